# Optimizing a Trainium2 kernel written in Bass

```python
import math
import jax, jax.numpy as jnp
from jax import lax
import numpy as np

D_MODEL = 1024
BATCH = 8
SEQ = 4096
DEPTH = 1
DEC_BATCH = 32
DEC_SEQ = 2048
PAST_LEN = 128

N_HEADS = 8
HEAD_DIM = 64
V_DIM = 2 * HEAD_DIM
QK_WIDTH = 2 * N_HEADS * HEAD_DIM
ATTN_WIDTH = N_HEADS * V_DIM
ROPE_THETA = 10000.0
Q_BLOCK = 128
RMS_EPS = 1e-5
LRU_WIDTH = 1024
LRU_BLOCKS = 16
LRU_BLOCK = LRU_WIDTH // LRU_BLOCKS
LRU_C = 8.0
CONV_WIDTH = 4
CONV_LEFT = 2
GATE_WIDTH = D_MODEL
IN_WIDTH = 2 * QK_WIDTH + ATTN_WIDTH + 2 * LRU_WIDTH + 2 * GATE_WIDTH
IN_SPLITS = [QK_WIDTH, 2 * QK_WIDTH, 2 * QK_WIDTH + ATTN_WIDTH,
             2 * QK_WIDTH + ATTN_WIDTH + LRU_WIDTH,
             2 * QK_WIDTH + ATTN_WIDTH + 2 * LRU_WIDTH,
             2 * QK_WIDTH + ATTN_WIDTH + 2 * LRU_WIDTH + GATE_WIDTH]
D_FF = ((8 * D_MODEL + 3 * 256 - 1) // (3 * 256)) * 256
ALPHA = (2.0 * DEPTH) ** 0.25
BETA = (8.0 * DEPTH) ** -0.25
LN_EPS = 1e-5

kernel_name = 'hybrid_diffattn_rglru_deepnorm_encoder'


def layer_norm(x, g=None, b=None):
    xf = x.astype(jnp.float32)
    mu = jnp.mean(xf, axis=-1, keepdims=True)
    var = jnp.mean(jnp.square(xf - mu), axis=-1, keepdims=True)
    y = (xf - mu) * lax.rsqrt(var + LN_EPS)
    if g is not None:
        y = y * g.astype(jnp.float32) + b.astype(jnp.float32)
    return y.astype(x.dtype)


def rope_tables(seq):
    inv = 1.0 / (ROPE_THETA ** (jnp.arange(0, HEAD_DIM, 2, dtype=jnp.float32) / HEAD_DIM))
    ang = jnp.arange(seq, dtype=jnp.float32)[:, None] * inv[None, :]
    return jnp.cos(ang), jnp.sin(ang)


def apply_rope(t, cos, sin):
    tf = t.astype(jnp.float32)
    t1, t2 = jnp.split(tf, 2, axis=-1)
    c = cos[:, None, :]
    s = sin[:, None, :]
    return jnp.concatenate([t1 * c - t2 * s, t2 * c + t1 * s], axis=-1).astype(t.dtype)


def diff_attention(q, k, v, lam, lambda_init, subln_g):
    B, S = q.shape[0], q.shape[1]
    q = q.reshape(B, S, N_HEADS, 2, HEAD_DIM) * (HEAD_DIM ** -0.5)
    k = k.reshape(B, S, N_HEADS, 2, HEAD_DIM)
    nblk = S // Q_BLOCK
    qb = q.reshape(B, nblk, Q_BLOCK, N_HEADS, 2, HEAD_DIM).transpose(1, 0, 2, 3, 4, 5)

    def block(qi):
        s = jnp.einsum('bqhcd,bkhcd->bhcqk', qi, k).astype(jnp.float32)
        p = jax.nn.softmax(s, axis=-1)
        w = p[:, :, 0] - lam * p[:, :, 1]
        return jnp.einsum('bhqk,bkhe->bqhe', w.astype(v.dtype), v)

    o = lax.map(block, qb)
    o = o.transpose(1, 0, 2, 3, 4).reshape(B, S, N_HEADS, V_DIM)
    of = o.astype(jnp.float32)
    of = of * lax.rsqrt(jnp.mean(jnp.square(of), axis=-1, keepdims=True) + RMS_EPS)
    of = of * subln_g.astype(jnp.float32) * (1.0 - lambda_init)
    return of.reshape(B, S, ATTN_WIDTH).astype(v.dtype)


def centred_dwconv(x, w, b):
    S = x.shape[1]
    xp = jnp.pad(x, ((0, 0), (CONV_LEFT, CONV_WIDTH - 1 - CONV_LEFT), (0, 0)))
    y = b
    for tap in range(CONV_WIDTH):
        y = y + w[tap] * xp[:, tap:tap + S]
    return y


def block_diag(x, w, b):
    B, S = x.shape[0], x.shape[1]
    xb = x.reshape(B, S, LRU_BLOCKS, LRU_BLOCK)
    return jnp.einsum('bsnd,nde->bsne', xb, w).reshape(B, S, LRU_WIDTH) + b


def rg_lru(x, w_gates, b_gates, lam, reverse):
    r = jax.nn.sigmoid(block_diag(x, w_gates[0], b_gates[0]).astype(jnp.float32))
    i = jax.nn.sigmoid(block_diag(x, w_gates[1], b_gates[1]).astype(jnp.float32))
    log_a = -LRU_C * r * jax.nn.softplus(-lam.astype(jnp.float32))
    a = jnp.exp(log_a)
    drive = jnp.sqrt(-jnp.expm1(2.0 * log_a)) * (i * x.astype(jnp.float32))

    def combine(left, right):
        a1, b1 = left
        a2, b2 = right
        return a1 * a2, a2 * b1 + b2

    _, h = lax.associative_scan(combine, (a, drive), axis=1, reverse=reverse)
    return h


def encoder_layer(x, c, cos, sin, lambda_init, w_ada, b_ada, w_in, lambda_q1, lambda_k1, lambda_q2,
                  lambda_k2, subln_g, conv_w, conv_b, w_lru_gates, b_lru_gates, lru_lambda,
                  w_attn_branch, w_lru_branch, w_out, ln1_g, ln1_b, w_ffn_in, w_ffn_out, ln2_g, ln2_b):
    B, S = x.shape[0], x.shape[1]
    mod = jax.nn.silu(c) @ w_ada + b_ada
    sh1, sc1, g1, sh2, sc2, g2 = [m[:, None, :] for m in jnp.split(mod, 6, axis=-1)]

    h = layer_norm(x) * (1.0 + sc1) + sh1
    proj = h @ w_in
    q, k, v, xr, yr, ga, gl = jnp.split(proj, IN_SPLITS, axis=-1)
    q = apply_rope(q.reshape(B, S, 2 * N_HEADS, HEAD_DIM), cos, sin)
    k = apply_rope(k.reshape(B, S, 2 * N_HEADS, HEAD_DIM), cos, sin)
    v = v.reshape(B, S, N_HEADS, V_DIM)
    lam = (jnp.exp(jnp.sum(lambda_q1.astype(jnp.float32) * lambda_k1.astype(jnp.float32)))
           - jnp.exp(jnp.sum(lambda_q2.astype(jnp.float32) * lambda_k2.astype(jnp.float32)))
           + lambda_init)
    attn = diff_attention(q, k, v, lam, lambda_init, subln_g)

    xc = centred_dwconv(xr, conv_w, conv_b)
    rec = (rg_lru(xc, w_lru_gates[0], b_lru_gates[0], lru_lambda[0], False)
           + rg_lru(xc, w_lru_gates[1], b_lru_gates[1], lru_lambda[1], True))
    rec = (rec.astype(x.dtype) * jax.nn.gelu(yr))

    merged = jax.nn.sigmoid(ga) * (attn @ w_attn_branch) + jax.nn.sigmoid(gl) * (rec @ w_lru_branch)
    x = layer_norm(ALPHA * x + g1 * (merged @ w_out), ln1_g, ln1_b)

    h = layer_norm(x) * (1.0 + sc2) + sh2
    gate, up = jnp.split(h @ w_ffn_in, 2, axis=-1)
    f = (jax.nn.silu(gate) * up) @ w_ffn_out
    x = layer_norm(ALPHA * x + g2 * f, ln2_g, ln2_b)
    return x


def run_trunk(x, c, w_ada, b_ada, w_in, lambda_q1, lambda_k1, lambda_q2, lambda_k2, subln_g,
              conv_w, conv_b, w_lru_gates, b_lru_gates, lru_lambda, w_attn_branch, w_lru_branch,
              w_out, ln1_g, ln1_b, w_ffn_in, w_ffn_out, ln2_g, ln2_b):
    cos, sin = rope_tables(x.shape[1])
    for l in range(DEPTH):
        lambda_init = 0.8 - 0.6 * math.exp(-0.3 * l)
        x = encoder_layer(x, c, cos, sin, lambda_init, w_ada[l], b_ada[l], w_in[l], lambda_q1[l],
                          lambda_k1[l], lambda_q2[l], lambda_k2[l], subln_g[l], conv_w[l], conv_b[l],
                          w_lru_gates[l], b_lru_gates[l], lru_lambda[l], w_attn_branch[l],
                          w_lru_branch[l], w_out[l], ln1_g[l], ln1_b[l], w_ffn_in[l], w_ffn_out[l],
                          ln2_g[l], ln2_b[l])
    return x


def setup_inputs(seed: int = 0) -> dict:
    key = jax.random.key(seed)
    ks = jax.random.split(key, 32)
    f32 = jnp.float32
    nrm = lambda k, shape, s: jax.random.normal(k, shape, f32) * s
    u = jax.random.uniform(ks[20], (DEPTH, 2, LRU_WIDTH), f32, minval=0.9, maxval=0.999)
    a = u ** (1.0 / LRU_C)
    lru_lambda = jnp.log(a) - jnp.log1p(-a)
    return {
        'x_prompt': nrm(ks[0], (BATCH, SEQ, D_MODEL), 1.0),
        'x_sample': nrm(ks[1], (DEC_BATCH, DEC_SEQ, D_MODEL), 1.0),
        'c_prompt': nrm(ks[2], (BATCH, D_MODEL), 1.0),
        'c_sample': nrm(ks[3], (DEC_BATCH, D_MODEL), 1.0),
        'w_ada': nrm(ks[4], (DEPTH, D_MODEL, 6 * D_MODEL), D_MODEL ** -0.5 * 0.5),
        'b_ada': nrm(ks[5], (DEPTH, 6 * D_MODEL), 0.02),
        'w_in': nrm(ks[6], (DEPTH, D_MODEL, IN_WIDTH), D_MODEL ** -0.5),
        'lambda_q1': nrm(ks[7], (DEPTH, HEAD_DIM), 0.1),
        'lambda_k1': nrm(ks[8], (DEPTH, HEAD_DIM), 0.1),
        'lambda_q2': nrm(ks[9], (DEPTH, HEAD_DIM), 0.1),
        'lambda_k2': nrm(ks[10], (DEPTH, HEAD_DIM), 0.1),
        'subln_g': 1.0 + nrm(ks[11], (DEPTH, V_DIM), 0.02),
        'conv_w': nrm(ks[12], (DEPTH, CONV_WIDTH, LRU_WIDTH), CONV_WIDTH ** -0.5),
        'conv_b': nrm(ks[13], (DEPTH, LRU_WIDTH), 0.02),
        'w_lru_gates': nrm(ks[14], (DEPTH, 2, 2, LRU_BLOCKS, LRU_BLOCK, LRU_BLOCK), LRU_BLOCK ** -0.5),
        'b_lru_gates': nrm(ks[15], (DEPTH, 2, 2, LRU_WIDTH), 0.02),
        'lru_lambda': lru_lambda,
        'w_attn_branch': nrm(ks[16], (DEPTH, ATTN_WIDTH, D_MODEL), ATTN_WIDTH ** -0.5),
        'w_lru_branch': nrm(ks[17], (DEPTH, LRU_WIDTH, D_MODEL), LRU_WIDTH ** -0.5),
        'w_out': nrm(ks[18], (DEPTH, D_MODEL, D_MODEL), D_MODEL ** -0.5 * BETA),
        'ln1_g': 1.0 + nrm(ks[19], (DEPTH, D_MODEL), 0.02),
        'ln1_b': nrm(ks[21], (DEPTH, D_MODEL), 0.02),
        'w_ffn_in': nrm(ks[22], (DEPTH, D_MODEL, 2 * D_FF), D_MODEL ** -0.5),
        'w_ffn_out': nrm(ks[23], (DEPTH, D_FF, D_MODEL), D_FF ** -0.5 * BETA),
        'ln2_g': 1.0 + nrm(ks[24], (DEPTH, D_MODEL), 0.02),
        'ln2_b': nrm(ks[25], (DEPTH, D_MODEL), 0.02),
    }


def reference(x_prompt, x_sample, c_prompt, c_sample, w_ada, b_ada, w_in, lambda_q1, lambda_k1,
              lambda_q2, lambda_k2, subln_g, conv_w, conv_b, w_lru_gates, b_lru_gates, lru_lambda,
              w_attn_branch, w_lru_branch, w_out, ln1_g, ln1_b, w_ffn_in, w_ffn_out, ln2_g, ln2_b):
    y_prompt = run_trunk(x_prompt, c_prompt, w_ada, b_ada, w_in, lambda_q1, lambda_k1, lambda_q2,
                         lambda_k2, subln_g, conv_w, conv_b, w_lru_gates, b_lru_gates, lru_lambda,
                         w_attn_branch, w_lru_branch, w_out, ln1_g, ln1_b, w_ffn_in, w_ffn_out,
                         ln2_g, ln2_b)
    y_sample = run_trunk(x_sample, c_sample, w_ada, b_ada, w_in, lambda_q1, lambda_k1, lambda_q2,
                         lambda_k2, subln_g, conv_w, conv_b, w_lru_gates, b_lru_gates, lru_lambda,
                         w_attn_branch, w_lru_branch, w_out, ln1_g, ln1_b, w_ffn_in, w_ffn_out,
                         ln2_g, ln2_b)
    return (y_prompt, y_sample)
```

```python
import numpy as np
from contextlib import ExitStack
import concourse.bass as bass
import concourse.mybir as mybir
from concourse.bass_utils import run_bass_kernel_spmd

F32 = mybir.dt.float32
BF16 = mybir.dt.bfloat16
AF = mybir.ActivationFunctionType
ALU = mybir.AluOpType

ENGS = ("pe", "act", "dve", "pool", "sp")
D = 1024
NCH = 8
DFF = 2816
NFF = 22
ALPHA = 2.0 ** 0.25
LN_EPS = 1e-5
RMS_EPS = 1e-5
LAMBDA_INIT = 0.2
QB = 2048
ARENA_BYTES = 174 * 1024


class Sem:
    def __init__(self, h, name):
        self.h = h
        self.name = name
        self.count = 0


class Buf:
    __slots__ = ("name", "w", "r", "const", "excl")

    def __init__(self, name, const=False, excl=False):
        self.name = name
        self.w = None
        self.r = {}
        self.const = const
        self.excl = excl


class _Rec:
    def __init__(self):
        self.call = None

    def __getattr__(self, name):
        def f(*a, **k):
            self.call = (name, a, k)
            return self
        return f


def _record(fn):
    r = _Rec()
    fn(r)
    return r.call


class Sched:
    def __init__(self, nc, stack):
        self.nc = nc
        self.stack = stack
        self.prog = {e: [] for e in ENGS}
        self.sems = []
        self.esem = {e: self.new_sem("e_" + e) for e in ENGS if e != "sp"}
        self.waited = {e: {} for e in ENGS}

    def new_sem(self, name):
        s = Sem(self.stack.enter_context(self.nc.semaphore(name)), name)
        self.sems.append(s)
        return s

    def _deps(self, eng, reads, writes):
        deps = {}
        for b in reads:
            if b.w is not None:
                s, v = b.w
                if deps.get(s, 0) < v:
                    deps[s] = v
        for b in writes:
            if b.w is not None:
                s, v = b.w
                if deps.get(s, 0) < v:
                    deps[s] = v
            for s, v in b.r.items():
                if deps.get(s, 0) < v:
                    deps[s] = v
        out = []
        wd = self.waited[eng]
        own = self.esem.get(eng)
        for s, v in deps.items():
            if s is own and eng == "pe":
                continue
            if wd.get(s, 0) >= v:
                continue
            assert s.count >= v, f"wait on future tick: eng={eng} sem={s.name} v={v} count={s.count}"
            wd[s] = v
            out.append((s, v))
        return out

    def op(self, eng, fn, reads=(), writes=(), mark=True):
        ex = [b for b in reads if b.excl]
        if ex:
            reads = [b for b in reads if not b.excl]
            writes = list(writes) + [b for b in ex if b not in writes]
        waits = self._deps(eng, reads, writes)
        sem = self.esem[eng]
        if mark:
            sem.count += 1
            tick = sem.count
        else:
            tick = sem.count + 1
        self.prog[eng].append((waits, _record(fn), (sem, 1) if mark else None))
        for b in reads:
            if not b.const and b.r.get(sem, 0) < tick:
                b.r[sem] = tick
        for b in writes:
            b.w = (sem, tick)
            b.r = {}

    def dma(self, queue, fn, dsem, reads=(), writes=()):
        waits = self._deps(queue, reads, writes)
        dsem.count += 16
        v = dsem.count
        self.prog[queue].append((waits, _record(fn), (dsem, 16)))
        for b in reads:
            if not b.const and b.r.get(dsem, 0) < v:
                b.r[dsem] = v
        for b in writes:
            b.w = (dsem, v)
            b.r = {}

    def fence(self, engs, bufs):
        for e in engs:
            waits = self._deps(e, [], bufs)
            self.prog[e].append((waits, None, None))

    def barrier(self, exclude=()):
        for e in ENGS:
            wd = self.waited[e]
            waits = []
            for s in self.sems:
                if s is self.esem.get(e) or s in exclude:
                    continue
                if s.count > wd.get(s, 0):
                    wd[s] = s.count
                    waits.append((s, s.count))
            self.prog[e].append((waits, None, None))

    def emit(self):
        nc = self.nc
        with nc.Block() as block:
            deco = {"pe": block.tensor, "act": block.scalar, "dve": block.vector, "pool": block.gpsimd, "sp": block.sync}
            for e in ENGS:
                prog = self.prog[e]

                def body(eng, prog=prog):
                    for waits, fn, inc in prog:
                        for s, v in waits:
                            eng.wait_ge(s.h, v)
                        if fn is not None:
                            ins = getattr(eng, fn[0])(*fn[1], **fn[2])
                            if inc is not None:
                                ins.then_inc(inc[0].h, inc[1])

                deco[e](body)


class _Stop(Exception):
    pass


def build(seq_lens, dbg=False, stop=None):
    nc = bass.Bass("TRN2", target_bir_lowering=False)
    NSEQ = len(seq_lens)
    NTOK = sum(seq_lens)
    SMAX = max(seq_lens)

    def din(name, shape, dt=F32):
        return nc.dram_tensor(name, list(shape), dt, kind="ExternalInput").ap()

    def dint(name, shape, dt):
        return nc.dram_tensor(name, list(shape), dt, kind="Internal").ap()

    x_d = din("x", [NTOK, D])
    c_d = din("c", [NSEQ, D])
    w_ada_d = din("w_ada", [D, 6 * D])
    b_ada_d = din("b_ada", [1, 6 * D])
    w_in_d = din("w_in", [56 * 128, 1024])
    lq1_d = din("lq1", [1, 64])
    lk1_d = din("lk1", [1, 64])
    lq2_d = din("lq2", [1, 64])
    lk2_d = din("lk2", [1, 64])
    subln_d = din("subln_g", [1, 128])
    conv_w_d = din("conv_w", [4, D])
    conv_b_d = din("conv_b", [1, D])
    wlg_d = din("w_lru_gates", [4, 16, 64, 64])
    blg_d = din("b_lru_gates", [4, D])
    llam_d = din("lru_lambda", [2, D])
    w_ab_d = din("w_ab", [8 * 128, 1024])
    w_lb_d = din("w_lb", [8 * 128, 1024])
    w_out_d = din("w_out", [8 * 128, 1024])
    ln1g_d = din("ln1_g", [1, D])
    ln1b_d = din("ln1_b", [1, D])
    w_fi_d = din("w_ffn_in", [44 * 128, 1024])
    w_fo_d = din("w_ffn_out", [8 * 128, DFF])
    ln2g_d = din("ln2_g", [1, D])
    ln2b_d = din("ln2_b", [1, D])
    ident_d = din("ident", [128, 128])
    prot_d = din("prot", [128, 128])
    ropec_d = din("rope_c", [128, SMAX])
    ropes_d = din("rope_s", [128, SMAX])
    y_d = nc.dram_tensor("y", [NTOK, D], F32, kind="ExternalOutput").ap()

    w_in_bf = dint("w_in_bf", [56 * 128, 1024], BF16)
    w_ab_bf = dint("w_ab_bf", [8 * 128, 1024], BF16)
    w_lb_bf = dint("w_lb_bf", [8 * 128, 1024], BF16)
    w_out_bf = dint("w_out_bf", [8 * 128, 1024], BF16)
    w_fi_bf = dint("w_fi_bf", [44 * 128, 1024], BF16)
    w_fo_bf = dint("w_fo_bf", [8 * 128, DFF], BF16)
    mod_d = dint("mod_d", [NSEQ, 6 * D], F32)
    lruw_d = dint("lruw_d", [128, NCH, 8, 128], BF16)
    rec_d = dint("rec_d", [NCH, 128, NTOK], BF16)

    dbg_out = {}
    if dbg:
        dbg_out["hT"] = nc.dram_tensor("dbg_hT", [128, NCH, NTOK], BF16, kind="ExternalOutput").ap()
        dbg_out["attnT"] = nc.dram_tensor("dbg_attnT", [128, NCH, NTOK], BF16, kind="ExternalOutput").ap()
        dbg_out["modT"] = nc.dram_tensor("dbg_modT", [128, 48, NSEQ], F32, kind="ExternalOutput").ap()
        dbg_out["rec"] = nc.dram_tensor("dbg_rec", [NCH, 128, NTOK], BF16, kind="ExternalOutput").ap()

    with ExitStack() as st:
        S = Sched(nc, st)

        def sb(name, shape, dt=F32):
            return st.enter_context(nc.sbuf_tensor(name, list(shape), dt))

        ident_bf = sb("ident_bf", [128, 128], BF16)
        ident_f = sb("ident_f", [128, 128], F32)
        prot_bf = sb("prot_bf", [128, 128], BF16)
        fmc = sb("fmc", [128, 11, NCH], F32)
        cdec = sb("cdec", [128, 4, NCH], F32)
        modT = sb("modT", [128, 48, NSEQ], F32)
        lamt = sb("lamt", [128, 8], F32)
        gsub = sb("gsub", [128, 128], F32)
        small = sb("small", [128, 64], F32)
        arena = sb("arena", [128, ARENA_BYTES // 2], BF16)
        ps = st.enter_context(nc.psum_tensor("ps", [128, 8 * 512], F32))

        b_const = Buf("const")
        b_hT = Buf("hT")

        def bank(b):
            return ps[:, b * 512:(b + 1) * 512]

        def bank_bf(b):
            return ps[:, b * 512:(b + 1) * 512].bitcast(BF16)

        pbuf = [Buf(f"psb{i}", excl=True) for i in range(8)]
        pstate = {"next": 0}

        def nextbank(lo=0, hi=8):
            n = pstate["next"]
            if n < lo or n >= hi:
                n = lo
            pstate["next"] = n + 1
            return n

        class Alloc:
            def __init__(self, off=0):
                self.off = off

            def get(self, shape, dt, name="t"):
                n = int(np.prod(shape[1:]))
                esz = 4 if dt == F32 else 2
                nbytes = (n * esz + 3) // 4 * 4
                assert self.off + nbytes <= ARENA_BYTES, f"arena overflow {name} {self.off + nbytes}"
                v = arena[0:shape[0], self.off // 2:(self.off + n * esz) // 2]
                if dt == F32:
                    v = v.bitcast(F32)
                if len(shape) == 3:
                    v = v.rearrange("p (a b) -> p a b", a=shape[1])
                elif len(shape) == 4:
                    v = v.rearrange("p (a b c) -> p a b c", a=shape[1], b=shape[2])
                self.off += nbytes
                return v

        sem_pool = {}

        def dsem(name):
            if name not in sem_pool:
                sem_pool[name] = S.new_sem("d_" + name)
            return sem_pool[name]

        ds_setup = dsem("setup")
        ds_cast = dsem("cast")
        b_wbf = Buf("wbf")

        import os
        SKIP = set(os.environ.get("KSKIP", "").split(","))

        cast_i = [0]
        ds_castk = [dsem("castk0"), dsem("castk1")]
        b_castk = [Buf("castk0"), Buf("castk1")]

        def cast_rows(dst, src, rows, blk):
            if "cast" in SKIP:
                return
            for r0 in range(0, rows, blk):
                r1 = min(rows, r0 + blk)
                k = cast_i[0] % 2
                cast_i[0] += 1
                S.dma("pool", lambda e, r0=r0, r1=r1: e.dma_start(out=dst[r0:r1, :], in_=src[r0:r1, :], max_dma_last_dim=4096),
                      ds_castk[k], writes=[b_castk[k]])

        al = Alloc()
        lv = al.get([128, 4, 64], F32)
        junk = al.get([128, 2, 64], F32)
        junk2 = al.get([128, 64], F32)
        wb = al.get([128, NCH, 8, 128], BF16)
        cT = al.get([128, NCH, NSEQ], F32)
        ones1 = al.get([1, NSEQ], F32)
        bada = al.get([1, 6 * D], F32)
        mod_sb = al.get([NSEQ, 6 * D], F32)
        wpan = [al.get([128, NCH, 512], F32) for _ in range(2)]
        b_wb = Buf("wb")
        b_lruw = Buf("lruw")
        b_modd = Buf("mod_d")
        b_mod = Buf("mod_sb")
        b_wpan = [Buf("wpan0"), Buf("wpan1")]
        ds_wp = [dsem("wp0"), dsem("wp1")]
        ds_wbd = dsem("wbd")

        S.op("pool", lambda e: e.memset(wb, 0.0), writes=[b_wb])
        S.dma("pool", lambda e: e.dma_start(out=ident_bf[:], in_=ident_d), ds_setup, writes=[b_const])
        S.dma("pool", lambda e: e.dma_start(out=prot_bf[:], in_=prot_d), ds_setup, writes=[b_const])
        for dg in (range(4) if "wbdma" not in SKIP else []):
            for j in range(2):
                src = wlg_d[dg].rearrange("(c j) d e -> j d c e", j=2)[j]
                S.dma("pool", lambda e, dg=dg, j=j, src=src: e.dma_start(out=wb[64 * j:64 * j + 64, :, dg, 64 * j:64 * j + 64], in_=src),
                      ds_wbd, reads=[], writes=[b_wb])
        cast_rows(w_in_bf, w_in_d, 56 * 128, 1024)
        S.dma("sp", lambda e: e.dma_start(out=ident_f[:], in_=ident_d), ds_setup, writes=[b_const])
        for s_ in range(NSEQ):
            S.dma("sp", lambda e, s_=s_: e.dma_start(out=cT[:, :, s_], in_=c_d[s_:s_ + 1, :].rearrange("o (c p) -> p (o c)", p=128),
                                                    allow_slow_non_contiguous=True), ds_setup, writes=[b_const])
        S.dma("sp", lambda e: e.dma_start(out=bada, in_=b_ada_d), ds_setup, writes=[b_const])
        vecs = [conv_w_d[0:1, :], conv_w_d[1:2, :], conv_w_d[2:3, :], conv_w_d[3:4, :], conv_b_d,
                blg_d[0:1, :], blg_d[1:2, :], blg_d[2:3, :], blg_d[3:4, :], llam_d[0:1, :], llam_d[1:2, :]]
        for i, v in (enumerate(vecs) if "slow" not in SKIP else []):
            S.dma("sp", lambda e, i=i, v=v: e.dma_start(out=fmc[:, i, :], in_=v.rearrange("o (c p) -> p (o c)", p=128),
                                                       allow_slow_non_contiguous=True), ds_setup, writes=[b_const])
        for i, v in (enumerate([lq1_d, lk1_d, lq2_d, lk2_d]) if "bcast" not in SKIP else []):
            S.dma("sp", lambda e, i=i, v=v: e.dma_start(out=lv[:, i, :], in_=v.broadcast_to([128, 64])), ds_setup, writes=[b_const])
        if "bcast" not in SKIP:
            S.dma("sp", lambda e: e.dma_start(out=gsub[:], in_=subln_d.broadcast_to([128, 128])), ds_setup, writes=[b_const])
        cast_rows(w_ab_bf, w_ab_d, 1024, 1024)
        cast_rows(w_lb_bf, w_lb_d, 1024, 1024)
        cast_rows(w_out_bf, w_out_d, 1024, 1024)
        cast_rows(w_fi_bf, w_fi_d, 44 * 128, 1024)
        cast_rows(w_fo_bf, w_fo_d, 1024, 512)
        S.barrier(exclude=ds_castk)
        S.op("dve", lambda e: e.tensor_tensor(out=junk[:, 0, :], in0=lv[:, 0, :], in1=lv[:, 1, :], op=ALU.mult), writes=[b_const])
        S.op("dve", lambda e: e.tensor_tensor(out=junk[:, 1, :], in0=lv[:, 2, :], in1=lv[:, 3, :], op=ALU.mult), reads=[b_const], writes=[b_const])
        S.op("dve", lambda e: e.tensor_scalar(out=gsub[:], in0=gsub[:], scalar1=1.0 - LAMBDA_INIT, scalar2=None, op0=ALU.mult),
             reads=[b_const], writes=[b_const])
        S.op("dve", lambda e: e.memset(ones1, 1.0), reads=[b_const], writes=[b_const])
        for c in range(NCH):
            for k in range(4):
                S.op("dve", lambda e, c=c, k=k: e.tensor_scalar(out=wb[:, c, 4 + k, :], in0=ident_bf[:], scalar1=fmc[:, k, c:c + 1],
                                                                scalar2=None, op0=ALU.mult), reads=[b_wb], writes=[b_wb])
        S.op("act", lambda e: e.activation(out=cT, in_=cT, func=AF.Silu), writes=[b_mod])
        S.op("act", lambda e: e.activation(out=small[:, 0:16], in_=fmc[:, 9:11, :].rearrange("p a c -> p (a c)"), func=AF.Exp, scale=-1.0),
             reads=[b_mod], writes=[b_mod])
        S.barrier(exclude=ds_castk)
        S.op("act", lambda e: e.activation(out=junk2, in_=junk[:, 0, :], func=AF.Identity, accum_out=lamt[:, 0:1]), writes=[b_mod])
        S.op("act", lambda e: e.activation(out=junk2, in_=junk[:, 1, :], func=AF.Identity, accum_out=lamt[:, 1:2]), reads=[b_mod], writes=[b_mod])
        S.op("act", lambda e: e.activation(out=lamt[:, 2:4], in_=lamt[:, 0:2], func=AF.Exp), reads=[b_mod], writes=[b_mod])
        S.op("act", lambda e: e.activation(out=small[:, 16:32], in_=small[:, 0:16], func=AF.Ln, bias=1.0, scale=1.0), reads=[b_mod], writes=[b_mod])
        S.dma("sp", lambda e: e.dma_start(out=lruw_d, in_=wb), ds_wbd, reads=[b_wb], writes=[b_lruw])
        S.barrier(exclude=ds_castk)
        S.op("dve", lambda e: e.tensor_tensor(out=lamt[:, 4:5], in0=lamt[:, 3:4], in1=lamt[:, 2:3], op=ALU.subtract), writes=[b_const])
        S.op("dve", lambda e: e.tensor_scalar(out=lamt[:, 4:5], in0=lamt[:, 4:5], scalar1=-LAMBDA_INIT, scalar2=None, op0=ALU.add),
             reads=[b_const], writes=[b_const])
        for d_ in range(2):
            S.op("dve", lambda e, d_=d_: e.tensor_scalar(out=cdec[:, 2 * d_, :], in0=small[:, 16 + 8 * d_:24 + 8 * d_], scalar1=-8.0,
                                                         scalar2=None, op0=ALU.mult), reads=[b_const], writes=[b_const])
            S.op("dve", lambda e, d_=d_: e.tensor_scalar(out=cdec[:, 2 * d_ + 1, :], in0=small[:, 16 + 8 * d_:24 + 8 * d_], scalar1=-16.0,
                                                         scalar2=None, op0=ALU.mult), reads=[b_const], writes=[b_const])
        for pn in (range(12) if "mod" not in SKIP else []):
            sl = pn % 2
            S.dma("sp", lambda e, pn=pn, sl=sl: e.dma_start(out=wpan[sl], in_=w_ada_d[:, pn * 512:(pn + 1) * 512].rearrange("(c p) n -> p c n", p=128)),
                  ds_wp[sl], writes=[b_wpan[sl]])
            bk = nextbank()
            for kc in range(NCH):
                S.op("pe", lambda e, bk=bk, kc=kc, sl=sl: e.matmul(bank(bk)[0:NSEQ, :], lhsT=cT[:, kc, :], rhs=wpan[sl][:, kc, :],
                                                                   start=(kc == 0), stop=False),
                     reads=[b_wpan[sl]], writes=[pbuf[bk]], mark=False)
            S.op("pe", lambda e, bk=bk, pn=pn: e.matmul(bank(bk)[0:NSEQ, :], lhsT=ones1, rhs=bada[:, pn * 512:(pn + 1) * 512],
                                                        start=False, stop=True), reads=[], writes=[pbuf[bk]])
            S.op("act", lambda e, bk=bk, pn=pn: e.activation(out=mod_sb[:, pn * 512:(pn + 1) * 512], in_=bank(bk)[0:NSEQ, :], func=AF.Identity),
                 reads=[pbuf[bk]], writes=[b_mod])
        S.dma("sp", lambda e: e.dma_start(out=mod_d, in_=mod_sb), ds_setup, reads=[b_mod], writes=[b_modd])
        for s_ in range(NSEQ):
            S.dma("sp", lambda e, s_=s_: e.dma_start(out=modT[:, :, s_], in_=mod_d[s_:s_ + 1, :].rearrange("o (c p) -> p (o c)", p=128),
                                                    allow_slow_non_contiguous=True), ds_setup, reads=[b_modd], writes=[b_modd])
        for lo in (8, 32):
            S.op("dve", lambda e, lo=lo: e.tensor_scalar(out=modT[:, lo:lo + 8, :], in0=modT[:, lo:lo + 8, :], scalar1=1.0, scalar2=None, op0=ALU.add),
                 reads=[b_modd], writes=[b_modd])
        ds_dbg = dsem("dbg")
        b_dbg = Buf("dbg")
        if dbg:
            S.dma("sp", lambda e: e.dma_start(out=dbg_out["modT"], in_=modT[:]), ds_dbg, reads=[b_modd], writes=[b_dbg])
        S.barrier()
        b_const = Buf("const2", const=True)
        b_wbf = Buf("wbf2", const=True)
        b_lruw = Buf("lruw2", const=True)
        b_recd_dummy = None

        def ln_stats(src_ap, b_src, st6, mv, rs, b_st):
            S.op("dve", lambda e: e.bn_stats(out=st6[:, 0, :], in_=src_ap[:, 0:512]), reads=[b_src], writes=[b_st])
            S.op("dve", lambda e: e.bn_stats(out=st6[:, 1, :], in_=src_ap[:, 512:1024]), reads=[b_src], writes=[b_st])
            S.op("dve", lambda e: e.bn_aggr(out=mv, in_=st6.rearrange("p a b -> p (a b)")), reads=[b_st], writes=[b_st])
            S.op("act", lambda e: e.activation(out=rs[:, 0:1], in_=mv[:, 1:2], func=AF.Sqrt, bias=LN_EPS, scale=1.0), reads=[b_st], writes=[b_st])
            S.op("dve", lambda e: e.reciprocal(out=rs[:, 0:1], in_=rs[:, 0:1]), reads=[b_st], writes=[b_st])
            S.op("dve", lambda e: e.tensor_scalar(out=rs[:, 1:2], in0=mv[:, 0:1], scalar1=rs[:, 0:1], scalar2=-1.0, op0=ALU.mult, op1=ALU.mult),
                 reads=[b_st], writes=[b_st])

        def mm_group(out_ap, b_out, pairs, reads):
            n = len(pairs)
            for i, (l, r) in enumerate(pairs):
                S.op("pe", lambda e, l=l, r=r, i=i: e.matmul(out_ap, lhsT=l, rhs=r, start=(i == 0), stop=(i == n - 1)),
                     reads=reads, writes=[b_out], mark=(i == n - 1))

        for si in ([] if stop == "setup" else range(NSEQ)):
            SL = seq_lens[si]
            tok0 = sum(seq_lens[:si])
            NT = SL // 512
            sc1p = lambda c: modT[:, 8 + c, si:si + 1]
            sh1 = lambda c: modT[:, 0 + c, si:si + 1]
            g1c = lambda c: modT[:, 16 + c, si:si + 1]
            sh2 = lambda c: modT[:, 24 + c, si:si + 1]
            sc2p = lambda c: modT[:, 32 + c, si:si + 1]
            g2c = lambda c: modT[:, 40 + c, si:si + 1]

            al = Alloc()
            hT = al.get([128, NCH, SL], BF16)
            HOFF = al.off
            XT = [al.get([128, D], F32) for _ in range(3)]
            b_XT = [Buf(f"XT{i}") for i in range(3)]
            ds_XT = [dsem(f"xt{i}") for i in range(3)]
            XN = [al.get([128, D], BF16) for _ in range(2)]
            b_XN = [Buf(f"XN{i}") for i in range(2)]
            STT_ = [(al.get([128, 2, 6], F32), al.get([128, 2], F32), al.get([128, 2], F32), Buf(f"st{i}")) for i in range(3)]
            for g in range(NT):
                base = (g % 2) * 4
                for j in range(4):
                    ti = g * 4 + j
                    sl = ti % 3
                    S.dma("sp", lambda e, sl=sl, ti=ti: e.dma_start(out=XT[sl], in_=x_d[tok0 + ti * 128: tok0 + (ti + 1) * 128, :]),
                          ds_XT[sl], writes=[b_XT[sl]])
                    st6, mv, rs, b_st = STT_[sl]
                    ln_stats(XT[sl], b_XT[sl], st6, mv, rs, b_st)
                    xs = ti % 2
                    S.op("act", lambda e, sl=sl, xs=xs, rs=rs: e.activation(out=XN[xs], in_=XT[sl], func=AF.Identity, scale=rs[:, 0:1], bias=rs[:, 1:2]),
                         reads=[b_XT[sl], b_st], writes=[b_XN[xs]])
                    for c in range(NCH):
                        bk = base + c // 2
                        S.op("pe", lambda e, bk=bk, c=c, j=j, xs=xs: e.transpose(
                            bank_bf(bk)[:, (c % 2) * 512 + j * 128:(c % 2) * 512 + (j + 1) * 128], XN[xs][:, c * 128:(c + 1) * 128], ident_bf[:]),
                            reads=[b_XN[xs], b_const], writes=[pbuf[bk]], mark=(c % 2 == 1))
                for c in range(NCH):
                    bk = base + c // 2
                    S.op("act", lambda e, bk=bk, c=c, g=g: e.activation(out=hT[:, c, g * 512:(g + 1) * 512], in_=bank_bf(bk)[:, (c % 2) * 512:(c % 2) * 512 + 512],
                                                                        func=AF.Identity, scale=sc1p(c), bias=sh1(c)),
                         reads=[pbuf[bk], b_const], writes=[b_hT])
            if dbg:
                S.dma("sp", lambda e: e.dma_start(out=dbg_out["hT"][:, :, tok0:tok0 + SL], in_=hT[:, :, 0:SL]), ds_dbg, reads=[b_hT], writes=[b_dbg])
            S.barrier()
            if stop == "p1":
                break

            al = Alloc(HOFF)
            WP = [al.get([128, 2, NCH, 128], BF16) for _ in range(2)]
            LW = [al.get([128, 8, 128], BF16) for _ in range(2)]
            b_WP = [Buf("WP0"), Buf("WP1")]
            ds_WP = [dsem("WP0"), dsem("WP1")]
            XR = al.get([128, SL + 4], BF16)
            G_ = al.get([128, SL], BF16)
            XC = al.get([128, SL], F32)
            XCb = al.get([128, SL], BF16)
            HF = al.get([128, SL], F32)
            HB = [al.get([128, 512], F32) for _ in range(2)]
            REC = al.get([128, SL], BF16)
            TMP = [[al.get([128, 512], F32) for _ in range(4)] for _ in range(2)]
            b_XR = [Buf(f"XR{t}") for t in range(NT)]
            b_G = [Buf(f"G{t}") for t in range(NT)]
            b_XC = [Buf(f"XC{t}") for t in range(NT)]
            b_XCb = [Buf(f"XCb{t}") for t in range(NT)]
            b_HF = [Buf(f"HF{t}") for t in range(NT)]
            b_HB = [Buf("HB0"), Buf("HB1")]
            b_REC = Buf("REC")
            b_TMP = [[Buf(f"TMP{i}{j}") for j in range(4)] for i in range(2)]
            ds_rec = dsem("rec")
            b_recd = Buf("rec_d")
            b_pad = Buf("XRpad")
            S.op("pool", lambda e: e.memset(XR[:, 0:2], 0.0), writes=[b_pad])
            S.op("pool", lambda e: e.memset(XR[:, SL + 2:SL + 4], 0.0), writes=[b_pad])

            def load_lru_panels(c):
                sl = c % 2
                S.dma("sp", lambda e: e.dma_start(out=WP[sl][:, 0].rearrange("p k n -> p (k n)"), in_=w_in_bf[(24 + c) * 128:(25 + c) * 128, :]),
                      ds_WP[sl], reads=[b_wbf], writes=[b_WP[sl]])
                S.dma("sp", lambda e: e.dma_start(out=WP[sl][:, 1].rearrange("p k n -> p (k n)"), in_=w_in_bf[(32 + c) * 128:(33 + c) * 128, :]),
                      ds_WP[sl], reads=[b_wbf], writes=[b_WP[sl]])
                S.dma("sp", lambda e: e.dma_start(out=LW[sl], in_=lruw_d[:, c]), ds_WP[sl], reads=[b_lruw], writes=[b_WP[sl]])

            load_lru_panels(0)
            for c in range(NCH):
                if c + 1 < NCH:
                    load_lru_panels(c + 1)
                sl = c % 2
                for t in range(NT):
                    bk = nextbank()
                    mm_group(bank(bk), pbuf[bk], [(WP[sl][:, 0, kc, :], hT[:, kc, t * 512:(t + 1) * 512]) for kc in range(NCH)], [b_WP[sl], b_hT])
                    S.op("act", lambda e, bk=bk, t=t: e.activation(out=XR[:, 2 + t * 512:2 + (t + 1) * 512], in_=bank(bk), func=AF.Identity),
                         reads=[pbuf[bk]], writes=[b_XR[t]])
                    bk = nextbank()
                    mm_group(bank(bk), pbuf[bk], [(WP[sl][:, 1, kc, :], hT[:, kc, t * 512:(t + 1) * 512]) for kc in range(NCH)], [b_WP[sl], b_hT])
                    S.op("act", lambda e, bk=bk, t=t: e.activation(out=G_[:, t * 512:(t + 1) * 512], in_=bank(bk), func=AF.Gelu_apprx_tanh),
                         reads=[pbuf[bk]], writes=[b_G[t]])
                for t in range(NT):
                    bk = nextbank()
                    rd = [b_WP[sl], b_pad, b_XR[t]] + ([b_XR[t - 1]] if t > 0 else []) + ([b_XR[t + 1]] if t + 1 < NT else [])
                    mm_group(bank(bk), pbuf[bk], [(LW[sl][:, 4 + k, :], XR[:, t * 512 + k:t * 512 + k + 512]) for k in range(4)], rd)
                    S.op("act", lambda e, bk=bk, t=t, c=c: e.activation(out=XC[:, t * 512:(t + 1) * 512], in_=bank(bk), func=AF.Identity,
                                                                        bias=fmc[:, 4, c:c + 1], scale=1.0),
                         reads=[pbuf[bk], b_const], writes=[b_XC[t]])
                    S.op("pool", lambda e, t=t: e.tensor_copy(out=XCb[:, t * 512:(t + 1) * 512], in_=XC[:, t * 512:(t + 1) * 512]),
                         reads=[b_XC[t]], writes=[b_XCb[t]])
                it = 0
                for d_ in range(2):
                    order = range(NT) if d_ == 0 else range(NT - 1, -1, -1)
                    for t in order:
                        tm = TMP[it % 2]
                        btm = b_TMP[it % 2]
                        it += 1
                        R_, I_, A_, Q_ = tm
                        tsl = slice(t * 512, (t + 1) * 512)
                        bkr = nextbank()
                        mm_group(bank(bkr), pbuf[bkr], [(LW[sl][:, 2 * d_, :], XCb[:, tsl])], [b_WP[sl], b_XCb[t]])
                        bki = nextbank()
                        mm_group(bank(bki), pbuf[bki], [(LW[sl][:, 2 * d_ + 1, :], XCb[:, tsl])], [b_WP[sl], b_XCb[t]])
                        S.op("act", lambda e, bkr=bkr, R_=R_, d_=d_, c=c: e.activation(out=R_, in_=bank(bkr), func=AF.Sigmoid, bias=fmc[:, 5 + 2 * d_, c:c + 1], scale=1.0),
                             reads=[pbuf[bkr], b_const], writes=[btm[0]])
                        S.op("act", lambda e, bki=bki, I_=I_, d_=d_, c=c: e.activation(out=I_, in_=bank(bki), func=AF.Sigmoid, bias=fmc[:, 6 + 2 * d_, c:c + 1], scale=1.0),
                             reads=[pbuf[bki], b_const], writes=[btm[1]])
                        S.op("act", lambda e, R_=R_, A_=A_, d_=d_, c=c: e.activation(out=A_, in_=R_, func=AF.Exp, scale=cdec[:, 2 * d_, c:c + 1]),
                             reads=[btm[0], b_const], writes=[btm[2]])
                        S.op("act", lambda e, R_=R_, Q_=Q_, d_=d_, c=c: e.activation(out=Q_, in_=R_, func=AF.Exp, scale=cdec[:, 2 * d_ + 1, c:c + 1]),
                             reads=[btm[0], b_const], writes=[btm[3]])
                        S.op("act", lambda e, Q_=Q_: e.activation(out=Q_, in_=Q_, func=AF.Sqrt, bias=1.0, scale=-1.0),
                             reads=[btm[3]], writes=[btm[3]])
                        S.op("pool", lambda e, I_=I_, tsl=tsl: e.tensor_tensor(out=I_, in0=I_, in1=XC[:, tsl], op=ALU.mult),
                             reads=[btm[1], b_XC[t]], writes=[btm[1]])
                        S.op("dve", lambda e, I_=I_, Q_=Q_: e.tensor_tensor(out=I_, in0=I_, in1=Q_, op=ALU.mult),
                             reads=[btm[1], btm[3]], writes=[btm[1]])
                        if d_ == 0:
                            init = 0.0 if t == 0 else HF[:, t * 512 - 1:t * 512]
                            rd = [btm[2], btm[1]] + ([b_HF[t - 1]] if t > 0 else [])
                            S.op("dve", lambda e, A_=A_, I_=I_, tsl=tsl, init=init: e.tensor_tensor_scan(out=HF[:, tsl], data0=A_, data1=I_, initial=init,
                                                                                                        op0=ALU.mult, op1=ALU.add),
                                 reads=rd, writes=[b_HF[t]])
                        else:
                            hb = HB[t % 2]
                            hbp = HB[(t + 1) % 2]
                            init = 0.0 if t == NT - 1 else hbp[:, 0:1]
                            rd = [btm[2], btm[1]] + ([b_HB[(t + 1) % 2]] if t < NT - 1 else [])
                            S.op("dve", lambda e, A_=A_, I_=I_, hb=hb, init=init: e.tensor_tensor_scan(out=hb[:, ::-1], data0=A_[:, ::-1], data1=I_[:, ::-1],
                                                                                                      initial=init, op0=ALU.mult, op1=ALU.add),
                                 reads=rd, writes=[b_HB[t % 2]])
                            S.op("pool", lambda e, R_=R_, hb=hb, tsl=tsl: e.tensor_tensor(out=R_, in0=hb, in1=HF[:, tsl], op=ALU.add),
                                 reads=[b_HB[t % 2], b_HF[t]], writes=[btm[0]])
                            S.op("pool", lambda e, R_=R_, tsl=tsl: e.tensor_tensor(out=REC[:, tsl], in0=R_, in1=G_[:, tsl], op=ALU.mult),
                                 reads=[btm[0], b_G[t]], writes=[b_REC])
                S.dma("sp", lambda e, c=c: e.dma_start(out=rec_d[c, :, tok0:tok0 + SL], in_=REC), ds_rec, reads=[b_REC], writes=[b_recd])
                if dbg:
                    S.dma("sp", lambda e, c=c: e.dma_start(out=dbg_out["rec"][c, :, tok0:tok0 + SL], in_=REC), ds_dbg, reads=[b_REC], writes=[b_dbg])
            S.barrier()
            if stop == "p2":
                break

            NQB = max(1, SL // QB)
            QBL = min(QB, SL)
            for qb in range(NQB):
                q0 = qb * QBL
                al = Alloc(HOFF)
                attnT = al.get([128, NCH, QBL], BF16)
                b_attnT = Buf("attnT")
                p4_off = al.off
                QT = al.get([128, QBL], BF16)
                KT = al.get([128, SL], BF16)
                NKT = SL // 128
                V_ = al.get([128, NKT, 130], BF16)
                WQ = [al.get([128, 3, NCH, 128], BF16) for _ in range(2)]
                b_WQ = [Buf("WQ0"), Buf("WQ1")]
                ds_WQ = [dsem("WQ0"), dsem("WQ1")]
                RC = [al.get([128, 2, 512], F32) for _ in range(2)]
                b_RC = [Buf("RC0"), Buf("RC1")]
                ds_RC = [dsem("RC0"), dsem("RC1")]
                QBF = [al.get([128, 512], BF16) for _ in range(2)]
                b_QBF = [Buf("QBF0"), Buf("QBF1")]
                T12 = [[al.get([128, 512], F32) for _ in range(2)] for _ in range(2)]
                b_T12 = [[Buf("T1"), Buf("T2")] for _ in range(2)]
                PT = [al.get([128, 2, 512], BF16) for _ in range(3)]
                b_PT = [Buf(f"PT{i}") for i in range(3)]
                O0 = [al.get([128, 128], F32) for _ in range(4)]
                OO = [al.get([128, 128], F32) for _ in range(4)]
                AT = [al.get([128, 128], BF16) for _ in range(4)]
                b_O0 = [Buf(f"O0{i}") for i in range(4)]
                b_OO = [Buf(f"OO{i}") for i in range(4)]
                b_AT = [Buf(f"AT{i}") for i in range(4)]
                SQJ = al.get([128, 128], F32)
                b_SQJ = Buf("SQJ")
                nrm = al.get([128, 32], F32)
                b_nrm = Buf("nrm")
                b_QT = Buf("QT")
                b_KT = Buf("KT")
                b_V = Buf("V")
                S.op("pool", lambda e: e.memset(V_[:, :, 128:130], 1.0), writes=[b_V])

                def load_head_w(h):
                    sl = h % 2
                    for i in range(3):
                        S.dma("sp", lambda e, i=i: e.dma_start(out=WQ[sl][:, i].rearrange("p k n -> p (k n)"), in_=w_in_bf[(i * 8 + h) * 128:(i * 8 + h + 1) * 128, :]),
                              ds_WQ[sl], reads=[b_wbf], writes=[b_WQ[sl]])

                rc_i = [0]

                def rope_proj(h, which, t_tok, dst_ap, b_dst):
                    if "rope" in SKIP:
                        return
                    sl = h % 2
                    r = rc_i[0] % 2
                    rc_i[0] += 1
                    S.dma("sp", lambda e: e.dma_start(out=RC[r][:, 0, :], in_=ropec_d[:, t_tok:t_tok + 512]), ds_RC[r], writes=[b_RC[r]])
                    S.dma("sp", lambda e: e.dma_start(out=RC[r][:, 1, :], in_=ropes_d[:, t_tok:t_tok + 512]), ds_RC[r], writes=[b_RC[r]])
                    bka = nextbank(0, 5)
                    mm_group(bank(bka), pbuf[bka], [(WQ[sl][:, which, kc, :], hT[:, kc, t_tok:t_tok + 512]) for kc in range(NCH)], [b_WQ[sl], b_hT])
                    S.op("act", lambda e: e.activation(out=QBF[r], in_=bank(bka), func=AF.Identity), reads=[pbuf[bka]], writes=[b_QBF[r]])
                    bkb = nextbank(0, 5)
                    mm_group(bank(bkb), pbuf[bkb], [(prot_bf[:], QBF[r])], [b_const, b_QBF[r]])
                    t1, t2 = T12[r]
                    S.op("dve", lambda e: e.tensor_tensor(out=t1, in0=bank(bka), in1=RC[r][:, 0, :], op=ALU.mult),
                         reads=[pbuf[bka], b_RC[r]], writes=[b_T12[r][0]])
                    S.op("dve", lambda e: e.tensor_tensor(out=t2, in0=bank(bkb), in1=RC[r][:, 1, :], op=ALU.mult),
                         reads=[pbuf[bkb], b_RC[r]], writes=[b_T12[r][1]])
                    S.op("pool", lambda e: e.tensor_tensor(out=dst_ap, in0=t1, in1=t2, op=ALU.add),
                         reads=[b_T12[r][0], b_T12[r][1]], writes=[b_dst])

                def acc_ap(a, lo=0, hi=129):
                    bk = 5 + a // 3
                    o = (a % 3) * 130
                    return ps[:, bk * 512 + o + lo: bk * 512 + o + hi]

                load_head_w(0)
                for h in range(NCH):
                    if h + 1 < NCH:
                        load_head_w(h + 1)
                    sl = h % 2
                    for t in range(NT):
                        rope_proj(h, 1, t * 512, KT[:, t * 512:(t + 1) * 512], b_KT)
                    for kt in (range(NKT) if "vproj" not in SKIP else []):
                        bk = nextbank(0, 5)
                        mm_group(bank(bk)[:, 0:128], pbuf[bk], [(hT[:, kc, kt * 128:(kt + 1) * 128], WQ[sl][:, 2, kc, :]) for kc in range(NCH)], [b_WQ[sl], b_hT])
                        S.op("act", lambda e, bk=bk, kt=kt: e.activation(out=V_[:, kt, 0:128], in_=bank(bk)[:, 0:128], func=AF.Identity),
                             reads=[pbuf[bk]], writes=[b_V])
                    for t in range(QBL // 512):
                        rope_proj(h, 0, q0 + t * 512, QT[:, t * 512:(t + 1) * 512], b_QT)
                    for qg in range(QBL // 512):
                        def emit_qk_exp(kt, qg=qg):
                            pr = kt % 2
                            stv = ps[:, pr * 1024:(pr + 1) * 1024].rearrange("p (a b) -> p a b", a=2)
                            bst = [pbuf[2 * pr], pbuf[2 * pr + 1]]
                            S.op("pe", lambda e: e.matmul(stv[:, 0, :], lhsT=KT[0:64, kt * 128:(kt + 1) * 128], rhs=QT[0:64, qg * 512:(qg + 1) * 512],
                                                          start=True, stop=True), reads=[b_KT, b_QT], writes=[bst[0]], mark=False)
                            S.op("pe", lambda e: e.matmul(stv[:, 1, :], lhsT=KT[64:128, kt * 128:(kt + 1) * 128], rhs=QT[64:128, qg * 512:(qg + 1) * 512],
                                                          start=True, stop=True), reads=[b_KT, b_QT], writes=[bst[1]], mark=True)
                            pi = kt % 3
                            S.op("act", lambda e: e.activation(out=PT[pi], in_=stv, func=AF.Exp, scale=0.125), reads=bst, writes=[b_PT[pi]])

                        def emit_pv(kt):
                            pi = kt % 3
                            for a in range(8):
                                cm, qs = a // 4, a % 4
                                S.op("pe", lambda e, a=a, cm=cm, qs=qs: e.matmul(acc_ap(a), lhsT=PT[pi][:, cm, qs * 128:(qs + 1) * 128], rhs=V_[:, kt, 0:129],
                                                                                 start=(kt == 0 and a % 3 == 0), stop=(kt == NKT - 1), skip_group_check=True),
                                     reads=[b_PT[pi], b_V], writes=[pbuf[5 + a // 3]], mark=(a == 7))

                        emit_qk_exp(0)
                        for kt in range(NKT):
                            if kt + 1 < NKT:
                                emit_qk_exp(kt + 1)
                            emit_pv(kt)
                        if "norm" in SKIP:
                            continue
                        for bkk in range(3):
                            n_ = 3 if bkk < 2 else 2
                            src = ps[:, (5 + bkk) * 512:(5 + bkk) * 512 + 390].rearrange("p (a b) -> p a b", a=3)[:, 0:n_, 128]
                            S.op("dve", lambda e, bkk=bkk, n_=n_, src=src: e.reciprocal(out=nrm[:, 3 * bkk:3 * bkk + n_], in_=src),
                                 reads=[pbuf[5 + bkk]], writes=[b_nrm])
                        S.op("dve", lambda e: e.tensor_scalar(out=nrm[:, 8:12], in0=nrm[:, 4:8], scalar1=lamt[:, 4:5], scalar2=None, op0=ALU.mult),
                             reads=[b_nrm, b_const], writes=[b_nrm])
                        for qs in range(4):
                            S.op("act", lambda e, qs=qs: e.activation(out=O0[qs], in_=acc_ap(qs, 0, 128), func=AF.Identity, scale=nrm[:, qs:qs + 1]),
                                 reads=[pbuf[5 + qs // 3], b_nrm], writes=[b_O0[qs]])
                            S.op("dve", lambda e, qs=qs: e.scalar_tensor_tensor(out=OO[qs], in0=acc_ap(4 + qs, 0, 128), scalar=nrm[:, 8 + qs:9 + qs], in1=O0[qs],
                                                                               op0=ALU.mult, op1=ALU.add),
                                 reads=[pbuf[5 + (4 + qs) // 3], b_nrm, b_O0[qs]], writes=[b_OO[qs]])
                            S.op("act", lambda e, qs=qs: e.activation(out=SQJ, in_=OO[qs], func=AF.Square, accum_out=nrm[:, 12 + qs:13 + qs]),
                                 reads=[b_OO[qs]], writes=[b_SQJ, b_nrm])
                        S.op("act", lambda e: e.activation(out=nrm[:, 16:20], in_=nrm[:, 12:16], func=AF.Sqrt, bias=RMS_EPS, scale=1.0 / 128.0),
                             reads=[b_nrm], writes=[b_nrm])
                        S.op("dve", lambda e: e.reciprocal(out=nrm[:, 16:20], in_=nrm[:, 16:20]), reads=[b_nrm], writes=[b_nrm])
                        bk4 = 4
                        for qs in range(4):
                            S.op("dve", lambda e, qs=qs: e.scalar_tensor_tensor(out=AT[qs], in0=OO[qs], scalar=nrm[:, 16 + qs:17 + qs], in1=gsub[:],
                                                                               op0=ALU.mult, op1=ALU.mult),
                                 reads=[b_OO[qs], b_nrm, b_const], writes=[b_AT[qs]])
                            S.op("pe", lambda e, qs=qs: e.transpose(bank_bf(bk4)[:, qs * 128:(qs + 1) * 128], AT[qs], ident_bf[:]),
                                 reads=[b_AT[qs], b_const], writes=[pbuf[bk4]], mark=(qs == 3))
                        S.op("act", lambda e, h=h, qg=qg: e.activation(out=attnT[:, h, qg * 512:(qg + 1) * 512], in_=bank_bf(bk4)[:, 0:512], func=AF.Identity),
                             reads=[pbuf[bk4]], writes=[b_attnT])
                if dbg:
                    S.dma("sp", lambda e: e.dma_start(out=dbg_out["attnT"][:, :, tok0 + q0:tok0 + q0 + QBL], in_=attnT), ds_dbg, reads=[b_attnT], writes=[b_dbg])
                S.barrier()
                if stop == "p3":
                    break

                T = 256 if SL > 2048 else 512
                NST = T // 128
                al = Alloc(p4_off)
                LNP = al.get([128, 4, D], F32)
                b_LNP = Buf("LNP", const=False)
                ds_lnp = dsem("lnp")
                for i, v in enumerate([ln1g_d, ln1b_d, ln2g_d, ln2b_d]):
                    S.dma("sp", lambda e, i=i, v=v: e.dma_start(out=LNP[:, i, :], in_=v.broadcast_to([128, D])), ds_lnp, writes=[b_LNP])
                X1 = al.get([128, NST, D], F32)
                b_X1 = [Buf(f"X1_{i}") for i in range(NST)]
                ds_X1 = [dsem(f"x1_{i}") for i in range(NST)]
                PAN = [al.get([128, 4096], BF16) for _ in range(3)]
                b_PAN = [Buf(f"PAN{i}") for i in range(3)]
                ds_PAN = [dsem(f"pan{i}") for i in range(3)]
                TT_ = [al.get([128, T], F32) for _ in range(2)]
                b_TT = [Buf("TT0"), Buf("TT1")]
                XN2 = [al.get([128, D], BF16) for _ in range(2)]
                b_XN2 = [Buf("XN20"), Buf("XN21")]
                STS = [(al.get([128, 2, 6], F32), al.get([128, 2], F32), al.get([128, 2], F32), Buf(f"sts{i}")) for i in range(2)]
                ov = al.off
                MG = al.get([128, NCH, T], BF16)
                RT = al.get([128, NCH, T], BF16)
                SG = [al.get([128, T], F32) for _ in range(2)]
                AB = [al.get([128, T], F32) for _ in range(2)]
                e_end = al.off
                al.off = ov
                H2 = al.get([128, NCH, T], BF16)
                UT = al.get([128, NFF, T], BF16)
                SU = [al.get([128, T], F32) for _ in range(2)]
                b_MG = Buf("MG")
                b_RT = Buf("RT")
                ds_RT = dsem("rt")
                b_SG = [Buf("SG0"), Buf("SG1")]
                b_AB = [Buf("AB0"), Buf("AB1")]
                b_H2 = Buf("H2")
                b_UT = Buf("UT")
                b_SU = [Buf("SU0"), Buf("SU1")]
                b_yd = Buf("y_d")
                ds_y = [dsem(f"y{i}") for i in range(4)]

                NG = QBL // T
                panels = []
                for g in range(NG):
                    for oc in range(NCH):
                        panels.append(("a", g, oc))
                    for oc in range(NCH):
                        panels.append(("b", g, oc))
                    for j in range(NFF):
                        panels.append(("d", g, j))
                    for oc in range(NCH):
                        panels.append(("e", g, oc))
                pstate4 = {"issued": 0}

                def issue_panel(i):
                    kind, g, k = panels[i]
                    sl = i % 3
                    pv = PAN[sl]
                    if kind == "a":
                        v4 = pv.rearrange("p (a k n) -> p a k n", a=4, k=NCH)
                        srcs = [w_in_bf[(40 + k) * 128:(41 + k) * 128, :], w_in_bf[(48 + k) * 128:(49 + k) * 128, :],
                                w_ab_bf[k * 128:(k + 1) * 128, :], w_lb_bf[k * 128:(k + 1) * 128, :]]
                        for a_, s_ in enumerate(srcs):
                            S.dma("sp", lambda e, a_=a_, s_=s_, v4=v4: e.dma_start(out=v4[:, a_].rearrange("p k n -> p (k n)"), in_=s_),
                                  ds_PAN[sl], reads=[b_wbf], writes=[b_PAN[sl]])
                    elif kind == "b":
                        v3 = pv[:, 0:NCH * 128].rearrange("p (k n) -> p k n", k=NCH)
                        S.dma("sp", lambda e, v3=v3, k=k: e.dma_start(out=v3.rearrange("p k n -> p (k n)"), in_=w_out_bf[k * 128:(k + 1) * 128, :]),
                              ds_PAN[sl], reads=[b_wbf], writes=[b_PAN[sl]])
                    elif kind == "d":
                        v4 = pv[:, 0:2 * NCH * 128].rearrange("p (a k n) -> p a k n", a=2, k=NCH)
                        for a_ in range(2):
                            S.dma("sp", lambda e, a_=a_, v4=v4, k=k: e.dma_start(out=v4[:, a_].rearrange("p k n -> p (k n)"), in_=w_fi_bf[(a_ * NFF + k) * 128:(a_ * NFF + k + 1) * 128, :]),
                                  ds_PAN[sl], reads=[b_wbf], writes=[b_PAN[sl]])
                    else:
                        v3 = pv[:, 0:NFF * 128].rearrange("p (k n) -> p k n", k=NFF)
                        S.dma("sp", lambda e, v3=v3, k=k: e.dma_start(out=v3.rearrange("p k n -> p (k n)"), in_=w_fo_bf[k * 128:(k + 1) * 128, :]),
                              ds_PAN[sl], reads=[b_wbf], writes=[b_PAN[sl]])

                def get_panel():
                    i = pstate4["cur"]
                    while pstate4["issued"] < min(len(panels), i + 3):
                        issue_panel(pstate4["issued"])
                        pstate4["issued"] += 1
                    pstate4["cur"] = i + 1
                    return PAN[i % 3], b_PAN[i % 3]

                pstate4["cur"] = 0

                def residual_block(g, pv3, nk, rhs_fn, rhs_bufs, b_pan, gcol, tti):
                    bk = nextbank()
                    mm_group(bank(bk)[:, 0:T], pbuf[bk], [(pv3[:, kc, :], rhs_fn(kc)) for kc in range(nk)], [b_pan] + rhs_bufs)
                    tt = TT_[tti % 2]
                    btt = b_TT[tti % 2]
                    S.op("act", lambda e: e.activation(out=tt, in_=bank(bk)[:, 0:T], func=AF.Identity, scale=gcol),
                         reads=[pbuf[bk], b_const], writes=[btt])
                    bk2 = nextbank()
                    for s_ in range(NST):
                        S.op("pe", lambda e, s_=s_: e.transpose(bank(bk2)[:, s_ * 128:(s_ + 1) * 128], tt[:, s_ * 128:(s_ + 1) * 128], ident_f[:]),
                             reads=[btt, b_const], writes=[pbuf[bk2]], mark=(s_ == NST - 1))
                    return bk2

                for g in range(NG):
                    gt0 = q0 + g * T
                    gl0 = g * T
                    S.fence(["sp", "act", "dve", "pool"], [b_H2, b_UT, b_SU[0], b_SU[1]])
                    for s_ in range(NST):
                        S.dma("pool", lambda e, s_=s_: e.dma_start(out=X1[:, s_, :], in_=x_d[tok0 + gt0 + s_ * 128:tok0 + gt0 + (s_ + 1) * 128, :]),
                              ds_X1[s_], writes=[b_X1[s_]])
                    S.dma("sp", lambda e: e.dma_start(out=RT, in_=rec_d[:, :, tok0 + gt0:tok0 + gt0 + T].rearrange("c p t -> p c t")),
                          ds_RT, reads=[b_recd], writes=[b_RT])
                    for oc in range(NCH):
                        pv, bp = get_panel()
                        v4 = pv.rearrange("p (a k n) -> p a k n", a=4, k=NCH)
                        bks = []
                        order = [(0, lambda kc: hT[:, kc, gt0:gt0 + T], b_hT), (2, lambda kc: attnT[:, kc, gl0:gl0 + T], b_attnT),
                                 (1, lambda kc: hT[:, kc, gt0:gt0 + T], b_hT), (3, lambda kc: RT[:, kc, :], b_RT)]
                        for a_, rf, rb in order:
                            bk = nextbank()
                            mm_group(bank(bk)[:, 0:T], pbuf[bk], [(v4[:, a_, kc, :], rf(kc)) for kc in range(NCH)], [bp, rb])
                            bks.append(bk)
                        for half in range(2):
                            sg = SG[half]
                            ab = AB[half]
                            bg, bb = bks[2 * half], bks[2 * half + 1]
                            S.op("act", lambda e, sg=sg, bg=bg: e.activation(out=sg, in_=bank(bg)[:, 0:T], func=AF.Sigmoid),
                                 reads=[pbuf[bg]], writes=[b_SG[half]])
                            S.op("dve", lambda e, sg=sg, ab=ab, bb=bb: e.tensor_tensor(out=ab, in0=bank(bb)[:, 0:T], in1=sg, op=ALU.mult),
                                 reads=[pbuf[bb], b_SG[half]], writes=[b_AB[half]])
                        S.op("pool", lambda e, oc=oc: e.tensor_tensor(out=MG[:, oc, :], in0=AB[0], in1=AB[1], op=ALU.add),
                             reads=[b_AB[0], b_AB[1]], writes=[b_MG])
                    for oc in range(NCH):
                        pv, bp = get_panel()
                        v3 = pv[:, 0:NCH * 128].rearrange("p (k n) -> p k n", k=NCH)
                        bk2 = residual_block(g, v3, NCH, lambda kc: MG[:, kc, :], [b_MG], bp, g1c(oc), oc)
                        S.op("dve", lambda e, oc=oc, bk2=bk2: e.scalar_tensor_tensor(
                            out=X1[:, :, oc * 128:(oc + 1) * 128], in0=X1[:, :, oc * 128:(oc + 1) * 128], scalar=ALPHA,
                            in1=bank(bk2)[:, 0:T].rearrange("p (s n) -> p s n", s=NST), op0=ALU.mult, op1=ALU.add),
                            reads=[pbuf[bk2]] + b_X1, writes=b_X1)
                    S.fence(["act", "dve"], [b_MG, b_RT, b_SG[0], b_SG[1], b_AB[0], b_AB[1]])
                    bks_c = [nextbank() for _ in range(4)]
                    for s_ in range(NST):
                        st6, mv, rs, b_st = STS[s_ % 2]
                        xs_ap = X1[:, s_, :]
                        ln_stats(xs_ap, b_X1[s_], st6, mv, rs, b_st)
                        S.op("act", lambda e, xs_ap=xs_ap, rs=rs: e.activation(out=xs_ap, in_=xs_ap, func=AF.Identity, scale=rs[:, 0:1], bias=rs[:, 1:2]),
                             reads=[b_X1[s_], b_st], writes=[b_X1[s_]])
                        S.op("dve", lambda e, xs_ap=xs_ap: e.tensor_tensor(out=xs_ap, in0=xs_ap, in1=LNP[:, 0, :], op=ALU.mult),
                             reads=[b_X1[s_], b_LNP], writes=[b_X1[s_]])
                        S.op("pool", lambda e, xs_ap=xs_ap: e.tensor_tensor(out=xs_ap, in0=xs_ap, in1=LNP[:, 1, :], op=ALU.add),
                             reads=[b_X1[s_], b_LNP], writes=[b_X1[s_]])
                        ln_stats(xs_ap, b_X1[s_], st6, mv, rs, b_st)
                        xn = XN2[s_ % 2]
                        S.op("act", lambda e, xs_ap=xs_ap, rs=rs, xn=xn: e.activation(out=xn, in_=xs_ap, func=AF.Identity, scale=rs[:, 0:1], bias=rs[:, 1:2]),
                             reads=[b_X1[s_], b_st], writes=[b_XN2[s_ % 2]])
                        for c in range(NCH):
                            bk = bks_c[c // 2]
                            S.op("pe", lambda e, bk=bk, c=c, s_=s_, xn=xn: e.transpose(
                                bank_bf(bk)[:, (c % 2) * 512 + s_ * 128:(c % 2) * 512 + (s_ + 1) * 128], xn[:, c * 128:(c + 1) * 128], ident_bf[:]),
                                reads=[b_XN2[s_ % 2], b_const], writes=[pbuf[bk]], mark=(c % 2 == 1))
                    for c in range(NCH):
                        bk = bks_c[c // 2]
                        S.op("act", lambda e, bk=bk, c=c: e.activation(out=H2[:, c, :], in_=bank_bf(bk)[:, (c % 2) * 512:(c % 2) * 512 + T],
                                                                       func=AF.Identity, scale=sc2p(c), bias=sh2(c)),
                             reads=[pbuf[bk], b_const], writes=[b_H2])
                    for j in range(NFF):
                        pv, bp = get_panel()
                        v4 = pv[:, 0:2 * NCH * 128].rearrange("p (a k n) -> p a k n", a=2, k=NCH)
                        bkg = nextbank()
                        mm_group(bank(bkg)[:, 0:T], pbuf[bkg], [(v4[:, 0, kc, :], H2[:, kc, :]) for kc in range(NCH)], [bp, b_H2])
                        bku = nextbank()
                        mm_group(bank(bku)[:, 0:T], pbuf[bku], [(v4[:, 1, kc, :], H2[:, kc, :]) for kc in range(NCH)], [bp, b_H2])
                        su = SU[j % 2]
                        S.op("act", lambda e, su=su, bkg=bkg: e.activation(out=su, in_=bank(bkg)[:, 0:T], func=AF.Silu),
                             reads=[pbuf[bkg]], writes=[b_SU[j % 2]])
                        S.op("dve", lambda e, su=su, bku=bku, j=j: e.tensor_tensor(out=UT[:, j, :], in0=bank(bku)[:, 0:T], in1=su, op=ALU.mult),
                             reads=[pbuf[bku], b_SU[j % 2]], writes=[b_UT])
                    for oc in range(NCH):
                        pv, bp = get_panel()
                        v3 = pv[:, 0:NFF * 128].rearrange("p (k n) -> p k n", k=NFF)
                        bk2 = residual_block(g, v3, NFF, lambda kc: UT[:, kc, :], [b_UT], bp, g2c(oc), oc)
                        S.op("dve", lambda e, oc=oc, bk2=bk2: e.scalar_tensor_tensor(
                            out=X1[:, :, oc * 128:(oc + 1) * 128], in0=X1[:, :, oc * 128:(oc + 1) * 128], scalar=ALPHA,
                            in1=bank(bk2)[:, 0:T].rearrange("p (s n) -> p s n", s=NST), op0=ALU.mult, op1=ALU.add),
                            reads=[pbuf[bk2]] + b_X1, writes=b_X1)
                    for s_ in range(NST):
                        st6, mv, rs, b_st = STS[s_ % 2]
                        xs_ap = X1[:, s_, :]
                        ln_stats(xs_ap, b_X1[s_], st6, mv, rs, b_st)
                        S.op("act", lambda e, xs_ap=xs_ap, rs=rs: e.activation(out=xs_ap, in_=xs_ap, func=AF.Identity, scale=rs[:, 0:1], bias=rs[:, 1:2]),
                             reads=[b_X1[s_], b_st], writes=[b_X1[s_]])
                        S.op("dve", lambda e, xs_ap=xs_ap: e.tensor_tensor(out=xs_ap, in0=xs_ap, in1=LNP[:, 2, :], op=ALU.mult),
                             reads=[b_X1[s_], b_LNP], writes=[b_X1[s_]])
                        S.op("pool", lambda e, xs_ap=xs_ap: e.tensor_tensor(out=xs_ap, in0=xs_ap, in1=LNP[:, 3, :], op=ALU.add),
                             reads=[b_X1[s_], b_LNP], writes=[b_X1[s_]])
                        S.dma("pool", lambda e, s_=s_, xs_ap=xs_ap: e.dma_start(out=y_d[tok0 + gt0 + s_ * 128:tok0 + gt0 + (s_ + 1) * 128, :], in_=xs_ap),
                              ds_y[s_], reads=[b_X1[s_]], writes=[b_yd])
                S.barrier()
        S.barrier()
        S.emit()
    return nc


def _rope_tables(smax):
    inv = (1.0 / (np.float32(10000.0) ** (np.arange(0, 64, 2, dtype=np.float32) / np.float32(64)))).astype(np.float32)
    ang = (np.arange(smax, dtype=np.float32)[:, None] * inv[None, :]).astype(np.float32)
    cos = np.cos(ang).astype(np.float32)
    sin = np.sin(ang).astype(np.float32)
    c = np.zeros((128, smax), np.float32)
    s = np.zeros((128, smax), np.float32)
    for p in range(128):
        d = p % 64
        j = d % 32
        c[p] = cos[:, j]
        s[p] = (-sin[:, j]) if d < 32 else sin[:, j]
    return c, s


def _consts(smax):
    ident = np.eye(128, dtype=np.float32)
    prot = np.zeros((128, 128), np.float32)
    for m in range(128):
        blk = (m // 64) * 64
        d = m % 64
        k = blk + ((d + 32) % 64)
        prot[k, m] = 1.0
    c, s = _rope_tables(smax)
    return ident, prot, c, s


def make_in_maps(inputs, n_cores, seq_plan):
    f = lambda a: np.ascontiguousarray(np.asarray(a, dtype=np.float32))
    def pan(a, nk):
        a = f(a)
        ncb = a.shape[1] // 128
        return np.ascontiguousarray(a.reshape(nk, 128, ncb, 128).transpose(2, 1, 0, 3).reshape(ncb * 128, nk * 128))

    w = {
        "w_ada": f(inputs["w_ada"][0]), "b_ada": f(inputs["b_ada"][0]).reshape(1, -1), "w_in": pan(inputs["w_in"][0], 8),
        "lq1": f(inputs["lambda_q1"][0]).reshape(1, 64), "lk1": f(inputs["lambda_k1"][0]).reshape(1, 64),
        "lq2": f(inputs["lambda_q2"][0]).reshape(1, 64), "lk2": f(inputs["lambda_k2"][0]).reshape(1, 64),
        "subln_g": f(inputs["subln_g"][0]).reshape(1, 128), "conv_w": f(inputs["conv_w"][0]), "conv_b": f(inputs["conv_b"][0]).reshape(1, -1),
        "w_lru_gates": f(inputs["w_lru_gates"][0]).reshape(4, 16, 64, 64), "b_lru_gates": f(inputs["b_lru_gates"][0]).reshape(4, -1),
        "lru_lambda": f(inputs["lru_lambda"][0]), "w_ab": pan(inputs["w_attn_branch"][0], 8), "w_lb": pan(inputs["w_lru_branch"][0], 8),
        "w_out": pan(inputs["w_out"][0], 8), "ln1_g": f(inputs["ln1_g"][0]).reshape(1, -1), "ln1_b": f(inputs["ln1_b"][0]).reshape(1, -1),
        "w_ffn_in": pan(inputs["w_ffn_in"][0], 8), "w_ffn_out": pan(inputs["w_ffn_out"][0], 22),
        "ln2_g": f(inputs["ln2_g"][0]).reshape(1, -1), "ln2_b": f(inputs["ln2_b"][0]).reshape(1, -1),
    }
    maps = []
    for core in range(n_cores):
        plan = seq_plan(core)
        smax = max(p[0].shape[0] for p in plan)
        ident, prot, c, s = _consts(smax)
        m = dict(w)
        m["x"] = np.ascontiguousarray(np.concatenate([p[0] for p in plan], axis=0))
        m["c"] = np.ascontiguousarray(np.stack([p[1] for p in plan], axis=0))
        m["ident"] = ident
        m["prot"] = prot
        m["rope_c"] = c
        m["rope_s"] = s
        maps.append(m)
    return maps


_NC_CACHE = {}


def kernel(**inputs):
    n = 8
    xp = np.asarray(inputs["x_prompt"], dtype=np.float32)
    xs = np.asarray(inputs["x_sample"], dtype=np.float32)
    cp = np.asarray(inputs["c_prompt"], dtype=np.float32)
    cs = np.asarray(inputs["c_sample"], dtype=np.float32)
    B, SP, _ = xp.shape
    DB, SS, _ = xs.shape
    per = DB // n
    seq_lens = [SP] + [SS] * per

    def plan(core):
        return [(xp[core], cp[core])] + [(xs[core * per + i], cs[core * per + i]) for i in range(per)]

    key = tuple(seq_lens)
    if key not in _NC_CACHE:
        _NC_CACHE[key] = build(seq_lens)
    nc = _NC_CACHE[key]
    maps = make_in_maps(inputs, n, plan)
    res = run_bass_kernel_spmd(nc, maps, core_ids=list(range(n)))
    yp = np.empty((B, SP, D), np.float32)
    ys = np.empty((DB, SS, D), np.float32)
    for core in range(n):
        y = np.asarray(res.results[core]["y"], dtype=np.float32)
        yp[core] = y[0:SP]
        ys[core * per:(core + 1) * per] = y[SP:].reshape(per, SS, D)
    return (yp, ys)
```

```python
import numpy as np
from contextlib import ExitStack
import concourse.bass as bass
import concourse.mybir as mybir
from concourse.bass_utils import run_bass_kernel_spmd

F32 = mybir.dt.float32
BF16 = mybir.dt.bfloat16
AF = mybir.ActivationFunctionType
ALU = mybir.AluOpType

ENGS = ("pe", "act", "dve", "pool", "sp")
D = 1024
NCH = 8
DFF = 2816
NFF = 22
ALPHA = 2.0 ** 0.25
LN_EPS = 1e-5
RMS_EPS = 1e-5
LAMBDA_INIT = 0.2
QB = 2048
ARENA_BYTES = 174 * 1024


class Sem:
    def __init__(self, h, name):
        self.h = h
        self.name = name
        self.count = 0


class Buf:
    __slots__ = ("name", "w", "r", "const", "excl")

    def __init__(self, name, const=False, excl=False):
        self.name = name
        self.w = None
        self.r = {}
        self.const = const
        self.excl = excl


class _Rec:
    def __init__(self):
        self.call = None

    def __getattr__(self, name):
        def f(*a, **k):
            self.call = (name, a, k)
            return self
        return f


def _record(fn):
    r = _Rec()
    fn(r)
    return r.call


class Sched:
    def __init__(self, nc, stack):
        self.nc = nc
        self.stack = stack
        self.prog = {e: [] for e in ENGS}
        self.sems = []
        self.esem = {e: self.new_sem("e_" + e) for e in ENGS if e != "sp"}
        self.waited = {e: {} for e in ENGS}

    def new_sem(self, name):
        s = Sem(self.stack.enter_context(self.nc.semaphore(name)), name)
        self.sems.append(s)
        return s

    def _deps(self, eng, reads, writes):
        deps = {}
        for b in reads:
            if b.w is not None:
                s, v = b.w
                if deps.get(s, 0) < v:
                    deps[s] = v
        for b in writes:
            if b.w is not None:
                s, v = b.w
                if deps.get(s, 0) < v:
                    deps[s] = v
            for s, v in b.r.items():
                if deps.get(s, 0) < v:
                    deps[s] = v
        out = []
        wd = self.waited[eng]
        own = self.esem.get(eng)
        for s, v in deps.items():
            if s is own and eng == "pe":
                continue
            if wd.get(s, 0) >= v:
                continue
            assert s.count >= v, f"wait on future tick: eng={eng} sem={s.name} v={v} count={s.count}"
            wd[s] = v
            out.append((s, v))
        return out

    def op(self, eng, fn, reads=(), writes=(), mark=True):
        ex = [b for b in reads if b.excl]
        if ex:
            reads = [b for b in reads if not b.excl]
            writes = list(writes) + [b for b in ex if b not in writes]
        waits = self._deps(eng, reads, writes)
        sem = self.esem[eng]
        if mark:
            sem.count += 1
            tick = sem.count
        else:
            tick = sem.count + 1
        self.prog[eng].append((waits, _record(fn), (sem, 1) if mark else None))
        for b in reads:
            if not b.const and b.r.get(sem, 0) < tick:
                b.r[sem] = tick
        for b in writes:
            b.w = (sem, tick)
            b.r = {}

    def dma(self, queue, fn, dsem, reads=(), writes=()):
        waits = self._deps(queue, reads, writes)
        dsem.count += 16
        v = dsem.count
        self.prog[queue].append((waits, _record(fn), (dsem, 16)))
        for b in reads:
            if not b.const and b.r.get(dsem, 0) < v:
                b.r[dsem] = v
        for b in writes:
            b.w = (dsem, v)
            b.r = {}

    def fence(self, engs, bufs):
        for e in engs:
            waits = self._deps(e, [], bufs)
            self.prog[e].append((waits, None, None))

    def barrier(self, exclude=()):
        for e in ENGS:
            wd = self.waited[e]
            waits = []
            for s in self.sems:
                if s is self.esem.get(e) or s in exclude:
                    continue
                if s.count > wd.get(s, 0):
                    wd[s] = s.count
                    waits.append((s, s.count))
            self.prog[e].append((waits, None, None))

    def emit(self):
        nc = self.nc
        with nc.Block() as block:
            deco = {"pe": block.tensor, "act": block.scalar, "dve": block.vector, "pool": block.gpsimd, "sp": block.sync}
            for e in ENGS:
                prog = self.prog[e]

                def body(eng, prog=prog):
                    for waits, fn, inc in prog:
                        for s, v in waits:
                            eng.wait_ge(s.h, v)
                        if fn is not None:
                            ins = getattr(eng, fn[0])(*fn[1], **fn[2])
                            if inc is not None:
                                ins.then_inc(inc[0].h, inc[1])

                deco[e](body)


class _Stop(Exception):
    pass


def build(seq_lens, dbg=False, stop=None):
    nc = bass.Bass("TRN2", target_bir_lowering=False)
    NSEQ = len(seq_lens)
    NTOK = sum(seq_lens)
    SMAX = max(seq_lens)

    def din(name, shape, dt=F32):
        return nc.dram_tensor(name, list(shape), dt, kind="ExternalInput").ap()

    def dint(name, shape, dt):
        return nc.dram_tensor(name, list(shape), dt, kind="Internal").ap()

    x_d = din("x", [NTOK, D])
    c_d = din("c", [NSEQ, D])
    w_ada_d = din("w_ada", [D, 6 * D])
    b_ada_d = din("b_ada", [1, 6 * D])
    w_in_d = din("w_in", [56 * 128, 1024])
    lq1_d = din("lq1", [1, 64])
    lk1_d = din("lk1", [1, 64])
    lq2_d = din("lq2", [1, 64])
    lk2_d = din("lk2", [1, 64])
    subln_d = din("subln_g", [1, 128])
    conv_w_d = din("conv_w", [4, D])
    conv_b_d = din("conv_b", [1, D])
    wlg_d = din("w_lru_gates", [4, 16, 64, 64])
    blg_d = din("b_lru_gates", [4, D])
    llam_d = din("lru_lambda", [2, D])
    w_ab_d = din("w_ab", [8 * 128, 1024])
    w_lb_d = din("w_lb", [8 * 128, 1024])
    w_out_d = din("w_out", [8 * 128, 1024])
    ln1g_d = din("ln1_g", [1, D])
    ln1b_d = din("ln1_b", [1, D])
    w_fi_d = din("w_ffn_in", [44 * 128, 1024])
    w_fo_d = din("w_ffn_out", [8 * 128, DFF])
    ln2g_d = din("ln2_g", [1, D])
    ln2b_d = din("ln2_b", [1, D])
    ident_d = din("ident", [128, 128])
    prot_d = din("prot", [128, 128])
    ropec_d = din("rope_c", [128, SMAX])
    ropes_d = din("rope_s", [128, SMAX])
    y_d = nc.dram_tensor("y", [NTOK, D], F32, kind="ExternalOutput").ap()

    w_in_bf = dint("w_in_bf", [56 * 128, 1024], BF16)
    w_ab_bf = dint("w_ab_bf", [8 * 128, 1024], BF16)
    w_lb_bf = dint("w_lb_bf", [8 * 128, 1024], BF16)
    w_out_bf = dint("w_out_bf", [8 * 128, 1024], BF16)
    w_fi_bf = dint("w_fi_bf", [44 * 128, 1024], BF16)
    w_fo_bf = dint("w_fo_bf", [8 * 128, DFF], BF16)
    mod_d = dint("mod_d", [NSEQ, 6 * D], F32)
    lruw_d = dint("lruw_d", [128, NCH, 8, 128], BF16)
    rec_d = dint("rec_d", [NCH, 128, NTOK], BF16)

    dbg_out = {}
    if dbg:
        dbg_out["hT"] = nc.dram_tensor("dbg_hT", [128, NCH, NTOK], BF16, kind="ExternalOutput").ap()
        dbg_out["attnT"] = nc.dram_tensor("dbg_attnT", [128, NCH, NTOK], BF16, kind="ExternalOutput").ap()
        dbg_out["modT"] = nc.dram_tensor("dbg_modT", [128, 48, NSEQ], F32, kind="ExternalOutput").ap()
        dbg_out["rec"] = nc.dram_tensor("dbg_rec", [NCH, 128, NTOK], BF16, kind="ExternalOutput").ap()

    with ExitStack() as st:
        S = Sched(nc, st)

        def sb(name, shape, dt=F32):
            return st.enter_context(nc.sbuf_tensor(name, list(shape), dt))

        ident_bf = sb("ident_bf", [128, 128], BF16)
        ident_f = sb("ident_f", [128, 128], F32)
        prot_bf = sb("prot_bf", [128, 128], BF16)
        fmc = sb("fmc", [128, 11, NCH], F32)
        cdec = sb("cdec", [128, 4, NCH], F32)
        hbias = sb("hbias", [128, 4, NCH], F32)
        modT = sb("modT", [128, 48, NSEQ], F32)
        lamt = sb("lamt", [128, 8], F32)
        gsub = sb("gsub", [128, 128], F32)
        small = sb("small", [128, 64], F32)
        arena = sb("arena", [128, ARENA_BYTES // 2], BF16)
        ps = st.enter_context(nc.psum_tensor("ps", [128, 8 * 512], F32))

        b_const = Buf("const")
        b_hT = Buf("hT")

        def bank(b):
            return ps[:, b * 512:(b + 1) * 512]

        def bank_bf(b):
            return ps[:, b * 512:(b + 1) * 512].bitcast(BF16)

        pbuf = [Buf(f"psb{i}", excl=True) for i in range(8)]
        pstate = {"next": 0}

        def nextbank(lo=0, hi=8):
            n = pstate["next"]
            if n < lo or n >= hi:
                n = lo
            pstate["next"] = n + 1
            return n

        class Alloc:
            def __init__(self, off=0):
                self.off = off

            def get(self, shape, dt, name="t"):
                n = int(np.prod(shape[1:]))
                esz = 4 if dt == F32 else 2
                nbytes = (n * esz + 3) // 4 * 4
                assert self.off + nbytes <= ARENA_BYTES, f"arena overflow {name} {self.off + nbytes}"
                v = arena[0:shape[0], self.off // 2:(self.off + n * esz) // 2]
                if dt == F32:
                    v = v.bitcast(F32)
                if len(shape) == 3:
                    v = v.rearrange("p (a b) -> p a b", a=shape[1])
                elif len(shape) == 4:
                    v = v.rearrange("p (a b c) -> p a b c", a=shape[1], b=shape[2])
                self.off += nbytes
                return v

        sem_pool = {}

        def dsem(name):
            if name not in sem_pool:
                sem_pool[name] = S.new_sem("d_" + name)
            return sem_pool[name]

        ds_setup = dsem("setup")
        ds_cast = dsem("cast")
        b_wbf = Buf("wbf")

        import os
        SKIP = set(os.environ.get("KSKIP", "").split(","))

        cast_i = [0]
        ds_castk = [dsem("castk0"), dsem("castk1")]
        b_castk = [Buf("castk0"), Buf("castk1")]

        def cast_rows(dst, src, rows, blk):
            if "cast" in SKIP:
                return
            for r0 in range(0, rows, blk):
                r1 = min(rows, r0 + blk)
                k = cast_i[0] % 2
                cast_i[0] += 1
                S.dma("pool", lambda e, r0=r0, r1=r1: e.dma_start(out=dst[r0:r1, :], in_=src[r0:r1, :], max_dma_last_dim=4096),
                      ds_castk[k], writes=[b_castk[k]])

        al = Alloc()
        lv = al.get([128, 4, 64], F32)
        junk = al.get([128, 2, 64], F32)
        junk2 = al.get([128, 64], F32)
        wb = al.get([128, NCH, 8, 128], BF16)
        cT = al.get([128, NCH, NSEQ], F32)
        ones1 = al.get([1, NSEQ], F32)
        bada = al.get([1, 6 * D], F32)
        mod_sb = al.get([NSEQ, 6 * D], F32)
        wpan = [al.get([128, NCH, 512], F32) for _ in range(2)]
        b_wb = Buf("wb")
        b_lruw = Buf("lruw")
        b_modd = Buf("mod_d")
        b_mod = Buf("mod_sb")
        b_wpan = [Buf("wpan0"), Buf("wpan1")]
        ds_wp = [dsem("wp0"), dsem("wp1")]
        ds_wbd = dsem("wbd")

        S.op("pool", lambda e: e.memset(wb, 0.0), writes=[b_wb])
        S.dma("pool", lambda e: e.dma_start(out=ident_bf[:], in_=ident_d), ds_setup, writes=[b_const])
        S.dma("pool", lambda e: e.dma_start(out=prot_bf[:], in_=prot_d), ds_setup, writes=[b_const])
        for dg in (range(4) if "wbdma" not in SKIP else []):
            for j in range(2):
                src = wlg_d[dg].rearrange("(c j) d e -> j d c e", j=2)[j]
                S.dma("pool", lambda e, dg=dg, j=j, src=src: e.dma_start(out=wb[64 * j:64 * j + 64, :, dg, 64 * j:64 * j + 64], in_=src),
                      ds_wbd, reads=[], writes=[b_wb])
        cast_rows(w_in_bf, w_in_d, 56 * 128, 1024)
        S.dma("sp", lambda e: e.dma_start(out=ident_f[:], in_=ident_d), ds_setup, writes=[b_const])
        for s_ in range(NSEQ):
            S.dma("sp", lambda e, s_=s_: e.dma_start(out=cT[:, :, s_], in_=c_d[s_:s_ + 1, :].rearrange("o (c p) -> p (o c)", p=128),
                                                    allow_slow_non_contiguous=True), ds_setup, writes=[b_const])
        S.dma("sp", lambda e: e.dma_start(out=bada, in_=b_ada_d), ds_setup, writes=[b_const])
        vecs = [conv_w_d[0:1, :], conv_w_d[1:2, :], conv_w_d[2:3, :], conv_w_d[3:4, :], conv_b_d,
                blg_d[0:1, :], blg_d[1:2, :], blg_d[2:3, :], blg_d[3:4, :], llam_d[0:1, :], llam_d[1:2, :]]
        for i, v in (enumerate(vecs) if "slow" not in SKIP else []):
            S.dma("sp", lambda e, i=i, v=v: e.dma_start(out=fmc[:, i, :], in_=v.rearrange("o (c p) -> p (o c)", p=128),
                                                       allow_slow_non_contiguous=True), ds_setup, writes=[b_const])
        for i, v in (enumerate([lq1_d, lk1_d, lq2_d, lk2_d]) if "bcast" not in SKIP else []):
            S.dma("sp", lambda e, i=i, v=v: e.dma_start(out=lv[:, i, :], in_=v.broadcast_to([128, 64])), ds_setup, writes=[b_const])
        if "bcast" not in SKIP:
            S.dma("sp", lambda e: e.dma_start(out=gsub[:], in_=subln_d.broadcast_to([128, 128])), ds_setup, writes=[b_const])
        cast_rows(w_ab_bf, w_ab_d, 1024, 1024)
        cast_rows(w_lb_bf, w_lb_d, 1024, 1024)
        cast_rows(w_out_bf, w_out_d, 1024, 1024)
        cast_rows(w_fi_bf, w_fi_d, 44 * 128, 1024)
        cast_rows(w_fo_bf, w_fo_d, 1024, 512)
        S.barrier(exclude=ds_castk)
        S.op("dve", lambda e: e.tensor_tensor(out=junk[:, 0, :], in0=lv[:, 0, :], in1=lv[:, 1, :], op=ALU.mult), writes=[b_const])
        S.op("dve", lambda e: e.tensor_tensor(out=junk[:, 1, :], in0=lv[:, 2, :], in1=lv[:, 3, :], op=ALU.mult), reads=[b_const], writes=[b_const])
        S.op("dve", lambda e: e.tensor_scalar(out=gsub[:], in0=gsub[:], scalar1=1.0 - LAMBDA_INIT, scalar2=None, op0=ALU.mult),
             reads=[b_const], writes=[b_const])
        S.op("dve", lambda e: e.memset(ones1, 1.0), reads=[b_const], writes=[b_const])
        for c in range(NCH):
            for k in range(4):
                S.op("dve", lambda e, c=c, k=k: e.tensor_scalar(out=wb[:, c, 4 + k, :], in0=ident_bf[:], scalar1=fmc[:, k, c:c + 1],
                                                                scalar2=None, op0=ALU.mult), reads=[b_wb], writes=[b_wb])
        S.op("act", lambda e: e.activation(out=cT, in_=cT, func=AF.Silu), writes=[b_mod])
        S.op("act", lambda e: e.activation(out=small[:, 0:16], in_=fmc[:, 9:11, :].rearrange("p a c -> p (a c)"), func=AF.Exp, scale=-1.0),
             reads=[b_mod], writes=[b_mod])
        S.barrier(exclude=ds_castk)
        S.op("act", lambda e: e.activation(out=junk2, in_=junk[:, 0, :], func=AF.Identity, accum_out=lamt[:, 0:1]), writes=[b_mod])
        S.op("act", lambda e: e.activation(out=junk2, in_=junk[:, 1, :], func=AF.Identity, accum_out=lamt[:, 1:2]), reads=[b_mod], writes=[b_mod])
        S.op("act", lambda e: e.activation(out=lamt[:, 2:4], in_=lamt[:, 0:2], func=AF.Exp), reads=[b_mod], writes=[b_mod])
        S.op("act", lambda e: e.activation(out=small[:, 16:32], in_=small[:, 0:16], func=AF.Ln, bias=1.0, scale=1.0), reads=[b_mod], writes=[b_mod])
        S.dma("sp", lambda e: e.dma_start(out=lruw_d, in_=wb), ds_wbd, reads=[b_wb], writes=[b_lruw])
        S.barrier(exclude=ds_castk)
        S.op("dve", lambda e: e.tensor_tensor(out=lamt[:, 4:5], in0=lamt[:, 3:4], in1=lamt[:, 2:3], op=ALU.subtract), writes=[b_const])
        S.op("dve", lambda e: e.tensor_scalar(out=lamt[:, 4:5], in0=lamt[:, 4:5], scalar1=-LAMBDA_INIT, scalar2=None, op0=ALU.add),
             reads=[b_const], writes=[b_const])
        S.op("dve", lambda e: e.tensor_scalar(out=hbias[:], in0=fmc[:, 5:9, :], scalar1=0.5, scalar2=None, op0=ALU.mult), reads=[b_const], writes=[b_const])
        for d_ in range(2):
            S.op("dve", lambda e, d_=d_: e.tensor_scalar(out=cdec[:, 2 * d_, :], in0=small[:, 16 + 8 * d_:24 + 8 * d_], scalar1=-4.0,
                                                         scalar2=None, op0=ALU.mult), reads=[b_const], writes=[b_const])
            S.op("dve", lambda e, d_=d_: e.tensor_scalar(out=cdec[:, 2 * d_ + 1, :], in0=small[:, 16 + 8 * d_:24 + 8 * d_], scalar1=-8.0,
                                                         scalar2=None, op0=ALU.mult), reads=[b_const], writes=[b_const])
        for pn in (range(12) if "mod" not in SKIP else []):
            sl = pn % 2
            S.dma("sp", lambda e, pn=pn, sl=sl: e.dma_start(out=wpan[sl], in_=w_ada_d[:, pn * 512:(pn + 1) * 512].rearrange("(c p) n -> p c n", p=128)),
                  ds_wp[sl], writes=[b_wpan[sl]])
            bk = nextbank()
            for kc in range(NCH):
                S.op("pe", lambda e, bk=bk, kc=kc, sl=sl: e.matmul(bank(bk)[0:NSEQ, :], lhsT=cT[:, kc, :], rhs=wpan[sl][:, kc, :],
                                                                   start=(kc == 0), stop=False),
                     reads=[b_wpan[sl]], writes=[pbuf[bk]], mark=False)
            S.op("pe", lambda e, bk=bk, pn=pn: e.matmul(bank(bk)[0:NSEQ, :], lhsT=ones1, rhs=bada[:, pn * 512:(pn + 1) * 512],
                                                        start=False, stop=True), reads=[], writes=[pbuf[bk]])
            S.op("act", lambda e, bk=bk, pn=pn: e.activation(out=mod_sb[:, pn * 512:(pn + 1) * 512], in_=bank(bk)[0:NSEQ, :], func=AF.Identity),
                 reads=[pbuf[bk]], writes=[b_mod])
        S.dma("sp", lambda e: e.dma_start(out=mod_d, in_=mod_sb), ds_setup, reads=[b_mod], writes=[b_modd])
        for s_ in range(NSEQ):
            S.dma("sp", lambda e, s_=s_: e.dma_start(out=modT[:, :, s_], in_=mod_d[s_:s_ + 1, :].rearrange("o (c p) -> p (o c)", p=128),
                                                    allow_slow_non_contiguous=True), ds_setup, reads=[b_modd], writes=[b_modd])
        for lo in (8, 32):
            S.op("dve", lambda e, lo=lo: e.tensor_scalar(out=modT[:, lo:lo + 8, :], in0=modT[:, lo:lo + 8, :], scalar1=1.0, scalar2=None, op0=ALU.add),
                 reads=[b_modd], writes=[b_modd])
        ds_dbg = dsem("dbg")
        b_dbg = Buf("dbg")
        if dbg:
            S.dma("sp", lambda e: e.dma_start(out=dbg_out["modT"], in_=modT[:]), ds_dbg, reads=[b_modd], writes=[b_dbg])
        S.barrier()
        b_const = Buf("const2", const=True)
        b_wbf = Buf("wbf2", const=True)
        b_lruw = Buf("lruw2", const=True)
        b_recd_dummy = None

        def ln_stats(src_ap, b_src, st6, mv, rs, b_st):
            S.op("dve", lambda e: e.bn_stats(out=st6[:, 0, :], in_=src_ap[:, 0:512]), reads=[b_src], writes=[b_st])
            S.op("dve", lambda e: e.bn_stats(out=st6[:, 1, :], in_=src_ap[:, 512:1024]), reads=[b_src], writes=[b_st])
            S.op("dve", lambda e: e.bn_aggr(out=mv, in_=st6.rearrange("p a b -> p (a b)")), reads=[b_st], writes=[b_st])
            S.op("act", lambda e: e.activation(out=rs[:, 0:1], in_=mv[:, 1:2], func=AF.Sqrt, bias=LN_EPS, scale=1.0), reads=[b_st], writes=[b_st])
            S.op("dve", lambda e: e.reciprocal(out=rs[:, 0:1], in_=rs[:, 0:1]), reads=[b_st], writes=[b_st])
            S.op("dve", lambda e: e.tensor_scalar(out=rs[:, 1:2], in0=mv[:, 0:1], scalar1=rs[:, 0:1], scalar2=-1.0, op0=ALU.mult, op1=ALU.mult),
                 reads=[b_st], writes=[b_st])

        def mm_group(out_ap, b_out, pairs, reads):
            n = len(pairs)
            for i, (l, r) in enumerate(pairs):
                S.op("pe", lambda e, l=l, r=r, i=i: e.matmul(out_ap, lhsT=l, rhs=r, start=(i == 0), stop=(i == n - 1)),
                     reads=reads, writes=[b_out], mark=(i == n - 1))

        for si in ([] if stop == "setup" else range(NSEQ)):
            SL = seq_lens[si]
            tok0 = sum(seq_lens[:si])
            NT = SL // 512
            sc1p = lambda c: modT[:, 8 + c, si:si + 1]
            sh1 = lambda c: modT[:, 0 + c, si:si + 1]
            g1c = lambda c: modT[:, 16 + c, si:si + 1]
            sh2 = lambda c: modT[:, 24 + c, si:si + 1]
            sc2p = lambda c: modT[:, 32 + c, si:si + 1]
            g2c = lambda c: modT[:, 40 + c, si:si + 1]

            al = Alloc()
            hT = al.get([128, NCH, SL], BF16)
            HOFF = al.off
            XT = [al.get([128, D], F32) for _ in range(3)]
            b_XT = [Buf(f"XT{i}") for i in range(3)]
            ds_XT = [dsem(f"xt{i}") for i in range(3)]
            XN = [al.get([128, D], BF16) for _ in range(2)]
            b_XN = [Buf(f"XN{i}") for i in range(2)]
            STT_ = [(al.get([128, 2, 6], F32), al.get([128, 2], F32), al.get([128, 2], F32), Buf(f"st{i}")) for i in range(3)]
            for g in range(NT):
                base = (g % 2) * 4
                for j in range(4):
                    ti = g * 4 + j
                    sl = ti % 3
                    S.dma("sp", lambda e, sl=sl, ti=ti: e.dma_start(out=XT[sl], in_=x_d[tok0 + ti * 128: tok0 + (ti + 1) * 128, :]),
                          ds_XT[sl], writes=[b_XT[sl]])
                    st6, mv, rs, b_st = STT_[sl]
                    ln_stats(XT[sl], b_XT[sl], st6, mv, rs, b_st)
                    xs = ti % 2
                    S.op("act", lambda e, sl=sl, xs=xs, rs=rs: e.activation(out=XN[xs], in_=XT[sl], func=AF.Identity, scale=rs[:, 0:1], bias=rs[:, 1:2]),
                         reads=[b_XT[sl], b_st], writes=[b_XN[xs]])
                    for c in range(NCH):
                        bk = base + c // 2
                        S.op("pe", lambda e, bk=bk, c=c, j=j, xs=xs: e.transpose(
                            bank_bf(bk)[:, (c % 2) * 512 + j * 128:(c % 2) * 512 + (j + 1) * 128], XN[xs][:, c * 128:(c + 1) * 128], ident_bf[:]),
                            reads=[b_XN[xs], b_const], writes=[pbuf[bk]], mark=(c % 2 == 1))
                for c in range(NCH):
                    bk = base + c // 2
                    S.op("act", lambda e, bk=bk, c=c, g=g: e.activation(out=hT[:, c, g * 512:(g + 1) * 512], in_=bank_bf(bk)[:, (c % 2) * 512:(c % 2) * 512 + 512],
                                                                        func=AF.Identity, scale=sc1p(c), bias=sh1(c)),
                         reads=[pbuf[bk], b_const], writes=[b_hT])
            if dbg:
                S.dma("sp", lambda e: e.dma_start(out=dbg_out["hT"][:, :, tok0:tok0 + SL], in_=hT[:, :, 0:SL]), ds_dbg, reads=[b_hT], writes=[b_dbg])
            S.barrier()
            if stop == "p1":
                break

            al = Alloc(HOFF)
            WP = [al.get([128, 2, NCH, 128], BF16) for _ in range(2)]
            LW = [al.get([128, 8, 128], BF16) for _ in range(2)]
            b_WP = [Buf("WP0"), Buf("WP1")]
            ds_WP = [dsem("WP0"), dsem("WP1")]
            XR = al.get([128, SL + 4], BF16)
            G_ = al.get([128, SL], BF16)
            XC = al.get([128, SL], F32)
            XCb = al.get([128, SL], BF16)
            HF = al.get([128, SL], F32)
            LB = min(SL, 2048) if SL <= 2048 else 1024
            if os.environ.get("KLB"):
                LB = int(os.environ["KLB"])
            NBLK = SL // LB
            TPB = LB // 512
            HBK = al.get([128, LB], F32)
            REC = al.get([128, SL], BF16)
            AA = al.get([128, LB], F32)
            QQ = al.get([128, LB], F32)
            DD = al.get([128, LB], F32)
            CAR = al.get([128, 2], F32)
            TMP = [[al.get([128, 512], F32) for _ in range(2)] for _ in range(2)]
            b_XR = [Buf(f"XR{t}") for t in range(NT)]
            b_G = [Buf(f"G{t}") for t in range(NT)]
            b_XC = [Buf(f"XC{t}") for t in range(NT)]
            b_XCb = [Buf(f"XCb{t}") for t in range(NT)]
            b_HF = [Buf(f"HF{t}") for t in range(NT)]
            b_HBK = Buf("HBK")
            b_REC = Buf("REC")
            b_AA = Buf("AA")
            b_QQ = Buf("QQ")
            b_DD = Buf("DD")
            b_CAR = Buf("CAR")
            b_TMP = [[Buf(f"TMP{i}{j}") for j in range(2)] for i in range(2)]
            ds_rec = dsem("rec")
            b_recd = Buf("rec_d")
            b_pad = Buf("XRpad")
            S.op("pool", lambda e: e.memset(XR[:, 0:2], 0.0), writes=[b_pad])
            S.op("pool", lambda e: e.memset(XR[:, SL + 2:SL + 4], 0.0), writes=[b_pad])

            def load_lru_panels(c):
                sl = c % 2
                S.dma("sp", lambda e: e.dma_start(out=WP[sl][:, 0].rearrange("p k n -> p (k n)"), in_=w_in_bf[(24 + c) * 128:(25 + c) * 128, :]),
                      ds_WP[sl], reads=[b_wbf], writes=[b_WP[sl]])
                S.dma("sp", lambda e: e.dma_start(out=WP[sl][:, 1].rearrange("p k n -> p (k n)"), in_=w_in_bf[(32 + c) * 128:(33 + c) * 128, :]),
                      ds_WP[sl], reads=[b_wbf], writes=[b_WP[sl]])
                S.dma("sp", lambda e: e.dma_start(out=LW[sl], in_=lruw_d[:, c]), ds_WP[sl], reads=[b_lruw], writes=[b_WP[sl]])

            load_lru_panels(0)
            for c in range(NCH):
                if c + 1 < NCH:
                    load_lru_panels(c + 1)
                sl = c % 2
                for t in range(NT):
                    bk = nextbank()
                    mm_group(bank(bk), pbuf[bk], [(WP[sl][:, 0, kc, :], hT[:, kc, t * 512:(t + 1) * 512]) for kc in range(NCH)], [b_WP[sl], b_hT])
                    S.op("act", lambda e, bk=bk, t=t: e.activation(out=XR[:, 2 + t * 512:2 + (t + 1) * 512], in_=bank(bk), func=AF.Identity),
                         reads=[pbuf[bk]], writes=[b_XR[t]])
                    bk = nextbank()
                    mm_group(bank(bk), pbuf[bk], [(WP[sl][:, 1, kc, :], hT[:, kc, t * 512:(t + 1) * 512]) for kc in range(NCH)], [b_WP[sl], b_hT])
                    S.op("act", lambda e, bk=bk, t=t: e.activation(out=G_[:, t * 512:(t + 1) * 512], in_=bank(bk), func=AF.Gelu_apprx_tanh),
                         reads=[pbuf[bk]], writes=[b_G[t]])
                for t in range(NT):
                    bk = nextbank()
                    rd = [b_WP[sl], b_pad, b_XR[t]] + ([b_XR[t - 1]] if t > 0 else []) + ([b_XR[t + 1]] if t + 1 < NT else [])
                    mm_group(bank(bk), pbuf[bk], [(LW[sl][:, 4 + k, :], XR[:, t * 512 + k:t * 512 + k + 512]) for k in range(4)], rd)
                    S.op("act", lambda e, bk=bk, t=t, c=c: e.activation(out=XC[:, t * 512:(t + 1) * 512], in_=bank(bk), func=AF.Identity,
                                                                        bias=fmc[:, 4, c:c + 1], scale=1.0),
                         reads=[pbuf[bk], b_const], writes=[b_XC[t]])
                    S.op("pool", lambda e, t=t: e.tensor_copy(out=XCb[:, t * 512:(t + 1) * 512], in_=XC[:, t * 512:(t + 1) * 512]),
                         reads=[b_XC[t]], writes=[b_XCb[t]])
                it = 0
                for d_ in range(2):
                    blocks = range(NBLK) if d_ == 0 else range(NBLK - 1, -1, -1)
                    for bi in blocks:
                        bsl = slice(bi * LB, (bi + 1) * LB)
                        for tl in range(TPB):
                            t = bi * TPB + tl
                            tm = TMP[it % 2]
                            btm = b_TMP[it % 2]
                            it += 1
                            TR, TI = tm
                            tsl = slice(t * 512, (t + 1) * 512)
                            lsl = slice(tl * 512, (tl + 1) * 512)
                            bkr = nextbank()
                            mm_group(bank(bkr), pbuf[bkr], [(LW[sl][:, 2 * d_, :], XCb[:, tsl])], [b_WP[sl], b_XCb[t]])
                            bki = nextbank()
                            mm_group(bank(bki), pbuf[bki], [(LW[sl][:, 2 * d_ + 1, :], XCb[:, tsl])], [b_WP[sl], b_XCb[t]])
                            S.op("act", lambda e: e.activation(out=TR, in_=bank(bkr), func=AF.Tanh, bias=hbias[:, 2 * d_, c:c + 1], scale=0.5),
                                 reads=[pbuf[bkr], b_const], writes=[btm[0]])
                            S.op("act", lambda e: e.activation(out=TI, in_=bank(bki), func=AF.Tanh, bias=hbias[:, 2 * d_ + 1, c:c + 1], scale=0.5),
                                 reads=[pbuf[bki], b_const], writes=[btm[1]])
                            S.op("act", lambda e: e.activation(out=AA[:, lsl], in_=TR, func=AF.Exp, scale=cdec[:, 2 * d_, c:c + 1], bias=cdec[:, 2 * d_, c:c + 1]),
                                 reads=[btm[0], b_const], writes=[b_AA])
                            S.op("act", lambda e: e.activation(out=QQ[:, lsl], in_=TR, func=AF.Exp, scale=cdec[:, 2 * d_ + 1, c:c + 1], bias=cdec[:, 2 * d_ + 1, c:c + 1]),
                                 reads=[btm[0], b_const], writes=[b_QQ])
                            S.op("dve", lambda e: e.scalar_tensor_tensor(out=DD[:, lsl], in0=TI, scalar=1.0, in1=XC[:, tsl], op0=ALU.add, op1=ALU.mult),
                                 reads=[btm[1], b_XC[t]], writes=[b_DD])
                        S.op("act", lambda e: e.activation(out=QQ, in_=QQ, func=AF.Sqrt, bias=0.25, scale=-0.25), reads=[b_QQ], writes=[b_QQ])
                        S.op("dve", lambda e: e.tensor_tensor(out=DD, in0=DD, in1=QQ, op=ALU.mult), reads=[b_DD, b_QQ], writes=[b_DD])
                        if d_ == 0:
                            init = 0.0 if bi == 0 else HF[:, bi * LB - 1:bi * LB]
                            S.op("dve", lambda e: e.tensor_tensor_scan(out=HF[:, bsl], data0=AA, data1=DD, initial=init, op0=ALU.mult, op1=ALU.add),
                                 reads=[b_AA, b_DD] + b_HF, writes=b_HF)
                        else:
                            init = 0.0 if bi == NBLK - 1 else CAR[:, 0:1]
                            S.op("dve", lambda e: e.tensor_tensor_scan(out=HBK[:, ::-1], data0=AA[:, ::-1], data1=DD[:, ::-1], initial=init,
                                                                        op0=ALU.mult, op1=ALU.add),
                                 reads=[b_AA, b_DD, b_CAR], writes=[b_HBK])
                            if bi > 0:
                                S.op("dve", lambda e: e.tensor_copy(out=CAR[:, 0:1], in_=HBK[:, 0:1]), reads=[b_HBK], writes=[b_CAR])
                            S.op("pool", lambda e: e.tensor_tensor(out=HBK, in0=HBK, in1=HF[:, bsl], op=ALU.add),
                                 reads=[b_HBK] + b_HF, writes=[b_HBK])
                            S.op("pool", lambda e: e.tensor_tensor(out=REC[:, bsl], in0=HBK, in1=G_[:, bsl], op=ALU.mult),
                                 reads=[b_HBK] + b_G, writes=[b_REC])
                S.dma("sp", lambda e, c=c: e.dma_start(out=rec_d[c, :, tok0:tok0 + SL], in_=REC), ds_rec, reads=[b_REC], writes=[b_recd])
                if dbg:
                    S.dma("sp", lambda e, c=c: e.dma_start(out=dbg_out["rec"][c, :, tok0:tok0 + SL], in_=REC), ds_dbg, reads=[b_REC], writes=[b_dbg])
            S.barrier()
            if stop == "p2":
                break

            NQB = max(1, SL // QB)
            QBL = min(QB, SL)
            for qb in range(NQB):
                q0 = qb * QBL
                al = Alloc(HOFF)
                attnT = al.get([128, NCH, QBL], BF16)
                b_attnT = Buf("attnT")
                p4_off = al.off
                QT = al.get([128, QBL], BF16)
                KT = al.get([128, SL], BF16)
                NKT = SL // 128
                V_ = al.get([128, NKT, 130], BF16)
                WQ = [al.get([128, 3, NCH, 128], BF16) for _ in range(2)]
                b_WQ = [Buf("WQ0"), Buf("WQ1")]
                ds_WQ = [dsem("WQ0"), dsem("WQ1")]
                RC = [al.get([128, 2, 512], F32) for _ in range(2)]
                b_RC = [Buf("RC0"), Buf("RC1")]
                ds_RC = [dsem("RC0"), dsem("RC1")]
                QBF = [al.get([128, 512], BF16) for _ in range(2)]
                b_QBF = [Buf("QBF0"), Buf("QBF1")]
                T12 = [[al.get([128, 512], F32) for _ in range(2)] for _ in range(2)]
                b_T12 = [[Buf("T1"), Buf("T2")] for _ in range(2)]
                PT = [al.get([128, 2, 512], BF16) for _ in range(3)]
                b_PT = [Buf(f"PT{i}") for i in range(3)]
                O0 = [al.get([128, 128], F32) for _ in range(4)]
                OO = [al.get([128, 128], F32) for _ in range(4)]
                AT = [al.get([128, 128], BF16) for _ in range(4)]
                b_O0 = [Buf(f"O0{i}") for i in range(4)]
                b_OO = [Buf(f"OO{i}") for i in range(4)]
                b_AT = [Buf(f"AT{i}") for i in range(4)]
                SQJ = al.get([128, 128], F32)
                b_SQJ = Buf("SQJ")
                nrm = al.get([128, 32], F32)
                b_nrm = Buf("nrm")
                b_QT = Buf("QT")
                b_KT = Buf("KT")
                b_V = Buf("V")
                S.op("pool", lambda e: e.memset(V_[:, :, 128:130], 1.0), writes=[b_V])

                def load_head_w(h):
                    sl = h % 2
                    for i in range(3):
                        S.dma("sp", lambda e, i=i: e.dma_start(out=WQ[sl][:, i].rearrange("p k n -> p (k n)"), in_=w_in_bf[(i * 8 + h) * 128:(i * 8 + h + 1) * 128, :]),
                              ds_WQ[sl], reads=[b_wbf], writes=[b_WQ[sl]])

                rc_i = [0]

                def rope_proj(h, which, t_tok, dst_ap, b_dst):
                    if "rope" in SKIP:
                        return
                    sl = h % 2
                    r = rc_i[0] % 2
                    rc_i[0] += 1
                    S.dma("sp", lambda e: e.dma_start(out=RC[r][:, 0, :], in_=ropec_d[:, t_tok:t_tok + 512]), ds_RC[r], writes=[b_RC[r]])
                    S.dma("sp", lambda e: e.dma_start(out=RC[r][:, 1, :], in_=ropes_d[:, t_tok:t_tok + 512]), ds_RC[r], writes=[b_RC[r]])
                    bka = nextbank(0, 5)
                    mm_group(bank(bka), pbuf[bka], [(WQ[sl][:, which, kc, :], hT[:, kc, t_tok:t_tok + 512]) for kc in range(NCH)], [b_WQ[sl], b_hT])
                    S.op("act", lambda e: e.activation(out=QBF[r], in_=bank(bka), func=AF.Identity), reads=[pbuf[bka]], writes=[b_QBF[r]])
                    bkb = nextbank(0, 5)
                    mm_group(bank(bkb), pbuf[bkb], [(prot_bf[:], QBF[r])], [b_const, b_QBF[r]])
                    t1, t2 = T12[r]
                    S.op("dve", lambda e: e.tensor_tensor(out=t1, in0=bank(bka), in1=RC[r][:, 0, :], op=ALU.mult),
                         reads=[pbuf[bka], b_RC[r]], writes=[b_T12[r][0]])
                    S.op("dve", lambda e: e.tensor_tensor(out=t2, in0=bank(bkb), in1=RC[r][:, 1, :], op=ALU.mult),
                         reads=[pbuf[bkb], b_RC[r]], writes=[b_T12[r][1]])
                    S.op("pool", lambda e: e.tensor_tensor(out=dst_ap, in0=t1, in1=t2, op=ALU.add),
                         reads=[b_T12[r][0], b_T12[r][1]], writes=[b_dst])

                def acc_ap(a, lo=0, hi=129):
                    bk = 5 + a // 3
                    o = (a % 3) * 130
                    return ps[:, bk * 512 + o + lo: bk * 512 + o + hi]

                load_head_w(0)
                for h in range(NCH):
                    if h + 1 < NCH:
                        load_head_w(h + 1)
                    sl = h % 2
                    for t in range(NT):
                        rope_proj(h, 1, t * 512, KT[:, t * 512:(t + 1) * 512], b_KT)
                    for kt in (range(NKT) if "vproj" not in SKIP else []):
                        bk = nextbank(0, 5)
                        mm_group(bank(bk)[:, 0:128], pbuf[bk], [(hT[:, kc, kt * 128:(kt + 1) * 128], WQ[sl][:, 2, kc, :]) for kc in range(NCH)], [b_WQ[sl], b_hT])
                        S.op("act", lambda e, bk=bk, kt=kt: e.activation(out=V_[:, kt, 0:128], in_=bank(bk)[:, 0:128], func=AF.Identity),
                             reads=[pbuf[bk]], writes=[b_V])
                    for t in range(QBL // 512):
                        rope_proj(h, 0, q0 + t * 512, QT[:, t * 512:(t + 1) * 512], b_QT)
                    for qg in range(QBL // 512):
                        def emit_qk_exp(kt, qg=qg):
                            pr = kt % 2
                            stv = ps[:, pr * 1024:(pr + 1) * 1024].rearrange("p (a b) -> p a b", a=2)
                            bst = [pbuf[2 * pr], pbuf[2 * pr + 1]]
                            S.op("pe", lambda e: e.matmul(stv[:, 0, :], lhsT=KT[0:64, kt * 128:(kt + 1) * 128], rhs=QT[0:64, qg * 512:(qg + 1) * 512],
                                                          start=True, stop=True), reads=[b_KT, b_QT], writes=[bst[0]], mark=False)
                            S.op("pe", lambda e: e.matmul(stv[:, 1, :], lhsT=KT[64:128, kt * 128:(kt + 1) * 128], rhs=QT[64:128, qg * 512:(qg + 1) * 512],
                                                          start=True, stop=True), reads=[b_KT, b_QT], writes=[bst[1]], mark=True)
                            pi = kt % 3
                            S.op("act", lambda e: e.activation(out=PT[pi], in_=stv, func=AF.Exp, scale=0.125), reads=bst, writes=[b_PT[pi]])

                        def emit_pv(kt):
                            pi = kt % 3
                            for a in range(8):
                                cm, qs = a // 4, a % 4
                                S.op("pe", lambda e, a=a, cm=cm, qs=qs: e.matmul(acc_ap(a), lhsT=PT[pi][:, cm, qs * 128:(qs + 1) * 128], rhs=V_[:, kt, 0:129],
                                                                                 start=(kt == 0 and a % 3 == 0), stop=(kt == NKT - 1), skip_group_check=True),
                                     reads=[b_PT[pi], b_V], writes=[pbuf[5 + a // 3]], mark=(a == 7))

                        emit_qk_exp(0)
                        for kt in range(NKT):
                            if kt + 1 < NKT:
                                emit_qk_exp(kt + 1)
                            emit_pv(kt)
                        if "norm" in SKIP:
                            continue
                        for bkk in range(3):
                            n_ = 3 if bkk < 2 else 2
                            src = ps[:, (5 + bkk) * 512:(5 + bkk) * 512 + 390].rearrange("p (a b) -> p a b", a=3)[:, 0:n_, 128]
                            S.op("dve", lambda e, bkk=bkk, n_=n_, src=src: e.reciprocal(out=nrm[:, 3 * bkk:3 * bkk + n_], in_=src),
                                 reads=[pbuf[5 + bkk]], writes=[b_nrm])
                        S.op("dve", lambda e: e.tensor_scalar(out=nrm[:, 8:12], in0=nrm[:, 4:8], scalar1=lamt[:, 4:5], scalar2=None, op0=ALU.mult),
                             reads=[b_nrm, b_const], writes=[b_nrm])
                        for qs in range(4):
                            S.op("act", lambda e, qs=qs: e.activation(out=O0[qs], in_=acc_ap(qs, 0, 128), func=AF.Identity, scale=nrm[:, qs:qs + 1]),
                                 reads=[pbuf[5 + qs // 3], b_nrm], writes=[b_O0[qs]])
                            S.op("dve", lambda e, qs=qs: e.scalar_tensor_tensor(out=OO[qs], in0=acc_ap(4 + qs, 0, 128), scalar=nrm[:, 8 + qs:9 + qs], in1=O0[qs],
                                                                               op0=ALU.mult, op1=ALU.add),
                                 reads=[pbuf[5 + (4 + qs) // 3], b_nrm, b_O0[qs]], writes=[b_OO[qs]])
                            S.op("act", lambda e, qs=qs: e.activation(out=SQJ, in_=OO[qs], func=AF.Square, accum_out=nrm[:, 12 + qs:13 + qs]),
                                 reads=[b_OO[qs]], writes=[b_SQJ, b_nrm])
                        S.op("act", lambda e: e.activation(out=nrm[:, 16:20], in_=nrm[:, 12:16], func=AF.Sqrt, bias=RMS_EPS, scale=1.0 / 128.0),
                             reads=[b_nrm], writes=[b_nrm])
                        S.op("dve", lambda e: e.reciprocal(out=nrm[:, 16:20], in_=nrm[:, 16:20]), reads=[b_nrm], writes=[b_nrm])
                        bk4 = 4
                        for qs in range(4):
                            S.op("dve", lambda e, qs=qs: e.scalar_tensor_tensor(out=AT[qs], in0=OO[qs], scalar=nrm[:, 16 + qs:17 + qs], in1=gsub[:],
                                                                               op0=ALU.mult, op1=ALU.mult),
                                 reads=[b_OO[qs], b_nrm, b_const], writes=[b_AT[qs]])
                            S.op("pe", lambda e, qs=qs: e.transpose(bank_bf(bk4)[:, qs * 128:(qs + 1) * 128], AT[qs], ident_bf[:]),
                                 reads=[b_AT[qs], b_const], writes=[pbuf[bk4]], mark=(qs == 3))
                        S.op("act", lambda e, h=h, qg=qg: e.activation(out=attnT[:, h, qg * 512:(qg + 1) * 512], in_=bank_bf(bk4)[:, 0:512], func=AF.Identity),
                             reads=[pbuf[bk4]], writes=[b_attnT])
                if dbg:
                    S.dma("sp", lambda e: e.dma_start(out=dbg_out["attnT"][:, :, tok0 + q0:tok0 + q0 + QBL], in_=attnT), ds_dbg, reads=[b_attnT], writes=[b_dbg])
                S.barrier()
                if stop == "p3":
                    break

                T = 256 if SL > 2048 else 512
                NST = T // 128
                al = Alloc(p4_off)
                LNP = al.get([128, 4, D], F32)
                b_LNP = Buf("LNP", const=False)
                ds_lnp = dsem("lnp")
                for i, v in enumerate([ln1g_d, ln1b_d, ln2g_d, ln2b_d]):
                    S.dma("sp", lambda e, i=i, v=v: e.dma_start(out=LNP[:, i, :], in_=v.broadcast_to([128, D])), ds_lnp, writes=[b_LNP])
                X1 = al.get([128, NST, D], F32)
                b_X1 = [Buf(f"X1_{i}") for i in range(NST)]
                ds_X1 = [dsem(f"x1_{i}") for i in range(NST)]
                PAN = [al.get([128, 4096], BF16) for _ in range(3)]
                b_PAN = [Buf(f"PAN{i}") for i in range(3)]
                ds_PAN = [dsem(f"pan{i}") for i in range(3)]
                TT_ = [al.get([128, T], F32) for _ in range(2)]
                b_TT = [Buf("TT0"), Buf("TT1")]
                XN2 = [al.get([128, D], BF16) for _ in range(2)]
                b_XN2 = [Buf("XN20"), Buf("XN21")]
                STS = [(al.get([128, 2, 6], F32), al.get([128, 2], F32), al.get([128, 2], F32), Buf(f"sts{i}")) for i in range(2)]
                ov = al.off
                MG = al.get([128, NCH, T], BF16)
                RT = al.get([128, NCH, T], BF16)
                SG = [al.get([128, T], F32) for _ in range(2)]
                AB = [al.get([128, T], F32) for _ in range(2)]
                e_end = al.off
                al.off = ov
                H2 = al.get([128, NCH, T], BF16)
                UT = al.get([128, NFF, T], BF16)
                SU = [al.get([128, T], F32) for _ in range(2)]
                b_MG = Buf("MG")
                b_RT = Buf("RT")
                ds_RT = dsem("rt")
                b_SG = [Buf("SG0"), Buf("SG1")]
                b_AB = [Buf("AB0"), Buf("AB1")]
                b_H2 = Buf("H2")
                b_UT = Buf("UT")
                b_SU = [Buf("SU0"), Buf("SU1")]
                b_yd = Buf("y_d")
                ds_y = [dsem(f"y{i}") for i in range(4)]

                NG = QBL // T
                panels = []
                for g in range(NG):
                    for oc in range(NCH):
                        panels.append(("a", g, oc))
                    for oc in range(NCH):
                        panels.append(("b", g, oc))
                    for j in range(NFF):
                        panels.append(("d", g, j))
                    for oc in range(NCH):
                        panels.append(("e", g, oc))
                pstate4 = {"issued": 0}

                def issue_panel(i):
                    kind, g, k = panels[i]
                    sl = i % 3
                    pv = PAN[sl]
                    if kind == "a":
                        v4 = pv.rearrange("p (a k n) -> p a k n", a=4, k=NCH)
                        srcs = [w_in_bf[(40 + k) * 128:(41 + k) * 128, :], w_in_bf[(48 + k) * 128:(49 + k) * 128, :],
                                w_ab_bf[k * 128:(k + 1) * 128, :], w_lb_bf[k * 128:(k + 1) * 128, :]]
                        for a_, s_ in enumerate(srcs):
                            S.dma("sp", lambda e, a_=a_, s_=s_, v4=v4: e.dma_start(out=v4[:, a_].rearrange("p k n -> p (k n)"), in_=s_),
                                  ds_PAN[sl], reads=[b_wbf], writes=[b_PAN[sl]])
                    elif kind == "b":
                        v3 = pv[:, 0:NCH * 128].rearrange("p (k n) -> p k n", k=NCH)
                        S.dma("sp", lambda e, v3=v3, k=k: e.dma_start(out=v3.rearrange("p k n -> p (k n)"), in_=w_out_bf[k * 128:(k + 1) * 128, :]),
                              ds_PAN[sl], reads=[b_wbf], writes=[b_PAN[sl]])
                    elif kind == "d":
                        v4 = pv[:, 0:2 * NCH * 128].rearrange("p (a k n) -> p a k n", a=2, k=NCH)
                        for a_ in range(2):
                            S.dma("sp", lambda e, a_=a_, v4=v4, k=k: e.dma_start(out=v4[:, a_].rearrange("p k n -> p (k n)"), in_=w_fi_bf[(a_ * NFF + k) * 128:(a_ * NFF + k + 1) * 128, :]),
                                  ds_PAN[sl], reads=[b_wbf], writes=[b_PAN[sl]])
                    else:
                        v3 = pv[:, 0:NFF * 128].rearrange("p (k n) -> p k n", k=NFF)
                        S.dma("sp", lambda e, v3=v3, k=k: e.dma_start(out=v3.rearrange("p k n -> p (k n)"), in_=w_fo_bf[k * 128:(k + 1) * 128, :]),
                              ds_PAN[sl], reads=[b_wbf], writes=[b_PAN[sl]])

                def get_panel():
                    i = pstate4["cur"]
                    while pstate4["issued"] < min(len(panels), i + 3):
                        issue_panel(pstate4["issued"])
                        pstate4["issued"] += 1
                    pstate4["cur"] = i + 1
                    return PAN[i % 3], b_PAN[i % 3]

                pstate4["cur"] = 0

                def residual_block(g, pv3, nk, rhs_fn, rhs_bufs, b_pan, gcol, tti):
                    bk = nextbank()
                    mm_group(bank(bk)[:, 0:T], pbuf[bk], [(pv3[:, kc, :], rhs_fn(kc)) for kc in range(nk)], [b_pan] + rhs_bufs)
                    tt = TT_[tti % 2]
                    btt = b_TT[tti % 2]
                    S.op("act", lambda e: e.activation(out=tt, in_=bank(bk)[:, 0:T], func=AF.Identity, scale=gcol),
                         reads=[pbuf[bk], b_const], writes=[btt])
                    bk2 = nextbank()
                    for s_ in range(NST):
                        S.op("pe", lambda e, s_=s_: e.transpose(bank(bk2)[:, s_ * 128:(s_ + 1) * 128], tt[:, s_ * 128:(s_ + 1) * 128], ident_f[:]),
                             reads=[btt, b_const], writes=[pbuf[bk2]], mark=(s_ == NST - 1))
                    return bk2

                for g in range(NG):
                    gt0 = q0 + g * T
                    gl0 = g * T
                    S.fence(["sp", "act", "dve", "pool"], [b_H2, b_UT, b_SU[0], b_SU[1]])
                    for s_ in range(NST):
                        S.dma("pool", lambda e, s_=s_: e.dma_start(out=X1[:, s_, :], in_=x_d[tok0 + gt0 + s_ * 128:tok0 + gt0 + (s_ + 1) * 128, :]),
                              ds_X1[s_], writes=[b_X1[s_]])
                    S.dma("sp", lambda e: e.dma_start(out=RT, in_=rec_d[:, :, tok0 + gt0:tok0 + gt0 + T].rearrange("c p t -> p c t")),
                          ds_RT, reads=[b_recd], writes=[b_RT])
                    for oc in range(NCH):
                        pv, bp = get_panel()
                        v4 = pv.rearrange("p (a k n) -> p a k n", a=4, k=NCH)
                        bks = []
                        order = [(0, lambda kc: hT[:, kc, gt0:gt0 + T], b_hT), (2, lambda kc: attnT[:, kc, gl0:gl0 + T], b_attnT),
                                 (1, lambda kc: hT[:, kc, gt0:gt0 + T], b_hT), (3, lambda kc: RT[:, kc, :], b_RT)]
                        for a_, rf, rb in order:
                            bk = nextbank()
                            mm_group(bank(bk)[:, 0:T], pbuf[bk], [(v4[:, a_, kc, :], rf(kc)) for kc in range(NCH)], [bp, rb])
                            bks.append(bk)
                        for half in range(2):
                            sg = SG[half]
                            ab = AB[half]
                            bg, bb = bks[2 * half], bks[2 * half + 1]
                            S.op("act", lambda e, sg=sg, bg=bg: e.activation(out=sg, in_=bank(bg)[:, 0:T], func=AF.Sigmoid),
                                 reads=[pbuf[bg]], writes=[b_SG[half]])
                            S.op("dve", lambda e, sg=sg, ab=ab, bb=bb: e.tensor_tensor(out=ab, in0=bank(bb)[:, 0:T], in1=sg, op=ALU.mult),
                                 reads=[pbuf[bb], b_SG[half]], writes=[b_AB[half]])
                        S.op("pool", lambda e, oc=oc: e.tensor_tensor(out=MG[:, oc, :], in0=AB[0], in1=AB[1], op=ALU.add),
                             reads=[b_AB[0], b_AB[1]], writes=[b_MG])
                    for oc in range(NCH):
                        pv, bp = get_panel()
                        v3 = pv[:, 0:NCH * 128].rearrange("p (k n) -> p k n", k=NCH)
                        bk2 = residual_block(g, v3, NCH, lambda kc: MG[:, kc, :], [b_MG], bp, g1c(oc), oc)
                        S.op("dve", lambda e, oc=oc, bk2=bk2: e.scalar_tensor_tensor(
                            out=X1[:, :, oc * 128:(oc + 1) * 128], in0=X1[:, :, oc * 128:(oc + 1) * 128], scalar=ALPHA,
                            in1=bank(bk2)[:, 0:T].rearrange("p (s n) -> p s n", s=NST), op0=ALU.mult, op1=ALU.add),
                            reads=[pbuf[bk2]] + b_X1, writes=b_X1)
                    S.fence(["act", "dve"], [b_MG, b_RT, b_SG[0], b_SG[1], b_AB[0], b_AB[1]])
                    bks_c = [nextbank() for _ in range(4)]
                    for s_ in range(NST):
                        st6, mv, rs, b_st = STS[s_ % 2]
                        xs_ap = X1[:, s_, :]
                        ln_stats(xs_ap, b_X1[s_], st6, mv, rs, b_st)
                        S.op("act", lambda e, xs_ap=xs_ap, rs=rs: e.activation(out=xs_ap, in_=xs_ap, func=AF.Identity, scale=rs[:, 0:1], bias=rs[:, 1:2]),
                             reads=[b_X1[s_], b_st], writes=[b_X1[s_]])
                        S.op("dve", lambda e, xs_ap=xs_ap: e.tensor_tensor(out=xs_ap, in0=xs_ap, in1=LNP[:, 0, :], op=ALU.mult),
                             reads=[b_X1[s_], b_LNP], writes=[b_X1[s_]])
                        S.op("pool", lambda e, xs_ap=xs_ap: e.tensor_tensor(out=xs_ap, in0=xs_ap, in1=LNP[:, 1, :], op=ALU.add),
                             reads=[b_X1[s_], b_LNP], writes=[b_X1[s_]])
                        ln_stats(xs_ap, b_X1[s_], st6, mv, rs, b_st)
                        xn = XN2[s_ % 2]
                        S.op("act", lambda e, xs_ap=xs_ap, rs=rs, xn=xn: e.activation(out=xn, in_=xs_ap, func=AF.Identity, scale=rs[:, 0:1], bias=rs[:, 1:2]),
                             reads=[b_X1[s_], b_st], writes=[b_XN2[s_ % 2]])
                        for c in range(NCH):
                            bk = bks_c[c // 2]
                            S.op("pe", lambda e, bk=bk, c=c, s_=s_, xn=xn: e.transpose(
                                bank_bf(bk)[:, (c % 2) * 512 + s_ * 128:(c % 2) * 512 + (s_ + 1) * 128], xn[:, c * 128:(c + 1) * 128], ident_bf[:]),
                                reads=[b_XN2[s_ % 2], b_const], writes=[pbuf[bk]], mark=(c % 2 == 1))
                    for c in range(NCH):
                        bk = bks_c[c // 2]
                        S.op("act", lambda e, bk=bk, c=c: e.activation(out=H2[:, c, :], in_=bank_bf(bk)[:, (c % 2) * 512:(c % 2) * 512 + T],
                                                                       func=AF.Identity, scale=sc2p(c), bias=sh2(c)),
                             reads=[pbuf[bk], b_const], writes=[b_H2])
                    for j in range(NFF):
                        pv, bp = get_panel()
                        v4 = pv[:, 0:2 * NCH * 128].rearrange("p (a k n) -> p a k n", a=2, k=NCH)
                        bkg = nextbank()
                        mm_group(bank(bkg)[:, 0:T], pbuf[bkg], [(v4[:, 0, kc, :], H2[:, kc, :]) for kc in range(NCH)], [bp, b_H2])
                        bku = nextbank()
                        mm_group(bank(bku)[:, 0:T], pbuf[bku], [(v4[:, 1, kc, :], H2[:, kc, :]) for kc in range(NCH)], [bp, b_H2])
                        su = SU[j % 2]
                        S.op("act", lambda e, su=su, bkg=bkg: e.activation(out=su, in_=bank(bkg)[:, 0:T], func=AF.Silu),
                             reads=[pbuf[bkg]], writes=[b_SU[j % 2]])
                        S.op("dve", lambda e, su=su, bku=bku, j=j: e.tensor_tensor(out=UT[:, j, :], in0=bank(bku)[:, 0:T], in1=su, op=ALU.mult),
                             reads=[pbuf[bku], b_SU[j % 2]], writes=[b_UT])
                    for oc in range(NCH):
                        pv, bp = get_panel()
                        v3 = pv[:, 0:NFF * 128].rearrange("p (k n) -> p k n", k=NFF)
                        bk2 = residual_block(g, v3, NFF, lambda kc: UT[:, kc, :], [b_UT], bp, g2c(oc), oc)
                        S.op("dve", lambda e, oc=oc, bk2=bk2: e.scalar_tensor_tensor(
                            out=X1[:, :, oc * 128:(oc + 1) * 128], in0=X1[:, :, oc * 128:(oc + 1) * 128], scalar=ALPHA,
                            in1=bank(bk2)[:, 0:T].rearrange("p (s n) -> p s n", s=NST), op0=ALU.mult, op1=ALU.add),
                            reads=[pbuf[bk2]] + b_X1, writes=b_X1)
                    for s_ in range(NST):
                        st6, mv, rs, b_st = STS[s_ % 2]
                        xs_ap = X1[:, s_, :]
                        ln_stats(xs_ap, b_X1[s_], st6, mv, rs, b_st)
                        S.op("act", lambda e, xs_ap=xs_ap, rs=rs: e.activation(out=xs_ap, in_=xs_ap, func=AF.Identity, scale=rs[:, 0:1], bias=rs[:, 1:2]),
                             reads=[b_X1[s_], b_st], writes=[b_X1[s_]])
                        S.op("dve", lambda e, xs_ap=xs_ap: e.tensor_tensor(out=xs_ap, in0=xs_ap, in1=LNP[:, 2, :], op=ALU.mult),
                             reads=[b_X1[s_], b_LNP], writes=[b_X1[s_]])
                        S.op("pool", lambda e, xs_ap=xs_ap: e.tensor_tensor(out=xs_ap, in0=xs_ap, in1=LNP[:, 3, :], op=ALU.add),
                             reads=[b_X1[s_], b_LNP], writes=[b_X1[s_]])
                        S.dma("pool", lambda e, s_=s_, xs_ap=xs_ap: e.dma_start(out=y_d[tok0 + gt0 + s_ * 128:tok0 + gt0 + (s_ + 1) * 128, :], in_=xs_ap),
                              ds_y[s_], reads=[b_X1[s_]], writes=[b_yd])
                S.barrier()
        S.barrier()
        S.emit()
    return nc


def _rope_tables(smax):
    inv = (1.0 / (np.float32(10000.0) ** (np.arange(0, 64, 2, dtype=np.float32) / np.float32(64)))).astype(np.float32)
    ang = (np.arange(smax, dtype=np.float32)[:, None] * inv[None, :]).astype(np.float32)
    cos = np.cos(ang).astype(np.float32)
    sin = np.sin(ang).astype(np.float32)
    c = np.zeros((128, smax), np.float32)
    s = np.zeros((128, smax), np.float32)
    for p in range(128):
        d = p % 64
        j = d % 32
        c[p] = cos[:, j]
        s[p] = (-sin[:, j]) if d < 32 else sin[:, j]
    return c, s


def _consts(smax):
    ident = np.eye(128, dtype=np.float32)
    prot = np.zeros((128, 128), np.float32)
    for m in range(128):
        blk = (m // 64) * 64
        d = m % 64
        k = blk + ((d + 32) % 64)
        prot[k, m] = 1.0
    c, s = _rope_tables(smax)
    return ident, prot, c, s


def make_in_maps(inputs, n_cores, seq_plan):
    f = lambda a: np.ascontiguousarray(np.asarray(a, dtype=np.float32))
    def pan(a, nk):
        a = f(a)
        ncb = a.shape[1] // 128
        return np.ascontiguousarray(a.reshape(nk, 128, ncb, 128).transpose(2, 1, 0, 3).reshape(ncb * 128, nk * 128))

    w = {
        "w_ada": f(inputs["w_ada"][0]), "b_ada": f(inputs["b_ada"][0]).reshape(1, -1), "w_in": pan(inputs["w_in"][0], 8),
        "lq1": f(inputs["lambda_q1"][0]).reshape(1, 64), "lk1": f(inputs["lambda_k1"][0]).reshape(1, 64),
        "lq2": f(inputs["lambda_q2"][0]).reshape(1, 64), "lk2": f(inputs["lambda_k2"][0]).reshape(1, 64),
        "subln_g": f(inputs["subln_g"][0]).reshape(1, 128), "conv_w": f(inputs["conv_w"][0]), "conv_b": f(inputs["conv_b"][0]).reshape(1, -1),
        "w_lru_gates": f(inputs["w_lru_gates"][0]).reshape(4, 16, 64, 64), "b_lru_gates": f(inputs["b_lru_gates"][0]).reshape(4, -1),
        "lru_lambda": f(inputs["lru_lambda"][0]), "w_ab": pan(inputs["w_attn_branch"][0], 8), "w_lb": pan(inputs["w_lru_branch"][0], 8),
        "w_out": pan(inputs["w_out"][0], 8), "ln1_g": f(inputs["ln1_g"][0]).reshape(1, -1), "ln1_b": f(inputs["ln1_b"][0]).reshape(1, -1),
        "w_ffn_in": pan(inputs["w_ffn_in"][0], 8), "w_ffn_out": pan(inputs["w_ffn_out"][0], 22),
        "ln2_g": f(inputs["ln2_g"][0]).reshape(1, -1), "ln2_b": f(inputs["ln2_b"][0]).reshape(1, -1),
    }
    maps = []
    for core in range(n_cores):
        plan = seq_plan(core)
        smax = max(p[0].shape[0] for p in plan)
        ident, prot, c, s = _consts(smax)
        m = dict(w)
        m["x"] = np.ascontiguousarray(np.concatenate([p[0] for p in plan], axis=0))
        m["c"] = np.ascontiguousarray(np.stack([p[1] for p in plan], axis=0))
        m["ident"] = ident
        m["prot"] = prot
        m["rope_c"] = c
        m["rope_s"] = s
        maps.append(m)
    return maps


_NC_CACHE = {}


def kernel(**inputs):
    n = 8
    xp = np.asarray(inputs["x_prompt"], dtype=np.float32)
    xs = np.asarray(inputs["x_sample"], dtype=np.float32)
    cp = np.asarray(inputs["c_prompt"], dtype=np.float32)
    cs = np.asarray(inputs["c_sample"], dtype=np.float32)
    B, SP, _ = xp.shape
    DB, SS, _ = xs.shape
    per = DB // n
    seq_lens = [SP] + [SS] * per

    def plan(core):
        return [(xp[core], cp[core])] + [(xs[core * per + i], cs[core * per + i]) for i in range(per)]

    key = tuple(seq_lens)
    if key not in _NC_CACHE:
        _NC_CACHE[key] = build(seq_lens)
    nc = _NC_CACHE[key]
    maps = make_in_maps(inputs, n, plan)
    res = run_bass_kernel_spmd(nc, maps, core_ids=list(range(n)))
    yp = np.empty((B, SP, D), np.float32)
    ys = np.empty((DB, SS, D), np.float32)
    for core in range(n):
        y = np.asarray(res.results[core]["y"], dtype=np.float32)
        yp[core] = y[0:SP]
        ys[core * per:(core + 1) * per] = y[SP:].reshape(per, SS, D)
    return (yp, ys)
```

```python
import numpy as np
from contextlib import ExitStack
import concourse.bass as bass
import concourse.mybir as mybir
from concourse.bass_utils import run_bass_kernel_spmd

F32 = mybir.dt.float32
BF16 = mybir.dt.bfloat16
AF = mybir.ActivationFunctionType
ALU = mybir.AluOpType

ENGS = ("pe", "act", "dve", "pool", "sp")
D = 1024
NCH = 8
DFF = 2816
NFF = 22
ALPHA = 2.0 ** 0.25
LN_EPS = 1e-5
RMS_EPS = 1e-5
LAMBDA_INIT = 0.2
QB = 2048
ARENA_BYTES = 174 * 1024


class Sem:
    def __init__(self, h, name):
        self.h = h
        self.name = name
        self.count = 0


class Buf:
    __slots__ = ("name", "w", "r", "const", "excl")

    def __init__(self, name, const=False, excl=False):
        self.name = name
        self.w = None
        self.r = {}
        self.const = const
        self.excl = excl


class _Rec:
    def __init__(self):
        self.call = None

    def __getattr__(self, name):
        def f(*a, **k):
            self.call = (name, a, k)
            return self
        return f


def _record(fn):
    r = _Rec()
    fn(r)
    return r.call


class Sched:
    def __init__(self, nc, stack):
        self.nc = nc
        self.stack = stack
        self.prog = {e: [] for e in ENGS}
        self.sems = []
        self.esem = {e: self.new_sem("e_" + e) for e in ENGS if e != "sp"}
        self.waited = {e: {} for e in ENGS}

    def new_sem(self, name):
        s = Sem(self.stack.enter_context(self.nc.semaphore(name)), name)
        self.sems.append(s)
        return s

    def _deps(self, eng, reads, writes):
        deps = {}
        for b in reads:
            if b.w is not None:
                s, v = b.w
                if deps.get(s, 0) < v:
                    deps[s] = v
        for b in writes:
            if b.w is not None:
                s, v = b.w
                if deps.get(s, 0) < v:
                    deps[s] = v
            for s, v in b.r.items():
                if deps.get(s, 0) < v:
                    deps[s] = v
        out = []
        wd = self.waited[eng]
        own = self.esem.get(eng)
        for s, v in deps.items():
            if s is own and eng == "pe":
                continue
            if wd.get(s, 0) >= v:
                continue
            assert s.count >= v, f"wait on future tick: eng={eng} sem={s.name} v={v} count={s.count}"
            wd[s] = v
            out.append((s, v))
        return out

    def op(self, eng, fn, reads=(), writes=(), mark=True):
        ex = [b for b in reads if b.excl]
        if ex:
            reads = [b for b in reads if not b.excl]
            writes = list(writes) + [b for b in ex if b not in writes]
        waits = self._deps(eng, reads, writes)
        sem = self.esem[eng]
        if mark:
            sem.count += 1
            tick = sem.count
        else:
            tick = sem.count + 1
        self.prog[eng].append((waits, _record(fn), (sem, 1) if mark else None))
        for b in reads:
            if not b.const and b.r.get(sem, 0) < tick:
                b.r[sem] = tick
        for b in writes:
            b.w = (sem, tick)
            b.r = {}

    def dma(self, queue, fn, dsem, reads=(), writes=()):
        waits = self._deps(queue, reads, writes)
        dsem.count += 16
        v = dsem.count
        self.prog[queue].append((waits, _record(fn), (dsem, 16)))
        for b in reads:
            if not b.const and b.r.get(dsem, 0) < v:
                b.r[dsem] = v
        for b in writes:
            b.w = (dsem, v)
            b.r = {}

    def fence(self, engs, bufs):
        for e in engs:
            waits = self._deps(e, [], bufs)
            self.prog[e].append((waits, None, None))

    def barrier(self, exclude=()):
        for e in ENGS:
            wd = self.waited[e]
            waits = []
            for s in self.sems:
                if s is self.esem.get(e) or s in exclude:
                    continue
                if s.count > wd.get(s, 0):
                    wd[s] = s.count
                    waits.append((s, s.count))
            self.prog[e].append((waits, None, None))

    def emit(self):
        nc = self.nc
        with nc.Block() as block:
            deco = {"pe": block.tensor, "act": block.scalar, "dve": block.vector, "pool": block.gpsimd, "sp": block.sync}
            for e in ENGS:
                prog = self.prog[e]

                def body(eng, prog=prog):
                    for waits, fn, inc in prog:
                        for s, v in waits:
                            eng.wait_ge(s.h, v)
                        if fn is not None:
                            ins = getattr(eng, fn[0])(*fn[1], **fn[2])
                            if inc is not None:
                                ins.then_inc(inc[0].h, inc[1])

                deco[e](body)


class _Stop(Exception):
    pass


def build(seq_lens, dbg=False, stop=None):
    nc = bass.Bass("TRN2", target_bir_lowering=False)
    NSEQ = len(seq_lens)
    NTOK = sum(seq_lens)
    SMAX = max(seq_lens)

    def din(name, shape, dt=F32):
        return nc.dram_tensor(name, list(shape), dt, kind="ExternalInput").ap()

    def dint(name, shape, dt):
        return nc.dram_tensor(name, list(shape), dt, kind="Internal").ap()

    x_d = din("x", [NTOK, D])
    c_d = din("c", [NSEQ, D])
    w_ada_d = din("w_ada", [D, 6 * D])
    b_ada_d = din("b_ada", [1, 6 * D])
    w_in_d = din("w_in", [56 * 128, 1024])
    lq1_d = din("lq1", [1, 64])
    lk1_d = din("lk1", [1, 64])
    lq2_d = din("lq2", [1, 64])
    lk2_d = din("lk2", [1, 64])
    subln_d = din("subln_g", [1, 128])
    conv_w_d = din("conv_w", [4, D])
    conv_b_d = din("conv_b", [1, D])
    wlg_d = din("w_lru_gates", [4, 16, 64, 64])
    blg_d = din("b_lru_gates", [4, D])
    llam_d = din("lru_lambda", [2, D])
    w_ab_d = din("w_ab", [8 * 128, 1024])
    w_lb_d = din("w_lb", [8 * 128, 1024])
    w_out_d = din("w_out", [8 * 128, 1024])
    ln1g_d = din("ln1_g", [1, D])
    ln1b_d = din("ln1_b", [1, D])
    w_fi_d = din("w_ffn_in", [44 * 128, 1024])
    w_fo_d = din("w_ffn_out", [8 * 128, DFF])
    ln2g_d = din("ln2_g", [1, D])
    ln2b_d = din("ln2_b", [1, D])
    ident_d = din("ident", [128, 128])
    prot_d = din("prot", [128, 128])
    ropec_d = din("rope_c", [128, SMAX])
    ropes_d = din("rope_s", [128, SMAX])
    y_d = nc.dram_tensor("y", [NTOK, D], F32, kind="ExternalOutput").ap()

    w_in_bf = dint("w_in_bf", [56 * 128, 1024], BF16)
    w_ab_bf = dint("w_ab_bf", [8 * 128, 1024], BF16)
    w_lb_bf = dint("w_lb_bf", [8 * 128, 1024], BF16)
    w_out_bf = dint("w_out_bf", [8 * 128, 1024], BF16)
    w_fi_bf = dint("w_fi_bf", [44 * 128, 1024], BF16)
    w_fo_bf = dint("w_fo_bf", [8 * 128, DFF], BF16)
    mod_d = dint("mod_d", [NSEQ, 6 * D], F32)
    lruw_d = dint("lruw_d", [128, NCH, 8, 128], BF16)
    rec_d = dint("rec_d", [NCH, 128, NTOK], BF16)

    dbg_out = {}
    if dbg:
        dbg_out["hT"] = nc.dram_tensor("dbg_hT", [128, NCH, NTOK], BF16, kind="ExternalOutput").ap()
        dbg_out["attnT"] = nc.dram_tensor("dbg_attnT", [128, NCH, NTOK], BF16, kind="ExternalOutput").ap()
        dbg_out["modT"] = nc.dram_tensor("dbg_modT", [128, 48, NSEQ], F32, kind="ExternalOutput").ap()
        dbg_out["rec"] = nc.dram_tensor("dbg_rec", [NCH, 128, NTOK], BF16, kind="ExternalOutput").ap()

    with ExitStack() as st:
        S = Sched(nc, st)

        def sb(name, shape, dt=F32):
            return st.enter_context(nc.sbuf_tensor(name, list(shape), dt))

        ident_bf = sb("ident_bf", [128, 128], BF16)
        ident_f = sb("ident_f", [128, 128], F32)
        prot_bf = sb("prot_bf", [128, 128], BF16)
        fmc = sb("fmc", [128, 11, NCH], F32)
        cdec = sb("cdec", [128, 4, NCH], F32)
        hbias = sb("hbias", [128, 4, NCH], F32)
        modT = sb("modT", [128, 48, NSEQ], F32)
        lamt = sb("lamt", [128, 8], F32)
        gsub = sb("gsub", [128, 128], F32)
        small = sb("small", [128, 64], F32)
        arena = sb("arena", [128, ARENA_BYTES // 2], BF16)
        ps = st.enter_context(nc.psum_tensor("ps", [128, 8 * 512], F32))

        b_const = Buf("const")
        b_hT = Buf("hT")

        def bank(b):
            return ps[:, b * 512:(b + 1) * 512]

        def bank_bf(b):
            return ps[:, b * 512:(b + 1) * 512].bitcast(BF16)

        pbuf = [Buf(f"psb{i}", excl=True) for i in range(8)]
        pstate = {"next": 0}

        def nextbank(lo=0, hi=8):
            n = pstate["next"]
            if n < lo or n >= hi:
                n = lo
            pstate["next"] = n + 1
            return n

        class Alloc:
            def __init__(self, off=0):
                self.off = off

            def get(self, shape, dt, name="t"):
                n = int(np.prod(shape[1:]))
                esz = 4 if dt == F32 else 2
                nbytes = (n * esz + 3) // 4 * 4
                assert self.off + nbytes <= ARENA_BYTES, f"arena overflow {name} {self.off + nbytes}"
                v = arena[0:shape[0], self.off // 2:(self.off + n * esz) // 2]
                if dt == F32:
                    v = v.bitcast(F32)
                if len(shape) == 3:
                    v = v.rearrange("p (a b) -> p a b", a=shape[1])
                elif len(shape) == 4:
                    v = v.rearrange("p (a b c) -> p a b c", a=shape[1], b=shape[2])
                self.off += nbytes
                return v

        sem_pool = {}

        def dsem(name):
            if name not in sem_pool:
                sem_pool[name] = S.new_sem("d_" + name)
            return sem_pool[name]

        ds_setup = dsem("setup")
        ds_cast = dsem("cast")
        b_wbf = Buf("wbf")

        import os
        SKIP = set(os.environ.get("KSKIP", "").split(","))

        cast_i = [0]
        ds_castk = [dsem("castk0"), dsem("castk1")]
        b_castk = [Buf("castk0"), Buf("castk1")]

        def cast_rows(dst, src, rows, blk):
            if "cast" in SKIP:
                return
            for r0 in range(0, rows, blk):
                r1 = min(rows, r0 + blk)
                k = cast_i[0] % 2
                cast_i[0] += 1
                S.dma("pool", lambda e, r0=r0, r1=r1: e.dma_start(out=dst[r0:r1, :], in_=src[r0:r1, :], max_dma_last_dim=4096),
                      ds_castk[k], writes=[b_castk[k]])

        al = Alloc()
        lv = al.get([128, 4, 64], F32)
        junk = al.get([128, 2, 64], F32)
        junk2 = al.get([128, 64], F32)
        wb = al.get([128, NCH, 8, 128], BF16)
        cT = al.get([128, NCH, NSEQ], F32)
        ones1 = al.get([1, NSEQ], F32)
        bada = al.get([1, 6 * D], F32)
        mod_sb = al.get([NSEQ, 6 * D], F32)
        wpan = [al.get([128, NCH, 512], F32) for _ in range(2)]
        b_wb = Buf("wb")
        b_lruw = Buf("lruw")
        b_modd = Buf("mod_d")
        b_mod = Buf("mod_sb")
        b_wpan = [Buf("wpan0"), Buf("wpan1")]
        ds_wp = [dsem("wp0"), dsem("wp1")]
        ds_wbd = dsem("wbd")

        S.op("pool", lambda e: e.memset(wb, 0.0), writes=[b_wb])
        S.dma("pool", lambda e: e.dma_start(out=ident_bf[:], in_=ident_d), ds_setup, writes=[b_const])
        S.dma("pool", lambda e: e.dma_start(out=prot_bf[:], in_=prot_d), ds_setup, writes=[b_const])
        for dg in (range(4) if "wbdma" not in SKIP else []):
            for j in range(2):
                src = wlg_d[dg].rearrange("(c j) d e -> j d c e", j=2)[j]
                S.dma("pool", lambda e, dg=dg, j=j, src=src: e.dma_start(out=wb[64 * j:64 * j + 64, :, dg, 64 * j:64 * j + 64], in_=src),
                      ds_wbd, reads=[], writes=[b_wb])
        cast_rows(w_in_bf, w_in_d, 56 * 128, 1024)
        S.dma("sp", lambda e: e.dma_start(out=ident_f[:], in_=ident_d), ds_setup, writes=[b_const])
        for s_ in range(NSEQ):
            S.dma("sp", lambda e, s_=s_: e.dma_start(out=cT[:, :, s_], in_=c_d[s_:s_ + 1, :].rearrange("o (c p) -> p (o c)", p=128),
                                                    allow_slow_non_contiguous=True), ds_setup, writes=[b_const])
        S.dma("sp", lambda e: e.dma_start(out=bada, in_=b_ada_d), ds_setup, writes=[b_const])
        vecs = [conv_w_d[0:1, :], conv_w_d[1:2, :], conv_w_d[2:3, :], conv_w_d[3:4, :], conv_b_d,
                blg_d[0:1, :], blg_d[1:2, :], blg_d[2:3, :], blg_d[3:4, :], llam_d[0:1, :], llam_d[1:2, :]]
        for i, v in (enumerate(vecs) if "slow" not in SKIP else []):
            S.dma("sp", lambda e, i=i, v=v: e.dma_start(out=fmc[:, i, :], in_=v.rearrange("o (c p) -> p (o c)", p=128),
                                                       allow_slow_non_contiguous=True), ds_setup, writes=[b_const])
        for i, v in (enumerate([lq1_d, lk1_d, lq2_d, lk2_d]) if "bcast" not in SKIP else []):
            S.dma("sp", lambda e, i=i, v=v: e.dma_start(out=lv[:, i, :], in_=v.broadcast_to([128, 64])), ds_setup, writes=[b_const])
        if "bcast" not in SKIP:
            S.dma("sp", lambda e: e.dma_start(out=gsub[:], in_=subln_d.broadcast_to([128, 128])), ds_setup, writes=[b_const])
        cast_rows(w_ab_bf, w_ab_d, 1024, 1024)
        cast_rows(w_lb_bf, w_lb_d, 1024, 1024)
        cast_rows(w_out_bf, w_out_d, 1024, 1024)
        cast_rows(w_fi_bf, w_fi_d, 44 * 128, 1024)
        cast_rows(w_fo_bf, w_fo_d, 1024, 512)
        S.barrier(exclude=ds_castk)
        S.op("dve", lambda e: e.tensor_tensor(out=junk[:, 0, :], in0=lv[:, 0, :], in1=lv[:, 1, :], op=ALU.mult), writes=[b_const])
        S.op("dve", lambda e: e.tensor_tensor(out=junk[:, 1, :], in0=lv[:, 2, :], in1=lv[:, 3, :], op=ALU.mult), reads=[b_const], writes=[b_const])
        S.op("dve", lambda e: e.tensor_scalar(out=gsub[:], in0=gsub[:], scalar1=1.0 - LAMBDA_INIT, scalar2=None, op0=ALU.mult),
             reads=[b_const], writes=[b_const])
        S.op("dve", lambda e: e.memset(ones1, 1.0), reads=[b_const], writes=[b_const])
        for c in range(NCH):
            for k in range(4):
                S.op("dve", lambda e, c=c, k=k: e.tensor_scalar(out=wb[:, c, 4 + k, :], in0=ident_bf[:], scalar1=fmc[:, k, c:c + 1],
                                                                scalar2=None, op0=ALU.mult), reads=[b_wb], writes=[b_wb])
        S.op("act", lambda e: e.activation(out=cT, in_=cT, func=AF.Silu), writes=[b_mod])
        S.op("act", lambda e: e.activation(out=small[:, 0:16], in_=fmc[:, 9:11, :].rearrange("p a c -> p (a c)"), func=AF.Exp, scale=-1.0),
             reads=[b_mod], writes=[b_mod])
        S.barrier(exclude=ds_castk)
        S.op("act", lambda e: e.activation(out=junk2, in_=junk[:, 0, :], func=AF.Identity, accum_out=lamt[:, 0:1]), writes=[b_mod])
        S.op("act", lambda e: e.activation(out=junk2, in_=junk[:, 1, :], func=AF.Identity, accum_out=lamt[:, 1:2]), reads=[b_mod], writes=[b_mod])
        S.op("act", lambda e: e.activation(out=lamt[:, 2:4], in_=lamt[:, 0:2], func=AF.Exp), reads=[b_mod], writes=[b_mod])
        S.op("act", lambda e: e.activation(out=small[:, 16:32], in_=small[:, 0:16], func=AF.Ln, bias=1.0, scale=1.0), reads=[b_mod], writes=[b_mod])
        S.dma("sp", lambda e: e.dma_start(out=lruw_d, in_=wb), ds_wbd, reads=[b_wb], writes=[b_lruw])
        S.barrier(exclude=ds_castk)
        S.op("dve", lambda e: e.tensor_tensor(out=lamt[:, 4:5], in0=lamt[:, 3:4], in1=lamt[:, 2:3], op=ALU.subtract), writes=[b_const])
        S.op("dve", lambda e: e.tensor_scalar(out=lamt[:, 4:5], in0=lamt[:, 4:5], scalar1=-LAMBDA_INIT, scalar2=None, op0=ALU.add),
             reads=[b_const], writes=[b_const])
        S.op("dve", lambda e: e.tensor_scalar(out=hbias[:], in0=fmc[:, 5:9, :], scalar1=0.5, scalar2=None, op0=ALU.mult), reads=[b_const], writes=[b_const])
        for d_ in range(2):
            S.op("dve", lambda e, d_=d_: e.tensor_scalar(out=cdec[:, 2 * d_, :], in0=small[:, 16 + 8 * d_:24 + 8 * d_], scalar1=-4.0,
                                                         scalar2=None, op0=ALU.mult), reads=[b_const], writes=[b_const])
            S.op("dve", lambda e, d_=d_: e.tensor_scalar(out=cdec[:, 2 * d_ + 1, :], in0=small[:, 16 + 8 * d_:24 + 8 * d_], scalar1=-8.0,
                                                         scalar2=None, op0=ALU.mult), reads=[b_const], writes=[b_const])
        for pn in (range(12) if "mod" not in SKIP else []):
            sl = pn % 2
            S.dma("sp", lambda e, pn=pn, sl=sl: e.dma_start(out=wpan[sl], in_=w_ada_d[:, pn * 512:(pn + 1) * 512].rearrange("(c p) n -> p c n", p=128)),
                  ds_wp[sl], writes=[b_wpan[sl]])
            bk = nextbank()
            for kc in range(NCH):
                S.op("pe", lambda e, bk=bk, kc=kc, sl=sl: e.matmul(bank(bk)[0:NSEQ, :], lhsT=cT[:, kc, :], rhs=wpan[sl][:, kc, :],
                                                                   start=(kc == 0), stop=False),
                     reads=[b_wpan[sl]], writes=[pbuf[bk]], mark=False)
            S.op("pe", lambda e, bk=bk, pn=pn: e.matmul(bank(bk)[0:NSEQ, :], lhsT=ones1, rhs=bada[:, pn * 512:(pn + 1) * 512],
                                                        start=False, stop=True), reads=[], writes=[pbuf[bk]])
            S.op("act", lambda e, bk=bk, pn=pn: e.activation(out=mod_sb[:, pn * 512:(pn + 1) * 512], in_=bank(bk)[0:NSEQ, :], func=AF.Identity),
                 reads=[pbuf[bk]], writes=[b_mod])
        S.dma("sp", lambda e: e.dma_start(out=mod_d, in_=mod_sb), ds_setup, reads=[b_mod], writes=[b_modd])
        for s_ in range(NSEQ):
            S.dma("sp", lambda e, s_=s_: e.dma_start(out=modT[:, :, s_], in_=mod_d[s_:s_ + 1, :].rearrange("o (c p) -> p (o c)", p=128),
                                                    allow_slow_non_contiguous=True), ds_setup, reads=[b_modd], writes=[b_modd])
        for lo in (8, 32):
            S.op("dve", lambda e, lo=lo: e.tensor_scalar(out=modT[:, lo:lo + 8, :], in0=modT[:, lo:lo + 8, :], scalar1=1.0, scalar2=None, op0=ALU.add),
                 reads=[b_modd], writes=[b_modd])
        ds_dbg = dsem("dbg")
        b_dbg = Buf("dbg")
        if dbg:
            S.dma("sp", lambda e: e.dma_start(out=dbg_out["modT"], in_=modT[:]), ds_dbg, reads=[b_modd], writes=[b_dbg])
        S.barrier()
        b_const = Buf("const2", const=True)
        b_wbf = Buf("wbf2", const=True)
        b_lruw = Buf("lruw2", const=True)
        b_recd_dummy = None

        def ln_stats(src_ap, b_src, st6, mv, rs, b_st):
            S.op("dve", lambda e: e.bn_stats(out=st6[:, 0, :], in_=src_ap[:, 0:512]), reads=[b_src], writes=[b_st])
            S.op("dve", lambda e: e.bn_stats(out=st6[:, 1, :], in_=src_ap[:, 512:1024]), reads=[b_src], writes=[b_st])
            S.op("dve", lambda e: e.bn_aggr(out=mv, in_=st6.rearrange("p a b -> p (a b)")), reads=[b_st], writes=[b_st])
            S.op("act", lambda e: e.activation(out=rs[:, 0:1], in_=mv[:, 1:2], func=AF.Sqrt, bias=LN_EPS, scale=1.0), reads=[b_st], writes=[b_st])
            S.op("dve", lambda e: e.reciprocal(out=rs[:, 0:1], in_=rs[:, 0:1]), reads=[b_st], writes=[b_st])
            S.op("dve", lambda e: e.tensor_scalar(out=rs[:, 1:2], in0=mv[:, 0:1], scalar1=rs[:, 0:1], scalar2=-1.0, op0=ALU.mult, op1=ALU.mult),
                 reads=[b_st], writes=[b_st])

        def ln_stats_multi(items):
            for src_ap, b_src, st6, mv, rs, b_st in items:
                S.op("dve", lambda e: e.bn_stats(out=st6[:, 0, :], in_=src_ap[:, 0:512]), reads=[b_src], writes=[b_st])
                S.op("dve", lambda e: e.bn_stats(out=st6[:, 1, :], in_=src_ap[:, 512:1024]), reads=[b_src], writes=[b_st])
                S.op("dve", lambda e: e.bn_aggr(out=mv, in_=st6.rearrange("p a b -> p (a b)")), reads=[b_st], writes=[b_st])
            for src_ap, b_src, st6, mv, rs, b_st in items:
                S.op("act", lambda e: e.activation(out=rs[:, 0:1], in_=mv[:, 1:2], func=AF.Sqrt, bias=LN_EPS, scale=1.0), reads=[b_st], writes=[b_st])
            for src_ap, b_src, st6, mv, rs, b_st in items:
                S.op("dve", lambda e: e.reciprocal(out=rs[:, 0:1], in_=rs[:, 0:1]), reads=[b_st], writes=[b_st])
                S.op("dve", lambda e: e.tensor_scalar(out=rs[:, 1:2], in0=mv[:, 0:1], scalar1=rs[:, 0:1], scalar2=-1.0, op0=ALU.mult, op1=ALU.mult),
                     reads=[b_st], writes=[b_st])

        def ln_affine_multi(items, gi, bi_, LNP, b_LNP):
            ln_stats_multi(items)
            for src_ap, b_src, st6, mv, rs, b_st in items:
                S.op("act", lambda e: e.activation(out=src_ap, in_=src_ap, func=AF.Identity, scale=rs[:, 0:1], bias=rs[:, 1:2]),
                     reads=[b_src, b_st], writes=[b_src])
            for src_ap, b_src, st6, mv, rs, b_st in items:
                S.op("dve", lambda e: e.tensor_tensor(out=src_ap, in0=src_ap, in1=LNP[:, gi, :], op=ALU.mult), reads=[b_src, b_LNP], writes=[b_src])
            for i, (src_ap, b_src, st6, mv, rs, b_st) in enumerate(items):
                eng = "pool" if i % 2 == 0 else "dve"
                S.op(eng, lambda e: e.tensor_tensor(out=src_ap, in0=src_ap, in1=LNP[:, bi_, :], op=ALU.add), reads=[b_src, b_LNP], writes=[b_src])

        def mm_group(out_ap, b_out, pairs, reads):
            n = len(pairs)
            for i, (l, r) in enumerate(pairs):
                S.op("pe", lambda e, l=l, r=r, i=i: e.matmul(out_ap, lhsT=l, rhs=r, start=(i == 0), stop=(i == n - 1)),
                     reads=reads, writes=[b_out], mark=(i == n - 1))

        for si in ([] if stop == "setup" else range(NSEQ)):
            SL = seq_lens[si]
            tok0 = sum(seq_lens[:si])
            NT = SL // 512
            sc1p = lambda c: modT[:, 8 + c, si:si + 1]
            sh1 = lambda c: modT[:, 0 + c, si:si + 1]
            g1c = lambda c: modT[:, 16 + c, si:si + 1]
            sh2 = lambda c: modT[:, 24 + c, si:si + 1]
            sc2p = lambda c: modT[:, 32 + c, si:si + 1]
            g2c = lambda c: modT[:, 40 + c, si:si + 1]

            al = Alloc()
            hT = al.get([128, NCH, SL], BF16)
            HOFF = al.off
            XT = [al.get([128, D], F32) for _ in range(8)]
            b_XT = [Buf(f"XT{i}") for i in range(8)]
            ds_XT = [dsem(f"xt{i}") for i in range(8)]
            XN = [al.get([128, D], BF16) for _ in range(4)]
            b_XN = [Buf(f"XN{i}") for i in range(4)]
            STT_ = [(al.get([128, 2, 6], F32), al.get([128, 2], F32), al.get([128, 2], F32), Buf(f"st{i}")) for i in range(4)]
            for g in range(NT):
                base = (g % 2) * 4
                items = []
                for j in range(4):
                    ti = g * 4 + j
                    sl = ti % 8
                    S.dma("sp", lambda e, sl=sl, ti=ti: e.dma_start(out=XT[sl], in_=x_d[tok0 + ti * 128: tok0 + (ti + 1) * 128, :]),
                          ds_XT[sl], writes=[b_XT[sl]])
                    items.append((XT[sl], b_XT[sl]) + STT_[j])
                ln_stats_multi(items)
                for j in range(4):
                    src_ap, b_src, st6, mv, rs, b_st = items[j]
                    S.op("act", lambda e: e.activation(out=XN[j], in_=src_ap, func=AF.Identity, scale=rs[:, 0:1], bias=rs[:, 1:2]),
                         reads=[b_src, b_st], writes=[b_XN[j]])
                for j in range(4):
                    for c in range(NCH):
                        bk = base + c // 2
                        S.op("pe", lambda e, bk=bk, c=c, j=j: e.transpose(
                            bank_bf(bk)[:, (c % 2) * 512 + j * 128:(c % 2) * 512 + (j + 1) * 128], XN[j][:, c * 128:(c + 1) * 128], ident_bf[:]),
                            reads=[b_XN[j], b_const], writes=[pbuf[bk]], mark=(c % 2 == 1))
                for c in range(NCH):
                    bk = base + c // 2
                    S.op("act", lambda e, bk=bk, c=c, g=g: e.activation(out=hT[:, c, g * 512:(g + 1) * 512], in_=bank_bf(bk)[:, (c % 2) * 512:(c % 2) * 512 + 512],
                                                                        func=AF.Identity, scale=sc1p(c), bias=sh1(c)),
                         reads=[pbuf[bk], b_const], writes=[b_hT])
            if dbg:
                S.dma("sp", lambda e: e.dma_start(out=dbg_out["hT"][:, :, tok0:tok0 + SL], in_=hT[:, :, 0:SL]), ds_dbg, reads=[b_hT], writes=[b_dbg])
            S.barrier()
            if stop == "p1":
                break

            al = Alloc(HOFF)
            WP = [al.get([128, 2, NCH, 128], BF16) for _ in range(2)]
            LW = [al.get([128, 8, 128], BF16) for _ in range(2)]
            b_WP = [Buf("WP0"), Buf("WP1")]
            ds_WP = [dsem("WP0"), dsem("WP1")]
            XR = al.get([128, SL + 4], BF16)
            G_ = al.get([128, SL], BF16)
            XC = al.get([128, SL], F32)
            XCb = al.get([128, SL], BF16)
            HF = al.get([128, SL], F32)
            LB = min(SL, 2048) if SL <= 2048 else 1024
            if os.environ.get("KLB"):
                LB = int(os.environ["KLB"])
            NBLK = SL // LB
            TPB = LB // 512
            HBK = al.get([128, LB], F32)
            REC = al.get([128, SL], BF16)
            AA = al.get([128, LB], F32)
            QQ = al.get([128, LB], F32)
            DD = al.get([128, LB], F32)
            CAR = al.get([128, 2], F32)
            TMP = [[al.get([128, 512], F32) for _ in range(2)] for _ in range(2)]
            b_XR = [Buf(f"XR{t}") for t in range(NT)]
            b_G = [Buf(f"G{t}") for t in range(NT)]
            b_XC = [Buf(f"XC{t}") for t in range(NT)]
            b_XCb = [Buf(f"XCb{t}") for t in range(NT)]
            b_HF = [Buf(f"HF{t}") for t in range(NT)]
            b_HBK = Buf("HBK")
            b_REC = Buf("REC")
            b_AA = Buf("AA")
            b_QQ = Buf("QQ")
            b_DD = Buf("DD")
            b_CAR = Buf("CAR")
            b_TMP = [[Buf(f"TMP{i}{j}") for j in range(2)] for i in range(2)]
            ds_rec = dsem("rec")
            b_recd = Buf("rec_d")
            b_pad = Buf("XRpad")
            S.op("pool", lambda e: e.memset(XR[:, 0:2], 0.0), writes=[b_pad])
            S.op("pool", lambda e: e.memset(XR[:, SL + 2:SL + 4], 0.0), writes=[b_pad])

            def load_lru_panels(c):
                sl = c % 2
                S.dma("sp", lambda e: e.dma_start(out=WP[sl][:, 0].rearrange("p k n -> p (k n)"), in_=w_in_bf[(24 + c) * 128:(25 + c) * 128, :]),
                      ds_WP[sl], reads=[b_wbf], writes=[b_WP[sl]])
                S.dma("sp", lambda e: e.dma_start(out=WP[sl][:, 1].rearrange("p k n -> p (k n)"), in_=w_in_bf[(32 + c) * 128:(33 + c) * 128, :]),
                      ds_WP[sl], reads=[b_wbf], writes=[b_WP[sl]])
                S.dma("sp", lambda e: e.dma_start(out=LW[sl], in_=lruw_d[:, c]), ds_WP[sl], reads=[b_lruw], writes=[b_WP[sl]])

            load_lru_panels(0)
            for c in range(NCH):
                if c + 1 < NCH:
                    load_lru_panels(c + 1)
                sl = c % 2
                for t in range(NT):
                    bk = nextbank()
                    mm_group(bank(bk), pbuf[bk], [(WP[sl][:, 0, kc, :], hT[:, kc, t * 512:(t + 1) * 512]) for kc in range(NCH)], [b_WP[sl], b_hT])
                    S.op("act", lambda e, bk=bk, t=t: e.activation(out=XR[:, 2 + t * 512:2 + (t + 1) * 512], in_=bank(bk), func=AF.Identity),
                         reads=[pbuf[bk]], writes=[b_XR[t]])
                    bk = nextbank()
                    mm_group(bank(bk), pbuf[bk], [(WP[sl][:, 1, kc, :], hT[:, kc, t * 512:(t + 1) * 512]) for kc in range(NCH)], [b_WP[sl], b_hT])
                    S.op("act", lambda e, bk=bk, t=t: e.activation(out=G_[:, t * 512:(t + 1) * 512], in_=bank(bk), func=AF.Gelu_apprx_tanh),
                         reads=[pbuf[bk]], writes=[b_G[t]])
                for t in range(NT):
                    bk = nextbank()
                    rd = [b_WP[sl], b_pad, b_XR[t]] + ([b_XR[t - 1]] if t > 0 else []) + ([b_XR[t + 1]] if t + 1 < NT else [])
                    mm_group(bank(bk), pbuf[bk], [(LW[sl][:, 4 + k, :], XR[:, t * 512 + k:t * 512 + k + 512]) for k in range(4)], rd)
                    S.op("act", lambda e, bk=bk, t=t, c=c: e.activation(out=XC[:, t * 512:(t + 1) * 512], in_=bank(bk), func=AF.Identity,
                                                                        bias=fmc[:, 4, c:c + 1], scale=1.0),
                         reads=[pbuf[bk], b_const], writes=[b_XC[t]])
                    S.op("pool", lambda e, t=t: e.tensor_copy(out=XCb[:, t * 512:(t + 1) * 512], in_=XC[:, t * 512:(t + 1) * 512]),
                         reads=[b_XC[t]], writes=[b_XCb[t]])
                it = 0
                for d_ in range(2):
                    blocks = range(NBLK) if d_ == 0 else range(NBLK - 1, -1, -1)
                    for bi in blocks:
                        bsl = slice(bi * LB, (bi + 1) * LB)
                        for tl in range(TPB):
                            t = bi * TPB + tl
                            tm = TMP[it % 2]
                            btm = b_TMP[it % 2]
                            it += 1
                            TR, TI = tm
                            tsl = slice(t * 512, (t + 1) * 512)
                            lsl = slice(tl * 512, (tl + 1) * 512)
                            bkr = nextbank()
                            mm_group(bank(bkr), pbuf[bkr], [(LW[sl][:, 2 * d_, :], XCb[:, tsl])], [b_WP[sl], b_XCb[t]])
                            bki = nextbank()
                            mm_group(bank(bki), pbuf[bki], [(LW[sl][:, 2 * d_ + 1, :], XCb[:, tsl])], [b_WP[sl], b_XCb[t]])
                            S.op("act", lambda e: e.activation(out=TR, in_=bank(bkr), func=AF.Tanh, bias=hbias[:, 2 * d_, c:c + 1], scale=0.5),
                                 reads=[pbuf[bkr], b_const], writes=[btm[0]])
                            S.op("act", lambda e: e.activation(out=TI, in_=bank(bki), func=AF.Tanh, bias=hbias[:, 2 * d_ + 1, c:c + 1], scale=0.5),
                                 reads=[pbuf[bki], b_const], writes=[btm[1]])
                            S.op("act", lambda e: e.activation(out=AA[:, lsl], in_=TR, func=AF.Exp, scale=cdec[:, 2 * d_, c:c + 1], bias=cdec[:, 2 * d_, c:c + 1]),
                                 reads=[btm[0], b_const], writes=[b_AA])
                            S.op("act", lambda e: e.activation(out=QQ[:, lsl], in_=TR, func=AF.Exp, scale=cdec[:, 2 * d_ + 1, c:c + 1], bias=cdec[:, 2 * d_ + 1, c:c + 1]),
                                 reads=[btm[0], b_const], writes=[b_QQ])
                            S.op("dve", lambda e: e.scalar_tensor_tensor(out=DD[:, lsl], in0=TI, scalar=1.0, in1=XC[:, tsl], op0=ALU.add, op1=ALU.mult),
                                 reads=[btm[1], b_XC[t]], writes=[b_DD])
                        S.op("act", lambda e: e.activation(out=QQ, in_=QQ, func=AF.Sqrt, bias=0.25, scale=-0.25), reads=[b_QQ], writes=[b_QQ])
                        S.op("dve", lambda e: e.tensor_tensor(out=DD, in0=DD, in1=QQ, op=ALU.mult), reads=[b_DD, b_QQ], writes=[b_DD])
                        if d_ == 0:
                            init = 0.0 if bi == 0 else HF[:, bi * LB - 1:bi * LB]
                            S.op("dve", lambda e: e.tensor_tensor_scan(out=HF[:, bsl], data0=AA, data1=DD, initial=init, op0=ALU.mult, op1=ALU.add),
                                 reads=[b_AA, b_DD] + b_HF, writes=b_HF)
                        else:
                            init = 0.0 if bi == NBLK - 1 else CAR[:, 0:1]
                            S.op("dve", lambda e: e.tensor_tensor_scan(out=HBK[:, ::-1], data0=AA[:, ::-1], data1=DD[:, ::-1], initial=init,
                                                                        op0=ALU.mult, op1=ALU.add),
                                 reads=[b_AA, b_DD, b_CAR], writes=[b_HBK])
                            if bi > 0:
                                S.op("dve", lambda e: e.tensor_copy(out=CAR[:, 0:1], in_=HBK[:, 0:1]), reads=[b_HBK], writes=[b_CAR])
                            S.op("pool", lambda e: e.tensor_tensor(out=HBK, in0=HBK, in1=HF[:, bsl], op=ALU.add),
                                 reads=[b_HBK] + b_HF, writes=[b_HBK])
                            S.op("pool", lambda e: e.tensor_tensor(out=REC[:, bsl], in0=HBK, in1=G_[:, bsl], op=ALU.mult),
                                 reads=[b_HBK] + b_G, writes=[b_REC])
                S.dma("sp", lambda e, c=c: e.dma_start(out=rec_d[c, :, tok0:tok0 + SL], in_=REC), ds_rec, reads=[b_REC], writes=[b_recd])
                if dbg:
                    S.dma("sp", lambda e, c=c: e.dma_start(out=dbg_out["rec"][c, :, tok0:tok0 + SL], in_=REC), ds_dbg, reads=[b_REC], writes=[b_dbg])
            S.barrier()
            if stop == "p2":
                break

            NQB = max(1, SL // QB)
            QBL = min(QB, SL)
            for qb in range(NQB):
                q0 = qb * QBL
                al = Alloc(HOFF)
                attnT = al.get([128, NCH, QBL], BF16)
                b_attnT = Buf("attnT")
                p4_off = al.off
                QT = al.get([128, QBL], BF16)
                KT = al.get([128, SL], BF16)
                NKT = SL // 128
                V_ = al.get([128, NKT, 130], BF16)
                WQ = [al.get([128, 3, NCH, 128], BF16) for _ in range(2)]
                b_WQ = [Buf("WQ0"), Buf("WQ1")]
                ds_WQ = [dsem("WQ0"), dsem("WQ1")]
                RC = [al.get([128, 2, 512], F32) for _ in range(2)]
                b_RC = [Buf("RC0"), Buf("RC1")]
                ds_RC = [dsem("RC0"), dsem("RC1")]
                QBF = [al.get([128, 512], BF16) for _ in range(2)]
                b_QBF = [Buf("QBF0"), Buf("QBF1")]
                T12 = [[al.get([128, 512], F32) for _ in range(2)] for _ in range(2)]
                b_T12 = [[Buf("T1"), Buf("T2")] for _ in range(2)]
                PT = [al.get([128, 2, 512], BF16) for _ in range(3)]
                b_PT = [Buf(f"PT{i}") for i in range(3)]
                NQG = QBL // 512
                ACCS = al.get([128, 8, 130], F32)
                b_ACCS = Buf("ACCS")
                O0 = al.get([128, 128], F32)
                b_O0 = Buf("O0")
                OO = [al.get([128, 128], F32) for _ in range(4 * NQG)]
                b_OO = [Buf(f"OO{i}") for i in range(4 * NQG)]
                AT = [al.get([128, 128], BF16) for _ in range(4)]
                b_AT = [Buf(f"AT{i}") for i in range(4)]
                nrm = al.get([128, 16], F32)
                b_nrm = Buf("nrm")
                st6n = al.get([128, 6], F32)
                mvn = al.get([128, 2], F32)
                b_stn = Buf("stn")
                msa = al.get([128, 4 * NQG], F32)
                b_msa = Buf("msa")
                b_QT = Buf("QT")
                b_KT = Buf("KT")
                b_V = Buf("V")
                S.op("pool", lambda e: e.memset(V_[:, :, 128:130], 1.0), writes=[b_V])

                def load_head_w(h):
                    sl = h % 2
                    for i in range(3):
                        S.dma("sp", lambda e, i=i: e.dma_start(out=WQ[sl][:, i].rearrange("p k n -> p (k n)"), in_=w_in_bf[(i * 8 + h) * 128:(i * 8 + h + 1) * 128, :]),
                              ds_WQ[sl], reads=[b_wbf], writes=[b_WQ[sl]])

                rc_i = [0]

                def rope_proj(h, which, t_tok, dst_ap, b_dst):
                    if "rope" in SKIP:
                        return
                    sl = h % 2
                    r = rc_i[0] % 2
                    rc_i[0] += 1
                    S.dma("sp", lambda e: e.dma_start(out=RC[r][:, 0, :], in_=ropec_d[:, t_tok:t_tok + 512]), ds_RC[r], writes=[b_RC[r]])
                    S.dma("sp", lambda e: e.dma_start(out=RC[r][:, 1, :], in_=ropes_d[:, t_tok:t_tok + 512]), ds_RC[r], writes=[b_RC[r]])
                    bka = nextbank(0, 5)
                    mm_group(bank(bka), pbuf[bka], [(WQ[sl][:, which, kc, :], hT[:, kc, t_tok:t_tok + 512]) for kc in range(NCH)], [b_WQ[sl], b_hT])
                    S.op("act", lambda e: e.activation(out=QBF[r], in_=bank(bka), func=AF.Identity), reads=[pbuf[bka]], writes=[b_QBF[r]])
                    bkb = nextbank(0, 5)
                    mm_group(bank(bkb), pbuf[bkb], [(prot_bf[:], QBF[r])], [b_const, b_QBF[r]])
                    t1, t2 = T12[r]
                    S.op("dve", lambda e: e.tensor_tensor(out=t1, in0=bank(bka), in1=RC[r][:, 0, :], op=ALU.mult),
                         reads=[pbuf[bka], b_RC[r]], writes=[b_T12[r][0]])
                    S.op("dve", lambda e: e.tensor_tensor(out=t2, in0=bank(bkb), in1=RC[r][:, 1, :], op=ALU.mult),
                         reads=[pbuf[bkb], b_RC[r]], writes=[b_T12[r][1]])
                    S.op("pool", lambda e: e.tensor_tensor(out=dst_ap, in0=t1, in1=t2, op=ALU.add),
                         reads=[b_T12[r][0], b_T12[r][1]], writes=[b_dst])

                def acc_ap(a, lo=0, hi=129):
                    bk = 5 + a // 3
                    o = (a % 3) * 130
                    return ps[:, bk * 512 + o + lo: bk * 512 + o + hi]

                load_head_w(0)
                pending_tail = []
                for h in range(NCH):
                    if h + 1 < NCH:
                        load_head_w(h + 1)
                    sl = h % 2
                    for t in range(NT):
                        rope_proj(h, 1, t * 512, KT[:, t * 512:(t + 1) * 512], b_KT)
                    for kt in (range(NKT) if "vproj" not in SKIP else []):
                        bk = nextbank(0, 5)
                        mm_group(bank(bk)[:, 0:128], pbuf[bk], [(hT[:, kc, kt * 128:(kt + 1) * 128], WQ[sl][:, 2, kc, :]) for kc in range(NCH)], [b_WQ[sl], b_hT])
                        S.op("act", lambda e, bk=bk, kt=kt: e.activation(out=V_[:, kt, 0:128], in_=bank(bk)[:, 0:128], func=AF.Identity),
                             reads=[pbuf[bk]], writes=[b_V])
                    for t in range(QBL // 512):
                        rope_proj(h, 0, q0 + t * 512, QT[:, t * 512:(t + 1) * 512], b_QT)
                    while pending_tail:
                        pending_tail.pop(0)()
                    for qg in range(QBL // 512):
                        def emit_qk_exp(kt, qg=qg):
                            pr = kt % 2
                            stv = ps[:, pr * 1024:(pr + 1) * 1024].rearrange("p (a b) -> p a b", a=2)
                            bst = [pbuf[2 * pr], pbuf[2 * pr + 1]]
                            S.op("pe", lambda e: e.matmul(stv[:, 0, :], lhsT=KT[0:64, kt * 128:(kt + 1) * 128], rhs=QT[0:64, qg * 512:(qg + 1) * 512],
                                                          start=True, stop=True), reads=[b_KT, b_QT], writes=[bst[0]], mark=False)
                            S.op("pe", lambda e: e.matmul(stv[:, 1, :], lhsT=KT[64:128, kt * 128:(kt + 1) * 128], rhs=QT[64:128, qg * 512:(qg + 1) * 512],
                                                          start=True, stop=True), reads=[b_KT, b_QT], writes=[bst[1]], mark=True)
                            pi = kt % 3
                            S.op("act", lambda e: e.activation(out=PT[pi], in_=stv, func=AF.Exp, scale=0.125), reads=bst, writes=[b_PT[pi]])

                        def emit_pv(kt):
                            pi = kt % 3
                            for a in range(8):
                                cm, qs = a // 4, a % 4
                                S.op("pe", lambda e, a=a, cm=cm, qs=qs: e.matmul(acc_ap(a), lhsT=PT[pi][:, cm, qs * 128:(qs + 1) * 128], rhs=V_[:, kt, 0:129],
                                                                                 start=(kt == 0 and a % 3 == 0), stop=(kt == NKT - 1), skip_group_check=True),
                                     reads=[b_PT[pi], b_V], writes=[pbuf[5 + a // 3]], mark=(a == 7))

                        emit_qk_exp(0)
                        for kt in range(NKT):
                            if kt + 1 < NKT:
                                emit_qk_exp(kt + 1)
                            emit_pv(kt)
                        for bkk in range(3):
                            n_ = 3 if bkk < 2 else 2
                            src = ps[:, (5 + bkk) * 512:(5 + bkk) * 512 + n_ * 130].rearrange("p (a b) -> p a b", a=n_)
                            S.op("dve", lambda e, bkk=bkk, n_=n_, src=src: e.tensor_copy(out=ACCS[:, 3 * bkk:3 * bkk + n_, :], in_=src),
                                 reads=[pbuf[5 + bkk]], writes=[b_ACCS])
                        S.op("dve", lambda e: e.reciprocal(out=nrm[:, 0:8], in_=ACCS[:, :, 128]), reads=[b_ACCS], writes=[b_nrm])
                        S.op("dve", lambda e: e.tensor_scalar(out=nrm[:, 8:12], in0=nrm[:, 4:8], scalar1=lamt[:, 4:5], scalar2=None, op0=ALU.mult),
                             reads=[b_nrm, b_const], writes=[b_nrm])
                        for qs in range(4):
                            oi = qg * 4 + qs
                            S.op("dve", lambda e, qs=qs: e.tensor_scalar(out=O0, in0=ACCS[:, qs, 0:128], scalar1=nrm[:, qs:qs + 1], scalar2=None, op0=ALU.mult),
                                 reads=[b_ACCS, b_nrm], writes=[b_O0])
                            S.op("dve", lambda e, qs=qs, oi=oi: e.scalar_tensor_tensor(out=OO[oi], in0=ACCS[:, 4 + qs, 0:128], scalar=nrm[:, 8 + qs:9 + qs], in1=O0,
                                                                                      op0=ALU.mult, op1=ALU.add),
                                 reads=[b_ACCS, b_nrm, b_O0], writes=[b_OO[oi]])
                            S.op("dve", lambda e, oi=oi: e.bn_stats(out=st6n, in_=OO[oi]), reads=[b_OO[oi]], writes=[b_stn])
                            S.op("dve", lambda e: e.bn_aggr(out=mvn, in_=st6n), reads=[b_stn], writes=[b_stn])
                            S.op("dve", lambda e, oi=oi: e.tensor_scalar(out=msa[:, oi:oi + 1], in0=mvn[:, 0:1], scalar1=mvn[:, 0:1], scalar2=mvn[:, 1:2],
                                                                        op0=ALU.mult, op1=ALU.add),
                                 reads=[b_stn], writes=[b_msa])

                    def head_tail(h=h):
                        S.op("act", lambda e: e.activation(out=msa, in_=msa, func=AF.Sqrt, bias=RMS_EPS, scale=1.0), reads=[b_msa], writes=[b_msa])
                        S.op("dve", lambda e: e.reciprocal(out=msa, in_=msa), reads=[b_msa], writes=[b_msa])
                        for qg2 in range(NQG):
                            for qs in range(4):
                                oi = qg2 * 4 + qs
                                S.op("dve", lambda e, qs=qs, oi=oi: e.scalar_tensor_tensor(out=AT[qs], in0=OO[oi], scalar=msa[:, oi:oi + 1], in1=gsub[:],
                                                                                          op0=ALU.mult, op1=ALU.mult),
                                     reads=[b_OO[oi], b_msa, b_const], writes=[b_AT[qs]])
                                S.op("pe", lambda e, qs=qs: e.transpose(bank_bf(4)[:, qs * 128:(qs + 1) * 128], AT[qs], ident_bf[:]),
                                     reads=[b_AT[qs], b_const], writes=[pbuf[4]], mark=(qs == 3))
                            S.op("dve", lambda e, qg2=qg2: e.tensor_copy(out=attnT[:, h, qg2 * 512:(qg2 + 1) * 512], in_=bank_bf(4)[:, 0:512]),
                                 reads=[pbuf[4]], writes=[b_attnT])

                    pending_tail.append(head_tail)
                while pending_tail:
                    pending_tail.pop(0)()
                if dbg:
                    S.dma("sp", lambda e: e.dma_start(out=dbg_out["attnT"][:, :, tok0 + q0:tok0 + q0 + QBL], in_=attnT), ds_dbg, reads=[b_attnT], writes=[b_dbg])
                S.barrier()
                if stop == "p3":
                    break

                T = 256 if SL > 2048 else 512
                NST = T // 128
                al = Alloc(p4_off)
                LNP = al.get([128, 4, D], F32)
                b_LNP = Buf("LNP", const=False)
                ds_lnp = dsem("lnp")
                for i, v in enumerate([ln1g_d, ln1b_d, ln2g_d, ln2b_d]):
                    S.dma("sp", lambda e, i=i, v=v: e.dma_start(out=LNP[:, i, :], in_=v.broadcast_to([128, D])), ds_lnp, writes=[b_LNP])
                X1 = al.get([128, NST, D], F32)
                b_X1 = [Buf(f"X1_{i}") for i in range(NST)]
                ds_X1 = [dsem(f"x1_{i}") for i in range(NST)]
                PAN = [al.get([128, 4096], BF16) for _ in range(3)]
                b_PAN = [Buf(f"PAN{i}") for i in range(3)]
                ds_PAN = [dsem(f"pan{i}") for i in range(3)]
                TT_ = [al.get([128, T], F32) for _ in range(2)]
                b_TT = [Buf("TT0"), Buf("TT1")]
                XN2 = [al.get([128, D], BF16) for _ in range(NST)]
                b_XN2 = [Buf(f"XN2{i}") for i in range(NST)]
                STS = [(al.get([128, 2, 6], F32), al.get([128, 2], F32), al.get([128, 2], F32), Buf(f"sts{i}")) for i in range(NST)]
                ov = al.off
                MG = al.get([128, NCH, T], BF16)
                RT = al.get([128, NCH, T], BF16)
                SG = [al.get([128, T], F32) for _ in range(2)]
                AB = [al.get([128, T], F32) for _ in range(2)]
                e_end = al.off
                al.off = ov
                H2 = al.get([128, NCH, T], BF16)
                UT = al.get([128, NFF, T], BF16)
                SU = [al.get([128, T], F32) for _ in range(2)]
                b_MG = Buf("MG")
                b_RT = Buf("RT")
                ds_RT = dsem("rt")
                b_SG = [Buf("SG0"), Buf("SG1")]
                b_AB = [Buf("AB0"), Buf("AB1")]
                b_H2 = Buf("H2")
                b_UT = Buf("UT")
                b_SU = [Buf("SU0"), Buf("SU1")]
                b_yd = Buf("y_d")
                ds_y = [dsem(f"y{i}") for i in range(4)]

                NG = QBL // T
                panels = []
                for g in range(NG):
                    for oc in range(NCH):
                        panels.append(("a", g, oc))
                    for oc in range(NCH):
                        panels.append(("b", g, oc))
                    for j in range(NFF):
                        panels.append(("d", g, j))
                    for oc in range(NCH):
                        panels.append(("e", g, oc))
                pstate4 = {"issued": 0}

                def issue_panel(i):
                    kind, g, k = panels[i]
                    sl = i % 3
                    pv = PAN[sl]
                    if kind == "a":
                        v4 = pv.rearrange("p (a k n) -> p a k n", a=4, k=NCH)
                        srcs = [w_in_bf[(40 + k) * 128:(41 + k) * 128, :], w_in_bf[(48 + k) * 128:(49 + k) * 128, :],
                                w_ab_bf[k * 128:(k + 1) * 128, :], w_lb_bf[k * 128:(k + 1) * 128, :]]
                        for a_, s_ in enumerate(srcs):
                            S.dma("sp", lambda e, a_=a_, s_=s_, v4=v4: e.dma_start(out=v4[:, a_].rearrange("p k n -> p (k n)"), in_=s_),
                                  ds_PAN[sl], reads=[b_wbf], writes=[b_PAN[sl]])
                    elif kind == "b":
                        v3 = pv[:, 0:NCH * 128].rearrange("p (k n) -> p k n", k=NCH)
                        S.dma("sp", lambda e, v3=v3, k=k: e.dma_start(out=v3.rearrange("p k n -> p (k n)"), in_=w_out_bf[k * 128:(k + 1) * 128, :]),
                              ds_PAN[sl], reads=[b_wbf], writes=[b_PAN[sl]])
                    elif kind == "d":
                        v4 = pv[:, 0:2 * NCH * 128].rearrange("p (a k n) -> p a k n", a=2, k=NCH)
                        for a_ in range(2):
                            S.dma("sp", lambda e, a_=a_, v4=v4, k=k: e.dma_start(out=v4[:, a_].rearrange("p k n -> p (k n)"), in_=w_fi_bf[(a_ * NFF + k) * 128:(a_ * NFF + k + 1) * 128, :]),
                                  ds_PAN[sl], reads=[b_wbf], writes=[b_PAN[sl]])
                    else:
                        v3 = pv[:, 0:NFF * 128].rearrange("p (k n) -> p k n", k=NFF)
                        S.dma("sp", lambda e, v3=v3, k=k: e.dma_start(out=v3.rearrange("p k n -> p (k n)"), in_=w_fo_bf[k * 128:(k + 1) * 128, :]),
                              ds_PAN[sl], reads=[b_wbf], writes=[b_PAN[sl]])

                def get_panel():
                    i = pstate4["cur"]
                    while pstate4["issued"] < min(len(panels), i + 3):
                        issue_panel(pstate4["issued"])
                        pstate4["issued"] += 1
                    pstate4["cur"] = i + 1
                    return PAN[i % 3], b_PAN[i % 3]

                pstate4["cur"] = 0

                def residual_block(g, pv3, nk, rhs_fn, rhs_bufs, b_pan, gcol, tti):
                    bk = nextbank()
                    mm_group(bank(bk)[:, 0:T], pbuf[bk], [(pv3[:, kc, :], rhs_fn(kc)) for kc in range(nk)], [b_pan] + rhs_bufs)
                    tt = TT_[tti % 2]
                    btt = b_TT[tti % 2]
                    S.op("act", lambda e: e.activation(out=tt, in_=bank(bk)[:, 0:T], func=AF.Identity, scale=gcol),
                         reads=[pbuf[bk], b_const], writes=[btt])
                    bk2 = nextbank()
                    for s_ in range(NST):
                        S.op("pe", lambda e, s_=s_: e.transpose(bank(bk2)[:, s_ * 128:(s_ + 1) * 128], tt[:, s_ * 128:(s_ + 1) * 128], ident_f[:]),
                             reads=[btt, b_const], writes=[pbuf[bk2]], mark=(s_ == NST - 1))
                    return bk2

                for g in range(NG):
                    gt0 = q0 + g * T
                    gl0 = g * T
                    S.fence(["sp", "act", "dve", "pool"], [b_H2, b_UT, b_SU[0], b_SU[1]])
                    for s_ in range(NST):
                        S.dma("pool", lambda e, s_=s_: e.dma_start(out=X1[:, s_, :], in_=x_d[tok0 + gt0 + s_ * 128:tok0 + gt0 + (s_ + 1) * 128, :]),
                              ds_X1[s_], writes=[b_X1[s_]])
                    S.dma("sp", lambda e: e.dma_start(out=RT, in_=rec_d[:, :, tok0 + gt0:tok0 + gt0 + T].rearrange("c p t -> p c t")),
                          ds_RT, reads=[b_recd], writes=[b_RT])
                    for oc in range(NCH):
                        pv, bp = get_panel()
                        v4 = pv.rearrange("p (a k n) -> p a k n", a=4, k=NCH)
                        bks = []
                        order = [(0, lambda kc: hT[:, kc, gt0:gt0 + T], b_hT), (2, lambda kc: attnT[:, kc, gl0:gl0 + T], b_attnT),
                                 (1, lambda kc: hT[:, kc, gt0:gt0 + T], b_hT), (3, lambda kc: RT[:, kc, :], b_RT)]
                        for a_, rf, rb in order:
                            bk = nextbank()
                            mm_group(bank(bk)[:, 0:T], pbuf[bk], [(v4[:, a_, kc, :], rf(kc)) for kc in range(NCH)], [bp, rb])
                            bks.append(bk)
                        for half in range(2):
                            sg = SG[half]
                            ab = AB[half]
                            bg, bb = bks[2 * half], bks[2 * half + 1]
                            S.op("act", lambda e, sg=sg, bg=bg: e.activation(out=sg, in_=bank(bg)[:, 0:T], func=AF.Sigmoid),
                                 reads=[pbuf[bg]], writes=[b_SG[half]])
                            S.op("dve", lambda e, sg=sg, ab=ab, bb=bb: e.tensor_tensor(out=ab, in0=bank(bb)[:, 0:T], in1=sg, op=ALU.mult),
                                 reads=[pbuf[bb], b_SG[half]], writes=[b_AB[half]])
                        S.op("pool", lambda e, oc=oc: e.tensor_tensor(out=MG[:, oc, :], in0=AB[0], in1=AB[1], op=ALU.add),
                             reads=[b_AB[0], b_AB[1]], writes=[b_MG])
                    for oc in range(NCH):
                        pv, bp = get_panel()
                        v3 = pv[:, 0:NCH * 128].rearrange("p (k n) -> p k n", k=NCH)
                        bk2 = residual_block(g, v3, NCH, lambda kc: MG[:, kc, :], [b_MG], bp, g1c(oc), oc)
                        S.op("dve", lambda e, oc=oc, bk2=bk2: e.scalar_tensor_tensor(
                            out=X1[:, :, oc * 128:(oc + 1) * 128], in0=X1[:, :, oc * 128:(oc + 1) * 128], scalar=ALPHA,
                            in1=bank(bk2)[:, 0:T].rearrange("p (s n) -> p s n", s=NST), op0=ALU.mult, op1=ALU.add),
                            reads=[pbuf[bk2]] + b_X1, writes=b_X1)
                    S.fence(["act", "dve"], [b_MG, b_RT, b_SG[0], b_SG[1], b_AB[0], b_AB[1]])
                    bks_c = [nextbank() for _ in range(4)]
                    items = [(X1[:, s_, :], b_X1[s_]) + STS[s_] for s_ in range(NST)]
                    ln_affine_multi(items, 0, 1, LNP, b_LNP)
                    ln_stats_multi(items)
                    for s_ in range(NST):
                        xs_ap, _, st6, mv, rs, b_st = items[s_]
                        xn = XN2[s_]
                        S.op("act", lambda e: e.activation(out=xn, in_=xs_ap, func=AF.Identity, scale=rs[:, 0:1], bias=rs[:, 1:2]),
                             reads=[b_X1[s_], b_st], writes=[b_XN2[s_]])
                    for s_ in range(NST):
                        xn = XN2[s_]
                        for c in range(NCH):
                            bk = bks_c[c // 2]
                            S.op("pe", lambda e, bk=bk, c=c: e.transpose(
                                bank_bf(bk)[:, (c % 2) * 512 + s_ * 128:(c % 2) * 512 + (s_ + 1) * 128], xn[:, c * 128:(c + 1) * 128], ident_bf[:]),
                                reads=[b_XN2[s_], b_const], writes=[pbuf[bk]], mark=(c % 2 == 1))
                    for c in range(NCH):
                        bk = bks_c[c // 2]
                        S.op("act", lambda e, bk=bk, c=c: e.activation(out=H2[:, c, :], in_=bank_bf(bk)[:, (c % 2) * 512:(c % 2) * 512 + T],
                                                                       func=AF.Identity, scale=sc2p(c), bias=sh2(c)),
                             reads=[pbuf[bk], b_const], writes=[b_H2])
                    for j in range(NFF):
                        pv, bp = get_panel()
                        v4 = pv[:, 0:2 * NCH * 128].rearrange("p (a k n) -> p a k n", a=2, k=NCH)
                        bkg = nextbank()
                        mm_group(bank(bkg)[:, 0:T], pbuf[bkg], [(v4[:, 0, kc, :], H2[:, kc, :]) for kc in range(NCH)], [bp, b_H2])
                        bku = nextbank()
                        mm_group(bank(bku)[:, 0:T], pbuf[bku], [(v4[:, 1, kc, :], H2[:, kc, :]) for kc in range(NCH)], [bp, b_H2])
                        su = SU[j % 2]
                        S.op("act", lambda e, su=su, bkg=bkg: e.activation(out=su, in_=bank(bkg)[:, 0:T], func=AF.Silu),
                             reads=[pbuf[bkg]], writes=[b_SU[j % 2]])
                        S.op("dve", lambda e, su=su, bku=bku, j=j: e.tensor_tensor(out=UT[:, j, :], in0=bank(bku)[:, 0:T], in1=su, op=ALU.mult),
                             reads=[pbuf[bku], b_SU[j % 2]], writes=[b_UT])
                    for oc in range(NCH):
                        pv, bp = get_panel()
                        v3 = pv[:, 0:NFF * 128].rearrange("p (k n) -> p k n", k=NFF)
                        bk2 = residual_block(g, v3, NFF, lambda kc: UT[:, kc, :], [b_UT], bp, g2c(oc), oc)
                        S.op("dve", lambda e, oc=oc, bk2=bk2: e.scalar_tensor_tensor(
                            out=X1[:, :, oc * 128:(oc + 1) * 128], in0=X1[:, :, oc * 128:(oc + 1) * 128], scalar=ALPHA,
                            in1=bank(bk2)[:, 0:T].rearrange("p (s n) -> p s n", s=NST), op0=ALU.mult, op1=ALU.add),
                            reads=[pbuf[bk2]] + b_X1, writes=b_X1)
                    items = [(X1[:, s_, :], b_X1[s_]) + STS[s_] for s_ in range(NST)]
                    ln_affine_multi(items, 2, 3, LNP, b_LNP)
                    for s_ in range(NST):
                        S.dma("pool", lambda e, s_=s_: e.dma_start(out=y_d[tok0 + gt0 + s_ * 128:tok0 + gt0 + (s_ + 1) * 128, :], in_=X1[:, s_, :]),
                              ds_y[s_], reads=[b_X1[s_]], writes=[b_yd])
                S.barrier()
        S.barrier()
        S.emit()
    return nc


def _rope_tables(smax):
    inv = (1.0 / (np.float32(10000.0) ** (np.arange(0, 64, 2, dtype=np.float32) / np.float32(64)))).astype(np.float32)
    ang = (np.arange(smax, dtype=np.float32)[:, None] * inv[None, :]).astype(np.float32)
    cos = np.cos(ang).astype(np.float32)
    sin = np.sin(ang).astype(np.float32)
    c = np.zeros((128, smax), np.float32)
    s = np.zeros((128, smax), np.float32)
    for p in range(128):
        d = p % 64
        j = d % 32
        c[p] = cos[:, j]
        s[p] = (-sin[:, j]) if d < 32 else sin[:, j]
    return c, s


def _consts(smax):
    ident = np.eye(128, dtype=np.float32)
    prot = np.zeros((128, 128), np.float32)
    for m in range(128):
        blk = (m // 64) * 64
        d = m % 64
        k = blk + ((d + 32) % 64)
        prot[k, m] = 1.0
    c, s = _rope_tables(smax)
    return ident, prot, c, s


def make_in_maps(inputs, n_cores, seq_plan):
    f = lambda a: np.ascontiguousarray(np.asarray(a, dtype=np.float32))
    def pan(a, nk):
        a = f(a)
        ncb = a.shape[1] // 128
        return np.ascontiguousarray(a.reshape(nk, 128, ncb, 128).transpose(2, 1, 0, 3).reshape(ncb * 128, nk * 128))

    w = {
        "w_ada": f(inputs["w_ada"][0]), "b_ada": f(inputs["b_ada"][0]).reshape(1, -1), "w_in": pan(inputs["w_in"][0], 8),
        "lq1": f(inputs["lambda_q1"][0]).reshape(1, 64), "lk1": f(inputs["lambda_k1"][0]).reshape(1, 64),
        "lq2": f(inputs["lambda_q2"][0]).reshape(1, 64), "lk2": f(inputs["lambda_k2"][0]).reshape(1, 64),
        "subln_g": f(inputs["subln_g"][0]).reshape(1, 128), "conv_w": f(inputs["conv_w"][0]), "conv_b": f(inputs["conv_b"][0]).reshape(1, -1),
        "w_lru_gates": f(inputs["w_lru_gates"][0]).reshape(4, 16, 64, 64), "b_lru_gates": f(inputs["b_lru_gates"][0]).reshape(4, -1),
        "lru_lambda": f(inputs["lru_lambda"][0]), "w_ab": pan(inputs["w_attn_branch"][0], 8), "w_lb": pan(inputs["w_lru_branch"][0], 8),
        "w_out": pan(inputs["w_out"][0], 8), "ln1_g": f(inputs["ln1_g"][0]).reshape(1, -1), "ln1_b": f(inputs["ln1_b"][0]).reshape(1, -1),
        "w_ffn_in": pan(inputs["w_ffn_in"][0], 8), "w_ffn_out": pan(inputs["w_ffn_out"][0], 22),
        "ln2_g": f(inputs["ln2_g"][0]).reshape(1, -1), "ln2_b": f(inputs["ln2_b"][0]).reshape(1, -1),
    }
    maps = []
    for core in range(n_cores):
        plan = seq_plan(core)
        smax = max(p[0].shape[0] for p in plan)
        ident, prot, c, s = _consts(smax)
        m = dict(w)
        m["x"] = np.ascontiguousarray(np.concatenate([p[0] for p in plan], axis=0))
        m["c"] = np.ascontiguousarray(np.stack([p[1] for p in plan], axis=0))
        m["ident"] = ident
        m["prot"] = prot
        m["rope_c"] = c
        m["rope_s"] = s
        maps.append(m)
    return maps


_NC_CACHE = {}


def kernel(**inputs):
    n = 8
    xp = np.asarray(inputs["x_prompt"], dtype=np.float32)
    xs = np.asarray(inputs["x_sample"], dtype=np.float32)
    cp = np.asarray(inputs["c_prompt"], dtype=np.float32)
    cs = np.asarray(inputs["c_sample"], dtype=np.float32)
    B, SP, _ = xp.shape
    DB, SS, _ = xs.shape
    per = DB // n
    seq_lens = [SP] + [SS] * per

    def plan(core):
        return [(xp[core], cp[core])] + [(xs[core * per + i], cs[core * per + i]) for i in range(per)]

    key = tuple(seq_lens)
    if key not in _NC_CACHE:
        _NC_CACHE[key] = build(seq_lens)
    nc = _NC_CACHE[key]
    maps = make_in_maps(inputs, n, plan)
    res = run_bass_kernel_spmd(nc, maps, core_ids=list(range(n)))
    yp = np.empty((B, SP, D), np.float32)
    ys = np.empty((DB, SS, D), np.float32)
    for core in range(n):
        y = np.asarray(res.results[core]["y"], dtype=np.float32)
        yp[core] = y[0:SP]
        ys[core * per:(core + 1) * per] = y[SP:].reshape(per, SS, D)
    return (yp, ys)
```

```python
import os
import numpy as np
from contextlib import ExitStack
import concourse.bass as bass
import concourse.mybir as mybir
from concourse.bass_utils import run_bass_kernel_spmd

F32 = mybir.dt.float32
BF16 = mybir.dt.bfloat16
AF = mybir.ActivationFunctionType
ALU = mybir.AluOpType

ENGS = ("pe", "act", "dve", "pool", "sp")
D = 1024
NCH = 8
DFF = 2816
NFF = 22
ALPHA = 2.0 ** 0.25
LN_EPS = 1e-5
RMS_EPS = 1e-5
LAMBDA_INIT = 0.2
QB = 2048
ARENA_BYTES = 174 * 1024


class Sem:
    def __init__(self, h, name):
        self.h = h
        self.name = name
        self.count = 0


class Buf:
    __slots__ = ("name", "w", "r", "const", "excl")

    def __init__(self, name, const=False, excl=False):
        self.name = name
        self.w = None
        self.r = {}
        self.const = const
        self.excl = excl


class _Rec:
    def __init__(self):
        self.call = None

    def __getattr__(self, name):
        def f(*a, **k):
            self.call = (name, a, k)
            return self
        return f


def _record(fn):
    r = _Rec()
    fn(r)
    return r.call


class Sched:
    def __init__(self, nc, stack):
        self.nc = nc
        self.stack = stack
        self.prog = {e: [] for e in ENGS}
        self.sems = []
        self.esem = {e: self.new_sem("e_" + e) for e in ENGS if e != "sp"}
        self.waited = {e: {} for e in ENGS}

    def new_sem(self, name):
        s = Sem(self.stack.enter_context(self.nc.semaphore(name)), name)
        self.sems.append(s)
        return s

    def _deps(self, eng, reads, writes):
        deps = {}
        for b in reads:
            if b.w is not None:
                s, v = b.w
                if deps.get(s, 0) < v:
                    deps[s] = v
        for b in writes:
            if b.w is not None:
                s, v = b.w
                if deps.get(s, 0) < v:
                    deps[s] = v
            for s, v in b.r.items():
                if deps.get(s, 0) < v:
                    deps[s] = v
        out = []
        wd = self.waited[eng]
        own = self.esem.get(eng)
        for s, v in deps.items():
            if s is own and eng == "pe":
                continue
            if wd.get(s, 0) >= v:
                continue
            assert s.count >= v, f"wait on future tick: eng={eng} sem={s.name} v={v} count={s.count}"
            wd[s] = v
            out.append((s, v))
        return out

    def op(self, eng, fn, reads=(), writes=(), mark=True):
        ex = [b for b in reads if b.excl]
        if ex:
            reads = [b for b in reads if not b.excl]
            writes = list(writes) + [b for b in ex if b not in writes]
        waits = self._deps(eng, reads, writes)
        sem = self.esem[eng]
        if mark:
            sem.count += 1
            tick = sem.count
        else:
            tick = sem.count + 1
        self.prog[eng].append((waits, _record(fn), (sem, 1) if mark else None))
        for b in reads:
            if not b.const and b.r.get(sem, 0) < tick:
                b.r[sem] = tick
        for b in writes:
            b.w = (sem, tick)
            b.r = {}

    def dma(self, queue, fn, dsem, reads=(), writes=()):
        waits = self._deps(queue, reads, writes)
        dsem.count += 16
        v = dsem.count
        self.prog[queue].append((waits, _record(fn), (dsem, 16)))
        for b in reads:
            if not b.const and b.r.get(dsem, 0) < v:
                b.r[dsem] = v
        for b in writes:
            b.w = (dsem, v)
            b.r = {}

    def fence(self, engs, bufs):
        for e in engs:
            waits = self._deps(e, [], bufs)
            self.prog[e].append((waits, None, None))

    def barrier(self, exclude=()):
        for e in ENGS:
            wd = self.waited[e]
            waits = []
            for s in self.sems:
                if s is self.esem.get(e) or s in exclude:
                    continue
                if s.count > wd.get(s, 0):
                    wd[s] = s.count
                    waits.append((s, s.count))
            self.prog[e].append((waits, None, None))

    def emit(self):
        nc = self.nc
        with nc.Block() as block:
            deco = {"pe": block.tensor, "act": block.scalar, "dve": block.vector, "pool": block.gpsimd, "sp": block.sync}
            for e in ENGS:
                prog = self.prog[e]

                def body(eng, prog=prog):
                    for waits, fn, inc in prog:
                        for s, v in waits:
                            eng.wait_ge(s.h, v)
                        if fn is not None:
                            ins = getattr(eng, fn[0])(*fn[1], **fn[2])
                            if inc is not None:
                                ins.then_inc(inc[0].h, inc[1])

                deco[e](body)


class _Stop(Exception):
    pass


def build(seq_lens, dbg=False, stop=None):
    nc = bass.Bass("TRN2", target_bir_lowering=False)
    NSEQ = len(seq_lens)
    NTOK = sum(seq_lens)
    SMAX = max(seq_lens)

    def din(name, shape, dt=F32):
        return nc.dram_tensor(name, list(shape), dt, kind="ExternalInput").ap()

    def dint(name, shape, dt):
        return nc.dram_tensor(name, list(shape), dt, kind="Internal").ap()

    x_d = din("x", [NTOK, D])
    c_d = din("c", [NSEQ, D])
    w_ada_d = din("w_ada", [D, 6 * D])
    b_ada_d = din("b_ada", [1, 6 * D])
    w_in_d = din("w_in", [56 * 128, 1024])
    lq1_d = din("lq1", [1, 64])
    lk1_d = din("lk1", [1, 64])
    lq2_d = din("lq2", [1, 64])
    lk2_d = din("lk2", [1, 64])
    subln_d = din("subln_g", [1, 128])
    conv_w_d = din("conv_w", [4, D])
    conv_b_d = din("conv_b", [1, D])
    wlg_d = din("w_lru_gates", [4, 16, 64, 64])
    blg_d = din("b_lru_gates", [4, D])
    llam_d = din("lru_lambda", [2, D])
    w_ab_d = din("w_ab", [8 * 128, 1024])
    w_lb_d = din("w_lb", [8 * 128, 1024])
    w_out_d = din("w_out", [8 * 128, 1024])
    ln1g_d = din("ln1_g", [1, D])
    ln1b_d = din("ln1_b", [1, D])
    w_fi_d = din("w_ffn_in", [44 * 128, 1024])
    w_fo_d = din("w_ffn_out", [8 * 128, DFF])
    ln2g_d = din("ln2_g", [1, D])
    ln2b_d = din("ln2_b", [1, D])
    ident_d = din("ident", [128, 128])
    prot_d = din("prot", [128, 128])
    ropec_d = din("rope_c", [128, SMAX])
    ropes_d = din("rope_s", [128, SMAX])
    y_d = nc.dram_tensor("y", [NTOK, D], F32, kind="ExternalOutput").ap()

    w_in_bf = dint("w_in_bf", [56 * 128, 1024], BF16)
    w_ab_bf = dint("w_ab_bf", [8 * 128, 1024], BF16)
    w_lb_bf = dint("w_lb_bf", [8 * 128, 1024], BF16)
    w_out_bf = dint("w_out_bf", [8 * 128, 1024], BF16)
    w_fi_bf = dint("w_fi_bf", [44 * 128, 1024], BF16)
    w_fo_bf = dint("w_fo_bf", [8 * 128, DFF], BF16)
    mod_d = dint("mod_d", [NSEQ, 6 * D], F32)
    lruw_d = dint("lruw_d", [128, NCH, 8, 128], BF16)
    rec_d = dint("rec_d", [NCH, 128, NTOK], BF16)
    hT_d = dint("hT_d", [128, NCH, SMAX], BF16)
    BIGTH = int(os.environ.get("KBIGTH", "2048"))
    QBX = int(os.environ.get("KQB", str(QB)))

    dbg_out = {}
    if dbg:
        dbg_out["hT"] = nc.dram_tensor("dbg_hT", [128, NCH, NTOK], BF16, kind="ExternalOutput").ap()
        dbg_out["attnT"] = nc.dram_tensor("dbg_attnT", [128, NCH, NTOK], BF16, kind="ExternalOutput").ap()
        dbg_out["modT"] = nc.dram_tensor("dbg_modT", [128, 48, NSEQ], F32, kind="ExternalOutput").ap()
        dbg_out["rec"] = nc.dram_tensor("dbg_rec", [NCH, 128, NTOK], BF16, kind="ExternalOutput").ap()

    with ExitStack() as st:
        S = Sched(nc, st)

        def sb(name, shape, dt=F32):
            return st.enter_context(nc.sbuf_tensor(name, list(shape), dt))

        ident_bf = sb("ident_bf", [128, 128], BF16)
        ident_f = sb("ident_f", [128, 128], F32)
        prot_bf = sb("prot_bf", [128, 128], BF16)
        fmc = sb("fmc", [128, 11, NCH], F32)
        cdec = sb("cdec", [128, 4, NCH], F32)
        hbias = sb("hbias", [128, 4, NCH], F32)
        modT = sb("modT", [128, 48, NSEQ], F32)
        lamt = sb("lamt", [128, 8], F32)
        gsub = sb("gsub", [128, 128], F32)
        small = sb("small", [128, 64], F32)
        arena = sb("arena", [128, ARENA_BYTES // 2], BF16)
        ps = st.enter_context(nc.psum_tensor("ps", [128, 8 * 512], F32))

        b_const = Buf("const")
        b_hT = Buf("hT")

        def bank(b):
            return ps[:, b * 512:(b + 1) * 512]

        def bank_bf(b):
            return ps[:, b * 512:(b + 1) * 512].bitcast(BF16)

        pbuf = [Buf(f"psb{i}", excl=True) for i in range(8)]
        pstate = {"next": 0}

        def nextbank(lo=0, hi=8):
            n = pstate["next"]
            if n < lo or n >= hi:
                n = lo
            pstate["next"] = n + 1
            return n

        class Alloc:
            def __init__(self, off=0):
                self.off = off

            def get(self, shape, dt, name="t"):
                n = int(np.prod(shape[1:]))
                esz = 4 if dt == F32 else 2
                nbytes = (n * esz + 3) // 4 * 4
                assert self.off + nbytes <= ARENA_BYTES, f"arena overflow {name} {self.off + nbytes}"
                v = arena[0:shape[0], self.off // 2:(self.off + n * esz) // 2]
                if dt == F32:
                    v = v.bitcast(F32)
                if len(shape) == 3:
                    v = v.rearrange("p (a b) -> p a b", a=shape[1])
                elif len(shape) == 4:
                    v = v.rearrange("p (a b c) -> p a b c", a=shape[1], b=shape[2])
                self.off += nbytes
                return v

        sem_pool = {}

        def dsem(name):
            if name not in sem_pool:
                sem_pool[name] = S.new_sem("d_" + name)
            return sem_pool[name]

        ds_setup = dsem("setup")
        ds_cast = dsem("cast")
        b_wbf = Buf("wbf")


        SKIP = set(os.environ.get("KSKIP", "").split(","))

        cast_i = [0]
        ds_castk = [dsem("castk0"), dsem("castk1")]
        b_castk = [Buf("castk0"), Buf("castk1")]

        def cast_rows(dst, src, rows, blk):
            if "cast" in SKIP:
                return
            for r0 in range(0, rows, blk):
                r1 = min(rows, r0 + blk)
                k = cast_i[0] % 2
                cast_i[0] += 1
                S.dma("pool", lambda e, r0=r0, r1=r1: e.dma_start(out=dst[r0:r1, :], in_=src[r0:r1, :], max_dma_last_dim=4096),
                      ds_castk[k], writes=[b_castk[k]])

        al = Alloc()
        lv = al.get([128, 4, 64], F32)
        junk = al.get([128, 2, 64], F32)
        junk2 = al.get([128, 64], F32)
        wb = al.get([128, NCH, 8, 128], BF16)
        cT = al.get([128, NCH, NSEQ], F32)
        ones1 = al.get([1, NSEQ], F32)
        bada = al.get([1, 6 * D], F32)
        mod_sb = al.get([NSEQ, 6 * D], F32)
        wpan = [al.get([128, NCH, 512], F32) for _ in range(2)]
        b_wb = Buf("wb")
        b_lruw = Buf("lruw")
        b_modd = Buf("mod_d")
        b_mod = Buf("mod_sb")
        b_wpan = [Buf("wpan0"), Buf("wpan1")]
        ds_wp = [dsem("wp0"), dsem("wp1")]
        ds_wbd = dsem("wbd")

        S.op("pool", lambda e: e.memset(wb, 0.0), writes=[b_wb])
        S.dma("pool", lambda e: e.dma_start(out=ident_bf[:], in_=ident_d), ds_setup, writes=[b_const])
        S.dma("pool", lambda e: e.dma_start(out=prot_bf[:], in_=prot_d), ds_setup, writes=[b_const])
        for dg in (range(4) if "wbdma" not in SKIP else []):
            for j in range(2):
                src = wlg_d[dg].rearrange("(c j) d e -> j d c e", j=2)[j]
                S.dma("pool", lambda e, dg=dg, j=j, src=src: e.dma_start(out=wb[64 * j:64 * j + 64, :, dg, 64 * j:64 * j + 64], in_=src),
                      ds_wbd, reads=[], writes=[b_wb])
        cast_rows(w_in_bf, w_in_d, 56 * 128, 1024)
        S.dma("sp", lambda e: e.dma_start(out=ident_f[:], in_=ident_d), ds_setup, writes=[b_const])
        for s_ in range(NSEQ):
            S.dma("sp", lambda e, s_=s_: e.dma_start(out=cT[:, :, s_], in_=c_d[s_:s_ + 1, :].rearrange("o (c p) -> p (o c)", p=128),
                                                    allow_slow_non_contiguous=True), ds_setup, writes=[b_const])
        S.dma("sp", lambda e: e.dma_start(out=bada, in_=b_ada_d), ds_setup, writes=[b_const])
        vecs = [conv_w_d[0:1, :], conv_w_d[1:2, :], conv_w_d[2:3, :], conv_w_d[3:4, :], conv_b_d,
                blg_d[0:1, :], blg_d[1:2, :], blg_d[2:3, :], blg_d[3:4, :], llam_d[0:1, :], llam_d[1:2, :]]
        for i, v in (enumerate(vecs) if "slow" not in SKIP else []):
            S.dma("sp", lambda e, i=i, v=v: e.dma_start(out=fmc[:, i, :], in_=v.rearrange("o (c p) -> p (o c)", p=128),
                                                       allow_slow_non_contiguous=True), ds_setup, writes=[b_const])
        for i, v in (enumerate([lq1_d, lk1_d, lq2_d, lk2_d]) if "bcast" not in SKIP else []):
            S.dma("sp", lambda e, i=i, v=v: e.dma_start(out=lv[:, i, :], in_=v.broadcast_to([128, 64])), ds_setup, writes=[b_const])
        if "bcast" not in SKIP:
            S.dma("sp", lambda e: e.dma_start(out=gsub[:], in_=subln_d.broadcast_to([128, 128])), ds_setup, writes=[b_const])
        cast_rows(w_ab_bf, w_ab_d, 1024, 1024)
        cast_rows(w_lb_bf, w_lb_d, 1024, 1024)
        cast_rows(w_out_bf, w_out_d, 1024, 1024)
        cast_rows(w_fi_bf, w_fi_d, 44 * 128, 1024)
        cast_rows(w_fo_bf, w_fo_d, 1024, 512)
        S.barrier(exclude=ds_castk)
        S.op("dve", lambda e: e.tensor_tensor(out=junk[:, 0, :], in0=lv[:, 0, :], in1=lv[:, 1, :], op=ALU.mult), writes=[b_const])
        S.op("dve", lambda e: e.tensor_tensor(out=junk[:, 1, :], in0=lv[:, 2, :], in1=lv[:, 3, :], op=ALU.mult), reads=[b_const], writes=[b_const])
        S.op("dve", lambda e: e.tensor_scalar(out=gsub[:], in0=gsub[:], scalar1=1.0 - LAMBDA_INIT, scalar2=None, op0=ALU.mult),
             reads=[b_const], writes=[b_const])
        S.op("dve", lambda e: e.memset(ones1, 1.0), reads=[b_const], writes=[b_const])
        for c in range(NCH):
            for k in range(4):
                S.op("dve", lambda e, c=c, k=k: e.tensor_scalar(out=wb[:, c, 4 + k, :], in0=ident_bf[:], scalar1=fmc[:, k, c:c + 1],
                                                                scalar2=None, op0=ALU.mult), reads=[b_wb], writes=[b_wb])
        S.op("act", lambda e: e.activation(out=cT, in_=cT, func=AF.Silu), writes=[b_mod])
        S.op("act", lambda e: e.activation(out=small[:, 0:16], in_=fmc[:, 9:11, :].rearrange("p a c -> p (a c)"), func=AF.Exp, scale=-1.0),
             reads=[b_mod], writes=[b_mod])
        S.barrier(exclude=ds_castk)
        S.op("act", lambda e: e.activation(out=junk2, in_=junk[:, 0, :], func=AF.Identity, accum_out=lamt[:, 0:1]), writes=[b_mod])
        S.op("act", lambda e: e.activation(out=junk2, in_=junk[:, 1, :], func=AF.Identity, accum_out=lamt[:, 1:2]), reads=[b_mod], writes=[b_mod])
        S.op("act", lambda e: e.activation(out=lamt[:, 2:4], in_=lamt[:, 0:2], func=AF.Exp), reads=[b_mod], writes=[b_mod])
        S.op("act", lambda e: e.activation(out=small[:, 16:32], in_=small[:, 0:16], func=AF.Ln, bias=1.0, scale=1.0), reads=[b_mod], writes=[b_mod])
        S.dma("sp", lambda e: e.dma_start(out=lruw_d, in_=wb), ds_wbd, reads=[b_wb], writes=[b_lruw])
        S.barrier(exclude=ds_castk)
        S.op("dve", lambda e: e.tensor_tensor(out=lamt[:, 4:5], in0=lamt[:, 3:4], in1=lamt[:, 2:3], op=ALU.subtract), writes=[b_const])
        S.op("dve", lambda e: e.tensor_scalar(out=lamt[:, 4:5], in0=lamt[:, 4:5], scalar1=-LAMBDA_INIT, scalar2=None, op0=ALU.add),
             reads=[b_const], writes=[b_const])
        S.op("dve", lambda e: e.tensor_scalar(out=hbias[:], in0=fmc[:, 5:9, :], scalar1=0.5, scalar2=None, op0=ALU.mult), reads=[b_const], writes=[b_const])
        for d_ in range(2):
            S.op("dve", lambda e, d_=d_: e.tensor_scalar(out=cdec[:, 2 * d_, :], in0=small[:, 16 + 8 * d_:24 + 8 * d_], scalar1=-4.0,
                                                         scalar2=None, op0=ALU.mult), reads=[b_const], writes=[b_const])
            S.op("dve", lambda e, d_=d_: e.tensor_scalar(out=cdec[:, 2 * d_ + 1, :], in0=small[:, 16 + 8 * d_:24 + 8 * d_], scalar1=-8.0,
                                                         scalar2=None, op0=ALU.mult), reads=[b_const], writes=[b_const])
        for pn in (range(12) if "mod" not in SKIP else []):
            sl = pn % 2
            S.dma("sp", lambda e, pn=pn, sl=sl: e.dma_start(out=wpan[sl], in_=w_ada_d[:, pn * 512:(pn + 1) * 512].rearrange("(c p) n -> p c n", p=128)),
                  ds_wp[sl], writes=[b_wpan[sl]])
            bk = nextbank()
            for kc in range(NCH):
                S.op("pe", lambda e, bk=bk, kc=kc, sl=sl: e.matmul(bank(bk)[0:NSEQ, :], lhsT=cT[:, kc, :], rhs=wpan[sl][:, kc, :],
                                                                   start=(kc == 0), stop=False),
                     reads=[b_wpan[sl]], writes=[pbuf[bk]], mark=False)
            S.op("pe", lambda e, bk=bk, pn=pn: e.matmul(bank(bk)[0:NSEQ, :], lhsT=ones1, rhs=bada[:, pn * 512:(pn + 1) * 512],
                                                        start=False, stop=True), reads=[], writes=[pbuf[bk]])
            S.op("act", lambda e, bk=bk, pn=pn: e.activation(out=mod_sb[:, pn * 512:(pn + 1) * 512], in_=bank(bk)[0:NSEQ, :], func=AF.Identity),
                 reads=[pbuf[bk]], writes=[b_mod])
        S.dma("sp", lambda e: e.dma_start(out=mod_d, in_=mod_sb), ds_setup, reads=[b_mod], writes=[b_modd])
        for s_ in range(NSEQ):
            S.dma("sp", lambda e, s_=s_: e.dma_start(out=modT[:, :, s_], in_=mod_d[s_:s_ + 1, :].rearrange("o (c p) -> p (o c)", p=128),
                                                    allow_slow_non_contiguous=True), ds_setup, reads=[b_modd], writes=[b_modd])
        for lo in (8, 32):
            S.op("dve", lambda e, lo=lo: e.tensor_scalar(out=modT[:, lo:lo + 8, :], in0=modT[:, lo:lo + 8, :], scalar1=1.0, scalar2=None, op0=ALU.add),
                 reads=[b_modd], writes=[b_modd])
        ds_dbg = dsem("dbg")
        b_dbg = Buf("dbg")
        if dbg:
            S.dma("sp", lambda e: e.dma_start(out=dbg_out["modT"], in_=modT[:]), ds_dbg, reads=[b_modd], writes=[b_dbg])
        S.barrier()
        b_const = Buf("const2", const=True)
        b_wbf = Buf("wbf2", const=True)
        b_lruw = Buf("lruw2", const=True)
        b_recd_dummy = None

        def ln_stats(src_ap, b_src, st6, mv, rs, b_st):
            S.op("dve", lambda e: e.bn_stats(out=st6[:, 0, :], in_=src_ap[:, 0:512]), reads=[b_src], writes=[b_st])
            S.op("dve", lambda e: e.bn_stats(out=st6[:, 1, :], in_=src_ap[:, 512:1024]), reads=[b_src], writes=[b_st])
            S.op("dve", lambda e: e.bn_aggr(out=mv, in_=st6.rearrange("p a b -> p (a b)")), reads=[b_st], writes=[b_st])
            S.op("act", lambda e: e.activation(out=rs[:, 0:1], in_=mv[:, 1:2], func=AF.Sqrt, bias=LN_EPS, scale=1.0), reads=[b_st], writes=[b_st])
            S.op("dve", lambda e: e.reciprocal(out=rs[:, 0:1], in_=rs[:, 0:1]), reads=[b_st], writes=[b_st])
            S.op("dve", lambda e: e.tensor_scalar(out=rs[:, 1:2], in0=mv[:, 0:1], scalar1=rs[:, 0:1], scalar2=-1.0, op0=ALU.mult, op1=ALU.mult),
                 reads=[b_st], writes=[b_st])

        def ln_stats_multi(items):
            for src_ap, b_src, st6, mv, rs, b_st in items:
                S.op("dve", lambda e: e.bn_stats(out=st6[:, 0, :], in_=src_ap[:, 0:512]), reads=[b_src], writes=[b_st])
                S.op("dve", lambda e: e.bn_stats(out=st6[:, 1, :], in_=src_ap[:, 512:1024]), reads=[b_src], writes=[b_st])
                S.op("dve", lambda e: e.bn_aggr(out=mv, in_=st6.rearrange("p a b -> p (a b)")), reads=[b_st], writes=[b_st])
            for src_ap, b_src, st6, mv, rs, b_st in items:
                S.op("act", lambda e: e.activation(out=rs[:, 0:1], in_=mv[:, 1:2], func=AF.Sqrt, bias=LN_EPS, scale=1.0), reads=[b_st], writes=[b_st])
            for src_ap, b_src, st6, mv, rs, b_st in items:
                S.op("dve", lambda e: e.reciprocal(out=rs[:, 0:1], in_=rs[:, 0:1]), reads=[b_st], writes=[b_st])
                S.op("dve", lambda e: e.tensor_scalar(out=rs[:, 1:2], in0=mv[:, 0:1], scalar1=rs[:, 0:1], scalar2=-1.0, op0=ALU.mult, op1=ALU.mult),
                     reads=[b_st], writes=[b_st])

        def ln_affine_multi(items, gi, bi_, LNP, b_LNP):
            ln_stats_multi(items)
            for src_ap, b_src, st6, mv, rs, b_st in items:
                S.op("act", lambda e: e.activation(out=src_ap, in_=src_ap, func=AF.Identity, scale=rs[:, 0:1], bias=rs[:, 1:2]),
                     reads=[b_src, b_st], writes=[b_src])
            for src_ap, b_src, st6, mv, rs, b_st in items:
                S.op("dve", lambda e: e.tensor_tensor(out=src_ap, in0=src_ap, in1=LNP[:, gi, :], op=ALU.mult), reads=[b_src, b_LNP], writes=[b_src])
            for i, (src_ap, b_src, st6, mv, rs, b_st) in enumerate(items):
                eng = "pool" if i % 2 == 0 else "dve"
                S.op(eng, lambda e: e.tensor_tensor(out=src_ap, in0=src_ap, in1=LNP[:, bi_, :], op=ALU.add), reads=[b_src, b_LNP], writes=[b_src])

        def mm_group(out_ap, b_out, pairs, reads):
            n = len(pairs)
            for i, (l, r) in enumerate(pairs):
                S.op("pe", lambda e, l=l, r=r, i=i: e.matmul(out_ap, lhsT=l, rhs=r, start=(i == 0), stop=(i == n - 1)),
                     reads=reads, writes=[b_out], mark=(i == n - 1))

        for si in ([] if stop == "setup" else range(NSEQ)):
            SL = seq_lens[si]
            tok0 = sum(seq_lens[:si])
            NT = SL // 512
            sc1p = lambda c: modT[:, 8 + c, si:si + 1]
            sh1 = lambda c: modT[:, 0 + c, si:si + 1]
            g1c = lambda c: modT[:, 16 + c, si:si + 1]
            sh2 = lambda c: modT[:, 24 + c, si:si + 1]
            sc2p = lambda c: modT[:, 32 + c, si:si + 1]
            g2c = lambda c: modT[:, 40 + c, si:si + 1]

            al = Alloc()
            hT = al.get([128, NCH, SL], BF16)
            HOFF = al.off
            XT = [al.get([128, D], F32) for _ in range(8)]
            b_XT = [Buf(f"XT{i}") for i in range(8)]
            ds_XT = [dsem(f"xt{i}") for i in range(8)]
            XN = [al.get([128, D], BF16) for _ in range(4)]
            b_XN = [Buf(f"XN{i}") for i in range(4)]
            STT_ = [(al.get([128, 2, 6], F32), al.get([128, 2], F32), al.get([128, 2], F32), Buf(f"st{i}")) for i in range(4)]
            for g in range(NT):
                base = (g % 2) * 4
                items = []
                for j in range(4):
                    ti = g * 4 + j
                    sl = ti % 8
                    S.dma("sp", lambda e, sl=sl, ti=ti: e.dma_start(out=XT[sl], in_=x_d[tok0 + ti * 128: tok0 + (ti + 1) * 128, :]),
                          ds_XT[sl], writes=[b_XT[sl]])
                    items.append((XT[sl], b_XT[sl]) + STT_[j])
                ln_stats_multi(items)
                for j in range(4):
                    src_ap, b_src, st6, mv, rs, b_st = items[j]
                    S.op("act", lambda e: e.activation(out=XN[j], in_=src_ap, func=AF.Identity, scale=rs[:, 0:1], bias=rs[:, 1:2]),
                         reads=[b_src, b_st], writes=[b_XN[j]])
                for j in range(4):
                    for c in range(NCH):
                        bk = base + c // 2
                        S.op("pe", lambda e, bk=bk, c=c, j=j: e.transpose(
                            bank_bf(bk)[:, (c % 2) * 512 + j * 128:(c % 2) * 512 + (j + 1) * 128], XN[j][:, c * 128:(c + 1) * 128], ident_bf[:]),
                            reads=[b_XN[j], b_const], writes=[pbuf[bk]], mark=(c % 2 == 1))
                for c in range(NCH):
                    bk = base + c // 2
                    S.op("act", lambda e, bk=bk, c=c, g=g: e.activation(out=hT[:, c, g * 512:(g + 1) * 512], in_=bank_bf(bk)[:, (c % 2) * 512:(c % 2) * 512 + 512],
                                                                        func=AF.Identity, scale=sc1p(c), bias=sh1(c)),
                         reads=[pbuf[bk], b_const], writes=[b_hT])
            if dbg:
                S.dma("sp", lambda e: e.dma_start(out=dbg_out["hT"][:, :, tok0:tok0 + SL], in_=hT[:, :, 0:SL]), ds_dbg, reads=[b_hT], writes=[b_dbg])
            S.barrier()
            if stop == "p1":
                break

            al = Alloc(HOFF)
            WP = [al.get([128, 2, NCH, 128], BF16) for _ in range(2)]
            LW = [al.get([128, 8, 128], BF16) for _ in range(2)]
            b_WP = [Buf("WP0"), Buf("WP1")]
            ds_WP = [dsem("WP0"), dsem("WP1")]
            XR = al.get([128, SL + 4], BF16)
            G_ = al.get([128, SL], BF16)
            XC = al.get([128, SL], F32)
            XCb = al.get([128, SL], BF16)
            HF = al.get([128, SL], F32)
            LB = min(SL, 2048) if SL <= 2048 else 1024
            if os.environ.get("KLB"):
                LB = int(os.environ["KLB"])
            NBLK = SL // LB
            TPB = LB // 512
            HBK = al.get([128, LB], F32)
            REC = al.get([128, SL], BF16)
            AA = al.get([128, LB], F32)
            QQ = al.get([128, LB], F32)
            DD = al.get([128, LB], F32)
            CAR = al.get([128, 2], F32)
            TMP = [[al.get([128, 512], F32) for _ in range(2)] for _ in range(2)]
            b_XR = [Buf(f"XR{t}") for t in range(NT)]
            b_G = [Buf(f"G{t}") for t in range(NT)]
            b_XC = [Buf(f"XC{t}") for t in range(NT)]
            b_XCb = [Buf(f"XCb{t}") for t in range(NT)]
            b_HF = [Buf(f"HF{t}") for t in range(NT)]
            b_HBK = Buf("HBK")
            b_REC = Buf("REC")
            b_AA = Buf("AA")
            b_QQ = Buf("QQ")
            b_DD = Buf("DD")
            b_CAR = Buf("CAR")
            b_TMP = [[Buf(f"TMP{i}{j}") for j in range(2)] for i in range(2)]
            ds_rec = dsem("rec")
            b_recd = Buf("rec_d")
            b_pad = Buf("XRpad")
            S.op("pool", lambda e: e.memset(XR[:, 0:2], 0.0), writes=[b_pad])
            S.op("pool", lambda e: e.memset(XR[:, SL + 2:SL + 4], 0.0), writes=[b_pad])

            def load_lru_panels(c):
                sl = c % 2
                S.dma("sp", lambda e: e.dma_start(out=WP[sl][:, 0].rearrange("p k n -> p (k n)"), in_=w_in_bf[(24 + c) * 128:(25 + c) * 128, :]),
                      ds_WP[sl], reads=[b_wbf], writes=[b_WP[sl]])
                S.dma("sp", lambda e: e.dma_start(out=WP[sl][:, 1].rearrange("p k n -> p (k n)"), in_=w_in_bf[(32 + c) * 128:(33 + c) * 128, :]),
                      ds_WP[sl], reads=[b_wbf], writes=[b_WP[sl]])
                S.dma("sp", lambda e: e.dma_start(out=LW[sl], in_=lruw_d[:, c]), ds_WP[sl], reads=[b_lruw], writes=[b_WP[sl]])

            load_lru_panels(0)
            for c in range(NCH):
                if c + 1 < NCH:
                    load_lru_panels(c + 1)
                sl = c % 2
                for t in range(NT):
                    bk = nextbank()
                    mm_group(bank(bk), pbuf[bk], [(WP[sl][:, 0, kc, :], hT[:, kc, t * 512:(t + 1) * 512]) for kc in range(NCH)], [b_WP[sl], b_hT])
                    S.op("act", lambda e, bk=bk, t=t: e.activation(out=XR[:, 2 + t * 512:2 + (t + 1) * 512], in_=bank(bk), func=AF.Identity),
                         reads=[pbuf[bk]], writes=[b_XR[t]])
                    bk = nextbank()
                    mm_group(bank(bk), pbuf[bk], [(WP[sl][:, 1, kc, :], hT[:, kc, t * 512:(t + 1) * 512]) for kc in range(NCH)], [b_WP[sl], b_hT])
                    S.op("act", lambda e, bk=bk, t=t: e.activation(out=G_[:, t * 512:(t + 1) * 512], in_=bank(bk), func=AF.Gelu_apprx_tanh),
                         reads=[pbuf[bk]], writes=[b_G[t]])
                for t in range(NT):
                    bk = nextbank()
                    rd = [b_WP[sl], b_pad, b_XR[t]] + ([b_XR[t - 1]] if t > 0 else []) + ([b_XR[t + 1]] if t + 1 < NT else [])
                    mm_group(bank(bk), pbuf[bk], [(LW[sl][:, 4 + k, :], XR[:, t * 512 + k:t * 512 + k + 512]) for k in range(4)], rd)
                    S.op("act", lambda e, bk=bk, t=t, c=c: e.activation(out=XC[:, t * 512:(t + 1) * 512], in_=bank(bk), func=AF.Identity,
                                                                        bias=fmc[:, 4, c:c + 1], scale=1.0),
                         reads=[pbuf[bk], b_const], writes=[b_XC[t]])
                    S.op("pool", lambda e, t=t: e.tensor_copy(out=XCb[:, t * 512:(t + 1) * 512], in_=XC[:, t * 512:(t + 1) * 512]),
                         reads=[b_XC[t]], writes=[b_XCb[t]])
                it = 0
                for d_ in range(2):
                    blocks = range(NBLK) if d_ == 0 else range(NBLK - 1, -1, -1)
                    for bi in blocks:
                        bsl = slice(bi * LB, (bi + 1) * LB)
                        for tl in range(TPB):
                            t = bi * TPB + tl
                            tm = TMP[it % 2]
                            btm = b_TMP[it % 2]
                            it += 1
                            TR, TI = tm
                            tsl = slice(t * 512, (t + 1) * 512)
                            lsl = slice(tl * 512, (tl + 1) * 512)
                            bkr = nextbank()
                            mm_group(bank(bkr), pbuf[bkr], [(LW[sl][:, 2 * d_, :], XCb[:, tsl])], [b_WP[sl], b_XCb[t]])
                            bki = nextbank()
                            mm_group(bank(bki), pbuf[bki], [(LW[sl][:, 2 * d_ + 1, :], XCb[:, tsl])], [b_WP[sl], b_XCb[t]])
                            S.op("act", lambda e: e.activation(out=TR, in_=bank(bkr), func=AF.Tanh, bias=hbias[:, 2 * d_, c:c + 1], scale=0.5),
                                 reads=[pbuf[bkr], b_const], writes=[btm[0]])
                            S.op("act", lambda e: e.activation(out=TI, in_=bank(bki), func=AF.Tanh, bias=hbias[:, 2 * d_ + 1, c:c + 1], scale=0.5),
                                 reads=[pbuf[bki], b_const], writes=[btm[1]])
                            S.op("act", lambda e: e.activation(out=AA[:, lsl], in_=TR, func=AF.Exp, scale=cdec[:, 2 * d_, c:c + 1], bias=cdec[:, 2 * d_, c:c + 1]),
                                 reads=[btm[0], b_const], writes=[b_AA])
                            S.op("act", lambda e: e.activation(out=QQ[:, lsl], in_=TR, func=AF.Exp, scale=cdec[:, 2 * d_ + 1, c:c + 1], bias=cdec[:, 2 * d_ + 1, c:c + 1]),
                                 reads=[btm[0], b_const], writes=[b_QQ])
                            S.op("dve", lambda e: e.scalar_tensor_tensor(out=DD[:, lsl], in0=TI, scalar=1.0, in1=XC[:, tsl], op0=ALU.add, op1=ALU.mult),
                                 reads=[btm[1], b_XC[t]], writes=[b_DD])
                        S.op("act", lambda e: e.activation(out=QQ, in_=QQ, func=AF.Sqrt, bias=0.25, scale=-0.25), reads=[b_QQ], writes=[b_QQ])
                        S.op("dve", lambda e: e.tensor_tensor(out=DD, in0=DD, in1=QQ, op=ALU.mult), reads=[b_DD, b_QQ], writes=[b_DD])
                        if d_ == 0:
                            init = 0.0 if bi == 0 else HF[:, bi * LB - 1:bi * LB]
                            S.op("dve", lambda e: e.tensor_tensor_scan(out=HF[:, bsl], data0=AA, data1=DD, initial=init, op0=ALU.mult, op1=ALU.add),
                                 reads=[b_AA, b_DD] + b_HF, writes=b_HF)
                        else:
                            init = 0.0 if bi == NBLK - 1 else CAR[:, 0:1]
                            S.op("dve", lambda e: e.tensor_tensor_scan(out=HBK[:, ::-1], data0=AA[:, ::-1], data1=DD[:, ::-1], initial=init,
                                                                        op0=ALU.mult, op1=ALU.add),
                                 reads=[b_AA, b_DD, b_CAR], writes=[b_HBK])
                            if bi > 0:
                                S.op("dve", lambda e: e.tensor_copy(out=CAR[:, 0:1], in_=HBK[:, 0:1]), reads=[b_HBK], writes=[b_CAR])
                            S.op("pool", lambda e: e.tensor_tensor(out=HBK, in0=HBK, in1=HF[:, bsl], op=ALU.add),
                                 reads=[b_HBK] + b_HF, writes=[b_HBK])
                            S.op("pool", lambda e: e.tensor_tensor(out=REC[:, bsl], in0=HBK, in1=G_[:, bsl], op=ALU.mult),
                                 reads=[b_HBK] + b_G, writes=[b_REC])
                S.dma("sp", lambda e, c=c: e.dma_start(out=rec_d[c, :, tok0:tok0 + SL], in_=REC), ds_rec, reads=[b_REC], writes=[b_recd])
                if dbg:
                    S.dma("sp", lambda e, c=c: e.dma_start(out=dbg_out["rec"][c, :, tok0:tok0 + SL], in_=REC), ds_dbg, reads=[b_REC], writes=[b_dbg])
            S.barrier()
            if stop == "p2":
                break

            NQB = max(1, SL // QBX)
            QBL = min(QBX, SL)
            BIG = SL > BIGTH
            ds_hsp = dsem("hsp")
            b_hTd = Buf("hT_d")
            if BIG:
                S.dma("sp", lambda e: e.dma_start(out=hT_d[:, :, 0:SL], in_=hT), ds_hsp, reads=[b_hT], writes=[b_hTd])
            for qb in range(NQB):
                q0 = qb * QBL
                al = Alloc(HOFF)
                attnT = al.get([128, NCH, QBL], BF16)
                b_attnT = Buf("attnT")
                p4_off = al.off
                QT = al.get([128, QBL], BF16)
                KT = al.get([128, SL], BF16)
                NKT = SL // 128
                V_ = al.get([128, NKT, 130], BF16)
                WQ = [al.get([128, 3, NCH, 128], BF16) for _ in range(2)]
                b_WQ = [Buf("WQ0"), Buf("WQ1")]
                ds_WQ = [dsem("WQ0"), dsem("WQ1")]
                RC = [al.get([128, 2, 512], F32) for _ in range(2)]
                b_RC = [Buf("RC0"), Buf("RC1")]
                ds_RC = [dsem("RC0"), dsem("RC1")]
                QBF = [al.get([128, 512], BF16) for _ in range(2)]
                b_QBF = [Buf("QBF0"), Buf("QBF1")]
                T12 = [[al.get([128, 512], F32) for _ in range(2)] for _ in range(2)]
                b_T12 = [[Buf("T1"), Buf("T2")] for _ in range(2)]
                PT = [al.get([128, 2, 512], BF16) for _ in range(3)]
                b_PT = [Buf(f"PT{i}") for i in range(3)]
                NQG = QBL // 512
                ACCS = al.get([128, 8, 130], F32)
                b_ACCS = Buf("ACCS")
                O0 = al.get([128, 128], F32)
                b_O0 = Buf("O0")
                OO = [al.get([128, 128], F32) for _ in range(4 * NQG)]
                b_OO = [Buf(f"OO{i}") for i in range(4 * NQG)]
                AT = [al.get([128, 128], BF16) for _ in range(4)]
                b_AT = [Buf(f"AT{i}") for i in range(4)]
                nrm = al.get([128, 16], F32)
                b_nrm = Buf("nrm")
                st6n = al.get([128, 6], F32)
                mvn = al.get([128, 2], F32)
                b_stn = Buf("stn")
                msa = al.get([128, 4 * NQG], F32)
                b_msa = Buf("msa")
                b_QT = Buf("QT")
                b_KT = Buf("KT")
                b_V = Buf("V")
                S.op("pool", lambda e: e.memset(V_[:, :, 128:130], 1.0), writes=[b_V])

                def load_head_w(h):
                    sl = h % 2
                    for i in range(3):
                        S.dma("sp", lambda e, i=i: e.dma_start(out=WQ[sl][:, i].rearrange("p k n -> p (k n)"), in_=w_in_bf[(i * 8 + h) * 128:(i * 8 + h + 1) * 128, :]),
                              ds_WQ[sl], reads=[b_wbf], writes=[b_WQ[sl]])

                rc_i = [0]

                def rope_proj(h, which, t_tok, dst_ap, b_dst):
                    sl = h % 2
                    r = rc_i[0] % 2
                    rc_i[0] += 1
                    S.dma("sp", lambda e: e.dma_start(out=RC[r][:, 0, :], in_=ropec_d[:, t_tok:t_tok + 512]), ds_RC[r], writes=[b_RC[r]])
                    S.dma("sp", lambda e: e.dma_start(out=RC[r][:, 1, :], in_=ropes_d[:, t_tok:t_tok + 512]), ds_RC[r], writes=[b_RC[r]])
                    bka = nextbank(0, 5)
                    mm_group(bank(bka), pbuf[bka], [(WQ[sl][:, which, kc, :], hT[:, kc, t_tok:t_tok + 512]) for kc in range(NCH)], [b_WQ[sl], b_hT])
                    S.op("act", lambda e: e.activation(out=QBF[r], in_=bank(bka), func=AF.Identity), reads=[pbuf[bka]], writes=[b_QBF[r]])
                    t1, t2 = T12[r]
                    S.op("dve", lambda e: e.tensor_tensor(out=t1, in0=bank(bka), in1=RC[r][:, 0, :], op=ALU.mult),
                         reads=[pbuf[bka], b_RC[r]], writes=[b_T12[r][0]])

                    def part2():
                        bkb = nextbank(0, 5)
                        mm_group(bank(bkb), pbuf[bkb], [(prot_bf[:], QBF[r])], [b_const, b_QBF[r]])
                        S.op("dve", lambda e: e.tensor_tensor(out=t2, in0=bank(bkb), in1=RC[r][:, 1, :], op=ALU.mult),
                             reads=[pbuf[bkb], b_RC[r]], writes=[b_T12[r][1]])
                        S.op("pool", lambda e: e.tensor_tensor(out=dst_ap, in0=t1, in1=t2, op=ALU.add),
                             reads=[b_T12[r][0], b_T12[r][1]], writes=[b_dst])
                    return part2

                def acc_ap(a, lo=0, hi=129):
                    bk = 5 + a // 3
                    o = (a % 3) * 130
                    return ps[:, bk * 512 + o + lo: bk * 512 + o + hi]

                load_head_w(0)
                pending_tail = []
                for h in range(NCH):
                    if h + 1 < NCH:
                        load_head_w(h + 1)
                    sl = h % 2
                    jobs = [(1, t * 512, KT[:, t * 512:(t + 1) * 512], b_KT) for t in range(NT)]
                    jobs += [(0, q0 + t * 512, QT[:, t * 512:(t + 1) * 512], b_QT) for t in range(QBL // 512)]
                    vper = -(-NKT // len(jobs))
                    vi = 0
                    prev = None
                    for (which, t_tok, dst_ap, b_dst) in jobs:
                        p2 = rope_proj(h, which, t_tok, dst_ap, b_dst)
                        if prev is not None:
                            prev()
                        prev = p2
                        for _ in range(vper):
                            if vi < NKT:
                                kt = vi
                                vi += 1
                                bk = nextbank(0, 5)
                                mm_group(bank(bk)[:, 0:128], pbuf[bk], [(hT[:, kc, kt * 128:(kt + 1) * 128], WQ[sl][:, 2, kc, :]) for kc in range(NCH)], [b_WQ[sl], b_hT])
                                S.op("dve", lambda e, bk=bk, kt=kt: e.tensor_copy(out=V_[:, kt, 0:128], in_=bank(bk)[:, 0:128]),
                                     reads=[pbuf[bk]], writes=[b_V])
                    prev()
                    assert vi == NKT
                    while pending_tail:
                        pending_tail.pop(0)()
                    for qg in range(QBL // 512):
                        def emit_qk_exp(kt, qg=qg):
                            pr = kt % 2
                            stv = ps[:, pr * 1024:(pr + 1) * 1024].rearrange("p (a b) -> p a b", a=2)
                            bst = [pbuf[2 * pr], pbuf[2 * pr + 1]]
                            S.op("pe", lambda e: e.matmul(stv[:, 0, :], lhsT=KT[0:64, kt * 128:(kt + 1) * 128], rhs=QT[0:64, qg * 512:(qg + 1) * 512],
                                                          start=True, stop=True), reads=[b_KT, b_QT], writes=[bst[0]], mark=False)
                            S.op("pe", lambda e: e.matmul(stv[:, 1, :], lhsT=KT[64:128, kt * 128:(kt + 1) * 128], rhs=QT[64:128, qg * 512:(qg + 1) * 512],
                                                          start=True, stop=True), reads=[b_KT, b_QT], writes=[bst[1]], mark=True)
                            pi = kt % 3
                            S.op("act", lambda e: e.activation(out=PT[pi], in_=stv, func=AF.Exp, scale=0.125), reads=bst, writes=[b_PT[pi]])

                        def emit_pv(kt):
                            pi = kt % 3
                            for a in range(8):
                                cm, qs = a // 4, a % 4
                                S.op("pe", lambda e, a=a, cm=cm, qs=qs: e.matmul(acc_ap(a), lhsT=PT[pi][:, cm, qs * 128:(qs + 1) * 128], rhs=V_[:, kt, 0:129],
                                                                                 start=(kt == 0 and a % 3 == 0), stop=(kt == NKT - 1), skip_group_check=True),
                                     reads=[b_PT[pi], b_V], writes=[pbuf[5 + a // 3]], mark=(a == 7))

                        emit_qk_exp(0)
                        for kt in range(NKT):
                            if kt + 1 < NKT:
                                emit_qk_exp(kt + 1)
                            emit_pv(kt)
                        for bkk in range(3):
                            n_ = 3 if bkk < 2 else 2
                            src = ps[:, (5 + bkk) * 512:(5 + bkk) * 512 + n_ * 130].rearrange("p (a b) -> p a b", a=n_)
                            S.op("dve", lambda e, bkk=bkk, n_=n_, src=src: e.tensor_copy(out=ACCS[:, 3 * bkk:3 * bkk + n_, :], in_=src),
                                 reads=[pbuf[5 + bkk]], writes=[b_ACCS])
                        S.op("dve", lambda e: e.reciprocal(out=nrm[:, 0:8], in_=ACCS[:, :, 128]), reads=[b_ACCS], writes=[b_nrm])
                        S.op("dve", lambda e: e.tensor_scalar(out=nrm[:, 8:12], in0=nrm[:, 4:8], scalar1=lamt[:, 4:5], scalar2=None, op0=ALU.mult),
                             reads=[b_nrm, b_const], writes=[b_nrm])
                        for qs in range(4):
                            oi = qg * 4 + qs
                            S.op("dve", lambda e, qs=qs: e.tensor_scalar(out=O0, in0=ACCS[:, qs, 0:128], scalar1=nrm[:, qs:qs + 1], scalar2=None, op0=ALU.mult),
                                 reads=[b_ACCS, b_nrm], writes=[b_O0])
                            S.op("dve", lambda e, qs=qs, oi=oi: e.scalar_tensor_tensor(out=OO[oi], in0=ACCS[:, 4 + qs, 0:128], scalar=nrm[:, 8 + qs:9 + qs], in1=O0,
                                                                                      op0=ALU.mult, op1=ALU.add),
                                 reads=[b_ACCS, b_nrm, b_O0], writes=[b_OO[oi]])
                            S.op("dve", lambda e, oi=oi: e.bn_stats(out=st6n, in_=OO[oi]), reads=[b_OO[oi]], writes=[b_stn])
                            S.op("dve", lambda e: e.bn_aggr(out=mvn, in_=st6n), reads=[b_stn], writes=[b_stn])
                            S.op("dve", lambda e, oi=oi: e.tensor_scalar(out=msa[:, oi:oi + 1], in0=mvn[:, 0:1], scalar1=mvn[:, 0:1], scalar2=mvn[:, 1:2],
                                                                        op0=ALU.mult, op1=ALU.add),
                                 reads=[b_stn], writes=[b_msa])

                    def head_tail(h=h):
                        S.op("act", lambda e: e.activation(out=msa, in_=msa, func=AF.Sqrt, bias=RMS_EPS, scale=1.0), reads=[b_msa], writes=[b_msa])
                        S.op("dve", lambda e: e.reciprocal(out=msa, in_=msa), reads=[b_msa], writes=[b_msa])
                        for qg2 in range(NQG):
                            for qs in range(4):
                                oi = qg2 * 4 + qs
                                S.op("dve", lambda e, qs=qs, oi=oi: e.scalar_tensor_tensor(out=AT[qs], in0=OO[oi], scalar=msa[:, oi:oi + 1], in1=gsub[:],
                                                                                          op0=ALU.mult, op1=ALU.mult),
                                     reads=[b_OO[oi], b_msa, b_const], writes=[b_AT[qs]])
                                S.op("pe", lambda e, qs=qs: e.transpose(bank_bf(4)[:, qs * 128:(qs + 1) * 128], AT[qs], ident_bf[:]),
                                     reads=[b_AT[qs], b_const], writes=[pbuf[4]], mark=(qs == 3))
                            S.op("dve", lambda e, qg2=qg2: e.tensor_copy(out=attnT[:, h, qg2 * 512:(qg2 + 1) * 512], in_=bank_bf(4)[:, 0:512]),
                                 reads=[pbuf[4]], writes=[b_attnT])

                    pending_tail.append(head_tail)
                while pending_tail:
                    pending_tail.pop(0)()
                if dbg:
                    S.dma("sp", lambda e: e.dma_start(out=dbg_out["attnT"][:, :, tok0 + q0:tok0 + q0 + QBL], in_=attnT), ds_dbg, reads=[b_attnT], writes=[b_dbg])
                S.barrier()
                if stop == "p3":
                    break

                T = 512
                NST = T // 128
                al = Alloc(0 if BIG else p4_off)
                LNP = al.get([128, 4, D], F32)
                b_LNP = Buf("LNP", const=False)
                ds_lnp = dsem("lnp")
                for i, v in enumerate([ln1g_d, ln1b_d, ln2g_d, ln2b_d]):
                    S.dma("sp", lambda e, i=i, v=v: e.dma_start(out=LNP[:, i, :], in_=v.broadcast_to([128, D])), ds_lnp, writes=[b_LNP])
                X1 = al.get([128, NST, D], F32)
                b_X1 = [Buf(f"X1_{i}") for i in range(NST)]
                ds_X1 = [dsem(f"x1_{i}") for i in range(NST)]
                PAN = [al.get([128, 4096], BF16) for _ in range(3)]
                b_PAN = [Buf(f"PAN{i}") for i in range(3)]
                ds_PAN = [dsem(f"pan{i}") for i in range(3)]
                if BIG:
                    assert al.off <= HOFF, al.off
                    al.off = p4_off
                    HTT = al.get([128, NCH, T], BF16)
                    b_HTT = Buf("HTT")
                    ds_HTT = dsem("htt")
                TT_ = [al.get([128, T], F32) for _ in range(2)]
                b_TT = [Buf("TT0"), Buf("TT1")]
                XN2 = [al.get([128, D], BF16) for _ in range(NST)]
                b_XN2 = [Buf(f"XN2{i}") for i in range(NST)]
                STS = [(al.get([128, 2, 6], F32), al.get([128, 2], F32), al.get([128, 2], F32), Buf(f"sts{i}")) for i in range(NST)]
                ov = al.off
                MG = al.get([128, NCH, T], BF16)
                RT = al.get([128, NCH, T], BF16)
                SG = [al.get([128, T], F32) for _ in range(2)]
                AB = [al.get([128, T], F32) for _ in range(2)]
                e_end = al.off
                al.off = ov
                H2 = al.get([128, NCH, T], BF16)
                UT = al.get([128, NFF, T], BF16)
                SU = [al.get([128, T], F32) for _ in range(2)]
                b_MG = Buf("MG")
                b_RT = Buf("RT")
                ds_RT = dsem("rt")
                b_SG = [Buf("SG0"), Buf("SG1")]
                b_AB = [Buf("AB0"), Buf("AB1")]
                b_H2 = Buf("H2")
                b_UT = Buf("UT")
                b_SU = [Buf("SU0"), Buf("SU1")]
                b_yd = Buf("y_d")
                ds_y = [dsem(f"y{i}") for i in range(4)]

                NG = QBL // T
                panels = []
                for g in range(NG):
                    for oc in range(NCH):
                        panels.append(("a", g, oc))
                    for oc in range(NCH):
                        panels.append(("b", g, oc))
                    for j in range(NFF):
                        panels.append(("d", g, j))
                    for oc in range(NCH):
                        panels.append(("e", g, oc))
                pstate4 = {"issued": 0}

                def issue_panel(i):
                    kind, g, k = panels[i]
                    sl = i % 3
                    pv = PAN[sl]
                    if kind == "a":
                        v4 = pv.rearrange("p (a k n) -> p a k n", a=4, k=NCH)
                        srcs = [w_in_bf[(40 + k) * 128:(41 + k) * 128, :], w_in_bf[(48 + k) * 128:(49 + k) * 128, :],
                                w_ab_bf[k * 128:(k + 1) * 128, :], w_lb_bf[k * 128:(k + 1) * 128, :]]
                        for a_, s_ in enumerate(srcs):
                            S.dma("sp", lambda e, a_=a_, s_=s_, v4=v4: e.dma_start(out=v4[:, a_].rearrange("p k n -> p (k n)"), in_=s_),
                                  ds_PAN[sl], reads=[b_wbf], writes=[b_PAN[sl]])
                    elif kind == "b":
                        v3 = pv[:, 0:NCH * 128].rearrange("p (k n) -> p k n", k=NCH)
                        S.dma("sp", lambda e, v3=v3, k=k: e.dma_start(out=v3.rearrange("p k n -> p (k n)"), in_=w_out_bf[k * 128:(k + 1) * 128, :]),
                              ds_PAN[sl], reads=[b_wbf], writes=[b_PAN[sl]])
                    elif kind == "d":
                        v4 = pv[:, 0:2 * NCH * 128].rearrange("p (a k n) -> p a k n", a=2, k=NCH)
                        for a_ in range(2):
                            S.dma("sp", lambda e, a_=a_, v4=v4, k=k: e.dma_start(out=v4[:, a_].rearrange("p k n -> p (k n)"), in_=w_fi_bf[(a_ * NFF + k) * 128:(a_ * NFF + k + 1) * 128, :]),
                                  ds_PAN[sl], reads=[b_wbf], writes=[b_PAN[sl]])
                    else:
                        v3 = pv[:, 0:NFF * 128].rearrange("p (k n) -> p k n", k=NFF)
                        S.dma("sp", lambda e, v3=v3, k=k: e.dma_start(out=v3.rearrange("p k n -> p (k n)"), in_=w_fo_bf[k * 128:(k + 1) * 128, :]),
                              ds_PAN[sl], reads=[b_wbf], writes=[b_PAN[sl]])

                def get_panel():
                    i = pstate4["cur"]
                    while pstate4["issued"] < min(len(panels), i + 3):
                        issue_panel(pstate4["issued"])
                        pstate4["issued"] += 1
                    pstate4["cur"] = i + 1
                    return PAN[i % 3], b_PAN[i % 3]

                pstate4["cur"] = 0

                def residual_block(g, pv3, nk, rhs_fn, rhs_bufs, b_pan, gcol, tti):
                    bk = nextbank()
                    mm_group(bank(bk)[:, 0:T], pbuf[bk], [(pv3[:, kc, :], rhs_fn(kc)) for kc in range(nk)], [b_pan] + rhs_bufs)
                    tt = TT_[tti % 2]
                    btt = b_TT[tti % 2]
                    S.op("act", lambda e: e.activation(out=tt, in_=bank(bk)[:, 0:T], func=AF.Identity, scale=gcol),
                         reads=[pbuf[bk], b_const], writes=[btt])
                    bk2 = nextbank()
                    for s_ in range(NST):
                        S.op("pe", lambda e, s_=s_: e.transpose(bank(bk2)[:, s_ * 128:(s_ + 1) * 128], tt[:, s_ * 128:(s_ + 1) * 128], ident_f[:]),
                             reads=[btt, b_const], writes=[pbuf[bk2]], mark=(s_ == NST - 1))
                    return bk2

                for g in range(NG):
                    gt0 = q0 + g * T
                    gl0 = g * T
                    S.fence(["sp", "act", "dve", "pool"], [b_H2, b_UT, b_SU[0], b_SU[1]])
                    for s_ in range(NST):
                        S.dma("pool", lambda e, s_=s_: e.dma_start(out=X1[:, s_, :], in_=x_d[tok0 + gt0 + s_ * 128:tok0 + gt0 + (s_ + 1) * 128, :]),
                              ds_X1[s_], writes=[b_X1[s_]])
                    S.dma("sp", lambda e: e.dma_start(out=RT, in_=rec_d[:, :, tok0 + gt0:tok0 + gt0 + T].rearrange("c p t -> p c t")),
                          ds_RT, reads=[b_recd], writes=[b_RT])
                    if BIG:
                        S.dma("sp", lambda e: e.dma_start(out=HTT, in_=hT_d[:, :, gt0:gt0 + T]), ds_HTT, reads=[b_hTd], writes=[b_HTT])
                        hsrc = lambda kc: HTT[:, kc, :]
                        b_hsrc = b_HTT
                    else:
                        hsrc = lambda kc: hT[:, kc, gt0:gt0 + T]
                        b_hsrc = b_hT
                    for oc in range(NCH):
                        pv, bp = get_panel()
                        v4 = pv.rearrange("p (a k n) -> p a k n", a=4, k=NCH)
                        bks = []
                        order = [(0, hsrc, b_hsrc), (2, lambda kc: attnT[:, kc, gl0:gl0 + T], b_attnT),
                                 (1, hsrc, b_hsrc), (3, lambda kc: RT[:, kc, :], b_RT)]
                        for a_, rf, rb in order:
                            bk = nextbank()
                            mm_group(bank(bk)[:, 0:T], pbuf[bk], [(v4[:, a_, kc, :], rf(kc)) for kc in range(NCH)], [bp, rb])
                            bks.append(bk)
                        for half in range(2):
                            sg = SG[half]
                            ab = AB[half]
                            bg, bb = bks[2 * half], bks[2 * half + 1]
                            S.op("act", lambda e, sg=sg, bg=bg: e.activation(out=sg, in_=bank(bg)[:, 0:T], func=AF.Sigmoid),
                                 reads=[pbuf[bg]], writes=[b_SG[half]])
                            S.op("dve", lambda e, sg=sg, ab=ab, bb=bb: e.tensor_tensor(out=ab, in0=bank(bb)[:, 0:T], in1=sg, op=ALU.mult),
                                 reads=[pbuf[bb], b_SG[half]], writes=[b_AB[half]])
                        S.op("pool", lambda e, oc=oc: e.tensor_tensor(out=MG[:, oc, :], in0=AB[0], in1=AB[1], op=ALU.add),
                             reads=[b_AB[0], b_AB[1]], writes=[b_MG])
                    for oc in range(NCH):
                        pv, bp = get_panel()
                        v3 = pv[:, 0:NCH * 128].rearrange("p (k n) -> p k n", k=NCH)
                        bk2 = residual_block(g, v3, NCH, lambda kc: MG[:, kc, :], [b_MG], bp, g1c(oc), oc)
                        S.op("dve", lambda e, oc=oc, bk2=bk2: e.scalar_tensor_tensor(
                            out=X1[:, :, oc * 128:(oc + 1) * 128], in0=X1[:, :, oc * 128:(oc + 1) * 128], scalar=ALPHA,
                            in1=bank(bk2)[:, 0:T].rearrange("p (s n) -> p s n", s=NST), op0=ALU.mult, op1=ALU.add),
                            reads=[pbuf[bk2]] + b_X1, writes=b_X1)
                    S.fence(["act", "dve"], [b_MG, b_RT, b_SG[0], b_SG[1], b_AB[0], b_AB[1]])
                    bks_c = [nextbank() for _ in range(4)]
                    items = [(X1[:, s_, :], b_X1[s_]) + STS[s_] for s_ in range(NST)]
                    ln_affine_multi(items, 0, 1, LNP, b_LNP)
                    ln_stats_multi(items)
                    for s_ in range(NST):
                        xs_ap, _, st6, mv, rs, b_st = items[s_]
                        xn = XN2[s_]
                        S.op("act", lambda e: e.activation(out=xn, in_=xs_ap, func=AF.Identity, scale=rs[:, 0:1], bias=rs[:, 1:2]),
                             reads=[b_X1[s_], b_st], writes=[b_XN2[s_]])
                    for s_ in range(NST):
                        xn = XN2[s_]
                        for c in range(NCH):
                            bk = bks_c[c // 2]
                            S.op("pe", lambda e, bk=bk, c=c: e.transpose(
                                bank_bf(bk)[:, (c % 2) * 512 + s_ * 128:(c % 2) * 512 + (s_ + 1) * 128], xn[:, c * 128:(c + 1) * 128], ident_bf[:]),
                                reads=[b_XN2[s_], b_const], writes=[pbuf[bk]], mark=(c % 2 == 1))
                    for c in range(NCH):
                        bk = bks_c[c // 2]
                        S.op("act", lambda e, bk=bk, c=c: e.activation(out=H2[:, c, :], in_=bank_bf(bk)[:, (c % 2) * 512:(c % 2) * 512 + T],
                                                                       func=AF.Identity, scale=sc2p(c), bias=sh2(c)),
                             reads=[pbuf[bk], b_const], writes=[b_H2])
                    for j in range(NFF):
                        pv, bp = get_panel()
                        v4 = pv[:, 0:2 * NCH * 128].rearrange("p (a k n) -> p a k n", a=2, k=NCH)
                        bkg = nextbank()
                        mm_group(bank(bkg)[:, 0:T], pbuf[bkg], [(v4[:, 0, kc, :], H2[:, kc, :]) for kc in range(NCH)], [bp, b_H2])
                        bku = nextbank()
                        mm_group(bank(bku)[:, 0:T], pbuf[bku], [(v4[:, 1, kc, :], H2[:, kc, :]) for kc in range(NCH)], [bp, b_H2])
                        su = SU[j % 2]
                        S.op("act", lambda e, su=su, bkg=bkg: e.activation(out=su, in_=bank(bkg)[:, 0:T], func=AF.Silu),
                             reads=[pbuf[bkg]], writes=[b_SU[j % 2]])
                        S.op("dve", lambda e, su=su, bku=bku, j=j: e.tensor_tensor(out=UT[:, j, :], in0=bank(bku)[:, 0:T], in1=su, op=ALU.mult),
                             reads=[pbuf[bku], b_SU[j % 2]], writes=[b_UT])
                    for oc in range(NCH):
                        pv, bp = get_panel()
                        v3 = pv[:, 0:NFF * 128].rearrange("p (k n) -> p k n", k=NFF)
                        bk2 = residual_block(g, v3, NFF, lambda kc: UT[:, kc, :], [b_UT], bp, g2c(oc), oc)
                        S.op("dve", lambda e, oc=oc, bk2=bk2: e.scalar_tensor_tensor(
                            out=X1[:, :, oc * 128:(oc + 1) * 128], in0=X1[:, :, oc * 128:(oc + 1) * 128], scalar=ALPHA,
                            in1=bank(bk2)[:, 0:T].rearrange("p (s n) -> p s n", s=NST), op0=ALU.mult, op1=ALU.add),
                            reads=[pbuf[bk2]] + b_X1, writes=b_X1)
                    items = [(X1[:, s_, :], b_X1[s_]) + STS[s_] for s_ in range(NST)]
                    ln_affine_multi(items, 2, 3, LNP, b_LNP)
                    for s_ in range(NST):
                        S.dma("pool", lambda e, s_=s_: e.dma_start(out=y_d[tok0 + gt0 + s_ * 128:tok0 + gt0 + (s_ + 1) * 128, :], in_=X1[:, s_, :]),
                              ds_y[s_], reads=[b_X1[s_]], writes=[b_yd])
                S.barrier()
                if BIG and qb + 1 < NQB:
                    S.dma("sp", lambda e: e.dma_start(out=hT, in_=hT_d[:, :, 0:SL]), ds_hsp, reads=[b_hTd], writes=[b_hT])
        S.barrier()
        S.emit()
    return nc


def _rope_tables(smax):
    inv = (1.0 / (np.float32(10000.0) ** (np.arange(0, 64, 2, dtype=np.float32) / np.float32(64)))).astype(np.float32)
    ang = (np.arange(smax, dtype=np.float32)[:, None] * inv[None, :]).astype(np.float32)
    cos = np.cos(ang).astype(np.float32)
    sin = np.sin(ang).astype(np.float32)
    c = np.zeros((128, smax), np.float32)
    s = np.zeros((128, smax), np.float32)
    for p in range(128):
        d = p % 64
        j = d % 32
        c[p] = cos[:, j]
        s[p] = (-sin[:, j]) if d < 32 else sin[:, j]
    return c, s


def _consts(smax):
    ident = np.eye(128, dtype=np.float32)
    prot = np.zeros((128, 128), np.float32)
    for m in range(128):
        blk = (m // 64) * 64
        d = m % 64
        k = blk + ((d + 32) % 64)
        prot[k, m] = 1.0
    c, s = _rope_tables(smax)
    return ident, prot, c, s


def make_in_maps(inputs, n_cores, seq_plan):
    f = lambda a: np.ascontiguousarray(np.asarray(a, dtype=np.float32))
    def pan(a, nk):
        a = f(a)
        ncb = a.shape[1] // 128
        return np.ascontiguousarray(a.reshape(nk, 128, ncb, 128).transpose(2, 1, 0, 3).reshape(ncb * 128, nk * 128))

    w = {
        "w_ada": f(inputs["w_ada"][0]), "b_ada": f(inputs["b_ada"][0]).reshape(1, -1), "w_in": pan(inputs["w_in"][0], 8),
        "lq1": f(inputs["lambda_q1"][0]).reshape(1, 64), "lk1": f(inputs["lambda_k1"][0]).reshape(1, 64),
        "lq2": f(inputs["lambda_q2"][0]).reshape(1, 64), "lk2": f(inputs["lambda_k2"][0]).reshape(1, 64),
        "subln_g": f(inputs["subln_g"][0]).reshape(1, 128), "conv_w": f(inputs["conv_w"][0]), "conv_b": f(inputs["conv_b"][0]).reshape(1, -1),
        "w_lru_gates": f(inputs["w_lru_gates"][0]).reshape(4, 16, 64, 64), "b_lru_gates": f(inputs["b_lru_gates"][0]).reshape(4, -1),
        "lru_lambda": f(inputs["lru_lambda"][0]), "w_ab": pan(inputs["w_attn_branch"][0], 8), "w_lb": pan(inputs["w_lru_branch"][0], 8),
        "w_out": pan(inputs["w_out"][0], 8), "ln1_g": f(inputs["ln1_g"][0]).reshape(1, -1), "ln1_b": f(inputs["ln1_b"][0]).reshape(1, -1),
        "w_ffn_in": pan(inputs["w_ffn_in"][0], 8), "w_ffn_out": pan(inputs["w_ffn_out"][0], 22),
        "ln2_g": f(inputs["ln2_g"][0]).reshape(1, -1), "ln2_b": f(inputs["ln2_b"][0]).reshape(1, -1),
    }
    maps = []
    for core in range(n_cores):
        plan = seq_plan(core)
        smax = max(p[0].shape[0] for p in plan)
        ident, prot, c, s = _consts(smax)
        m = dict(w)
        m["x"] = np.ascontiguousarray(np.concatenate([p[0] for p in plan], axis=0))
        m["c"] = np.ascontiguousarray(np.stack([p[1] for p in plan], axis=0))
        m["ident"] = ident
        m["prot"] = prot
        m["rope_c"] = c
        m["rope_s"] = s
        maps.append(m)
    return maps


_NC_CACHE = {}


def kernel(**inputs):
    n = 8
    xp = np.asarray(inputs["x_prompt"], dtype=np.float32)
    xs = np.asarray(inputs["x_sample"], dtype=np.float32)
    cp = np.asarray(inputs["c_prompt"], dtype=np.float32)
    cs = np.asarray(inputs["c_sample"], dtype=np.float32)
    B, SP, _ = xp.shape
    DB, SS, _ = xs.shape
    per = DB // n
    seq_lens = [SP] + [SS] * per

    def plan(core):
        return [(xp[core], cp[core])] + [(xs[core * per + i], cs[core * per + i]) for i in range(per)]

    key = tuple(seq_lens)
    if key not in _NC_CACHE:
        _NC_CACHE[key] = build(seq_lens)
    nc = _NC_CACHE[key]
    maps = make_in_maps(inputs, n, plan)
    res = run_bass_kernel_spmd(nc, maps, core_ids=list(range(n)))
    yp = np.empty((B, SP, D), np.float32)
    ys = np.empty((DB, SS, D), np.float32)
    for core in range(n):
        y = np.asarray(res.results[core]["y"], dtype=np.float32)
        yp[core] = y[0:SP]
        ys[core * per:(core + 1) * per] = y[SP:].reshape(per, SS, D)
    return (yp, ys)
```

```python
import os
import numpy as np
from contextlib import ExitStack
import concourse.bass as bass
import concourse.mybir as mybir
from concourse.bass_utils import run_bass_kernel_spmd

F32 = mybir.dt.float32
BF16 = mybir.dt.bfloat16
AF = mybir.ActivationFunctionType
ALU = mybir.AluOpType

ENGS = ("pe", "act", "dve", "pool", "sp")
D = 1024
NCH = 8
DFF = 2816
NFF = 22
ALPHA = 2.0 ** 0.25
LN_EPS = 1e-5
RMS_EPS = 1e-5
LAMBDA_INIT = 0.2
QB = 2048
ARENA_BYTES = 174 * 1024


class Sem:
    def __init__(self, h, name):
        self.h = h
        self.name = name
        self.count = 0


class Buf:
    __slots__ = ("name", "w", "r", "const", "excl")

    def __init__(self, name, const=False, excl=False):
        self.name = name
        self.w = None
        self.r = {}
        self.const = const
        self.excl = excl


class _Rec:
    def __init__(self):
        self.call = None

    def __getattr__(self, name):
        def f(*a, **k):
            self.call = (name, a, k)
            return self
        return f


def _record(fn):
    r = _Rec()
    fn(r)
    return r.call


class Sched:
    def __init__(self, nc, stack):
        self.nc = nc
        self.stack = stack
        self.prog = {e: [] for e in ENGS}
        self.sems = []
        self.esem = {e: self.new_sem("e_" + e) for e in ENGS if e != "sp"}
        self.waited = {e: {} for e in ENGS}

    def new_sem(self, name):
        s = Sem(self.stack.enter_context(self.nc.semaphore(name)), name)
        self.sems.append(s)
        return s

    def _deps(self, eng, reads, writes):
        deps = {}
        for b in reads:
            if b.w is not None:
                s, v = b.w
                if deps.get(s, 0) < v:
                    deps[s] = v
        for b in writes:
            if b.w is not None:
                s, v = b.w
                if deps.get(s, 0) < v:
                    deps[s] = v
            for s, v in b.r.items():
                if deps.get(s, 0) < v:
                    deps[s] = v
        out = []
        wd = self.waited[eng]
        own = self.esem.get(eng)
        for s, v in deps.items():
            if s is own and eng == "pe":
                continue
            if wd.get(s, 0) >= v:
                continue
            assert s.count >= v, f"wait on future tick: eng={eng} sem={s.name} v={v} count={s.count}"
            wd[s] = v
            out.append((s, v))
        return out

    def op(self, eng, fn, reads=(), writes=(), mark=True):
        ex = [b for b in reads if b.excl]
        if ex:
            reads = [b for b in reads if not b.excl]
            writes = list(writes) + [b for b in ex if b not in writes]
        waits = self._deps(eng, reads, writes)
        sem = self.esem[eng]
        if mark:
            sem.count += 1
            tick = sem.count
        else:
            tick = sem.count + 1
        self.prog[eng].append((waits, _record(fn), (sem, 1) if mark else None))
        for b in reads:
            if not b.const and b.r.get(sem, 0) < tick:
                b.r[sem] = tick
        for b in writes:
            b.w = (sem, tick)
            b.r = {}

    def dma(self, queue, fn, dsem, reads=(), writes=()):
        waits = self._deps(queue, reads, writes)
        dsem.count += 16
        v = dsem.count
        self.prog[queue].append((waits, _record(fn), (dsem, 16)))
        for b in reads:
            if not b.const and b.r.get(dsem, 0) < v:
                b.r[dsem] = v
        for b in writes:
            b.w = (dsem, v)
            b.r = {}

    def fence(self, engs, bufs):
        for e in engs:
            waits = self._deps(e, [], bufs)
            self.prog[e].append((waits, None, None))

    def barrier(self, exclude=()):
        for e in ENGS:
            wd = self.waited[e]
            waits = []
            for s in self.sems:
                if s is self.esem.get(e) or s in exclude:
                    continue
                if s.count > wd.get(s, 0):
                    wd[s] = s.count
                    waits.append((s, s.count))
            self.prog[e].append((waits, None, None))

    def emit(self):
        nc = self.nc
        with nc.Block() as block:
            deco = {"pe": block.tensor, "act": block.scalar, "dve": block.vector, "pool": block.gpsimd, "sp": block.sync}
            for e in ENGS:
                prog = self.prog[e]

                def body(eng, prog=prog):
                    for waits, fn, inc in prog:
                        for s, v in waits:
                            eng.wait_ge(s.h, v)
                        if fn is not None:
                            ins = getattr(eng, fn[0])(*fn[1], **fn[2])
                            if inc is not None:
                                ins.then_inc(inc[0].h, inc[1])

                deco[e](body)


class _Stop(Exception):
    pass


def build(seq_lens, dbg=False, stop=None):
    nc = bass.Bass("TRN2", target_bir_lowering=False)
    NSEQ = len(seq_lens)
    NTOK = sum(seq_lens)
    SMAX = max(seq_lens)

    def din(name, shape, dt=F32):
        return nc.dram_tensor(name, list(shape), dt, kind="ExternalInput").ap()

    def dint(name, shape, dt):
        return nc.dram_tensor(name, list(shape), dt, kind="Internal").ap()

    x_d = din("x", [NTOK, D])
    c_d = din("c", [NSEQ, D])
    w_ada_d = din("w_ada", [D, 6 * D])
    b_ada_d = din("b_ada", [1, 6 * D])
    w_in_d = din("w_in", [56 * 128, 1024])
    lq1_d = din("lq1", [1, 64])
    lk1_d = din("lk1", [1, 64])
    lq2_d = din("lq2", [1, 64])
    lk2_d = din("lk2", [1, 64])
    subln_d = din("subln_g", [1, 128])
    conv_w_d = din("conv_w", [4, D])
    conv_b_d = din("conv_b", [1, D])
    wlg_d = din("w_lru_gates", [4, 16, 64, 64])
    blg_d = din("b_lru_gates", [4, D])
    llam_d = din("lru_lambda", [2, D])
    w_ab_d = din("w_ab", [8 * 128, 1024])
    w_lb_d = din("w_lb", [8 * 128, 1024])
    w_out_d = din("w_out", [8 * 128, 1024])
    ln1g_d = din("ln1_g", [1, D])
    ln1b_d = din("ln1_b", [1, D])
    w_fi_d = din("w_ffn_in", [44 * 128, 1024])
    w_fo_d = din("w_ffn_out", [8 * 128, DFF])
    ln2g_d = din("ln2_g", [1, D])
    ln2b_d = din("ln2_b", [1, D])
    ident_d = din("ident", [128, 128])
    prot_d = din("prot", [128, 128])
    ropec_d = din("rope_c", [128, SMAX])
    ropes_d = din("rope_s", [128, SMAX])
    y_d = nc.dram_tensor("y", [NTOK, D], F32, kind="ExternalOutput").ap()

    w_in_bf = dint("w_in_bf", [56 * 128, 1024], BF16)
    w_ab_bf = dint("w_ab_bf", [8 * 128, 1024], BF16)
    w_lb_bf = dint("w_lb_bf", [8 * 128, 1024], BF16)
    w_out_bf = dint("w_out_bf", [8 * 128, 1024], BF16)
    w_fi_bf = dint("w_fi_bf", [44 * 128, 1024], BF16)
    w_fo_bf = dint("w_fo_bf", [8 * 128, DFF], BF16)
    mod_d = dint("mod_d", [NSEQ, 6 * D], F32)
    lruw_d = dint("lruw_d", [128, NCH, 8, 128], BF16)
    rec_d = dint("rec_d", [NCH, 128, NTOK], BF16)
    hT_d = dint("hT_d", [128, NCH, SMAX], BF16)
    rope_bf = dint("rope_bf", [2, 128, SMAX], BF16)
    BIGTH = int(os.environ.get("KBIGTH", "2048"))
    QBX = int(os.environ.get("KQB", str(QB)))

    dbg_out = {}
    if dbg:
        dbg_out["hT"] = nc.dram_tensor("dbg_hT", [128, NCH, NTOK], BF16, kind="ExternalOutput").ap()
        dbg_out["attnT"] = nc.dram_tensor("dbg_attnT", [128, NCH, NTOK], BF16, kind="ExternalOutput").ap()
        dbg_out["modT"] = nc.dram_tensor("dbg_modT", [128, 48, NSEQ], F32, kind="ExternalOutput").ap()
        dbg_out["rec"] = nc.dram_tensor("dbg_rec", [NCH, 128, NTOK], BF16, kind="ExternalOutput").ap()

    with ExitStack() as st:
        S = Sched(nc, st)

        def sb(name, shape, dt=F32):
            return st.enter_context(nc.sbuf_tensor(name, list(shape), dt))

        ident_bf = sb("ident_bf", [128, 128], BF16)
        ident_f = sb("ident_f", [128, 128], F32)
        prot_bf = sb("prot_bf", [128, 128], BF16)
        fmc = sb("fmc", [128, 11, NCH], F32)
        cdec = sb("cdec", [128, 4, NCH], F32)
        hbias = sb("hbias", [128, 4, NCH], F32)
        modT = sb("modT", [128, 48, NSEQ], F32)
        lamt = sb("lamt", [128, 8], F32)
        gsub = sb("gsub", [128, 128], F32)
        small = sb("small", [128, 64], F32)
        arena = sb("arena", [128, ARENA_BYTES // 2], BF16)
        ps = st.enter_context(nc.psum_tensor("ps", [128, 8 * 512], F32))

        b_const = Buf("const")
        b_hT = Buf("hT")

        def bank(b):
            return ps[:, b * 512:(b + 1) * 512]

        def bank_bf(b):
            return ps[:, b * 512:(b + 1) * 512].bitcast(BF16)

        pbuf = [Buf(f"psb{i}", excl=True) for i in range(8)]
        pstate = {"next": 0}

        def nextbank(lo=0, hi=8):
            n = pstate["next"]
            if n < lo or n >= hi:
                n = lo
            pstate["next"] = n + 1
            return n

        class Alloc:
            def __init__(self, off=0):
                self.off = off

            def get(self, shape, dt, name="t"):
                n = int(np.prod(shape[1:]))
                esz = 4 if dt == F32 else 2
                nbytes = (n * esz + 3) // 4 * 4
                assert self.off + nbytes <= ARENA_BYTES, f"arena overflow {name} {self.off + nbytes}"
                v = arena[0:shape[0], self.off // 2:(self.off + n * esz) // 2]
                if dt == F32:
                    v = v.bitcast(F32)
                if len(shape) == 3:
                    v = v.rearrange("p (a b) -> p a b", a=shape[1])
                elif len(shape) == 4:
                    v = v.rearrange("p (a b c) -> p a b c", a=shape[1], b=shape[2])
                self.off += nbytes
                return v

        sem_pool = {}

        def dsem(name):
            if name not in sem_pool:
                sem_pool[name] = S.new_sem("d_" + name)
            return sem_pool[name]

        ds_setup = dsem("setup")
        ds_cast = dsem("cast")
        b_wbf = Buf("wbf")


        SKIP = set(os.environ.get("KSKIP", "").split(","))

        cast_i = [0]
        ds_castk = [dsem("castk0"), dsem("castk1")]
        b_castk = [Buf("castk0"), Buf("castk1")]

        def cast_rows(dst, src, rows, blk):
            if "cast" in SKIP:
                return
            for r0 in range(0, rows, blk):
                r1 = min(rows, r0 + blk)
                k = cast_i[0] % 2
                cast_i[0] += 1
                S.dma("pool", lambda e, r0=r0, r1=r1: e.dma_start(out=dst[r0:r1, :], in_=src[r0:r1, :], max_dma_last_dim=4096),
                      ds_castk[k], writes=[b_castk[k]])

        al = Alloc()
        lv = al.get([128, 4, 64], F32)
        junk = al.get([128, 2, 64], F32)
        junk2 = al.get([128, 64], F32)
        wb = al.get([128, NCH, 8, 128], BF16)
        cT = al.get([128, NCH, NSEQ], F32)
        ones1 = al.get([1, NSEQ], F32)
        bada = al.get([1, 6 * D], F32)
        mod_sb = al.get([NSEQ, 6 * D], F32)
        wpan = [al.get([128, NCH, 512], F32) for _ in range(2)]
        b_wb = Buf("wb")
        b_lruw = Buf("lruw")
        b_modd = Buf("mod_d")
        b_mod = Buf("mod_sb")
        b_wpan = [Buf("wpan0"), Buf("wpan1")]
        ds_wp = [dsem("wp0"), dsem("wp1")]
        ds_wbd = dsem("wbd")

        S.op("pool", lambda e: e.memset(wb, 0.0), writes=[b_wb])
        S.dma("pool", lambda e: e.dma_start(out=ident_bf[:], in_=ident_d), ds_setup, writes=[b_const])
        S.dma("pool", lambda e: e.dma_start(out=prot_bf[:], in_=prot_d), ds_setup, writes=[b_const])
        for dg in (range(4) if "wbdma" not in SKIP else []):
            for j in range(2):
                src = wlg_d[dg].rearrange("(c j) d e -> j d c e", j=2)[j]
                S.dma("pool", lambda e, dg=dg, j=j, src=src: e.dma_start(out=wb[64 * j:64 * j + 64, :, dg, 64 * j:64 * j + 64], in_=src),
                      ds_wbd, reads=[], writes=[b_wb])
        cast_rows(w_in_bf, w_in_d, 56 * 128, 1024)
        S.dma("sp", lambda e: e.dma_start(out=ident_f[:], in_=ident_d), ds_setup, writes=[b_const])
        for s_ in range(NSEQ):
            S.dma("sp", lambda e, s_=s_: e.dma_start(out=cT[:, :, s_], in_=c_d[s_:s_ + 1, :].rearrange("o (c p) -> p (o c)", p=128),
                                                    allow_slow_non_contiguous=True), ds_setup, writes=[b_const])
        S.dma("sp", lambda e: e.dma_start(out=bada, in_=b_ada_d), ds_setup, writes=[b_const])
        vecs = [conv_w_d[0:1, :], conv_w_d[1:2, :], conv_w_d[2:3, :], conv_w_d[3:4, :], conv_b_d,
                blg_d[0:1, :], blg_d[1:2, :], blg_d[2:3, :], blg_d[3:4, :], llam_d[0:1, :], llam_d[1:2, :]]
        for i, v in (enumerate(vecs) if "slow" not in SKIP else []):
            S.dma("sp", lambda e, i=i, v=v: e.dma_start(out=fmc[:, i, :], in_=v.rearrange("o (c p) -> p (o c)", p=128),
                                                       allow_slow_non_contiguous=True), ds_setup, writes=[b_const])
        for i, v in (enumerate([lq1_d, lk1_d, lq2_d, lk2_d]) if "bcast" not in SKIP else []):
            S.dma("sp", lambda e, i=i, v=v: e.dma_start(out=lv[:, i, :], in_=v.broadcast_to([128, 64])), ds_setup, writes=[b_const])
        if "bcast" not in SKIP:
            S.dma("sp", lambda e: e.dma_start(out=gsub[:], in_=subln_d.broadcast_to([128, 128])), ds_setup, writes=[b_const])
        cast_rows(rope_bf[0], ropec_d, 128, 128)
        cast_rows(rope_bf[1], ropes_d, 128, 128)
        cast_rows(w_ab_bf, w_ab_d, 1024, 1024)
        cast_rows(w_lb_bf, w_lb_d, 1024, 1024)
        cast_rows(w_out_bf, w_out_d, 1024, 1024)
        cast_rows(w_fi_bf, w_fi_d, 44 * 128, 1024)
        cast_rows(w_fo_bf, w_fo_d, 1024, 512)
        S.barrier(exclude=ds_castk)
        S.op("dve", lambda e: e.tensor_tensor(out=junk[:, 0, :], in0=lv[:, 0, :], in1=lv[:, 1, :], op=ALU.mult), writes=[b_const])
        S.op("dve", lambda e: e.tensor_tensor(out=junk[:, 1, :], in0=lv[:, 2, :], in1=lv[:, 3, :], op=ALU.mult), reads=[b_const], writes=[b_const])
        S.op("dve", lambda e: e.tensor_scalar(out=gsub[:], in0=gsub[:], scalar1=1.0 - LAMBDA_INIT, scalar2=None, op0=ALU.mult),
             reads=[b_const], writes=[b_const])
        S.op("dve", lambda e: e.memset(ones1, 1.0), reads=[b_const], writes=[b_const])
        for c in range(NCH):
            for k in range(4):
                S.op("dve", lambda e, c=c, k=k: e.tensor_scalar(out=wb[:, c, 4 + k, :], in0=ident_bf[:], scalar1=fmc[:, k, c:c + 1],
                                                                scalar2=None, op0=ALU.mult), reads=[b_wb], writes=[b_wb])
        S.op("act", lambda e: e.activation(out=cT, in_=cT, func=AF.Silu), writes=[b_mod])
        S.op("act", lambda e: e.activation(out=small[:, 0:16], in_=fmc[:, 9:11, :].rearrange("p a c -> p (a c)"), func=AF.Exp, scale=-1.0),
             reads=[b_mod], writes=[b_mod])
        S.barrier(exclude=ds_castk)
        S.op("act", lambda e: e.activation(out=junk2, in_=junk[:, 0, :], func=AF.Identity, accum_out=lamt[:, 0:1]), writes=[b_mod])
        S.op("act", lambda e: e.activation(out=junk2, in_=junk[:, 1, :], func=AF.Identity, accum_out=lamt[:, 1:2]), reads=[b_mod], writes=[b_mod])
        S.op("act", lambda e: e.activation(out=lamt[:, 2:4], in_=lamt[:, 0:2], func=AF.Exp), reads=[b_mod], writes=[b_mod])
        S.op("act", lambda e: e.activation(out=small[:, 16:32], in_=small[:, 0:16], func=AF.Ln, bias=1.0, scale=1.0), reads=[b_mod], writes=[b_mod])
        S.dma("sp", lambda e: e.dma_start(out=lruw_d, in_=wb), ds_wbd, reads=[b_wb], writes=[b_lruw])
        S.barrier(exclude=ds_castk)
        S.op("dve", lambda e: e.tensor_tensor(out=lamt[:, 4:5], in0=lamt[:, 3:4], in1=lamt[:, 2:3], op=ALU.subtract), writes=[b_const])
        S.op("dve", lambda e: e.tensor_scalar(out=lamt[:, 4:5], in0=lamt[:, 4:5], scalar1=-LAMBDA_INIT, scalar2=None, op0=ALU.add),
             reads=[b_const], writes=[b_const])
        S.op("dve", lambda e: e.tensor_scalar(out=hbias[:], in0=fmc[:, 5:9, :], scalar1=0.5, scalar2=None, op0=ALU.mult), reads=[b_const], writes=[b_const])
        for d_ in range(2):
            S.op("dve", lambda e, d_=d_: e.tensor_scalar(out=cdec[:, 2 * d_, :], in0=small[:, 16 + 8 * d_:24 + 8 * d_], scalar1=-4.0,
                                                         scalar2=None, op0=ALU.mult), reads=[b_const], writes=[b_const])
            S.op("dve", lambda e, d_=d_: e.tensor_scalar(out=cdec[:, 2 * d_ + 1, :], in0=small[:, 16 + 8 * d_:24 + 8 * d_], scalar1=-8.0,
                                                         scalar2=None, op0=ALU.mult), reads=[b_const], writes=[b_const])
        for pn in (range(12) if "mod" not in SKIP else []):
            sl = pn % 2
            S.dma("sp", lambda e, pn=pn, sl=sl: e.dma_start(out=wpan[sl], in_=w_ada_d[:, pn * 512:(pn + 1) * 512].rearrange("(c p) n -> p c n", p=128)),
                  ds_wp[sl], writes=[b_wpan[sl]])
            bk = nextbank()
            for kc in range(NCH):
                S.op("pe", lambda e, bk=bk, kc=kc, sl=sl: e.matmul(bank(bk)[0:NSEQ, :], lhsT=cT[:, kc, :], rhs=wpan[sl][:, kc, :],
                                                                   start=(kc == 0), stop=False),
                     reads=[b_wpan[sl]], writes=[pbuf[bk]], mark=False)
            S.op("pe", lambda e, bk=bk, pn=pn: e.matmul(bank(bk)[0:NSEQ, :], lhsT=ones1, rhs=bada[:, pn * 512:(pn + 1) * 512],
                                                        start=False, stop=True), reads=[], writes=[pbuf[bk]])
            S.op("act", lambda e, bk=bk, pn=pn: e.activation(out=mod_sb[:, pn * 512:(pn + 1) * 512], in_=bank(bk)[0:NSEQ, :], func=AF.Identity),
                 reads=[pbuf[bk]], writes=[b_mod])
        S.dma("sp", lambda e: e.dma_start(out=mod_d, in_=mod_sb), ds_setup, reads=[b_mod], writes=[b_modd])
        for s_ in range(NSEQ):
            S.dma("sp", lambda e, s_=s_: e.dma_start(out=modT[:, :, s_], in_=mod_d[s_:s_ + 1, :].rearrange("o (c p) -> p (o c)", p=128),
                                                    allow_slow_non_contiguous=True), ds_setup, reads=[b_modd], writes=[b_modd])
        for lo in (8, 32):
            S.op("dve", lambda e, lo=lo: e.tensor_scalar(out=modT[:, lo:lo + 8, :], in0=modT[:, lo:lo + 8, :], scalar1=1.0, scalar2=None, op0=ALU.add),
                 reads=[b_modd], writes=[b_modd])
        ds_dbg = dsem("dbg")
        b_dbg = Buf("dbg")
        if dbg:
            S.dma("sp", lambda e: e.dma_start(out=dbg_out["modT"], in_=modT[:]), ds_dbg, reads=[b_modd], writes=[b_dbg])
        S.barrier()
        b_const = Buf("const2", const=True)
        b_wbf = Buf("wbf2", const=True)
        b_lruw = Buf("lruw2", const=True)
        b_recd_dummy = None

        def ln_stats(src_ap, b_src, st6, mv, rs, b_st):
            S.op("dve", lambda e: e.bn_stats(out=st6[:, 0, :], in_=src_ap[:, 0:512]), reads=[b_src], writes=[b_st])
            S.op("dve", lambda e: e.bn_stats(out=st6[:, 1, :], in_=src_ap[:, 512:1024]), reads=[b_src], writes=[b_st])
            S.op("dve", lambda e: e.bn_aggr(out=mv, in_=st6.rearrange("p a b -> p (a b)")), reads=[b_st], writes=[b_st])
            S.op("act", lambda e: e.activation(out=rs[:, 0:1], in_=mv[:, 1:2], func=AF.Sqrt, bias=LN_EPS, scale=1.0), reads=[b_st], writes=[b_st])
            S.op("dve", lambda e: e.reciprocal(out=rs[:, 0:1], in_=rs[:, 0:1]), reads=[b_st], writes=[b_st])
            S.op("dve", lambda e: e.tensor_scalar(out=rs[:, 1:2], in0=mv[:, 0:1], scalar1=rs[:, 0:1], scalar2=-1.0, op0=ALU.mult, op1=ALU.mult),
                 reads=[b_st], writes=[b_st])

        def ln_stats_multi(items):
            for src_ap, b_src, st6, mv, rs, b_st in items:
                S.op("dve", lambda e: e.bn_stats(out=st6[:, 0, :], in_=src_ap[:, 0:512]), reads=[b_src], writes=[b_st])
                S.op("dve", lambda e: e.bn_stats(out=st6[:, 1, :], in_=src_ap[:, 512:1024]), reads=[b_src], writes=[b_st])
                S.op("dve", lambda e: e.bn_aggr(out=mv, in_=st6.rearrange("p a b -> p (a b)")), reads=[b_st], writes=[b_st])
            for src_ap, b_src, st6, mv, rs, b_st in items:
                S.op("act", lambda e: e.activation(out=rs[:, 0:1], in_=mv[:, 1:2], func=AF.Sqrt, bias=LN_EPS, scale=1.0), reads=[b_st], writes=[b_st])
            for src_ap, b_src, st6, mv, rs, b_st in items:
                S.op("dve", lambda e: e.reciprocal(out=rs[:, 0:1], in_=rs[:, 0:1]), reads=[b_st], writes=[b_st])
                S.op("dve", lambda e: e.tensor_scalar(out=rs[:, 1:2], in0=mv[:, 0:1], scalar1=rs[:, 0:1], scalar2=-1.0, op0=ALU.mult, op1=ALU.mult),
                     reads=[b_st], writes=[b_st])

        def ln_affine_multi(items, gi, bi_, LNP, b_LNP):
            ln_stats_multi(items)
            for src_ap, b_src, st6, mv, rs, b_st in items:
                S.op("act", lambda e: e.activation(out=src_ap, in_=src_ap, func=AF.Identity, scale=rs[:, 0:1], bias=rs[:, 1:2]),
                     reads=[b_src, b_st], writes=[b_src])
            for src_ap, b_src, st6, mv, rs, b_st in items:
                S.op("dve", lambda e: e.tensor_tensor(out=src_ap, in0=src_ap, in1=LNP[:, gi, :], op=ALU.mult), reads=[b_src, b_LNP], writes=[b_src])
            for i, (src_ap, b_src, st6, mv, rs, b_st) in enumerate(items):
                eng = "pool" if i % 2 == 0 else "dve"
                S.op(eng, lambda e: e.tensor_tensor(out=src_ap, in0=src_ap, in1=LNP[:, bi_, :], op=ALU.add), reads=[b_src, b_LNP], writes=[b_src])

        def mm_group(out_ap, b_out, pairs, reads):
            n = len(pairs)
            for i, (l, r) in enumerate(pairs):
                S.op("pe", lambda e, l=l, r=r, i=i: e.matmul(out_ap, lhsT=l, rhs=r, start=(i == 0), stop=(i == n - 1)),
                     reads=reads, writes=[b_out], mark=(i == n - 1))

        for si in ([] if stop == "setup" else range(NSEQ)):
            SL = seq_lens[si]
            tok0 = sum(seq_lens[:si])
            NT = SL // 512
            sc1p = lambda c: modT[:, 8 + c, si:si + 1]
            sh1 = lambda c: modT[:, 0 + c, si:si + 1]
            g1c = lambda c: modT[:, 16 + c, si:si + 1]
            sh2 = lambda c: modT[:, 24 + c, si:si + 1]
            sc2p = lambda c: modT[:, 32 + c, si:si + 1]
            g2c = lambda c: modT[:, 40 + c, si:si + 1]

            al = Alloc()
            hT = al.get([128, NCH, SL], BF16)
            HOFF = al.off
            XT = [al.get([128, D], F32) for _ in range(8)]
            b_XT = [Buf(f"XT{i}") for i in range(8)]
            ds_XT = [dsem(f"xt{i}") for i in range(8)]
            XN = [al.get([128, D], BF16) for _ in range(4)]
            b_XN = [Buf(f"XN{i}") for i in range(4)]
            STT_ = [(al.get([128, 2, 6], F32), al.get([128, 2], F32), al.get([128, 2], F32), Buf(f"st{i}")) for i in range(4)]
            for g in range(NT):
                base = (g % 2) * 4
                items = []
                for j in range(4):
                    ti = g * 4 + j
                    sl = ti % 8
                    S.dma("sp", lambda e, sl=sl, ti=ti: e.dma_start(out=XT[sl], in_=x_d[tok0 + ti * 128: tok0 + (ti + 1) * 128, :]),
                          ds_XT[sl], writes=[b_XT[sl]])
                    items.append((XT[sl], b_XT[sl]) + STT_[j])
                ln_stats_multi(items)
                for j in range(4):
                    src_ap, b_src, st6, mv, rs, b_st = items[j]
                    S.op("act", lambda e: e.activation(out=XN[j], in_=src_ap, func=AF.Identity, scale=rs[:, 0:1], bias=rs[:, 1:2]),
                         reads=[b_src, b_st], writes=[b_XN[j]])
                for j in range(4):
                    for c in range(NCH):
                        bk = base + c // 2
                        S.op("pe", lambda e, bk=bk, c=c, j=j: e.transpose(
                            bank_bf(bk)[:, (c % 2) * 512 + j * 128:(c % 2) * 512 + (j + 1) * 128], XN[j][:, c * 128:(c + 1) * 128], ident_bf[:]),
                            reads=[b_XN[j], b_const], writes=[pbuf[bk]], mark=(c % 2 == 1))
                for c in range(NCH):
                    bk = base + c // 2
                    S.op("act", lambda e, bk=bk, c=c, g=g: e.activation(out=hT[:, c, g * 512:(g + 1) * 512], in_=bank_bf(bk)[:, (c % 2) * 512:(c % 2) * 512 + 512],
                                                                        func=AF.Identity, scale=sc1p(c), bias=sh1(c)),
                         reads=[pbuf[bk], b_const], writes=[b_hT])
            if dbg:
                S.dma("sp", lambda e: e.dma_start(out=dbg_out["hT"][:, :, tok0:tok0 + SL], in_=hT[:, :, 0:SL]), ds_dbg, reads=[b_hT], writes=[b_dbg])
            S.barrier()
            if stop == "p1":
                break

            al = Alloc(HOFF)
            WP = [al.get([128, 2, NCH, 128], BF16) for _ in range(2)]
            LW = [al.get([128, 8, 128], BF16) for _ in range(2)]
            b_WP = [Buf("WP0"), Buf("WP1")]
            ds_WP = [dsem("WP0"), dsem("WP1")]
            XR = al.get([128, SL + 4], BF16)
            G_ = al.get([128, SL], BF16)
            XC = al.get([128, SL], F32)
            XCb = al.get([128, SL], BF16)
            HF = al.get([128, SL], F32)
            LB = min(SL, 2048) if SL <= 2048 else 1024
            if os.environ.get("KLB"):
                LB = int(os.environ["KLB"])
            NBLK = SL // LB
            TPB = LB // 512
            HBK = al.get([128, LB], F32)
            REC = al.get([128, SL], BF16)
            AA = al.get([128, LB], F32)
            QQ = al.get([128, LB], F32)
            DD = al.get([128, LB], F32)
            CAR = al.get([128, 2], F32)
            TMP = [[al.get([128, 512], F32) for _ in range(2)] for _ in range(2)]
            b_XR = [Buf(f"XR{t}") for t in range(NT)]
            b_G = [Buf(f"G{t}") for t in range(NT)]
            b_XC = [Buf(f"XC{t}") for t in range(NT)]
            b_XCb = [Buf(f"XCb{t}") for t in range(NT)]
            b_HF = [Buf(f"HF{t}") for t in range(NT)]
            b_HBK = Buf("HBK")
            b_REC = Buf("REC")
            b_AA = Buf("AA")
            b_QQ = Buf("QQ")
            b_DD = Buf("DD")
            b_CAR = Buf("CAR")
            b_TMP = [[Buf(f"TMP{i}{j}") for j in range(2)] for i in range(2)]
            ds_rec = dsem("rec")
            b_recd = Buf("rec_d")
            b_pad = Buf("XRpad")
            S.op("pool", lambda e: e.memset(XR[:, 0:2], 0.0), writes=[b_pad])
            S.op("pool", lambda e: e.memset(XR[:, SL + 2:SL + 4], 0.0), writes=[b_pad])

            def load_lru_panels(c):
                sl = c % 2
                S.dma("sp", lambda e: e.dma_start(out=WP[sl][:, 0].rearrange("p k n -> p (k n)"), in_=w_in_bf[(24 + c) * 128:(25 + c) * 128, :]),
                      ds_WP[sl], reads=[b_wbf], writes=[b_WP[sl]])
                S.dma("sp", lambda e: e.dma_start(out=WP[sl][:, 1].rearrange("p k n -> p (k n)"), in_=w_in_bf[(32 + c) * 128:(33 + c) * 128, :]),
                      ds_WP[sl], reads=[b_wbf], writes=[b_WP[sl]])
                S.dma("sp", lambda e: e.dma_start(out=LW[sl], in_=lruw_d[:, c]), ds_WP[sl], reads=[b_lruw], writes=[b_WP[sl]])

            load_lru_panels(0)
            for c in range(NCH):
                if c + 1 < NCH:
                    load_lru_panels(c + 1)
                sl = c % 2
                for t in range(NT):
                    bk = nextbank()
                    mm_group(bank(bk), pbuf[bk], [(WP[sl][:, 0, kc, :], hT[:, kc, t * 512:(t + 1) * 512]) for kc in range(NCH)], [b_WP[sl], b_hT])
                    S.op("act", lambda e, bk=bk, t=t: e.activation(out=XR[:, 2 + t * 512:2 + (t + 1) * 512], in_=bank(bk), func=AF.Identity),
                         reads=[pbuf[bk]], writes=[b_XR[t]])
                    bk = nextbank()
                    mm_group(bank(bk), pbuf[bk], [(WP[sl][:, 1, kc, :], hT[:, kc, t * 512:(t + 1) * 512]) for kc in range(NCH)], [b_WP[sl], b_hT])
                    S.op("act", lambda e, bk=bk, t=t: e.activation(out=G_[:, t * 512:(t + 1) * 512], in_=bank(bk), func=AF.Gelu_apprx_tanh),
                         reads=[pbuf[bk]], writes=[b_G[t]])
                for t in range(NT):
                    bk = nextbank()
                    rd = [b_WP[sl], b_pad, b_XR[t]] + ([b_XR[t - 1]] if t > 0 else []) + ([b_XR[t + 1]] if t + 1 < NT else [])
                    mm_group(bank(bk), pbuf[bk], [(LW[sl][:, 4 + k, :], XR[:, t * 512 + k:t * 512 + k + 512]) for k in range(4)], rd)
                    S.op("act", lambda e, bk=bk, t=t, c=c: e.activation(out=XC[:, t * 512:(t + 1) * 512], in_=bank(bk), func=AF.Identity,
                                                                        bias=fmc[:, 4, c:c + 1], scale=1.0),
                         reads=[pbuf[bk], b_const], writes=[b_XC[t]])
                    S.op("pool", lambda e, t=t: e.tensor_copy(out=XCb[:, t * 512:(t + 1) * 512], in_=XC[:, t * 512:(t + 1) * 512]),
                         reads=[b_XC[t]], writes=[b_XCb[t]])
                it = 0
                for d_ in range(2):
                    blocks = range(NBLK) if d_ == 0 else range(NBLK - 1, -1, -1)
                    for bi in blocks:
                        bsl = slice(bi * LB, (bi + 1) * LB)
                        for tl in range(TPB):
                            t = bi * TPB + tl
                            tm = TMP[it % 2]
                            btm = b_TMP[it % 2]
                            it += 1
                            TR, TI = tm
                            tsl = slice(t * 512, (t + 1) * 512)
                            lsl = slice(tl * 512, (tl + 1) * 512)
                            bkr = nextbank()
                            mm_group(bank(bkr), pbuf[bkr], [(LW[sl][:, 2 * d_, :], XCb[:, tsl])], [b_WP[sl], b_XCb[t]])
                            bki = nextbank()
                            mm_group(bank(bki), pbuf[bki], [(LW[sl][:, 2 * d_ + 1, :], XCb[:, tsl])], [b_WP[sl], b_XCb[t]])
                            S.op("act", lambda e: e.activation(out=TR, in_=bank(bkr), func=AF.Tanh, bias=hbias[:, 2 * d_, c:c + 1], scale=0.5),
                                 reads=[pbuf[bkr], b_const], writes=[btm[0]])
                            S.op("act", lambda e: e.activation(out=TI, in_=bank(bki), func=AF.Tanh, bias=hbias[:, 2 * d_ + 1, c:c + 1], scale=0.5),
                                 reads=[pbuf[bki], b_const], writes=[btm[1]])
                            S.op("act", lambda e: e.activation(out=AA[:, lsl], in_=TR, func=AF.Exp, scale=cdec[:, 2 * d_, c:c + 1], bias=cdec[:, 2 * d_, c:c + 1]),
                                 reads=[btm[0], b_const], writes=[b_AA])
                            S.op("act", lambda e: e.activation(out=QQ[:, lsl], in_=TR, func=AF.Exp, scale=cdec[:, 2 * d_ + 1, c:c + 1], bias=cdec[:, 2 * d_ + 1, c:c + 1]),
                                 reads=[btm[0], b_const], writes=[b_QQ])
                            S.op("dve", lambda e: e.scalar_tensor_tensor(out=DD[:, lsl], in0=TI, scalar=1.0, in1=XC[:, tsl], op0=ALU.add, op1=ALU.mult),
                                 reads=[btm[1], b_XC[t]], writes=[b_DD])
                        S.op("act", lambda e: e.activation(out=QQ, in_=QQ, func=AF.Sqrt, bias=0.25, scale=-0.25), reads=[b_QQ], writes=[b_QQ])
                        S.op("dve", lambda e: e.tensor_tensor(out=DD, in0=DD, in1=QQ, op=ALU.mult), reads=[b_DD, b_QQ], writes=[b_DD])
                        if d_ == 0:
                            init = 0.0 if bi == 0 else HF[:, bi * LB - 1:bi * LB]
                            S.op("dve", lambda e: e.tensor_tensor_scan(out=HF[:, bsl], data0=AA, data1=DD, initial=init, op0=ALU.mult, op1=ALU.add),
                                 reads=[b_AA, b_DD] + b_HF, writes=b_HF)
                        else:
                            init = 0.0 if bi == NBLK - 1 else CAR[:, 0:1]
                            S.op("dve", lambda e: e.tensor_tensor_scan(out=HBK[:, ::-1], data0=AA[:, ::-1], data1=DD[:, ::-1], initial=init,
                                                                        op0=ALU.mult, op1=ALU.add),
                                 reads=[b_AA, b_DD, b_CAR], writes=[b_HBK])
                            if bi > 0:
                                S.op("dve", lambda e: e.tensor_copy(out=CAR[:, 0:1], in_=HBK[:, 0:1]), reads=[b_HBK], writes=[b_CAR])
                            S.op("pool", lambda e: e.tensor_tensor(out=HBK, in0=HBK, in1=HF[:, bsl], op=ALU.add),
                                 reads=[b_HBK] + b_HF, writes=[b_HBK])
                            S.op("pool", lambda e: e.tensor_tensor(out=REC[:, bsl], in0=HBK, in1=G_[:, bsl], op=ALU.mult),
                                 reads=[b_HBK] + b_G, writes=[b_REC])
                S.dma("sp", lambda e, c=c: e.dma_start(out=rec_d[c, :, tok0:tok0 + SL], in_=REC), ds_rec, reads=[b_REC], writes=[b_recd])
                if dbg:
                    S.dma("sp", lambda e, c=c: e.dma_start(out=dbg_out["rec"][c, :, tok0:tok0 + SL], in_=REC), ds_dbg, reads=[b_REC], writes=[b_dbg])
            S.barrier()
            if stop == "p2":
                break

            NQB = max(1, SL // QBX)
            QBL = min(QBX, SL)
            BIG = SL > BIGTH
            ds_hsp = dsem("hsp")
            b_hTd = Buf("hT_d")
            if BIG:
                S.dma("sp", lambda e: e.dma_start(out=hT_d[:, :, 0:SL], in_=hT), ds_hsp, reads=[b_hT], writes=[b_hTd])
            for qb in range(NQB):
                q0 = qb * QBL
                al = Alloc(HOFF)
                attnT = al.get([128, NCH, QBL], BF16)
                b_attnT = Buf("attnT")
                p4_off = al.off
                QT = al.get([128, QBL], BF16)
                KT = al.get([128, SL], BF16)
                NKT = SL // 128
                V_ = al.get([128, NKT, 130], BF16)
                WQ = [al.get([128, 3, NCH, 128], BF16) for _ in range(2)]
                b_WQ = [Buf("WQ0"), Buf("WQ1")]
                ds_WQ = [dsem("WQ0"), dsem("WQ1")]
                RC = [al.get([128, 2, 512], BF16) for _ in range(4)]
                b_RC = [Buf(f"RC{i}") for i in range(4)]
                ds_RC = [dsem(f"RC{i}") for i in range(4)]
                QBF = [al.get([128, 512], BF16) for _ in range(2)]
                b_QBF = [Buf("QBF0"), Buf("QBF1")]
                T12 = [[al.get([128, 512], F32) for _ in range(2)] for _ in range(2)]
                b_T12 = [[Buf("T1"), Buf("T2")] for _ in range(2)]
                PT = [al.get([128, 2, 512], BF16) for _ in range(3)]
                b_PT = [Buf(f"PT{i}") for i in range(3)]
                NQG = QBL // 512
                ACCS = al.get([128, 8, 130], F32)
                b_ACCS = Buf("ACCS")
                O0 = al.get([128, 128], F32)
                b_O0 = Buf("O0")
                OO = [al.get([128, 128], F32) for _ in range(4 * NQG)]
                b_OO = [Buf(f"OO{i}") for i in range(4 * NQG)]
                AT = [al.get([128, 128], BF16) for _ in range(4)]
                b_AT = [Buf(f"AT{i}") for i in range(4)]
                nrm = al.get([128, 16], F32)
                b_nrm = Buf("nrm")
                st6n = al.get([128, 6], F32)
                mvn = al.get([128, 2], F32)
                b_stn = Buf("stn")
                msa = al.get([128, 4 * NQG], F32)
                b_msa = Buf("msa")
                b_QT = Buf("QT")
                b_KT = Buf("KT")
                b_V = Buf("V")
                S.op("pool", lambda e: e.memset(V_[:, :, 128:130], 1.0), writes=[b_V])

                def load_head_w(h):
                    sl = h % 2
                    for i in range(3):
                        S.dma("sp", lambda e, i=i: e.dma_start(out=WQ[sl][:, i].rearrange("p k n -> p (k n)"), in_=w_in_bf[(i * 8 + h) * 128:(i * 8 + h + 1) * 128, :]),
                              ds_WQ[sl], reads=[b_wbf], writes=[b_WQ[sl]])

                rc_i = [0]

                def rope_proj(h, which, t_tok, dst_ap, b_dst):
                    sl = h % 2
                    r = rc_i[0] % 2
                    r4 = rc_i[0] % 4
                    rc_i[0] += 1
                    S.dma("sp", lambda e: e.dma_start(out=RC[r4][:, 0, :], in_=rope_bf[0][:, t_tok:t_tok + 512]), ds_RC[r4], reads=[b_wbf], writes=[b_RC[r4]])
                    S.dma("sp", lambda e: e.dma_start(out=RC[r4][:, 1, :], in_=rope_bf[1][:, t_tok:t_tok + 512]), ds_RC[r4], reads=[b_wbf], writes=[b_RC[r4]])
                    bka = nextbank(0, 5)
                    mm_group(bank(bka), pbuf[bka], [(WQ[sl][:, which, kc, :], hT[:, kc, t_tok:t_tok + 512]) for kc in range(NCH)], [b_WQ[sl], b_hT])
                    S.op("act", lambda e: e.activation(out=QBF[r], in_=bank(bka), func=AF.Identity), reads=[pbuf[bka]], writes=[b_QBF[r]])
                    t1, t2 = T12[r]
                    S.op("dve", lambda e: e.tensor_tensor(out=t1, in0=bank(bka), in1=RC[r4][:, 0, :], op=ALU.mult),
                         reads=[pbuf[bka], b_RC[r4]], writes=[b_T12[r][0]])

                    def part2():
                        bkb = nextbank(0, 5)
                        mm_group(bank(bkb), pbuf[bkb], [(prot_bf[:], QBF[r])], [b_const, b_QBF[r]])
                        S.op("dve", lambda e: e.tensor_tensor(out=t2, in0=bank(bkb), in1=RC[r4][:, 1, :], op=ALU.mult),
                             reads=[pbuf[bkb], b_RC[r4]], writes=[b_T12[r][1]])
                        S.op("pool", lambda e: e.tensor_tensor(out=dst_ap, in0=t1, in1=t2, op=ALU.add),
                             reads=[b_T12[r][0], b_T12[r][1]], writes=[b_dst])
                    return part2

                def acc_ap(a, lo=0, hi=129):
                    bk = 5 + a // 3
                    o = (a % 3) * 130
                    return ps[:, bk * 512 + o + lo: bk * 512 + o + hi]

                load_head_w(0)
                pending_tail = []
                pending_tail2 = []
                for h in range(NCH):
                    if h + 1 < NCH:
                        load_head_w(h + 1)
                    sl = h % 2
                    jobs = [(1, t * 512, KT[:, t * 512:(t + 1) * 512], b_KT) for t in range(NT)]
                    jobs += [(0, q0 + t * 512, QT[:, t * 512:(t + 1) * 512], b_QT) for t in range(QBL // 512)]
                    vper = -(-NKT // len(jobs))
                    vi = 0
                    prev = None
                    for (which, t_tok, dst_ap, b_dst) in jobs:
                        p2 = rope_proj(h, which, t_tok, dst_ap, b_dst)
                        if prev is not None:
                            prev()
                        prev = p2
                        for _ in range(vper):
                            if vi < NKT:
                                kt = vi
                                vi += 1
                                bk = nextbank(0, 5)
                                mm_group(bank(bk)[:, 0:128], pbuf[bk], [(hT[:, kc, kt * 128:(kt + 1) * 128], WQ[sl][:, 2, kc, :]) for kc in range(NCH)], [b_WQ[sl], b_hT])
                                S.op("dve", lambda e, bk=bk, kt=kt: e.tensor_copy(out=V_[:, kt, 0:128], in_=bank(bk)[:, 0:128]),
                                     reads=[pbuf[bk]], writes=[b_V])
                    prev()
                    assert vi == NKT
                    while pending_tail:
                        pending_tail.pop(0)()
                    for qg in range(QBL // 512):
                        def emit_qk_exp(kt, qg=qg):
                            pr = kt % 2
                            stv = ps[:, pr * 1024:(pr + 1) * 1024].rearrange("p (a b) -> p a b", a=2)
                            bst = [pbuf[2 * pr], pbuf[2 * pr + 1]]
                            S.op("pe", lambda e: e.matmul(stv[:, 0, :], lhsT=KT[0:64, kt * 128:(kt + 1) * 128], rhs=QT[0:64, qg * 512:(qg + 1) * 512],
                                                          start=True, stop=True), reads=[b_KT, b_QT], writes=[bst[0]], mark=False)
                            S.op("pe", lambda e: e.matmul(stv[:, 1, :], lhsT=KT[64:128, kt * 128:(kt + 1) * 128], rhs=QT[64:128, qg * 512:(qg + 1) * 512],
                                                          start=True, stop=True), reads=[b_KT, b_QT], writes=[bst[1]], mark=True)
                            pi = kt % 3
                            S.op("act", lambda e: e.activation(out=PT[pi], in_=stv, func=AF.Exp, scale=0.125), reads=bst, writes=[b_PT[pi]])

                        def emit_pv(kt):
                            pi = kt % 3
                            for a in range(8):
                                cm, qs = a // 4, a % 4
                                S.op("pe", lambda e, a=a, cm=cm, qs=qs: e.matmul(acc_ap(a), lhsT=PT[pi][:, cm, qs * 128:(qs + 1) * 128], rhs=V_[:, kt, 0:129],
                                                                                 start=(kt == 0 and a % 3 == 0), stop=(kt == NKT - 1), skip_group_check=True),
                                     reads=[b_PT[pi], b_V], writes=[pbuf[5 + a // 3]], mark=(a == 7))

                        emit_qk_exp(0)
                        for kt in range(NKT):
                            if kt + 1 < NKT:
                                emit_qk_exp(kt + 1)
                            emit_pv(kt)
                            if kt % 3 == 2 and pending_tail2:
                                pending_tail2.pop(0)()
                        for bkk in range(3):
                            n_ = 3 if bkk < 2 else 2
                            src = ps[:, (5 + bkk) * 512:(5 + bkk) * 512 + n_ * 130].rearrange("p (a b) -> p a b", a=n_)
                            S.op("dve", lambda e, bkk=bkk, n_=n_, src=src: e.tensor_copy(out=ACCS[:, 3 * bkk:3 * bkk + n_, :], in_=src),
                                 reads=[pbuf[5 + bkk]], writes=[b_ACCS])
                        S.op("dve", lambda e: e.reciprocal(out=nrm[:, 0:8], in_=ACCS[:, :, 128]), reads=[b_ACCS], writes=[b_nrm])
                        S.op("dve", lambda e: e.tensor_scalar(out=nrm[:, 8:12], in0=nrm[:, 4:8], scalar1=lamt[:, 4:5], scalar2=None, op0=ALU.mult),
                             reads=[b_nrm, b_const], writes=[b_nrm])
                        for qs in range(4):
                            oi = qg * 4 + qs
                            S.op("dve", lambda e, qs=qs: e.tensor_scalar(out=O0, in0=ACCS[:, qs, 0:128], scalar1=nrm[:, qs:qs + 1], scalar2=None, op0=ALU.mult),
                                 reads=[b_ACCS, b_nrm], writes=[b_O0])
                            S.op("dve", lambda e, qs=qs, oi=oi: e.scalar_tensor_tensor(out=OO[oi], in0=ACCS[:, 4 + qs, 0:128], scalar=nrm[:, 8 + qs:9 + qs], in1=O0,
                                                                                      op0=ALU.mult, op1=ALU.add),
                                 reads=[b_ACCS, b_nrm, b_O0], writes=[b_OO[oi]])
                            S.op("dve", lambda e, oi=oi: e.bn_stats(out=st6n, in_=OO[oi]), reads=[b_OO[oi]], writes=[b_stn])
                            S.op("dve", lambda e: e.bn_aggr(out=mvn, in_=st6n), reads=[b_stn], writes=[b_stn])
                            S.op("dve", lambda e, oi=oi: e.tensor_scalar(out=msa[:, oi:oi + 1], in0=mvn[:, 0:1], scalar1=mvn[:, 0:1], scalar2=mvn[:, 1:2],
                                                                        op0=ALU.mult, op1=ALU.add),
                                 reads=[b_stn], writes=[b_msa])

                    def head_tail(h=h):
                        S.op("act", lambda e: e.activation(out=msa, in_=msa, func=AF.Sqrt, bias=RMS_EPS, scale=1.0), reads=[b_msa], writes=[b_msa])
                        S.op("dve", lambda e: e.reciprocal(out=msa, in_=msa), reads=[b_msa], writes=[b_msa])
                        def tail_qg(qg2):
                            for qs in range(4):
                                oi = qg2 * 4 + qs
                                S.op("dve", lambda e, qs=qs, oi=oi: e.scalar_tensor_tensor(out=AT[qs], in0=OO[oi], scalar=msa[:, oi:oi + 1], in1=gsub[:],
                                                                                          op0=ALU.mult, op1=ALU.mult),
                                     reads=[b_OO[oi], b_msa, b_const], writes=[b_AT[qs]])
                                S.op("pe", lambda e, qs=qs: e.transpose(bank_bf(4)[:, qs * 128:(qs + 1) * 128], AT[qs], ident_bf[:]),
                                     reads=[b_AT[qs], b_const], writes=[pbuf[4]], mark=(qs == 3))
                            S.op("dve", lambda e, qg2=qg2: e.tensor_copy(out=attnT[:, h, qg2 * 512:(qg2 + 1) * 512], in_=bank_bf(4)[:, 0:512]),
                                 reads=[pbuf[4]], writes=[b_attnT])
                        for qg2 in range(NQG):
                            pending_tail2.append(lambda qg2=qg2: tail_qg(qg2))

                    pending_tail.append(head_tail)
                while pending_tail:
                    pending_tail.pop(0)()
                while pending_tail2:
                    pending_tail2.pop(0)()
                if dbg:
                    S.dma("sp", lambda e: e.dma_start(out=dbg_out["attnT"][:, :, tok0 + q0:tok0 + q0 + QBL], in_=attnT), ds_dbg, reads=[b_attnT], writes=[b_dbg])
                S.barrier()
                if stop == "p3":
                    break

                T = 512
                NST = T // 128
                al = Alloc(0 if BIG else p4_off)
                LNP = al.get([128, 4, D], F32)
                b_LNP = Buf("LNP", const=False)
                ds_lnp = dsem("lnp")
                for i, v in enumerate([ln1g_d, ln1b_d, ln2g_d, ln2b_d]):
                    S.dma("sp", lambda e, i=i, v=v: e.dma_start(out=LNP[:, i, :], in_=v.broadcast_to([128, D])), ds_lnp, writes=[b_LNP])
                X1 = al.get([128, NST, D], F32)
                b_X1 = [Buf(f"X1_{i}") for i in range(NST)]
                ds_X1 = [dsem(f"x1_{i}") for i in range(NST)]
                PAN = [al.get([128, 4096], BF16) for _ in range(3)]
                b_PAN = [Buf(f"PAN{i}") for i in range(3)]
                ds_PAN = [dsem(f"pan{i}") for i in range(3)]
                if BIG:
                    assert al.off <= HOFF, al.off
                    al.off = p4_off
                    HTT = al.get([128, NCH, T], BF16)
                    b_HTT = Buf("HTT")
                    ds_HTT = dsem("htt")
                TT_ = [al.get([128, T], F32) for _ in range(2)]
                b_TT = [Buf("TT0"), Buf("TT1")]
                XN2 = [al.get([128, D], BF16) for _ in range(NST)]
                b_XN2 = [Buf(f"XN2{i}") for i in range(NST)]
                STS = [(al.get([128, 2, 6], F32), al.get([128, 2], F32), al.get([128, 2], F32), Buf(f"sts{i}")) for i in range(NST)]
                ov = al.off
                MG = al.get([128, NCH, T], BF16)
                RT = al.get([128, NCH, T], BF16)
                SG = [al.get([128, T], F32) for _ in range(2)]
                AB = [al.get([128, T], F32) for _ in range(2)]
                e_end = al.off
                al.off = ov
                H2 = al.get([128, NCH, T], BF16)
                UT = al.get([128, NFF, T], BF16)
                SU = [al.get([128, T], F32) for _ in range(2)]
                b_MG = Buf("MG")
                b_RT = Buf("RT")
                ds_RT = dsem("rt")
                b_SG = [Buf("SG0"), Buf("SG1")]
                b_AB = [Buf("AB0"), Buf("AB1")]
                b_H2 = Buf("H2")
                b_UT = Buf("UT")
                b_SU = [Buf("SU0"), Buf("SU1")]
                b_yd = Buf("y_d")
                ds_y = [dsem(f"y{i}") for i in range(4)]

                NG = QBL // T
                panels = []
                for g in range(NG):
                    for oc in range(NCH):
                        panels.append(("a", g, oc))
                    for oc in range(NCH):
                        panels.append(("b", g, oc))
                    for j in range(NFF):
                        panels.append(("d", g, j))
                    for oc in range(NCH):
                        panels.append(("e", g, oc))
                pstate4 = {"issued": 0}

                def issue_panel(i):
                    kind, g, k = panels[i]
                    sl = i % 3
                    pv = PAN[sl]
                    if kind == "a":
                        v4 = pv.rearrange("p (a k n) -> p a k n", a=4, k=NCH)
                        srcs = [w_in_bf[(40 + k) * 128:(41 + k) * 128, :], w_in_bf[(48 + k) * 128:(49 + k) * 128, :],
                                w_ab_bf[k * 128:(k + 1) * 128, :], w_lb_bf[k * 128:(k + 1) * 128, :]]
                        for a_, s_ in enumerate(srcs):
                            S.dma("sp", lambda e, a_=a_, s_=s_, v4=v4: e.dma_start(out=v4[:, a_].rearrange("p k n -> p (k n)"), in_=s_),
                                  ds_PAN[sl], reads=[b_wbf], writes=[b_PAN[sl]])
                    elif kind == "b":
                        v3 = pv[:, 0:NCH * 128].rearrange("p (k n) -> p k n", k=NCH)
                        S.dma("sp", lambda e, v3=v3, k=k: e.dma_start(out=v3.rearrange("p k n -> p (k n)"), in_=w_out_bf[k * 128:(k + 1) * 128, :]),
                              ds_PAN[sl], reads=[b_wbf], writes=[b_PAN[sl]])
                    elif kind == "d":
                        v4 = pv[:, 0:2 * NCH * 128].rearrange("p (a k n) -> p a k n", a=2, k=NCH)
                        for a_ in range(2):
                            S.dma("sp", lambda e, a_=a_, v4=v4, k=k: e.dma_start(out=v4[:, a_].rearrange("p k n -> p (k n)"), in_=w_fi_bf[(a_ * NFF + k) * 128:(a_ * NFF + k + 1) * 128, :]),
                                  ds_PAN[sl], reads=[b_wbf], writes=[b_PAN[sl]])
                    else:
                        v3 = pv[:, 0:NFF * 128].rearrange("p (k n) -> p k n", k=NFF)
                        S.dma("sp", lambda e, v3=v3, k=k: e.dma_start(out=v3.rearrange("p k n -> p (k n)"), in_=w_fo_bf[k * 128:(k + 1) * 128, :]),
                              ds_PAN[sl], reads=[b_wbf], writes=[b_PAN[sl]])

                def get_panel():
                    i = pstate4["cur"]
                    while pstate4["issued"] < min(len(panels), i + 3):
                        issue_panel(pstate4["issued"])
                        pstate4["issued"] += 1
                    pstate4["cur"] = i + 1
                    return PAN[i % 3], b_PAN[i % 3]

                pstate4["cur"] = 0

                def residual_block(g, pv3, nk, rhs_fn, rhs_bufs, b_pan, gcol, tti):
                    bk = nextbank()
                    mm_group(bank(bk)[:, 0:T], pbuf[bk], [(pv3[:, kc, :], rhs_fn(kc)) for kc in range(nk)], [b_pan] + rhs_bufs)
                    tt = TT_[tti % 2]
                    btt = b_TT[tti % 2]
                    S.op("act", lambda e: e.activation(out=tt, in_=bank(bk)[:, 0:T], func=AF.Identity, scale=gcol),
                         reads=[pbuf[bk], b_const], writes=[btt])
                    bk2 = nextbank()
                    for s_ in range(NST):
                        S.op("pe", lambda e, s_=s_: e.transpose(bank(bk2)[:, s_ * 128:(s_ + 1) * 128], tt[:, s_ * 128:(s_ + 1) * 128], ident_f[:]),
                             reads=[btt, b_const], writes=[pbuf[bk2]], mark=(s_ == NST - 1))
                    return bk2

                for g in range(NG):
                    gt0 = q0 + g * T
                    gl0 = g * T
                    S.fence(["sp", "act", "dve", "pool"], [b_H2, b_UT, b_SU[0], b_SU[1]])
                    for s_ in range(NST):
                        S.dma("pool", lambda e, s_=s_: e.dma_start(out=X1[:, s_, :], in_=x_d[tok0 + gt0 + s_ * 128:tok0 + gt0 + (s_ + 1) * 128, :]),
                              ds_X1[s_], writes=[b_X1[s_]])
                    S.dma("sp", lambda e: e.dma_start(out=RT, in_=rec_d[:, :, tok0 + gt0:tok0 + gt0 + T].rearrange("c p t -> p c t")),
                          ds_RT, reads=[b_recd], writes=[b_RT])
                    if BIG:
                        S.dma("sp", lambda e: e.dma_start(out=HTT, in_=hT_d[:, :, gt0:gt0 + T]), ds_HTT, reads=[b_hTd], writes=[b_HTT])
                        hsrc = lambda kc: HTT[:, kc, :]
                        b_hsrc = b_HTT
                    else:
                        hsrc = lambda kc: hT[:, kc, gt0:gt0 + T]
                        b_hsrc = b_hT
                    for oc in range(NCH):
                        pv, bp = get_panel()
                        v4 = pv.rearrange("p (a k n) -> p a k n", a=4, k=NCH)
                        bks = []
                        order = [(0, hsrc, b_hsrc), (2, lambda kc: attnT[:, kc, gl0:gl0 + T], b_attnT),
                                 (1, hsrc, b_hsrc), (3, lambda kc: RT[:, kc, :], b_RT)]
                        for a_, rf, rb in order:
                            bk = nextbank()
                            mm_group(bank(bk)[:, 0:T], pbuf[bk], [(v4[:, a_, kc, :], rf(kc)) for kc in range(NCH)], [bp, rb])
                            bks.append(bk)
                        for half in range(2):
                            sg = SG[half]
                            ab = AB[half]
                            bg, bb = bks[2 * half], bks[2 * half + 1]
                            S.op("act", lambda e, sg=sg, bg=bg: e.activation(out=sg, in_=bank(bg)[:, 0:T], func=AF.Sigmoid),
                                 reads=[pbuf[bg]], writes=[b_SG[half]])
                            S.op("dve", lambda e, sg=sg, ab=ab, bb=bb: e.tensor_tensor(out=ab, in0=bank(bb)[:, 0:T], in1=sg, op=ALU.mult),
                                 reads=[pbuf[bb], b_SG[half]], writes=[b_AB[half]])
                        S.op("pool", lambda e, oc=oc: e.tensor_tensor(out=MG[:, oc, :], in0=AB[0], in1=AB[1], op=ALU.add),
                             reads=[b_AB[0], b_AB[1]], writes=[b_MG])
                    for oc in range(NCH):
                        pv, bp = get_panel()
                        v3 = pv[:, 0:NCH * 128].rearrange("p (k n) -> p k n", k=NCH)
                        bk2 = residual_block(g, v3, NCH, lambda kc: MG[:, kc, :], [b_MG], bp, g1c(oc), oc)
                        S.op("dve", lambda e, oc=oc, bk2=bk2: e.scalar_tensor_tensor(
                            out=X1[:, :, oc * 128:(oc + 1) * 128], in0=X1[:, :, oc * 128:(oc + 1) * 128], scalar=ALPHA,
                            in1=bank(bk2)[:, 0:T].rearrange("p (s n) -> p s n", s=NST), op0=ALU.mult, op1=ALU.add),
                            reads=[pbuf[bk2]] + b_X1, writes=b_X1)
                    S.fence(["act", "dve"], [b_MG, b_RT, b_SG[0], b_SG[1], b_AB[0], b_AB[1]])
                    bks_c = [nextbank() for _ in range(4)]
                    items = [(X1[:, s_, :], b_X1[s_]) + STS[s_] for s_ in range(NST)]
                    ln_affine_multi(items, 0, 1, LNP, b_LNP)
                    ln_stats_multi(items)
                    for s_ in range(NST):
                        xs_ap, _, st6, mv, rs, b_st = items[s_]
                        xn = XN2[s_]
                        S.op("act", lambda e: e.activation(out=xn, in_=xs_ap, func=AF.Identity, scale=rs[:, 0:1], bias=rs[:, 1:2]),
                             reads=[b_X1[s_], b_st], writes=[b_XN2[s_]])
                    for s_ in range(NST):
                        xn = XN2[s_]
                        for c in range(NCH):
                            bk = bks_c[c // 2]
                            S.op("pe", lambda e, bk=bk, c=c: e.transpose(
                                bank_bf(bk)[:, (c % 2) * 512 + s_ * 128:(c % 2) * 512 + (s_ + 1) * 128], xn[:, c * 128:(c + 1) * 128], ident_bf[:]),
                                reads=[b_XN2[s_], b_const], writes=[pbuf[bk]], mark=(c % 2 == 1))
                    for c in range(NCH):
                        bk = bks_c[c // 2]
                        S.op("act", lambda e, bk=bk, c=c: e.activation(out=H2[:, c, :], in_=bank_bf(bk)[:, (c % 2) * 512:(c % 2) * 512 + T],
                                                                       func=AF.Identity, scale=sc2p(c), bias=sh2(c)),
                             reads=[pbuf[bk], b_const], writes=[b_H2])
                    for j in range(NFF):
                        pv, bp = get_panel()
                        v4 = pv[:, 0:2 * NCH * 128].rearrange("p (a k n) -> p a k n", a=2, k=NCH)
                        bkg = nextbank()
                        mm_group(bank(bkg)[:, 0:T], pbuf[bkg], [(v4[:, 0, kc, :], H2[:, kc, :]) for kc in range(NCH)], [bp, b_H2])
                        bku = nextbank()
                        mm_group(bank(bku)[:, 0:T], pbuf[bku], [(v4[:, 1, kc, :], H2[:, kc, :]) for kc in range(NCH)], [bp, b_H2])
                        su = SU[j % 2]
                        S.op("act", lambda e, su=su, bkg=bkg: e.activation(out=su, in_=bank(bkg)[:, 0:T], func=AF.Silu),
                             reads=[pbuf[bkg]], writes=[b_SU[j % 2]])
                        S.op("dve", lambda e, su=su, bku=bku, j=j: e.tensor_tensor(out=UT[:, j, :], in0=bank(bku)[:, 0:T], in1=su, op=ALU.mult),
                             reads=[pbuf[bku], b_SU[j % 2]], writes=[b_UT])
                    for oc in range(NCH):
                        pv, bp = get_panel()
                        v3 = pv[:, 0:NFF * 128].rearrange("p (k n) -> p k n", k=NFF)
                        bk2 = residual_block(g, v3, NFF, lambda kc: UT[:, kc, :], [b_UT], bp, g2c(oc), oc)
                        S.op("dve", lambda e, oc=oc, bk2=bk2: e.scalar_tensor_tensor(
                            out=X1[:, :, oc * 128:(oc + 1) * 128], in0=X1[:, :, oc * 128:(oc + 1) * 128], scalar=ALPHA,
                            in1=bank(bk2)[:, 0:T].rearrange("p (s n) -> p s n", s=NST), op0=ALU.mult, op1=ALU.add),
                            reads=[pbuf[bk2]] + b_X1, writes=b_X1)
                    items = [(X1[:, s_, :], b_X1[s_]) + STS[s_] for s_ in range(NST)]
                    ln_affine_multi(items, 2, 3, LNP, b_LNP)
                    for s_ in range(NST):
                        S.dma("pool", lambda e, s_=s_: e.dma_start(out=y_d[tok0 + gt0 + s_ * 128:tok0 + gt0 + (s_ + 1) * 128, :], in_=X1[:, s_, :]),
                              ds_y[s_], reads=[b_X1[s_]], writes=[b_yd])
                S.barrier()
                if BIG and qb + 1 < NQB:
                    S.dma("sp", lambda e: e.dma_start(out=hT, in_=hT_d[:, :, 0:SL]), ds_hsp, reads=[b_hTd], writes=[b_hT])
        S.barrier()
        S.emit()
    return nc


def _rope_tables(smax):
    inv = (1.0 / (np.float32(10000.0) ** (np.arange(0, 64, 2, dtype=np.float32) / np.float32(64)))).astype(np.float32)
    ang = (np.arange(smax, dtype=np.float32)[:, None] * inv[None, :]).astype(np.float32)
    cos = np.cos(ang).astype(np.float32)
    sin = np.sin(ang).astype(np.float32)
    c = np.zeros((128, smax), np.float32)
    s = np.zeros((128, smax), np.float32)
    for p in range(128):
        d = p % 64
        j = d % 32
        c[p] = cos[:, j]
        s[p] = (-sin[:, j]) if d < 32 else sin[:, j]
    return c, s


def _consts(smax):
    ident = np.eye(128, dtype=np.float32)
    prot = np.zeros((128, 128), np.float32)
    for m in range(128):
        blk = (m // 64) * 64
        d = m % 64
        k = blk + ((d + 32) % 64)
        prot[k, m] = 1.0
    c, s = _rope_tables(smax)
    return ident, prot, c, s


def make_in_maps(inputs, n_cores, seq_plan):
    f = lambda a: np.ascontiguousarray(np.asarray(a, dtype=np.float32))
    def pan(a, nk):
        a = f(a)
        ncb = a.shape[1] // 128
        return np.ascontiguousarray(a.reshape(nk, 128, ncb, 128).transpose(2, 1, 0, 3).reshape(ncb * 128, nk * 128))

    w = {
        "w_ada": f(inputs["w_ada"][0]), "b_ada": f(inputs["b_ada"][0]).reshape(1, -1), "w_in": pan(inputs["w_in"][0], 8),
        "lq1": f(inputs["lambda_q1"][0]).reshape(1, 64), "lk1": f(inputs["lambda_k1"][0]).reshape(1, 64),
        "lq2": f(inputs["lambda_q2"][0]).reshape(1, 64), "lk2": f(inputs["lambda_k2"][0]).reshape(1, 64),
        "subln_g": f(inputs["subln_g"][0]).reshape(1, 128), "conv_w": f(inputs["conv_w"][0]), "conv_b": f(inputs["conv_b"][0]).reshape(1, -1),
        "w_lru_gates": f(inputs["w_lru_gates"][0]).reshape(4, 16, 64, 64), "b_lru_gates": f(inputs["b_lru_gates"][0]).reshape(4, -1),
        "lru_lambda": f(inputs["lru_lambda"][0]), "w_ab": pan(inputs["w_attn_branch"][0], 8), "w_lb": pan(inputs["w_lru_branch"][0], 8),
        "w_out": pan(inputs["w_out"][0], 8), "ln1_g": f(inputs["ln1_g"][0]).reshape(1, -1), "ln1_b": f(inputs["ln1_b"][0]).reshape(1, -1),
        "w_ffn_in": pan(inputs["w_ffn_in"][0], 8), "w_ffn_out": pan(inputs["w_ffn_out"][0], 22),
        "ln2_g": f(inputs["ln2_g"][0]).reshape(1, -1), "ln2_b": f(inputs["ln2_b"][0]).reshape(1, -1),
    }
    maps = []
    for core in range(n_cores):
        plan = seq_plan(core)
        smax = max(p[0].shape[0] for p in plan)
        ident, prot, c, s = _consts(smax)
        m = dict(w)
        m["x"] = np.ascontiguousarray(np.concatenate([p[0] for p in plan], axis=0))
        m["c"] = np.ascontiguousarray(np.stack([p[1] for p in plan], axis=0))
        m["ident"] = ident
        m["prot"] = prot
        m["rope_c"] = c
        m["rope_s"] = s
        maps.append(m)
    return maps


_NC_CACHE = {}


def kernel(**inputs):
    n = 8
    xp = np.asarray(inputs["x_prompt"], dtype=np.float32)
    xs = np.asarray(inputs["x_sample"], dtype=np.float32)
    cp = np.asarray(inputs["c_prompt"], dtype=np.float32)
    cs = np.asarray(inputs["c_sample"], dtype=np.float32)
    B, SP, _ = xp.shape
    DB, SS, _ = xs.shape
    per = DB // n
    seq_lens = [SP] + [SS] * per

    def plan(core):
        return [(xp[core], cp[core])] + [(xs[core * per + i], cs[core * per + i]) for i in range(per)]

    key = tuple(seq_lens)
    if key not in _NC_CACHE:
        _NC_CACHE[key] = build(seq_lens)
    nc = _NC_CACHE[key]
    maps = make_in_maps(inputs, n, plan)
    res = run_bass_kernel_spmd(nc, maps, core_ids=list(range(n)))
    yp = np.empty((B, SP, D), np.float32)
    ys = np.empty((DB, SS, D), np.float32)
    for core in range(n):
        y = np.asarray(res.results[core]["y"], dtype=np.float32)
        yp[core] = y[0:SP]
        ys[core * per:(core + 1) * per] = y[SP:].reshape(per, SS, D)
    return (yp, ys)
```

```python
import os
import numpy as np
from contextlib import ExitStack
import concourse.bass as bass
import concourse.mybir as mybir
from concourse.bass_utils import run_bass_kernel_spmd

F32 = mybir.dt.float32
BF16 = mybir.dt.bfloat16
AF = mybir.ActivationFunctionType
ALU = mybir.AluOpType

ENGS = ("pe", "act", "dve", "pool", "sp")
D = 1024
NCH = 8
DFF = 2816
NFF = 22
ALPHA = 2.0 ** 0.25
LN_EPS = 1e-5
RMS_EPS = 1e-5
LAMBDA_INIT = 0.2
QB = 2048
ARENA_BYTES = 174 * 1024


class Sem:
    def __init__(self, h, name):
        self.h = h
        self.name = name
        self.count = 0


class Buf:
    __slots__ = ("name", "w", "r", "const", "excl")

    def __init__(self, name, const=False, excl=False):
        self.name = name
        self.w = None
        self.r = {}
        self.const = const
        self.excl = excl


class _Rec:
    def __init__(self):
        self.call = None

    def __getattr__(self, name):
        def f(*a, **k):
            self.call = (name, a, k)
            return self
        return f


def _record(fn):
    r = _Rec()
    fn(r)
    return r.call


class Sched:
    def __init__(self, nc, stack):
        self.nc = nc
        self.stack = stack
        self.prog = {e: [] for e in ENGS}
        self.sems = []
        self.esem = {e: self.new_sem("e_" + e) for e in ENGS if e != "sp"}
        self.waited = {e: {} for e in ENGS}

    def new_sem(self, name):
        s = Sem(self.stack.enter_context(self.nc.semaphore(name)), name)
        self.sems.append(s)
        return s

    def _deps(self, eng, reads, writes):
        deps = {}
        for b in reads:
            if b.w is not None:
                s, v = b.w
                if deps.get(s, 0) < v:
                    deps[s] = v
        for b in writes:
            if b.w is not None:
                s, v = b.w
                if deps.get(s, 0) < v:
                    deps[s] = v
            for s, v in b.r.items():
                if deps.get(s, 0) < v:
                    deps[s] = v
        out = []
        wd = self.waited[eng]
        own = self.esem.get(eng)
        for s, v in deps.items():
            if s is own and eng == "pe":
                continue
            if wd.get(s, 0) >= v:
                continue
            assert s.count >= v, f"wait on future tick: eng={eng} sem={s.name} v={v} count={s.count}"
            wd[s] = v
            out.append((s, v))
        return out

    def op(self, eng, fn, reads=(), writes=(), mark=True):
        ex = [b for b in reads if b.excl]
        if ex:
            reads = [b for b in reads if not b.excl]
            writes = list(writes) + [b for b in ex if b not in writes]
        waits = self._deps(eng, reads, writes)
        sem = self.esem[eng]
        if mark:
            sem.count += 1
            tick = sem.count
        else:
            tick = sem.count + 1
        self.prog[eng].append((waits, _record(fn), (sem, 1) if mark else None))
        for b in reads:
            if not b.const and b.r.get(sem, 0) < tick:
                b.r[sem] = tick
        for b in writes:
            b.w = (sem, tick)
            b.r = {}

    def dma(self, queue, fn, dsem, reads=(), writes=()):
        waits = self._deps(queue, reads, writes)
        dsem.count += 16
        v = dsem.count
        self.prog[queue].append((waits, _record(fn), (dsem, 16)))
        for b in reads:
            if not b.const and b.r.get(dsem, 0) < v:
                b.r[dsem] = v
        for b in writes:
            b.w = (dsem, v)
            b.r = {}

    def fence(self, engs, bufs):
        for e in engs:
            waits = self._deps(e, [], bufs)
            self.prog[e].append((waits, None, None))

    def barrier(self, exclude=()):
        for e in ENGS:
            wd = self.waited[e]
            waits = []
            for s in self.sems:
                if s is self.esem.get(e) or s in exclude:
                    continue
                if s.count > wd.get(s, 0):
                    wd[s] = s.count
                    waits.append((s, s.count))
            self.prog[e].append((waits, None, None))

    def emit(self):
        nc = self.nc
        with nc.Block() as block:
            deco = {"pe": block.tensor, "act": block.scalar, "dve": block.vector, "pool": block.gpsimd, "sp": block.sync}
            for e in ENGS:
                prog = self.prog[e]

                def body(eng, prog=prog):
                    for waits, fn, inc in prog:
                        for s, v in waits:
                            eng.wait_ge(s.h, v)
                        if fn is not None:
                            ins = getattr(eng, fn[0])(*fn[1], **fn[2])
                            if inc is not None:
                                ins.then_inc(inc[0].h, inc[1])

                deco[e](body)


class _Stop(Exception):
    pass


def build(seq_lens, dbg=False, stop=None):
    nc = bass.Bass("TRN2", target_bir_lowering=False)
    NSEQ = len(seq_lens)
    NTOK = sum(seq_lens)
    SMAX = max(seq_lens)

    def din(name, shape, dt=F32):
        return nc.dram_tensor(name, list(shape), dt, kind="ExternalInput").ap()

    def dint(name, shape, dt):
        return nc.dram_tensor(name, list(shape), dt, kind="Internal").ap()

    x_d = din("x", [NTOK, D])
    c_d = din("c", [NSEQ, D])
    w_ada_d = din("w_ada", [D, 6 * D])
    b_ada_d = din("b_ada", [1, 6 * D])
    w_in_d = din("w_in", [56 * 128, 1024])
    lq1_d = din("lq1", [1, 64])
    lk1_d = din("lk1", [1, 64])
    lq2_d = din("lq2", [1, 64])
    lk2_d = din("lk2", [1, 64])
    subln_d = din("subln_g", [1, 128])
    conv_w_d = din("conv_w", [4, D])
    conv_b_d = din("conv_b", [1, D])
    wlg_d = din("w_lru_gates", [4, 16, 64, 64])
    blg_d = din("b_lru_gates", [4, D])
    llam_d = din("lru_lambda", [2, D])
    w_ab_d = din("w_ab", [8 * 128, 1024])
    w_lb_d = din("w_lb", [8 * 128, 1024])
    w_out_d = din("w_out", [8 * 128, 1024])
    ln1g_d = din("ln1_g", [1, D])
    ln1b_d = din("ln1_b", [1, D])
    w_fi_d = din("w_ffn_in", [44 * 128, 1024])
    w_fo_d = din("w_ffn_out", [8 * 128, DFF])
    ln2g_d = din("ln2_g", [1, D])
    ln2b_d = din("ln2_b", [1, D])
    ident_d = din("ident", [128, 128])
    prot_d = din("prot", [128, 128])
    ropec_d = din("rope_c", [128, SMAX])
    ropes_d = din("rope_s", [128, SMAX])
    y_d = nc.dram_tensor("y", [NTOK, D], F32, kind="ExternalOutput").ap()

    w_in_bf = dint("w_in_bf", [56 * 128, 1024], BF16)
    w_ab_bf = dint("w_ab_bf", [8 * 128, 1024], BF16)
    w_lb_bf = dint("w_lb_bf", [8 * 128, 1024], BF16)
    w_out_bf = dint("w_out_bf", [8 * 128, 1024], BF16)
    w_fi_bf = dint("w_fi_bf", [44 * 128, 1024], BF16)
    w_fo_bf = dint("w_fo_bf", [8 * 128, DFF], BF16)
    mod_d = dint("mod_d", [NSEQ, 6 * D], F32)
    lruw_d = dint("lruw_d", [128, NCH, 8, 128], BF16)
    rec_d = dint("rec_d", [NCH, 128, NTOK], BF16)
    hT_d = dint("hT_d", [128, NCH, SMAX], BF16)
    rope_bf = dint("rope_bf", [2, 128, SMAX], BF16)
    BIGTH = int(os.environ.get("KBIGTH", "2048"))
    QBX = int(os.environ.get("KQB", str(QB)))

    dbg_out = {}
    if dbg:
        dbg_out["hT"] = nc.dram_tensor("dbg_hT", [128, NCH, NTOK], BF16, kind="ExternalOutput").ap()
        dbg_out["attnT"] = nc.dram_tensor("dbg_attnT", [128, NCH, NTOK], BF16, kind="ExternalOutput").ap()
        dbg_out["modT"] = nc.dram_tensor("dbg_modT", [128, 48, NSEQ], F32, kind="ExternalOutput").ap()
        dbg_out["rec"] = nc.dram_tensor("dbg_rec", [NCH, 128, NTOK], BF16, kind="ExternalOutput").ap()

    with ExitStack() as st:
        S = Sched(nc, st)

        def sb(name, shape, dt=F32):
            return st.enter_context(nc.sbuf_tensor(name, list(shape), dt))

        ident_bf = sb("ident_bf", [128, 128], BF16)
        ident_f = sb("ident_f", [128, 128], F32)
        prot_bf = sb("prot_bf", [128, 128], BF16)
        fmc = sb("fmc", [128, 11, NCH], F32)
        cdec = sb("cdec", [128, 4, NCH], F32)
        hbias = sb("hbias", [128, 4, NCH], F32)
        modT = sb("modT", [128, 48, NSEQ], F32)
        lamt = sb("lamt", [128, 8], F32)
        gsub = sb("gsub", [128, 128], F32)
        small = sb("small", [128, 64], F32)
        arena = sb("arena", [128, ARENA_BYTES // 2], BF16)
        ps = st.enter_context(nc.psum_tensor("ps", [128, 8 * 512], F32))

        b_const = Buf("const")
        b_hT = Buf("hT")

        def bank(b):
            return ps[:, b * 512:(b + 1) * 512]

        def bank_bf(b):
            return ps[:, b * 512:(b + 1) * 512].bitcast(BF16)

        pbuf = [Buf(f"psb{i}", excl=True) for i in range(8)]
        pstate = {"next": 0}

        def nextbank(lo=0, hi=8):
            n = pstate["next"]
            if n < lo or n >= hi:
                n = lo
            pstate["next"] = n + 1
            return n

        class Alloc:
            def __init__(self, off=0):
                self.off = off

            def get(self, shape, dt, name="t"):
                n = int(np.prod(shape[1:]))
                esz = 4 if dt == F32 else 2
                nbytes = (n * esz + 3) // 4 * 4
                assert self.off + nbytes <= ARENA_BYTES, f"arena overflow {name} {self.off + nbytes}"
                v = arena[0:shape[0], self.off // 2:(self.off + n * esz) // 2]
                if dt == F32:
                    v = v.bitcast(F32)
                if len(shape) == 3:
                    v = v.rearrange("p (a b) -> p a b", a=shape[1])
                elif len(shape) == 4:
                    v = v.rearrange("p (a b c) -> p a b c", a=shape[1], b=shape[2])
                self.off += nbytes
                return v

        sem_pool = {}

        def dsem(name):
            if name not in sem_pool:
                sem_pool[name] = S.new_sem("d_" + name)
            return sem_pool[name]

        ds_setup = dsem("setup")
        ds_cast = dsem("cast")
        b_wbf = Buf("wbf")


        SKIP = set(os.environ.get("KSKIP", "").split(","))

        cast_i = [0]
        ds_castk = [dsem("castk0"), dsem("castk1")]
        b_castk = [Buf("castk0"), Buf("castk1")]

        def cast_rows(dst, src, rows, blk):
            if "cast" in SKIP:
                return
            for r0 in range(0, rows, blk):
                r1 = min(rows, r0 + blk)
                k = cast_i[0] % 2
                cast_i[0] += 1
                S.dma("pool", lambda e, r0=r0, r1=r1: e.dma_start(out=dst[r0:r1, :], in_=src[r0:r1, :], max_dma_last_dim=4096),
                      ds_castk[k], writes=[b_castk[k]])

        al = Alloc()
        lv = al.get([128, 4, 64], F32)
        junk = al.get([128, 2, 64], F32)
        junk2 = al.get([128, 64], F32)
        wb = al.get([128, NCH, 8, 128], BF16)
        cT = al.get([128, NCH, NSEQ], F32)
        ones1 = al.get([1, NSEQ], F32)
        bada = al.get([1, 6 * D], F32)
        mod_sb = al.get([NSEQ, 6 * D], F32)
        wpan = [al.get([128, NCH, 512], F32) for _ in range(2)]
        b_wb = Buf("wb")
        b_lruw = Buf("lruw")
        b_modd = Buf("mod_d")
        b_mod = Buf("mod_sb")
        b_wpan = [Buf("wpan0"), Buf("wpan1")]
        ds_wp = [dsem("wp0"), dsem("wp1")]
        ds_wbd = dsem("wbd")

        S.op("pool", lambda e: e.memset(wb, 0.0), writes=[b_wb])
        S.dma("pool", lambda e: e.dma_start(out=ident_bf[:], in_=ident_d), ds_setup, writes=[b_const])
        S.dma("pool", lambda e: e.dma_start(out=prot_bf[:], in_=prot_d), ds_setup, writes=[b_const])
        for dg in (range(4) if "wbdma" not in SKIP else []):
            for j in range(2):
                src = wlg_d[dg].rearrange("(c j) d e -> j d c e", j=2)[j]
                S.dma("pool", lambda e, dg=dg, j=j, src=src: e.dma_start(out=wb[64 * j:64 * j + 64, :, dg, 64 * j:64 * j + 64], in_=src),
                      ds_wbd, reads=[], writes=[b_wb])
        cast_rows(w_in_bf, w_in_d, 56 * 128, 1024)
        S.dma("sp", lambda e: e.dma_start(out=ident_f[:], in_=ident_d), ds_setup, writes=[b_const])
        for s_ in range(NSEQ):
            S.dma("sp", lambda e, s_=s_: e.dma_start(out=cT[:, :, s_], in_=c_d[s_:s_ + 1, :].rearrange("o (c p) -> p (o c)", p=128),
                                                    allow_slow_non_contiguous=True), ds_setup, writes=[b_const])
        S.dma("sp", lambda e: e.dma_start(out=bada, in_=b_ada_d), ds_setup, writes=[b_const])
        vecs = [conv_w_d[0:1, :], conv_w_d[1:2, :], conv_w_d[2:3, :], conv_w_d[3:4, :], conv_b_d,
                blg_d[0:1, :], blg_d[1:2, :], blg_d[2:3, :], blg_d[3:4, :], llam_d[0:1, :], llam_d[1:2, :]]
        for i, v in (enumerate(vecs) if "slow" not in SKIP else []):
            S.dma("sp", lambda e, i=i, v=v: e.dma_start(out=fmc[:, i, :], in_=v.rearrange("o (c p) -> p (o c)", p=128),
                                                       allow_slow_non_contiguous=True), ds_setup, writes=[b_const])
        for i, v in (enumerate([lq1_d, lk1_d, lq2_d, lk2_d]) if "bcast" not in SKIP else []):
            S.dma("sp", lambda e, i=i, v=v: e.dma_start(out=lv[:, i, :], in_=v.broadcast_to([128, 64])), ds_setup, writes=[b_const])
        if "bcast" not in SKIP:
            S.dma("sp", lambda e: e.dma_start(out=gsub[:], in_=subln_d.broadcast_to([128, 128])), ds_setup, writes=[b_const])
        cast_rows(rope_bf[0], ropec_d, 128, 128)
        cast_rows(rope_bf[1], ropes_d, 128, 128)
        cast_rows(w_ab_bf, w_ab_d, 1024, 1024)
        cast_rows(w_lb_bf, w_lb_d, 1024, 1024)
        cast_rows(w_out_bf, w_out_d, 1024, 1024)
        cast_rows(w_fi_bf, w_fi_d, 44 * 128, 1024)
        cast_rows(w_fo_bf, w_fo_d, 1024, 512)
        S.barrier(exclude=ds_castk)
        S.op("dve", lambda e: e.tensor_tensor(out=junk[:, 0, :], in0=lv[:, 0, :], in1=lv[:, 1, :], op=ALU.mult), writes=[b_const])
        S.op("dve", lambda e: e.tensor_tensor(out=junk[:, 1, :], in0=lv[:, 2, :], in1=lv[:, 3, :], op=ALU.mult), reads=[b_const], writes=[b_const])
        S.op("dve", lambda e: e.tensor_scalar(out=gsub[:], in0=gsub[:], scalar1=1.0 - LAMBDA_INIT, scalar2=None, op0=ALU.mult),
             reads=[b_const], writes=[b_const])
        S.op("dve", lambda e: e.memset(ones1, 1.0), reads=[b_const], writes=[b_const])
        for c in range(NCH):
            for k in range(4):
                S.op("dve", lambda e, c=c, k=k: e.tensor_scalar(out=wb[:, c, 4 + k, :], in0=ident_bf[:], scalar1=fmc[:, k, c:c + 1],
                                                                scalar2=None, op0=ALU.mult), reads=[b_wb], writes=[b_wb])
        S.op("act", lambda e: e.activation(out=cT, in_=cT, func=AF.Silu), writes=[b_mod])
        S.op("act", lambda e: e.activation(out=small[:, 0:16], in_=fmc[:, 9:11, :].rearrange("p a c -> p (a c)"), func=AF.Exp, scale=-1.0),
             reads=[b_mod], writes=[b_mod])
        S.barrier(exclude=ds_castk)
        S.op("act", lambda e: e.activation(out=junk2, in_=junk[:, 0, :], func=AF.Identity, accum_out=lamt[:, 0:1]), writes=[b_mod])
        S.op("act", lambda e: e.activation(out=junk2, in_=junk[:, 1, :], func=AF.Identity, accum_out=lamt[:, 1:2]), reads=[b_mod], writes=[b_mod])
        S.op("act", lambda e: e.activation(out=lamt[:, 2:4], in_=lamt[:, 0:2], func=AF.Exp), reads=[b_mod], writes=[b_mod])
        S.op("act", lambda e: e.activation(out=small[:, 16:32], in_=small[:, 0:16], func=AF.Ln, bias=1.0, scale=1.0), reads=[b_mod], writes=[b_mod])
        S.dma("sp", lambda e: e.dma_start(out=lruw_d, in_=wb), ds_wbd, reads=[b_wb], writes=[b_lruw])
        S.barrier(exclude=ds_castk)
        S.op("dve", lambda e: e.tensor_tensor(out=lamt[:, 4:5], in0=lamt[:, 3:4], in1=lamt[:, 2:3], op=ALU.subtract), writes=[b_const])
        S.op("dve", lambda e: e.tensor_scalar(out=lamt[:, 4:5], in0=lamt[:, 4:5], scalar1=-LAMBDA_INIT, scalar2=None, op0=ALU.add),
             reads=[b_const], writes=[b_const])
        S.op("dve", lambda e: e.tensor_scalar(out=hbias[:], in0=fmc[:, 5:9, :], scalar1=0.5, scalar2=None, op0=ALU.mult), reads=[b_const], writes=[b_const])
        for d_ in range(2):
            S.op("dve", lambda e, d_=d_: e.tensor_scalar(out=cdec[:, 2 * d_, :], in0=small[:, 16 + 8 * d_:24 + 8 * d_], scalar1=-4.0,
                                                         scalar2=None, op0=ALU.mult), reads=[b_const], writes=[b_const])
            S.op("dve", lambda e, d_=d_: e.tensor_scalar(out=cdec[:, 2 * d_ + 1, :], in0=small[:, 16 + 8 * d_:24 + 8 * d_], scalar1=-8.0,
                                                         scalar2=None, op0=ALU.mult), reads=[b_const], writes=[b_const])
        for pn in (range(12) if "mod" not in SKIP else []):
            sl = pn % 2
            S.dma("sp", lambda e, pn=pn, sl=sl: e.dma_start(out=wpan[sl], in_=w_ada_d[:, pn * 512:(pn + 1) * 512].rearrange("(c p) n -> p c n", p=128)),
                  ds_wp[sl], writes=[b_wpan[sl]])
            bk = nextbank()
            for kc in range(NCH):
                S.op("pe", lambda e, bk=bk, kc=kc, sl=sl: e.matmul(bank(bk)[0:NSEQ, :], lhsT=cT[:, kc, :], rhs=wpan[sl][:, kc, :],
                                                                   start=(kc == 0), stop=False),
                     reads=[b_wpan[sl]], writes=[pbuf[bk]], mark=False)
            S.op("pe", lambda e, bk=bk, pn=pn: e.matmul(bank(bk)[0:NSEQ, :], lhsT=ones1, rhs=bada[:, pn * 512:(pn + 1) * 512],
                                                        start=False, stop=True), reads=[], writes=[pbuf[bk]])
            S.op("act", lambda e, bk=bk, pn=pn: e.activation(out=mod_sb[:, pn * 512:(pn + 1) * 512], in_=bank(bk)[0:NSEQ, :], func=AF.Identity),
                 reads=[pbuf[bk]], writes=[b_mod])
        S.dma("sp", lambda e: e.dma_start(out=mod_d, in_=mod_sb), ds_setup, reads=[b_mod], writes=[b_modd])
        for s_ in range(NSEQ):
            S.dma("sp", lambda e, s_=s_: e.dma_start(out=modT[:, :, s_], in_=mod_d[s_:s_ + 1, :].rearrange("o (c p) -> p (o c)", p=128),
                                                    allow_slow_non_contiguous=True), ds_setup, reads=[b_modd], writes=[b_modd])
        for lo in (8, 32):
            S.op("dve", lambda e, lo=lo: e.tensor_scalar(out=modT[:, lo:lo + 8, :], in0=modT[:, lo:lo + 8, :], scalar1=1.0, scalar2=None, op0=ALU.add),
                 reads=[b_modd], writes=[b_modd])
        ds_dbg = dsem("dbg")
        b_dbg = Buf("dbg")
        if dbg:
            S.dma("sp", lambda e: e.dma_start(out=dbg_out["modT"], in_=modT[:]), ds_dbg, reads=[b_modd], writes=[b_dbg])
        S.barrier(exclude=ds_castk)
        b_const = Buf("const2", const=True)
        b_wbf = Buf("wbf2", const=True)
        b_lruw = Buf("lruw2", const=True)
        b_recd_dummy = None

        def ln_stats(src_ap, b_src, st6, mv, rs, b_st):
            S.op("dve", lambda e: e.bn_stats(out=st6[:, 0, :], in_=src_ap[:, 0:512]), reads=[b_src], writes=[b_st])
            S.op("dve", lambda e: e.bn_stats(out=st6[:, 1, :], in_=src_ap[:, 512:1024]), reads=[b_src], writes=[b_st])
            S.op("dve", lambda e: e.bn_aggr(out=mv, in_=st6.rearrange("p a b -> p (a b)")), reads=[b_st], writes=[b_st])
            S.op("act", lambda e: e.activation(out=rs[:, 0:1], in_=mv[:, 1:2], func=AF.Sqrt, bias=LN_EPS, scale=1.0), reads=[b_st], writes=[b_st])
            S.op("dve", lambda e: e.reciprocal(out=rs[:, 0:1], in_=rs[:, 0:1]), reads=[b_st], writes=[b_st])
            S.op("dve", lambda e: e.tensor_scalar(out=rs[:, 1:2], in0=mv[:, 0:1], scalar1=rs[:, 0:1], scalar2=-1.0, op0=ALU.mult, op1=ALU.mult),
                 reads=[b_st], writes=[b_st])

        def ln_stats_multi(items):
            for src_ap, b_src, st6, mv, rs, b_st in items:
                S.op("dve", lambda e: e.bn_stats(out=st6[:, 0, :], in_=src_ap[:, 0:512]), reads=[b_src], writes=[b_st])
                S.op("dve", lambda e: e.bn_stats(out=st6[:, 1, :], in_=src_ap[:, 512:1024]), reads=[b_src], writes=[b_st])
                S.op("dve", lambda e: e.bn_aggr(out=mv, in_=st6.rearrange("p a b -> p (a b)")), reads=[b_st], writes=[b_st])
            for src_ap, b_src, st6, mv, rs, b_st in items:
                S.op("act", lambda e: e.activation(out=rs[:, 0:1], in_=mv[:, 1:2], func=AF.Sqrt, bias=LN_EPS, scale=1.0), reads=[b_st], writes=[b_st])
            for src_ap, b_src, st6, mv, rs, b_st in items:
                S.op("dve", lambda e: e.reciprocal(out=rs[:, 0:1], in_=rs[:, 0:1]), reads=[b_st], writes=[b_st])
                S.op("dve", lambda e: e.tensor_scalar(out=rs[:, 1:2], in0=mv[:, 0:1], scalar1=rs[:, 0:1], scalar2=-1.0, op0=ALU.mult, op1=ALU.mult),
                     reads=[b_st], writes=[b_st])

        def ln_affine_multi(items, gi, bi_, LNP, b_LNP):
            ln_stats_multi(items)
            for src_ap, b_src, st6, mv, rs, b_st in items:
                S.op("act", lambda e: e.activation(out=src_ap, in_=src_ap, func=AF.Identity, scale=rs[:, 0:1], bias=rs[:, 1:2]),
                     reads=[b_src, b_st], writes=[b_src])
            for src_ap, b_src, st6, mv, rs, b_st in items:
                S.op("dve", lambda e: e.tensor_tensor(out=src_ap, in0=src_ap, in1=LNP[:, gi, :], op=ALU.mult), reads=[b_src, b_LNP], writes=[b_src])
            for i, (src_ap, b_src, st6, mv, rs, b_st) in enumerate(items):
                eng = "pool" if i % 2 == 0 else "dve"
                S.op(eng, lambda e: e.tensor_tensor(out=src_ap, in0=src_ap, in1=LNP[:, bi_, :], op=ALU.add), reads=[b_src, b_LNP], writes=[b_src])

        def mm_group(out_ap, b_out, pairs, reads):
            n = len(pairs)
            for i, (l, r) in enumerate(pairs):
                S.op("pe", lambda e, l=l, r=r, i=i: e.matmul(out_ap, lhsT=l, rhs=r, start=(i == 0), stop=(i == n - 1)),
                     reads=reads, writes=[b_out], mark=(i == n - 1))

        for si in ([] if stop == "setup" else range(NSEQ)):
            SL = seq_lens[si]
            tok0 = sum(seq_lens[:si])
            NT = SL // 512
            sc1p = lambda c: modT[:, 8 + c, si:si + 1]
            sh1 = lambda c: modT[:, 0 + c, si:si + 1]
            g1c = lambda c: modT[:, 16 + c, si:si + 1]
            sh2 = lambda c: modT[:, 24 + c, si:si + 1]
            sc2p = lambda c: modT[:, 32 + c, si:si + 1]
            g2c = lambda c: modT[:, 40 + c, si:si + 1]

            al = Alloc()
            hT = al.get([128, NCH, SL], BF16)
            HOFF = al.off
            XT = [al.get([128, D], F32) for _ in range(8)]
            b_XT = [Buf(f"XT{i}") for i in range(8)]
            ds_XT = [dsem(f"xt{i}") for i in range(8)]
            XN = [al.get([128, D], BF16) for _ in range(4)]
            b_XN = [Buf(f"XN{i}") for i in range(4)]
            STT_ = [(al.get([128, 2, 6], F32), al.get([128, 2], F32), al.get([128, 2], F32), Buf(f"st{i}")) for i in range(4)]
            for g in range(NT):
                base = (g % 2) * 4
                items = []
                for j in range(4):
                    ti = g * 4 + j
                    sl = ti % 8
                    S.dma("sp", lambda e, sl=sl, ti=ti: e.dma_start(out=XT[sl], in_=x_d[tok0 + ti * 128: tok0 + (ti + 1) * 128, :]),
                          ds_XT[sl], writes=[b_XT[sl]])
                    items.append((XT[sl], b_XT[sl]) + STT_[j])
                ln_stats_multi(items)
                for j in range(4):
                    src_ap, b_src, st6, mv, rs, b_st = items[j]
                    S.op("act", lambda e: e.activation(out=XN[j], in_=src_ap, func=AF.Identity, scale=rs[:, 0:1], bias=rs[:, 1:2]),
                         reads=[b_src, b_st], writes=[b_XN[j]])
                for j in range(4):
                    for c in range(NCH):
                        bk = base + c // 2
                        S.op("pe", lambda e, bk=bk, c=c, j=j: e.transpose(
                            bank_bf(bk)[:, (c % 2) * 512 + j * 128:(c % 2) * 512 + (j + 1) * 128], XN[j][:, c * 128:(c + 1) * 128], ident_bf[:]),
                            reads=[b_XN[j], b_const], writes=[pbuf[bk]], mark=(c % 2 == 1))
                for c in range(NCH):
                    bk = base + c // 2
                    S.op("act", lambda e, bk=bk, c=c, g=g: e.activation(out=hT[:, c, g * 512:(g + 1) * 512], in_=bank_bf(bk)[:, (c % 2) * 512:(c % 2) * 512 + 512],
                                                                        func=AF.Identity, scale=sc1p(c), bias=sh1(c)),
                         reads=[pbuf[bk], b_const], writes=[b_hT])
            if dbg:
                S.dma("sp", lambda e: e.dma_start(out=dbg_out["hT"][:, :, tok0:tok0 + SL], in_=hT[:, :, 0:SL]), ds_dbg, reads=[b_hT], writes=[b_dbg])
            S.barrier()
            if stop == "p1":
                break

            al = Alloc(HOFF)
            WP = [al.get([128, 2, NCH, 128], BF16) for _ in range(2)]
            LW = [al.get([128, 8, 128], BF16) for _ in range(2)]
            b_WP = [Buf("WP0"), Buf("WP1")]
            ds_WP = [dsem("WP0"), dsem("WP1")]
            XR = al.get([128, SL + 4], BF16)
            G_ = al.get([128, SL], BF16)
            XC = al.get([128, SL], F32)
            XCb = al.get([128, SL], BF16)
            HF = al.get([128, SL], F32)
            LB = min(SL, 2048) if SL <= 2048 else 1024
            if os.environ.get("KLB"):
                LB = int(os.environ["KLB"])
            NBLK = SL // LB
            TPB = LB // 512
            HBK = al.get([128, LB], F32)
            REC = al.get([128, SL], BF16)
            AA = al.get([128, LB], F32)
            QQ = al.get([128, LB], F32)
            DD = al.get([128, LB], F32)
            CAR = al.get([128, 2], F32)
            TMP = [[al.get([128, 512], F32) for _ in range(2)] for _ in range(2)]
            b_XR = [Buf(f"XR{t}") for t in range(NT)]
            b_G = [Buf(f"G{t}") for t in range(NT)]
            b_XC = [Buf(f"XC{t}") for t in range(NT)]
            b_XCb = [Buf(f"XCb{t}") for t in range(NT)]
            b_HF = [Buf(f"HF{t}") for t in range(NT)]
            b_HBK = Buf("HBK")
            b_REC = Buf("REC")
            b_AA = Buf("AA")
            b_QQ = Buf("QQ")
            b_DD = Buf("DD")
            b_CAR = Buf("CAR")
            b_TMP = [[Buf(f"TMP{i}{j}") for j in range(2)] for i in range(2)]
            ds_rec = dsem("rec")
            b_recd = Buf("rec_d")
            b_pad = Buf("XRpad")
            S.op("pool", lambda e: e.memset(XR[:, 0:2], 0.0), writes=[b_pad])
            S.op("pool", lambda e: e.memset(XR[:, SL + 2:SL + 4], 0.0), writes=[b_pad])

            def load_lru_panels(c):
                sl = c % 2
                S.dma("sp", lambda e: e.dma_start(out=WP[sl][:, 0].rearrange("p k n -> p (k n)"), in_=w_in_bf[(24 + c) * 128:(25 + c) * 128, :]),
                      ds_WP[sl], reads=[b_wbf], writes=[b_WP[sl]])
                S.dma("sp", lambda e: e.dma_start(out=WP[sl][:, 1].rearrange("p k n -> p (k n)"), in_=w_in_bf[(32 + c) * 128:(33 + c) * 128, :]),
                      ds_WP[sl], reads=[b_wbf], writes=[b_WP[sl]])
                S.dma("sp", lambda e: e.dma_start(out=LW[sl], in_=lruw_d[:, c]), ds_WP[sl], reads=[b_lruw], writes=[b_WP[sl]])

            load_lru_panels(0)
            for c in range(NCH):
                if c + 1 < NCH:
                    load_lru_panels(c + 1)
                sl = c % 2
                for t in range(NT):
                    bk = nextbank()
                    mm_group(bank(bk), pbuf[bk], [(WP[sl][:, 0, kc, :], hT[:, kc, t * 512:(t + 1) * 512]) for kc in range(NCH)], [b_WP[sl], b_hT])
                    S.op("act", lambda e, bk=bk, t=t: e.activation(out=XR[:, 2 + t * 512:2 + (t + 1) * 512], in_=bank(bk), func=AF.Identity),
                         reads=[pbuf[bk]], writes=[b_XR[t]])
                    bk = nextbank()
                    mm_group(bank(bk), pbuf[bk], [(WP[sl][:, 1, kc, :], hT[:, kc, t * 512:(t + 1) * 512]) for kc in range(NCH)], [b_WP[sl], b_hT])
                    S.op("act", lambda e, bk=bk, t=t: e.activation(out=G_[:, t * 512:(t + 1) * 512], in_=bank(bk), func=AF.Gelu_apprx_tanh),
                         reads=[pbuf[bk]], writes=[b_G[t]])
                for t in range(NT):
                    bk = nextbank()
                    rd = [b_WP[sl], b_pad, b_XR[t]] + ([b_XR[t - 1]] if t > 0 else []) + ([b_XR[t + 1]] if t + 1 < NT else [])
                    mm_group(bank(bk), pbuf[bk], [(LW[sl][:, 4 + k, :], XR[:, t * 512 + k:t * 512 + k + 512]) for k in range(4)], rd)
                    S.op("act", lambda e, bk=bk, t=t, c=c: e.activation(out=XC[:, t * 512:(t + 1) * 512], in_=bank(bk), func=AF.Identity,
                                                                        bias=fmc[:, 4, c:c + 1], scale=1.0),
                         reads=[pbuf[bk], b_const], writes=[b_XC[t]])
                    S.op("pool", lambda e, t=t: e.tensor_copy(out=XCb[:, t * 512:(t + 1) * 512], in_=XC[:, t * 512:(t + 1) * 512]),
                         reads=[b_XC[t]], writes=[b_XCb[t]])
                it = 0
                for d_ in range(2):
                    blocks = range(NBLK) if d_ == 0 else range(NBLK - 1, -1, -1)
                    for bi in blocks:
                        bsl = slice(bi * LB, (bi + 1) * LB)
                        for tl in range(TPB):
                            t = bi * TPB + tl
                            tm = TMP[it % 2]
                            btm = b_TMP[it % 2]
                            it += 1
                            TR, TI = tm
                            tsl = slice(t * 512, (t + 1) * 512)
                            lsl = slice(tl * 512, (tl + 1) * 512)
                            bkr = nextbank()
                            mm_group(bank(bkr), pbuf[bkr], [(LW[sl][:, 2 * d_, :], XCb[:, tsl])], [b_WP[sl], b_XCb[t]])
                            bki = nextbank()
                            mm_group(bank(bki), pbuf[bki], [(LW[sl][:, 2 * d_ + 1, :], XCb[:, tsl])], [b_WP[sl], b_XCb[t]])
                            S.op("act", lambda e: e.activation(out=TR, in_=bank(bkr), func=AF.Tanh, bias=hbias[:, 2 * d_, c:c + 1], scale=0.5),
                                 reads=[pbuf[bkr], b_const], writes=[btm[0]])
                            S.op("act", lambda e: e.activation(out=TI, in_=bank(bki), func=AF.Tanh, bias=hbias[:, 2 * d_ + 1, c:c + 1], scale=0.5),
                                 reads=[pbuf[bki], b_const], writes=[btm[1]])
                            S.op("act", lambda e: e.activation(out=AA[:, lsl], in_=TR, func=AF.Exp, scale=cdec[:, 2 * d_, c:c + 1], bias=cdec[:, 2 * d_, c:c + 1]),
                                 reads=[btm[0], b_const], writes=[b_AA])
                            S.op("act", lambda e: e.activation(out=QQ[:, lsl], in_=TR, func=AF.Exp, scale=cdec[:, 2 * d_ + 1, c:c + 1], bias=cdec[:, 2 * d_ + 1, c:c + 1]),
                                 reads=[btm[0], b_const], writes=[b_QQ])
                            S.op("dve", lambda e: e.scalar_tensor_tensor(out=DD[:, lsl], in0=TI, scalar=1.0, in1=XC[:, tsl], op0=ALU.add, op1=ALU.mult),
                                 reads=[btm[1], b_XC[t]], writes=[b_DD])
                        S.op("act", lambda e: e.activation(out=QQ, in_=QQ, func=AF.Sqrt, bias=0.25, scale=-0.25), reads=[b_QQ], writes=[b_QQ])
                        S.op("dve", lambda e: e.tensor_tensor(out=DD, in0=DD, in1=QQ, op=ALU.mult), reads=[b_DD, b_QQ], writes=[b_DD])
                        if d_ == 0:
                            init = 0.0 if bi == 0 else HF[:, bi * LB - 1:bi * LB]
                            S.op("dve", lambda e: e.tensor_tensor_scan(out=HF[:, bsl], data0=AA, data1=DD, initial=init, op0=ALU.mult, op1=ALU.add),
                                 reads=[b_AA, b_DD] + b_HF, writes=b_HF)
                        else:
                            init = 0.0 if bi == NBLK - 1 else CAR[:, 0:1]
                            S.op("dve", lambda e: e.tensor_tensor_scan(out=HBK[:, ::-1], data0=AA[:, ::-1], data1=DD[:, ::-1], initial=init,
                                                                        op0=ALU.mult, op1=ALU.add),
                                 reads=[b_AA, b_DD, b_CAR], writes=[b_HBK])
                            if bi > 0:
                                S.op("dve", lambda e: e.tensor_copy(out=CAR[:, 0:1], in_=HBK[:, 0:1]), reads=[b_HBK], writes=[b_CAR])
                            S.op("pool", lambda e: e.tensor_tensor(out=HBK, in0=HBK, in1=HF[:, bsl], op=ALU.add),
                                 reads=[b_HBK] + b_HF, writes=[b_HBK])
                            S.op("pool", lambda e: e.tensor_tensor(out=REC[:, bsl], in0=HBK, in1=G_[:, bsl], op=ALU.mult),
                                 reads=[b_HBK] + b_G, writes=[b_REC])
                S.dma("sp", lambda e, c=c: e.dma_start(out=rec_d[c, :, tok0:tok0 + SL], in_=REC), ds_rec, reads=[b_REC], writes=[b_recd])
                if dbg:
                    S.dma("sp", lambda e, c=c: e.dma_start(out=dbg_out["rec"][c, :, tok0:tok0 + SL], in_=REC), ds_dbg, reads=[b_REC], writes=[b_dbg])
            S.barrier()
            if stop == "p2":
                break

            NQB = max(1, SL // QBX)
            QBL = min(QBX, SL)
            BIG = SL > BIGTH
            ds_hsp = dsem("hsp")
            b_hTd = Buf("hT_d")
            if BIG:
                S.dma("sp", lambda e: e.dma_start(out=hT_d[:, :, 0:SL], in_=hT), ds_hsp, reads=[b_hT], writes=[b_hTd])
            for qb in range(NQB):
                q0 = qb * QBL
                al = Alloc(HOFF)
                attnT = al.get([128, NCH, QBL], BF16)
                b_attnT = Buf("attnT")
                p4_off = al.off
                QT = al.get([128, QBL], BF16)
                KT = al.get([128, SL], BF16)
                NKT = SL // 128
                V_ = al.get([128, NKT, 130], BF16)
                WQ = [al.get([128, 3, NCH, 128], BF16) for _ in range(2)]
                b_WQ = [Buf("WQ0"), Buf("WQ1")]
                ds_WQ = [dsem("WQ0"), dsem("WQ1")]
                RC = [al.get([128, 2, 512], BF16) for _ in range(4)]
                b_RC = [Buf(f"RC{i}") for i in range(4)]
                ds_RC = [dsem(f"RC{i}") for i in range(4)]
                QBF = [al.get([128, 512], BF16) for _ in range(2)]
                b_QBF = [Buf("QBF0"), Buf("QBF1")]
                T12 = [[al.get([128, 512], F32) for _ in range(2)] for _ in range(2)]
                b_T12 = [[Buf("T1"), Buf("T2")] for _ in range(2)]
                PT = [al.get([128, 2, 512], BF16) for _ in range(3)]
                b_PT = [Buf(f"PT{i}") for i in range(3)]
                NQG = QBL // 512
                ACCS = al.get([128, 8, 130], F32)
                b_ACCS = Buf("ACCS")
                O0 = al.get([128, 128], F32)
                b_O0 = Buf("O0")
                OO = [al.get([128, 128], F32) for _ in range(4 * NQG)]
                b_OO = [Buf(f"OO{i}") for i in range(4 * NQG)]
                AT = [al.get([128, 128], BF16) for _ in range(4)]
                b_AT = [Buf(f"AT{i}") for i in range(4)]
                nrm = al.get([128, 16], F32)
                b_nrm = Buf("nrm")
                st6n = al.get([128, 6], F32)
                mvn = al.get([128, 2], F32)
                b_stn = Buf("stn")
                msa = al.get([128, 4 * NQG], F32)
                b_msa = Buf("msa")
                b_QT = Buf("QT")
                b_KT = Buf("KT")
                b_V = Buf("V")
                S.op("pool", lambda e: e.memset(V_[:, :, 128:130], 1.0), writes=[b_V])

                def load_head_w(h):
                    sl = h % 2
                    for i in range(3):
                        S.dma("sp", lambda e, i=i: e.dma_start(out=WQ[sl][:, i].rearrange("p k n -> p (k n)"), in_=w_in_bf[(i * 8 + h) * 128:(i * 8 + h + 1) * 128, :]),
                              ds_WQ[sl], reads=[b_wbf], writes=[b_WQ[sl]])

                rc_i = [0]

                def rope_proj(h, which, t_tok, dst_ap, b_dst):
                    sl = h % 2
                    r = rc_i[0] % 2
                    r4 = rc_i[0] % 4
                    rc_i[0] += 1
                    S.dma("sp", lambda e: e.dma_start(out=RC[r4][:, 0, :], in_=rope_bf[0][:, t_tok:t_tok + 512]), ds_RC[r4], reads=[b_wbf], writes=[b_RC[r4]])
                    S.dma("sp", lambda e: e.dma_start(out=RC[r4][:, 1, :], in_=rope_bf[1][:, t_tok:t_tok + 512]), ds_RC[r4], reads=[b_wbf], writes=[b_RC[r4]])
                    bka = nextbank(0, 5)
                    mm_group(bank(bka), pbuf[bka], [(WQ[sl][:, which, kc, :], hT[:, kc, t_tok:t_tok + 512]) for kc in range(NCH)], [b_WQ[sl], b_hT])
                    S.op("act", lambda e: e.activation(out=QBF[r], in_=bank(bka), func=AF.Identity), reads=[pbuf[bka]], writes=[b_QBF[r]])
                    t1, t2 = T12[r]
                    S.op("dve", lambda e: e.tensor_tensor(out=t1, in0=bank(bka), in1=RC[r4][:, 0, :], op=ALU.mult),
                         reads=[pbuf[bka], b_RC[r4]], writes=[b_T12[r][0]])

                    def part2():
                        bkb = nextbank(0, 5)
                        mm_group(bank(bkb), pbuf[bkb], [(prot_bf[:], QBF[r])], [b_const, b_QBF[r]])
                        S.op("dve", lambda e: e.tensor_tensor(out=t2, in0=bank(bkb), in1=RC[r4][:, 1, :], op=ALU.mult),
                             reads=[pbuf[bkb], b_RC[r4]], writes=[b_T12[r][1]])
                        S.op("pool", lambda e: e.tensor_tensor(out=dst_ap, in0=t1, in1=t2, op=ALU.add),
                             reads=[b_T12[r][0], b_T12[r][1]], writes=[b_dst])
                    return part2

                def acc_ap(a, lo=0, hi=129):
                    bk = 5 + a // 3
                    o = (a % 3) * 130
                    return ps[:, bk * 512 + o + lo: bk * 512 + o + hi]

                load_head_w(0)
                pending_tail = []
                pending_tail2 = []
                for h in range(NCH):
                    if h + 1 < NCH:
                        load_head_w(h + 1)
                    sl = h % 2
                    jobs = [(1, t * 512, KT[:, t * 512:(t + 1) * 512], b_KT) for t in range(NT)]
                    jobs += [(0, q0 + t * 512, QT[:, t * 512:(t + 1) * 512], b_QT) for t in range(QBL // 512)]
                    vper = -(-NKT // len(jobs))
                    vi = 0
                    prev = None
                    for (which, t_tok, dst_ap, b_dst) in jobs:
                        p2 = rope_proj(h, which, t_tok, dst_ap, b_dst)
                        if prev is not None:
                            prev()
                        prev = p2
                        for _ in range(vper):
                            if vi < NKT:
                                kt = vi
                                vi += 1
                                bk = nextbank(0, 5)
                                mm_group(bank(bk)[:, 0:128], pbuf[bk], [(hT[:, kc, kt * 128:(kt + 1) * 128], WQ[sl][:, 2, kc, :]) for kc in range(NCH)], [b_WQ[sl], b_hT])
                                S.op("dve", lambda e, bk=bk, kt=kt: e.tensor_copy(out=V_[:, kt, 0:128], in_=bank(bk)[:, 0:128]),
                                     reads=[pbuf[bk]], writes=[b_V])
                    prev()
                    assert vi == NKT
                    while pending_tail:
                        pending_tail.pop(0)()
                    for qg in range(QBL // 512):
                        def emit_qk_exp(kt, qg=qg):
                            pr = kt % 2
                            stv = ps[:, pr * 1024:(pr + 1) * 1024].rearrange("p (a b) -> p a b", a=2)
                            bst = [pbuf[2 * pr], pbuf[2 * pr + 1]]
                            S.op("pe", lambda e: e.matmul(stv[:, 0, :], lhsT=KT[0:64, kt * 128:(kt + 1) * 128], rhs=QT[0:64, qg * 512:(qg + 1) * 512],
                                                          start=True, stop=True), reads=[b_KT, b_QT], writes=[bst[0]], mark=False)
                            S.op("pe", lambda e: e.matmul(stv[:, 1, :], lhsT=KT[64:128, kt * 128:(kt + 1) * 128], rhs=QT[64:128, qg * 512:(qg + 1) * 512],
                                                          start=True, stop=True), reads=[b_KT, b_QT], writes=[bst[1]], mark=True)
                            pi = kt % 3
                            S.op("act", lambda e: e.activation(out=PT[pi], in_=stv, func=AF.Exp, scale=0.125), reads=bst, writes=[b_PT[pi]])

                        def emit_pv(kt):
                            pi = kt % 3
                            for a in range(8):
                                cm, qs = a // 4, a % 4
                                S.op("pe", lambda e, a=a, cm=cm, qs=qs: e.matmul(acc_ap(a), lhsT=PT[pi][:, cm, qs * 128:(qs + 1) * 128], rhs=V_[:, kt, 0:129],
                                                                                 start=(kt == 0 and a % 3 == 0), stop=(kt == NKT - 1), skip_group_check=True),
                                     reads=[b_PT[pi], b_V], writes=[pbuf[5 + a // 3]], mark=(a == 7))

                        emit_qk_exp(0)
                        for kt in range(NKT):
                            if kt + 1 < NKT:
                                emit_qk_exp(kt + 1)
                            emit_pv(kt)
                            if kt % 3 == 2 and pending_tail2:
                                pending_tail2.pop(0)()
                        for bkk in range(3):
                            n_ = 3 if bkk < 2 else 2
                            src = ps[:, (5 + bkk) * 512:(5 + bkk) * 512 + n_ * 130].rearrange("p (a b) -> p a b", a=n_)
                            S.op("dve", lambda e, bkk=bkk, n_=n_, src=src: e.tensor_copy(out=ACCS[:, 3 * bkk:3 * bkk + n_, :], in_=src),
                                 reads=[pbuf[5 + bkk]], writes=[b_ACCS])
                        S.op("dve", lambda e: e.reciprocal(out=nrm[:, 0:8], in_=ACCS[:, :, 128]), reads=[b_ACCS], writes=[b_nrm])
                        S.op("dve", lambda e: e.tensor_scalar(out=nrm[:, 8:12], in0=nrm[:, 4:8], scalar1=lamt[:, 4:5], scalar2=None, op0=ALU.mult),
                             reads=[b_nrm, b_const], writes=[b_nrm])
                        for qs in range(4):
                            oi = qg * 4 + qs
                            S.op("dve", lambda e, qs=qs: e.tensor_scalar(out=O0, in0=ACCS[:, qs, 0:128], scalar1=nrm[:, qs:qs + 1], scalar2=None, op0=ALU.mult),
                                 reads=[b_ACCS, b_nrm], writes=[b_O0])
                            S.op("dve", lambda e, qs=qs, oi=oi: e.scalar_tensor_tensor(out=OO[oi], in0=ACCS[:, 4 + qs, 0:128], scalar=nrm[:, 8 + qs:9 + qs], in1=O0,
                                                                                      op0=ALU.mult, op1=ALU.add),
                                 reads=[b_ACCS, b_nrm, b_O0], writes=[b_OO[oi]])
                            S.op("dve", lambda e, oi=oi: e.bn_stats(out=st6n, in_=OO[oi]), reads=[b_OO[oi]], writes=[b_stn])
                            S.op("dve", lambda e: e.bn_aggr(out=mvn, in_=st6n), reads=[b_stn], writes=[b_stn])
                            S.op("dve", lambda e, oi=oi: e.tensor_scalar(out=msa[:, oi:oi + 1], in0=mvn[:, 0:1], scalar1=mvn[:, 0:1], scalar2=mvn[:, 1:2],
                                                                        op0=ALU.mult, op1=ALU.add),
                                 reads=[b_stn], writes=[b_msa])

                    def head_tail(h=h):
                        S.op("act", lambda e: e.activation(out=msa, in_=msa, func=AF.Sqrt, bias=RMS_EPS, scale=1.0), reads=[b_msa], writes=[b_msa])
                        S.op("dve", lambda e: e.reciprocal(out=msa, in_=msa), reads=[b_msa], writes=[b_msa])
                        def tail_qg(qg2):
                            for qs in range(4):
                                oi = qg2 * 4 + qs
                                S.op("dve", lambda e, qs=qs, oi=oi: e.scalar_tensor_tensor(out=AT[qs], in0=OO[oi], scalar=msa[:, oi:oi + 1], in1=gsub[:],
                                                                                          op0=ALU.mult, op1=ALU.mult),
                                     reads=[b_OO[oi], b_msa, b_const], writes=[b_AT[qs]])
                                S.op("pe", lambda e, qs=qs: e.transpose(bank_bf(4)[:, qs * 128:(qs + 1) * 128], AT[qs], ident_bf[:]),
                                     reads=[b_AT[qs], b_const], writes=[pbuf[4]], mark=(qs == 3))
                            S.op("dve", lambda e, qg2=qg2: e.tensor_copy(out=attnT[:, h, qg2 * 512:(qg2 + 1) * 512], in_=bank_bf(4)[:, 0:512]),
                                 reads=[pbuf[4]], writes=[b_attnT])
                        for qg2 in range(NQG):
                            pending_tail2.append(lambda qg2=qg2: tail_qg(qg2))

                    pending_tail.append(head_tail)
                while pending_tail:
                    pending_tail.pop(0)()
                while pending_tail2:
                    pending_tail2.pop(0)()
                if dbg:
                    S.dma("sp", lambda e: e.dma_start(out=dbg_out["attnT"][:, :, tok0 + q0:tok0 + q0 + QBL], in_=attnT), ds_dbg, reads=[b_attnT], writes=[b_dbg])
                S.barrier()
                if stop == "p3":
                    break

                T = 512
                NST = T // 128
                al = Alloc(0 if BIG else p4_off)
                LNP = al.get([128, 4, D], F32)
                b_LNP = Buf("LNP", const=False)
                ds_lnp = dsem("lnp")
                for i, v in enumerate([ln1g_d, ln1b_d, ln2g_d, ln2b_d]):
                    S.dma("sp", lambda e, i=i, v=v: e.dma_start(out=LNP[:, i, :], in_=v.broadcast_to([128, D])), ds_lnp, writes=[b_LNP])
                X1 = al.get([128, NST, D], F32)
                b_X1 = [Buf(f"X1_{i}") for i in range(NST)]
                ds_X1 = [dsem(f"x1_{i}") for i in range(NST)]
                PAN = [al.get([128, 4096], BF16) for _ in range(3)]
                b_PAN = [Buf(f"PAN{i}") for i in range(3)]
                ds_PAN = [dsem(f"pan{i}") for i in range(3)]
                if BIG:
                    assert al.off <= HOFF, al.off
                    al.off = p4_off
                    HTT = al.get([128, NCH, T], BF16)
                    b_HTT = Buf("HTT")
                    ds_HTT = dsem("htt")
                TT_ = [al.get([128, T], F32) for _ in range(2)]
                b_TT = [Buf("TT0"), Buf("TT1")]
                XN2 = [al.get([128, D], BF16) for _ in range(NST)]
                b_XN2 = [Buf(f"XN2{i}") for i in range(NST)]
                STS = [(al.get([128, 2, 6], F32), al.get([128, 2], F32), al.get([128, 2], F32), Buf(f"sts{i}")) for i in range(NST)]
                ov = al.off
                MG = al.get([128, NCH, T], BF16)
                RT = al.get([128, NCH, T], BF16)
                SG = [al.get([128, T], F32) for _ in range(2)]
                AB = [al.get([128, T], F32) for _ in range(2)]
                e_end = al.off
                al.off = ov
                H2 = al.get([128, NCH, T], BF16)
                UT = al.get([128, NFF, T], BF16)
                SU = [al.get([128, T], F32) for _ in range(2)]
                b_MG = Buf("MG")
                b_RT = Buf("RT")
                ds_RT = dsem("rt")
                b_SG = [Buf("SG0"), Buf("SG1")]
                b_AB = [Buf("AB0"), Buf("AB1")]
                b_H2 = Buf("H2")
                b_UT = Buf("UT")
                b_SU = [Buf("SU0"), Buf("SU1")]
                b_yd = Buf("y_d")
                ds_y = [dsem(f"y{i}") for i in range(4)]

                NG = QBL // T
                panels = []
                for g in range(NG):
                    for oc in range(NCH):
                        panels.append(("a", g, oc))
                    for oc in range(NCH):
                        panels.append(("b", g, oc))
                    for j in range(NFF):
                        panels.append(("d", g, j))
                    for oc in range(NCH):
                        panels.append(("e", g, oc))
                pstate4 = {"issued": 0}

                def issue_panel(i):
                    kind, g, k = panels[i]
                    sl = i % 3
                    pv = PAN[sl]
                    if kind == "a":
                        v4 = pv.rearrange("p (a k n) -> p a k n", a=4, k=NCH)
                        srcs = [w_in_bf[(40 + k) * 128:(41 + k) * 128, :], w_in_bf[(48 + k) * 128:(49 + k) * 128, :],
                                w_ab_bf[k * 128:(k + 1) * 128, :], w_lb_bf[k * 128:(k + 1) * 128, :]]
                        for a_, s_ in enumerate(srcs):
                            S.dma("sp", lambda e, a_=a_, s_=s_, v4=v4: e.dma_start(out=v4[:, a_].rearrange("p k n -> p (k n)"), in_=s_),
                                  ds_PAN[sl], reads=[b_wbf], writes=[b_PAN[sl]])
                    elif kind == "b":
                        v3 = pv[:, 0:NCH * 128].rearrange("p (k n) -> p k n", k=NCH)
                        S.dma("sp", lambda e, v3=v3, k=k: e.dma_start(out=v3.rearrange("p k n -> p (k n)"), in_=w_out_bf[k * 128:(k + 1) * 128, :]),
                              ds_PAN[sl], reads=[b_wbf], writes=[b_PAN[sl]])
                    elif kind == "d":
                        v4 = pv[:, 0:2 * NCH * 128].rearrange("p (a k n) -> p a k n", a=2, k=NCH)
                        for a_ in range(2):
                            S.dma("sp", lambda e, a_=a_, v4=v4, k=k: e.dma_start(out=v4[:, a_].rearrange("p k n -> p (k n)"), in_=w_fi_bf[(a_ * NFF + k) * 128:(a_ * NFF + k + 1) * 128, :]),
                                  ds_PAN[sl], reads=[b_wbf], writes=[b_PAN[sl]])
                    else:
                        v3 = pv[:, 0:NFF * 128].rearrange("p (k n) -> p k n", k=NFF)
                        S.dma("sp", lambda e, v3=v3, k=k: e.dma_start(out=v3.rearrange("p k n -> p (k n)"), in_=w_fo_bf[k * 128:(k + 1) * 128, :]),
                              ds_PAN[sl], reads=[b_wbf], writes=[b_PAN[sl]])

                def get_panel():
                    i = pstate4["cur"]
                    while pstate4["issued"] < min(len(panels), i + 3):
                        issue_panel(pstate4["issued"])
                        pstate4["issued"] += 1
                    pstate4["cur"] = i + 1
                    return PAN[i % 3], b_PAN[i % 3]

                pstate4["cur"] = 0

                def residual_block(g, pv3, nk, rhs_fn, rhs_bufs, b_pan, gcol, tti, oc):
                    bk = nextbank()
                    mm_group(bank(bk)[:, 0:T], pbuf[bk], [(pv3[:, kc, :], rhs_fn(kc)) for kc in range(nk)], [b_pan] + rhs_bufs)
                    tt = TT_[tti % 2]
                    btt = b_TT[tti % 2]
                    S.op("act", lambda e: e.activation(out=tt, in_=bank(bk)[:, 0:T], func=AF.Identity, scale=gcol),
                         reads=[pbuf[bk], b_const], writes=[btt])

                    def part2():
                        bk2 = nextbank()
                        for s_ in range(NST):
                            S.op("pe", lambda e, s_=s_: e.transpose(bank(bk2)[:, s_ * 128:(s_ + 1) * 128], tt[:, s_ * 128:(s_ + 1) * 128], ident_f[:]),
                                 reads=[btt, b_const], writes=[pbuf[bk2]], mark=(s_ == NST - 1))
                        S.op("dve", lambda e: e.scalar_tensor_tensor(
                            out=X1[:, :, oc * 128:(oc + 1) * 128], in0=X1[:, :, oc * 128:(oc + 1) * 128], scalar=ALPHA,
                            in1=bank(bk2)[:, 0:T].rearrange("p (s n) -> p s n", s=NST), op0=ALU.mult, op1=ALU.add),
                            reads=[pbuf[bk2]] + b_X1, writes=b_X1)
                    return part2

                for g in range(NG):
                    gt0 = q0 + g * T
                    gl0 = g * T
                    S.fence(["sp", "act", "dve", "pool"], [b_H2, b_UT, b_SU[0], b_SU[1]])
                    for s_ in range(NST):
                        S.dma("pool", lambda e, s_=s_: e.dma_start(out=X1[:, s_, :], in_=x_d[tok0 + gt0 + s_ * 128:tok0 + gt0 + (s_ + 1) * 128, :]),
                              ds_X1[s_], writes=[b_X1[s_]])
                    S.dma("sp", lambda e: e.dma_start(out=RT, in_=rec_d[:, :, tok0 + gt0:tok0 + gt0 + T].rearrange("c p t -> p c t")),
                          ds_RT, reads=[b_recd], writes=[b_RT])
                    if BIG:
                        S.dma("sp", lambda e: e.dma_start(out=HTT, in_=hT_d[:, :, gt0:gt0 + T]), ds_HTT, reads=[b_hTd], writes=[b_HTT])
                        hsrc = lambda kc: HTT[:, kc, :]
                        b_hsrc = b_HTT
                    else:
                        hsrc = lambda kc: hT[:, kc, gt0:gt0 + T]
                        b_hsrc = b_hT
                    for oc in range(NCH):
                        pv, bp = get_panel()
                        v4 = pv.rearrange("p (a k n) -> p a k n", a=4, k=NCH)
                        bks = []
                        order = [(0, hsrc, b_hsrc), (2, lambda kc: attnT[:, kc, gl0:gl0 + T], b_attnT),
                                 (1, hsrc, b_hsrc), (3, lambda kc: RT[:, kc, :], b_RT)]
                        for a_, rf, rb in order:
                            bk = nextbank()
                            mm_group(bank(bk)[:, 0:T], pbuf[bk], [(v4[:, a_, kc, :], rf(kc)) for kc in range(NCH)], [bp, rb])
                            bks.append(bk)
                        for half in range(2):
                            sg = SG[half]
                            ab = AB[half]
                            bg, bb = bks[2 * half], bks[2 * half + 1]
                            S.op("act", lambda e, sg=sg, bg=bg: e.activation(out=sg, in_=bank(bg)[:, 0:T], func=AF.Sigmoid),
                                 reads=[pbuf[bg]], writes=[b_SG[half]])
                            S.op("dve", lambda e, sg=sg, ab=ab, bb=bb: e.tensor_tensor(out=ab, in0=bank(bb)[:, 0:T], in1=sg, op=ALU.mult),
                                 reads=[pbuf[bb], b_SG[half]], writes=[b_AB[half]])
                        S.op("pool", lambda e, oc=oc: e.tensor_tensor(out=MG[:, oc, :], in0=AB[0], in1=AB[1], op=ALU.add),
                             reads=[b_AB[0], b_AB[1]], writes=[b_MG])
                    prev2 = None
                    for oc in range(NCH):
                        pv, bp = get_panel()
                        v3 = pv[:, 0:NCH * 128].rearrange("p (k n) -> p k n", k=NCH)
                        p2 = residual_block(g, v3, NCH, lambda kc: MG[:, kc, :], [b_MG], bp, g1c(oc), oc, oc)
                        if prev2 is not None:
                            prev2()
                        prev2 = p2
                    prev2()
                    S.fence(["act", "dve"], [b_MG, b_RT, b_SG[0], b_SG[1], b_AB[0], b_AB[1]])
                    bks_c = [nextbank() for _ in range(4)]
                    items = [(X1[:, s_, :], b_X1[s_]) + STS[s_] for s_ in range(NST)]
                    ln_affine_multi(items, 0, 1, LNP, b_LNP)
                    ln_stats_multi(items)
                    for s_ in range(NST):
                        xs_ap, _, st6, mv, rs, b_st = items[s_]
                        xn = XN2[s_]
                        S.op("act", lambda e: e.activation(out=xn, in_=xs_ap, func=AF.Identity, scale=rs[:, 0:1], bias=rs[:, 1:2]),
                             reads=[b_X1[s_], b_st], writes=[b_XN2[s_]])
                    for s_ in range(NST):
                        xn = XN2[s_]
                        for c in range(NCH):
                            bk = bks_c[c // 2]
                            S.op("pe", lambda e, bk=bk, c=c: e.transpose(
                                bank_bf(bk)[:, (c % 2) * 512 + s_ * 128:(c % 2) * 512 + (s_ + 1) * 128], xn[:, c * 128:(c + 1) * 128], ident_bf[:]),
                                reads=[b_XN2[s_], b_const], writes=[pbuf[bk]], mark=(c % 2 == 1))
                    for c in range(NCH):
                        bk = bks_c[c // 2]
                        S.op("act", lambda e, bk=bk, c=c: e.activation(out=H2[:, c, :], in_=bank_bf(bk)[:, (c % 2) * 512:(c % 2) * 512 + T],
                                                                       func=AF.Identity, scale=sc2p(c), bias=sh2(c)),
                             reads=[pbuf[bk], b_const], writes=[b_H2])
                    for j in range(NFF):
                        pv, bp = get_panel()
                        v4 = pv[:, 0:2 * NCH * 128].rearrange("p (a k n) -> p a k n", a=2, k=NCH)
                        bkg = nextbank()
                        mm_group(bank(bkg)[:, 0:T], pbuf[bkg], [(v4[:, 0, kc, :], H2[:, kc, :]) for kc in range(NCH)], [bp, b_H2])
                        bku = nextbank()
                        mm_group(bank(bku)[:, 0:T], pbuf[bku], [(v4[:, 1, kc, :], H2[:, kc, :]) for kc in range(NCH)], [bp, b_H2])
                        su = SU[j % 2]
                        S.op("act", lambda e, su=su, bkg=bkg: e.activation(out=su, in_=bank(bkg)[:, 0:T], func=AF.Silu),
                             reads=[pbuf[bkg]], writes=[b_SU[j % 2]])
                        S.op("dve", lambda e, su=su, bku=bku, j=j: e.tensor_tensor(out=UT[:, j, :], in0=bank(bku)[:, 0:T], in1=su, op=ALU.mult),
                             reads=[pbuf[bku], b_SU[j % 2]], writes=[b_UT])
                    prev2 = None
                    for oc in range(NCH):
                        pv, bp = get_panel()
                        v3 = pv[:, 0:NFF * 128].rearrange("p (k n) -> p k n", k=NFF)
                        p2 = residual_block(g, v3, NFF, lambda kc: UT[:, kc, :], [b_UT], bp, g2c(oc), oc, oc)
                        if prev2 is not None:
                            prev2()
                        prev2 = p2
                    prev2()
                    items = [(X1[:, s_, :], b_X1[s_]) + STS[s_] for s_ in range(NST)]
                    ln_affine_multi(items, 2, 3, LNP, b_LNP)
                    for s_ in range(NST):
                        S.dma("pool", lambda e, s_=s_: e.dma_start(out=y_d[tok0 + gt0 + s_ * 128:tok0 + gt0 + (s_ + 1) * 128, :], in_=X1[:, s_, :]),
                              ds_y[s_], reads=[b_X1[s_]], writes=[b_yd])
                S.barrier()
                if BIG and qb + 1 < NQB:
                    S.dma("sp", lambda e: e.dma_start(out=hT, in_=hT_d[:, :, 0:SL]), ds_hsp, reads=[b_hTd], writes=[b_hT])
        S.barrier()
        S.emit()
    return nc


def _rope_tables(smax):
    inv = (1.0 / (np.float32(10000.0) ** (np.arange(0, 64, 2, dtype=np.float32) / np.float32(64)))).astype(np.float32)
    ang = (np.arange(smax, dtype=np.float32)[:, None] * inv[None, :]).astype(np.float32)
    cos = np.cos(ang).astype(np.float32)
    sin = np.sin(ang).astype(np.float32)
    c = np.zeros((128, smax), np.float32)
    s = np.zeros((128, smax), np.float32)
    for p in range(128):
        d = p % 64
        j = d % 32
        c[p] = cos[:, j]
        s[p] = (-sin[:, j]) if d < 32 else sin[:, j]
    return c, s


def _consts(smax):
    ident = np.eye(128, dtype=np.float32)
    prot = np.zeros((128, 128), np.float32)
    for m in range(128):
        blk = (m // 64) * 64
        d = m % 64
        k = blk + ((d + 32) % 64)
        prot[k, m] = 1.0
    c, s = _rope_tables(smax)
    return ident, prot, c, s


def make_in_maps(inputs, n_cores, seq_plan):
    f = lambda a: np.ascontiguousarray(np.asarray(a, dtype=np.float32))
    def pan(a, nk):
        a = f(a)
        ncb = a.shape[1] // 128
        return np.ascontiguousarray(a.reshape(nk, 128, ncb, 128).transpose(2, 1, 0, 3).reshape(ncb * 128, nk * 128))

    w = {
        "w_ada": f(inputs["w_ada"][0]), "b_ada": f(inputs["b_ada"][0]).reshape(1, -1), "w_in": pan(inputs["w_in"][0], 8),
        "lq1": f(inputs["lambda_q1"][0]).reshape(1, 64), "lk1": f(inputs["lambda_k1"][0]).reshape(1, 64),
        "lq2": f(inputs["lambda_q2"][0]).reshape(1, 64), "lk2": f(inputs["lambda_k2"][0]).reshape(1, 64),
        "subln_g": f(inputs["subln_g"][0]).reshape(1, 128), "conv_w": f(inputs["conv_w"][0]), "conv_b": f(inputs["conv_b"][0]).reshape(1, -1),
        "w_lru_gates": f(inputs["w_lru_gates"][0]).reshape(4, 16, 64, 64), "b_lru_gates": f(inputs["b_lru_gates"][0]).reshape(4, -1),
        "lru_lambda": f(inputs["lru_lambda"][0]), "w_ab": pan(inputs["w_attn_branch"][0], 8), "w_lb": pan(inputs["w_lru_branch"][0], 8),
        "w_out": pan(inputs["w_out"][0], 8), "ln1_g": f(inputs["ln1_g"][0]).reshape(1, -1), "ln1_b": f(inputs["ln1_b"][0]).reshape(1, -1),
        "w_ffn_in": pan(inputs["w_ffn_in"][0], 8), "w_ffn_out": pan(inputs["w_ffn_out"][0], 22),
        "ln2_g": f(inputs["ln2_g"][0]).reshape(1, -1), "ln2_b": f(inputs["ln2_b"][0]).reshape(1, -1),
    }
    maps = []
    for core in range(n_cores):
        plan = seq_plan(core)
        smax = max(p[0].shape[0] for p in plan)
        ident, prot, c, s = _consts(smax)
        m = dict(w)
        m["x"] = np.ascontiguousarray(np.concatenate([p[0] for p in plan], axis=0))
        m["c"] = np.ascontiguousarray(np.stack([p[1] for p in plan], axis=0))
        m["ident"] = ident
        m["prot"] = prot
        m["rope_c"] = c
        m["rope_s"] = s
        maps.append(m)
    return maps


_NC_CACHE = {}


def kernel(**inputs):
    n = 8
    xp = np.asarray(inputs["x_prompt"], dtype=np.float32)
    xs = np.asarray(inputs["x_sample"], dtype=np.float32)
    cp = np.asarray(inputs["c_prompt"], dtype=np.float32)
    cs = np.asarray(inputs["c_sample"], dtype=np.float32)
    B, SP, _ = xp.shape
    DB, SS, _ = xs.shape
    per = DB // n
    seq_lens = [SP] + [SS] * per

    def plan(core):
        return [(xp[core], cp[core])] + [(xs[core * per + i], cs[core * per + i]) for i in range(per)]

    key = tuple(seq_lens)
    if key not in _NC_CACHE:
        _NC_CACHE[key] = build(seq_lens)
    nc = _NC_CACHE[key]
    maps = make_in_maps(inputs, n, plan)
    res = run_bass_kernel_spmd(nc, maps, core_ids=list(range(n)))
    yp = np.empty((B, SP, D), np.float32)
    ys = np.empty((DB, SS, D), np.float32)
    for core in range(n):
        y = np.asarray(res.results[core]["y"], dtype=np.float32)
        yp[core] = y[0:SP]
        ys[core * per:(core + 1) * per] = y[SP:].reshape(per, SS, D)
    return (yp, ys)
```
